# Optimizing a Trainium2 kernel written in Bass

```python
import math
import jax
import jax.numpy as jnp
from jax import lax
import numpy as np

D_MODEL = 1024
BATCH = 32
SEQ = 256
DEPTH = 4
DEC_BATCH = 4
DEC_SEQ = 4096
PAST_LEN = 512

GRID_W = 64
MLA_H = 8
MLA_DN = 64
MLA_DR = 32
MLA_DV = 64
Q_RANK = 256
KV_RANK = 128
ML_H = 4
ML_D = 64
ML_W = 256
HG_H = 4
HG_DK = 64
HG_DV = 64
HG_W = 256
HG_KW = 256
MIX_W = 1024
FF = 2816
CHUNK = 64
Q_BLOCK = 128
ROPE_BASE = 10000.0
ALPHA = (2 * DEPTH) ** 0.25
BETA = (8 * DEPTH) ** -0.25
EPS = 1e-6
MASK_NEG = -1e30
TINY = 1e-30
IN_SIZES = (Q_RANK, KV_RANK, MLA_DR, ML_W, ML_W, ML_W, ML_W, ML_H, ML_H, ML_H, ML_H, HG_W, HG_KW, HG_KW, HG_W, HG_W)
IN_COLS = 2736

kernel_name = 'hybrid_mla_mlstm_hgrn2_diffusion_step'


def layernorm(x, g, b):
    xf = x.astype(jnp.float32)
    mu = xf.mean(-1, keepdims=True)
    var = jnp.mean(jnp.square(xf - mu), -1, keepdims=True)
    return ((xf - mu) * lax.rsqrt(var + EPS) * g.astype(jnp.float32) + b.astype(jnp.float32)).astype(x.dtype)


def rmsnorm(x, g):
    xf = x.astype(jnp.float32)
    return (xf * lax.rsqrt(jnp.mean(xf * xf, -1, keepdims=True) + EPS) * g.astype(jnp.float32)).astype(x.dtype)


def to_heads(x, n_heads):
    B, T, W = x.shape
    return x.reshape(B, T, n_heads, W // n_heads).transpose(0, 2, 1, 3)


def from_heads(x):
    B, H, T, d = x.shape
    return x.transpose(0, 2, 1, 3).reshape(B, T, H * d)


def flip(a):
    return jnp.flip(a, axis=2)


def axial_rope(n):
    n_rows = n // GRID_W
    row = jnp.repeat(jnp.arange(n_rows, dtype=jnp.float32), GRID_W)
    col = jnp.tile(jnp.arange(GRID_W, dtype=jnp.float32), n_rows)
    half = MLA_DR // 2
    freqs = ROPE_BASE ** (-jnp.arange(half // 2, dtype=jnp.float32) * (2.0 / half))
    ar = row[:, None] * freqs
    ac = col[:, None] * freqs
    ang = jnp.concatenate([ar, ar, ac, ac], -1)
    return jnp.cos(ang), jnp.sin(ang)


def apply_rope(x, cos, sin):
    x1, x2, x3, x4 = jnp.split(x, 4, axis=-1)
    rot = jnp.concatenate([-x2, x1, -x4, x3], -1)
    return (x.astype(jnp.float32) * cos + rot.astype(jnp.float32) * sin).astype(x.dtype)


def attend(q_nope, q_pe, k_nope, k_pe, v):
    B, T, H, _ = q_nope.shape
    nb = T // Q_BLOCK
    scale = (MLA_DN + MLA_DR) ** -0.5

    def blocks(a):
        return jnp.moveaxis(a.reshape((B, nb, Q_BLOCK) + a.shape[2:]), 1, 0)

    def one_block(qs):
        qn, qp = qs
        s = (jnp.einsum('bqhd,bkhd->bhqk', qn, k_nope).astype(jnp.float32)
             + jnp.einsum('bqhr,bkr->bhqk', qp, k_pe).astype(jnp.float32))
        p = jax.nn.softmax(s * scale, axis=-1).astype(v.dtype)
        return jnp.einsum('bhqk,bkhd->bqhd', p, v)

    o = lax.map(one_block, (blocks(q_nope), blocks(q_pe)))
    return jnp.moveaxis(o, 0, 1).reshape(B, T, H * MLA_DV)


def mlstm_scan(q, k, v, ig, lf, C0, n0, m0):
    B, H, T, d = q.shape
    nc = T // CHUNK

    def chunks(a):
        return jnp.moveaxis(a.reshape(a.shape[:2] + (nc, CHUNK) + a.shape[3:]), 2, 0)

    causal = jnp.tril(jnp.ones((CHUNK, CHUNK), dtype=bool))

    def step(carry, inp):
        C, n, m = carry
        qc, kc, vc, ic, fc = inp
        b = jnp.cumsum(fc, axis=-1)
        a = b + m[..., None]
        Dm = jnp.where(causal, b[..., :, None] - b[..., None, :] + ic[..., None, :], MASK_NEG)
        mt = jnp.maximum(a, Dm.max(-1))
        ws = jnp.exp(a - mt)
        qk = jnp.einsum('bhtd,bhsd->bhts', qc, kc) * jnp.exp(Dm - mt[..., None])
        num = ws[..., None] * jnp.einsum('bhtd,bhde->bhte', qc, C) + jnp.einsum('bhts,bhse->bhte', qk, vc)
        den = ws * jnp.einsum('bhtd,bhd->bht', qc, n) + qk.sum(-1)
        h = num / jnp.maximum(jnp.abs(den), jnp.exp(-mt))[..., None]
        g = b[..., -1]
        u = g[..., None] - b + ic
        m_new = jnp.maximum(g + m, u.max(-1))
        sc = jnp.exp(g + m - m_new)
        ku = kc * jnp.exp(u - m_new[..., None])[..., None]
        C_new = sc[..., None, None] * C + jnp.einsum('bhsd,bhse->bhde', ku, vc)
        n_new = sc[..., None] * n + ku.sum(2)
        return (C_new, n_new, m_new), h

    (C, n, m), hs = lax.scan(step, (C0, n0, m0), tuple(chunks(a) for a in (q, k, v, ig, lf)))
    return jnp.moveaxis(hs, 0, 2).reshape(B, H, T, d), C, n, m


def hgrn_scan(q, k, v, lg, S0):
    B, H, T, dk = q.shape
    dv = v.shape[-1]
    nc = T // CHUNK

    def chunks(a):
        return jnp.moveaxis(a.reshape(a.shape[:2] + (nc, CHUNK) + a.shape[3:]), 2, 0)

    causal = jnp.tril(jnp.ones((CHUNK, CHUNK), dtype=bool))[:, :, None]

    def step(S, inp):
        qc, kc, vc, gc = inp
        Bc = jnp.cumsum(gc, axis=2)
        inter = jnp.einsum('bhtc,bhce->bhte', qc * jnp.exp(Bc), S)
        diff = Bc[:, :, :, None, :] - Bc[:, :, None, :, :]
        decay = jnp.where(causal, jnp.exp(jnp.where(causal, diff, 0.0)), 0.0)
        A = jnp.einsum('bhtsc,bhsc->bhts', qc[:, :, :, None, :] * decay, kc)
        o = inter + jnp.einsum('bhts,bhse->bhte', A, vc)
        gL = Bc[:, :, -1:, :]
        S_new = jnp.exp(gL[:, :, 0, :])[..., None] * S + jnp.einsum('bhsc,bhse->bhce', kc * jnp.exp(gL - Bc), vc)
        return S_new, o

    S, os_ = lax.scan(step, S0, tuple(chunks(a) for a in (q, k, v, lg)))
    return jnp.moveaxis(os_, 0, 2).reshape(B, H, T, dv), S


def mla_expand(ckv_n, w_ukv):
    B, T, _ = ckv_n.shape
    kv = (ckv_n @ w_ukv).reshape(B, T, MLA_H, MLA_DN + MLA_DV)
    return kv[..., :MLA_DN], kv[..., MLA_DN:]


def mla_mixer(cq, ckv, kpe, q_norm, w_uq, kv_norm, w_ukv, cache, rope):
    B, T, _ = cq.shape
    q = (rmsnorm(cq, q_norm) @ w_uq).reshape(B, T, MLA_H, MLA_DN + MLA_DR)
    q_nope, q_pe = q[..., :MLA_DN], q[..., MLA_DN:]
    ckv_n = rmsnorm(ckv, kv_norm)
    k_nope, v = mla_expand(ckv_n, w_ukv)
    if cache is None:
        return attend(q_nope, q_pe, k_nope, kpe, v), (ckv_n, kpe)
    cos, sin = rope
    q_pe = apply_rope(q_pe, cos[:, None, :], sin[:, None, :])
    kpe = apply_rope(kpe, cos, sin)
    ckv_c, kpe_c = cache
    kn_c, v_c = mla_expand(ckv_c, w_ukv)
    out = attend(q_nope, q_pe,
                 jnp.concatenate([k_nope, kn_c], axis=1),
                 jnp.concatenate([kpe, kpe_c], axis=1),
                 jnp.concatenate([v, v_c], axis=1))
    return out, None


def mlstm_mixer(mq, mk, mv, mo, mi_f, mi_b, mf_f, mf_b, norm_g, init):
    f32 = jnp.float32
    q = to_heads(mq, ML_H).astype(f32)
    k = to_heads(mk, ML_H).astype(f32) * (ML_D ** -0.5)
    v = to_heads(mv, ML_H).astype(f32)

    def gate(a):
        return a.astype(f32).transpose(0, 2, 1)

    ig_f, ig_b = gate(mi_f), gate(mi_b)
    lf_f, lf_b = jax.nn.log_sigmoid(gate(mf_f)), jax.nn.log_sigmoid(gate(mf_b))
    (Cf0, nf0, mf0), (Cb0, nb0, mb0) = init
    hf, Cf, nf, mf = mlstm_scan(q, k, v, ig_f, lf_f, Cf0.astype(f32), nf0.astype(f32), mf0.astype(f32))
    hb, Cb, nb, mb = mlstm_scan(flip(q), flip(k), flip(v), flip(ig_b), flip(lf_b),
                                Cb0.astype(f32), nb0.astype(f32), mb0.astype(f32))
    h = hf + flip(hb)
    mu = h.mean(-1, keepdims=True)
    var = jnp.mean(jnp.square(h - mu), -1, keepdims=True)
    h = (h - mu) * lax.rsqrt(var + EPS)
    out = from_heads(h) * norm_g.astype(f32) * jax.nn.sigmoid(mo.astype(f32))
    states = (jnp.stack([Cf, Cb], 1), jnp.stack([nf, nb], 1), jnp.stack([mf, mb], 1))
    return out.astype(mq.dtype), states


def hgrn_mixer(gq, gf_f, gf_b, gi, gg, lb, norm_g, init):
    f32 = jnp.float32
    q = jax.nn.silu(to_heads(gq, HG_H).astype(f32))
    v = to_heads(gi, HG_H).astype(f32)
    lb = lb.astype(f32).reshape(1, HG_H, 1, HG_DK)

    def decay(fr):
        fr = to_heads(fr, HG_H).astype(f32)
        f = lb + (1.0 - lb) * jax.nn.sigmoid(fr)
        return (1.0 - lb) * jax.nn.sigmoid(-fr), jnp.log(jnp.maximum(f, TINY))

    kf, lgf = decay(gf_f)
    kb, lgb = decay(gf_b)
    S0f, S0b = init
    of, Sf = hgrn_scan(q, kf, v, lgf, S0f.astype(f32))
    ob, Sb = hgrn_scan(flip(q), flip(kb), flip(v), flip(lgb), S0b.astype(f32))
    o = of + flip(ob)
    o = o * lax.rsqrt(jnp.mean(o * o, -1, keepdims=True) + EPS)
    out = from_heads(o) * norm_g.astype(f32) * jax.nn.silu(gg.astype(f32))
    return out.astype(gq.dtype), jnp.stack([Sf, Sb], 1)


def trunk_layer(x, cvec, lp, lb, cache, rope):
    B, T, _ = x.shape
    mod = (jax.nn.silu(cvec) @ lp['w_mod'] + lp['b_mod'])[:, None, :]
    sh1, sc1, g1, sh2, sc2, g2 = jnp.split(mod, 6, axis=-1)
    h = x * (1 + sc1) + sh1
    offsets = [int(o) for o in np.cumsum(IN_SIZES)[:-1]]
    (cq, ckv, kpe, mq, mk, mv, mo, mi_f, mi_b, mf_f, mf_b,
     gq, gf_f, gf_b, gi, gg) = jnp.split(h @ lp['w_in'] + lp['b_in'], offsets, axis=-1)
    if cache is None:
        f32 = jnp.float32
        ml0 = (jnp.zeros((B, ML_H, ML_D, ML_D), f32), jnp.zeros((B, ML_H, ML_D), f32), jnp.zeros((B, ML_H), f32))
        ml_init = (ml0, ml0)
        hg0 = jnp.zeros((B, HG_H, HG_DK, HG_DV), f32)
        hg_init = (hg0, hg0)
        mla_cache = None
    else:
        mla_cache, ml_init, hg_init = cache
    a_out, mla_st = mla_mixer(cq, ckv, kpe, lp['mla_q_norm'], lp['w_uq'], lp['mla_kv_norm'], lp['w_ukv'], mla_cache, rope)
    m_out, ml_st = mlstm_mixer(mq, mk, mv, mo, mi_f, mi_b, mf_f, mf_b, lp['mlstm_norm'], ml_init)
    g_out, hg_st = hgrn_mixer(gq, gf_f, gf_b, gi, gg, lb, lp['hgrn_norm'], hg_init)
    mix = jnp.concatenate([a_out, m_out, g_out], axis=-1) @ lp['w_out']
    x = layernorm(ALPHA * x + g1 * mix, lp['ln1_g'], lp['ln1_b'])
    h2 = x * (1 + sc2) + sh2
    gate, up = jnp.split(h2 @ lp['w_ffn_in'], 2, axis=-1)
    x = layernorm(ALPHA * x + g2 * ((jax.nn.silu(gate) * up) @ lp['w_ffn_out']), lp['ln2_g'], lp['ln2_b'])
    return x, (mla_st, ml_st, hg_st)


def setup_inputs(seed: int = 0) -> dict:
    key = jax.random.key(seed)
    ks = jax.random.split(key, 32)
    D = D_MODEL

    def nrm(k, shape, scale=1.0):
        return jax.random.normal(k, shape, jnp.float32) * scale

    b_mod = nrm(ks[11], (DEPTH, 6 * D), 0.02)
    b_mod = b_mod.at[:, 2 * D:3 * D].add(1.0).at[:, 5 * D:6 * D].add(1.0)
    off = np.cumsum((0,) + IN_SIZES)
    b_in = nrm(ks[13], (DEPTH, IN_COLS), 0.02)
    b_in = b_in.at[:, int(off[9]):int(off[11])].add(jnp.tile(jnp.linspace(3.0, 6.0, ML_H), 2))
    return {
        'x_prompt': nrm(ks[0], (BATCH, SEQ, D)),
        'x_sample': nrm(ks[1], (DEC_BATCH, DEC_SEQ, D)),
        'cache_mla_ckv': nrm(ks[2], (DEC_BATCH, DEPTH, PAST_LEN, KV_RANK)),
        'cache_mla_kpe': nrm(ks[3], (DEC_BATCH, DEPTH, PAST_LEN, MLA_DR)),
        'state_mlstm_C': nrm(ks[4], (DEC_BATCH, DEPTH, 2, ML_H, ML_D, ML_D), 0.3),
        'state_mlstm_n': nrm(ks[5], (DEC_BATCH, DEPTH, 2, ML_H, ML_D), 0.3),
        'state_mlstm_m': nrm(ks[6], (DEC_BATCH, DEPTH, 2, ML_H)),
        'state_hgrn_S': nrm(ks[7], (DEC_BATCH, DEPTH, 2, HG_H, HG_DK, HG_DV), 0.5),
        'c': nrm(ks[8], (DEC_BATCH, D)),
        'c_ctx': nrm(ks[9], (D,)),
        'w_mod': nrm(ks[10], (DEPTH, D, 6 * D), 0.5 * D ** -0.5),
        'b_mod': b_mod,
        'w_in': nrm(ks[12], (DEPTH, D, IN_COLS), D ** -0.5),
        'b_in': b_in,
        'mla_q_norm': 1.0 + nrm(ks[14], (DEPTH, Q_RANK), 0.02),
        'w_uq': nrm(ks[15], (DEPTH, Q_RANK, MLA_H * (MLA_DN + MLA_DR)), Q_RANK ** -0.5),
        'mla_kv_norm': 1.0 + nrm(ks[16], (DEPTH, KV_RANK), 0.02),
        'w_ukv': nrm(ks[17], (DEPTH, KV_RANK, MLA_H * (MLA_DN + MLA_DV)), KV_RANK ** -0.5),
        'mlstm_norm': 1.0 + nrm(ks[18], (DEPTH, ML_W), 0.02),
        'hgrn_lb_logits': 1.0 + nrm(ks[19], (DEPTH, HG_KW), 0.1),
        'hgrn_norm': 1.0 + nrm(ks[20], (DEPTH, HG_W), 0.02),
        'w_out': nrm(ks[21], (DEPTH, MIX_W, D), BETA * MIX_W ** -0.5),
        'ln1_g': 1.0 + nrm(ks[22], (DEPTH, D), 0.02),
        'ln1_b': nrm(ks[23], (DEPTH, D), 0.02),
        'w_ffn_in': nrm(ks[24], (DEPTH, D, 2 * FF), D ** -0.5),
        'w_ffn_out': nrm(ks[25], (DEPTH, FF, D), BETA * FF ** -0.5),
        'ln2_g': 1.0 + nrm(ks[26], (DEPTH, D), 0.02),
        'ln2_b': nrm(ks[27], (DEPTH, D), 0.02),
    }


def reference(x_prompt, x_sample, cache_mla_ckv, cache_mla_kpe, state_mlstm_C, state_mlstm_n, state_mlstm_m,
              state_hgrn_S, c, c_ctx, w_mod, b_mod, w_in, b_in, mla_q_norm, w_uq, mla_kv_norm, w_ukv,
              mlstm_norm, hgrn_lb_logits, hgrn_norm, w_out, ln1_g, ln1_b, w_ffn_in, w_ffn_out, ln2_g, ln2_b):
    lb_p = jax.nn.softmax(hgrn_lb_logits.astype(jnp.float32), axis=0)
    lbs = jnp.cumsum(lb_p, axis=0) - lb_p[0]
    rope = axial_rope(x_sample.shape[1])
    yp = x_prompt
    ys = x_sample
    ckv_l, kpe_l, mC_l, mn_l, mm_l, hS_l = [], [], [], [], [], []
    for l in range(DEPTH):
        lp = dict(w_mod=w_mod[l], b_mod=b_mod[l], w_in=w_in[l], b_in=b_in[l],
                  mla_q_norm=mla_q_norm[l], w_uq=w_uq[l], mla_kv_norm=mla_kv_norm[l], w_ukv=w_ukv[l],
                  mlstm_norm=mlstm_norm[l], hgrn_norm=hgrn_norm[l], w_out=w_out[l],
                  ln1_g=ln1_g[l], ln1_b=ln1_b[l], w_ffn_in=w_ffn_in[l], w_ffn_out=w_ffn_out[l],
                  ln2_g=ln2_g[l], ln2_b=ln2_b[l])
        yp, (mla_st, ml_st, hg_st) = trunk_layer(yp, c_ctx[None, :], lp, lbs[l], None, None)
        ckv_l.append(mla_st[0])
        kpe_l.append(mla_st[1])
        mC_l.append(ml_st[0])
        mn_l.append(ml_st[1])
        mm_l.append(ml_st[2])
        hS_l.append(hg_st)
        cache_l = ((cache_mla_ckv[:, l], cache_mla_kpe[:, l]),
                   ((state_mlstm_C[:, l, 0], state_mlstm_n[:, l, 0], state_mlstm_m[:, l, 0]),
                    (state_mlstm_C[:, l, 1], state_mlstm_n[:, l, 1], state_mlstm_m[:, l, 1])),
                   (state_hgrn_S[:, l, 0], state_hgrn_S[:, l, 1]))
        ys, _ = trunk_layer(ys, c, lp, lbs[l], cache_l, rope)
    new_mla_ckv = jnp.stack(ckv_l, axis=1)
    new_mla_kpe = jnp.stack(kpe_l, axis=1)
    new_mlstm_C = jnp.stack(mC_l, axis=1)
    new_mlstm_n = jnp.stack(mn_l, axis=1)
    new_mlstm_m = jnp.stack(mm_l, axis=1)
    new_hgrn_S = jnp.stack(hS_l, axis=1)
    return (yp, ys, new_mla_ckv, new_mla_kpe, new_mlstm_C, new_mlstm_n, new_mlstm_m, new_hgrn_S)
```

```python
import math
from contextlib import ExitStack
import numpy as np
import concourse.bass as bass
import concourse.mybir as mybir
from concourse.bass_utils import run_bass_kernel_spmd

F32 = mybir.dt.float32
BF16 = mybir.dt.bfloat16
AF = mybir.ActivationFunctionType
ALU = mybir.AluOpType
AX = mybir.AxisListType

D = 1024
DEPTH = 4
NKC = 8
FF = 2816
NFC = 22
ALPHA = (2 * DEPTH) ** 0.25
EPS = 1e-6
EPS_LN = EPS / (ALPHA * ALPHA)
ATT_SCALE = 96 ** -0.5

_IN_SIZES = (256, 128, 32, 256, 256, 256, 256, 4, 4, 4, 4, 256, 256, 256, 256, 256)
_OFF = np.cumsum((0,) + _IN_SIZES)
_NAMES = ['cq', 'ckv', 'kpe', 'mq', 'mk', 'mv', 'mo', 'mi_f', 'mi_b', 'mf_f', 'mf_b', 'gq', 'gf_f', 'gf_b', 'gi', 'gg']
_COL = {n: np.arange(_OFF[i], _OFF[i + 1]) for i, n in enumerate(_NAMES)}
_PERM = np.concatenate([np.arange(8, 16), np.arange(0, 8), np.arange(24, 32), np.arange(16, 24)])
_ROT_SIGN = np.concatenate([-np.ones(8), np.ones(8), -np.ones(8), np.ones(8)]).astype(np.float32)
FM_GROUPS = []
_fm_cols = []


def _add_fm(cols, dst, r0):
    c0 = sum(len(c) for c in _fm_cols)
    FM_GROUPS.append((c0, len(cols), dst, r0))
    _fm_cols.append(np.asarray(cols))


_add_fm(_COL['cq'][:128], 'CQT', 0)
_add_fm(_COL['cq'][128:], 'CQT', 128)
_add_fm(_COL['ckv'], 'CKVT', 0)
_add_fm(_COL['kpe'], 'KPET', 0)
_add_fm(_COL['kpe'][_PERM], 'KPEP', 0)
for _n, _d in (('mq', 'MQT'), ('mk', 'MKT'), ('gq', 'GQT'), ('gf_f', 'GFF'), ('gf_b', 'GFB')):
    _add_fm(_COL[_n][:128], _d, 0)
    _add_fm(_COL[_n][128:], _d, 128)
for _n, _d in (('mi_f', 'MIF'), ('mi_b', 'MIB'), ('mf_f', 'MFF'), ('mf_b', 'MFB')):
    _add_fm(_COL[_n], _d, 0)
NFM = sum(len(c) for c in _fm_cols)
_tm_cols = [_COL['mv'], _COL['mo'], _COL['mk'], _COL['gi'], _COL['gg']]
NTM = 1280
TM_GROUPS = [(0, 512), (512, 512), (1024, 256)]
W1_COLS = np.concatenate(_fm_cols + _tm_cols)
NW1 = len(W1_COLS)
NG = len(FM_GROUPS)
FM_SCR = {'CQT': 256, 'CKVT': 128, 'KPET': 32, 'KPEP': 32, 'MQT': 256, 'MKT': 256, 'GQT': 256, 'GFF': 256,
          'GFB': 256, 'MIF': 4, 'MIB': 4, 'MFF': 4, 'MFB': 4}


class Sched:
    EPOCH = 40000

    def __init__(self, nc, stack):
        self.nc = nc
        self.stack = stack
        self.eng = {'pe': nc.tensor, 'act': nc.scalar, 'dve': nc.vector, 'pool': nc.gpsimd, 'sp': nc.sync}
        self.prog = {k: [] for k in self.eng}
        self.seq = {k: 0 for k in self.eng}
        self.esems = {k: [] for k in self.eng}
        self.dsem = {}
        self.free_dsems = {}
        self.lastw = {}
        self.readers = {}
        self.waited = {}
        self.nsem = 0

    def _newsem(self, name):
        self.nsem += 1
        return self.stack.enter_context(self.nc.semaphore(name))

    def _deps(self, engine, reads, writes):
        deps = {}

        def add(ev):
            k = id(ev[0])
            if k not in deps or deps[k][1] < ev[1]:
                deps[k] = ev
        for k in reads:
            if k in self.lastw:
                add(self.lastw[k])
        for k in writes:
            if k in self.lastw:
                add(self.lastw[k])
            for ev in self.readers.get(k, {}).values():
                add(ev)
        waits = []
        own = set(id(s) for s in self.esems[engine]) if engine == 'pe' else set()
        for k, (sem, val) in deps.items():
            if k in own:
                continue
            wk = (engine, k)
            if self.waited.get(wk, 0) < val:
                self.waited[wk] = val
                waits.append((sem, val))
        return waits

    def _commit(self, ev, reads, writes):
        for k in writes:
            self.lastw[k] = ev
            self.readers[k] = {}
        for k in reads:
            self.readers.setdefault(k, {})[id(ev[0])] = ev

    def op(self, engine, fn, reads=(), writes=()):
        waits = self._deps(engine, reads, writes)
        n = self.seq[engine]
        e = n // self.EPOCH
        while len(self.esems[engine]) <= e:
            self.esems[engine].append(self._newsem(f"e_{engine}_{len(self.esems[engine])}"))
        sem = self.esems[engine][e]
        val = n % self.EPOCH + 1
        self.seq[engine] = n + 1
        self.prog[engine].append((waits, fn, sem, 1))
        self._commit((sem, val), reads, writes)

    def dma(self, queue, out, in_, reads=(), writes=(), sk=None, **kw):
        self.dma_group(queue, [(out, in_)], reads, writes, sk, **kw)

    def dma_group(self, queue, pairs, reads=(), writes=(), sk=None, **kw):
        assert sk is not None
        sk = (queue, sk)
        if sk not in self.dsem:
            fl = self.free_dsems.setdefault(queue, [])
            self.dsem[sk] = fl.pop() if fl else [self._newsem(f"d_{self.nsem}"), 0]
        ent = self.dsem[sk]
        waits = self._deps(queue, reads, writes)
        for i, (o, a) in enumerate(pairs):
            ent[1] += 16
            self.prog[queue].append((waits if i == 0 else [],
                                     (lambda eng, o=o, a=a: eng.dma_start(out=o, in_=a, **kw)), ent[0], 16))
        self._commit((ent[0], ent[1]), reads, writes)

    def barrier(self):
        evs = []
        for e, sems in self.esems.items():
            n = self.seq[e]
            if n > 0:
                evs.append((sems[(n - 1) // self.EPOCH], (n - 1) % self.EPOCH + 1))
        for sk, (sem, cnt) in self.dsem.items():
            if cnt > 0:
                evs.append((sem, cnt))
        for fl in self.free_dsems.values():
            for sem, cnt in fl:
                if cnt > 0:
                    evs.append((sem, cnt))
        for engine in self.eng:
            waits = []
            for sem, val in evs:
                wk = (engine, id(sem))
                if self.waited.get(wk, 0) < val:
                    self.waited[wk] = val
                    waits.append((sem, val))
            if waits:
                self.prog[engine].append((waits, None, None, 0))
        for (q, _), ent in self.dsem.items():
            self.free_dsems.setdefault(q, []).append(ent)
        self.dsem = {}
        self.lastw = {}
        self.readers = {}

    def emit(self):
        self.barrier()
        nc = self.nc
        with nc.Block() as block:
            def run(name):
                def f(eng):
                    for waits, fn, sem, amt in self.prog[name]:
                        for ws, wv in waits:
                            eng.wait_ge(ws, wv)
                        if fn is not None:
                            fn(eng).then_inc(sem, amt)
                return f
            block.sync(run('sp'))
            block.tensor(run('pe'))
            block.scalar(run('act'))
            block.vector(run('dve'))
            block.gpsimd(run('pool'))


class Mem:
    def __init__(self, big, words):
        self.big = big
        self.words = words
        self.top = 0
        self.n = 0

    def mark(self):
        return self.top

    def release(self, m):
        self.top = m

    def alloc(self, shape, dt=F32, name='t'):
        P = shape[0]
        n = int(np.prod(shape[1:]))
        w = n if dt == F32 else (n + 1) // 2
        w = (w + 15) // 16 * 16
        assert self.top + w <= self.words, f"SBUF overflow allocating {name} {shape}"
        a = self.big[0:P, self.top:self.top + w]
        self.top += w
        if dt != F32:
            a = a.bitcast(dt)
        a = a[:, 0:n]
        if len(shape) == 3:
            a = a.rearrange("p (a b) -> p a b", a=shape[1], b=shape[2])
        elif len(shape) == 4:
            a = a.rearrange("p (a b c) -> p a b c", a=shape[1], b=shape[2], c=shape[3])
        self.n += 1
        return a, f"{name}#{self.n}"


def build_program(cfg):
    NL = cfg.get('n_layers', DEPTH)
    NST = cfg.get('n_sample_tiles', 8)
    NPR = cfg.get('n_prompts', 4)
    DBG = cfg.get('debug', [])
    PH = cfg.get('phases', ['p1', 'mla', 'mlstm', 'hgrn', 'dense'])
    MIXIN = cfg.get('mix_input', False)
    HG_NOINT = cfg.get('hg_noint', False)
    TS = NST * 512
    TT = TS + NPR * 256
    NTILE = TT // 512
    nc = bass.Bass("TRN2", target_bir_lowering=False)

    def din(name, shape, dt=F32):
        return nc.dram_tensor(name, list(shape), dt, kind="ExternalInput").ap()

    def dout(name, shape, dt=F32):
        return nc.dram_tensor(name, list(shape), dt, kind="ExternalOutput").ap()

    def dscr(name, shape, dt=F32):
        kind = "ExternalOutput" if name in DBG else "Internal"
        return nc.dram_tensor(name, list(shape), dt, kind=kind).ap()

    xin = din("xin", [TT, D])
    cvT = din("cvT", [128, NKC, 2])
    w_mod = din("w_mod", [NL, D, 6 * D])
    b_modT = din("b_modT", [NL, 128, 48])
    w1 = din("w1", [NL, D, NW1])
    b1fm = din("b1fm", [NL, 128, NG])
    b1tm = din("b1tm", [NL, 128, NTM])
    w_out = din("w_out", [NL, D, D])
    lnp = din("lnp", [NL, 128, 4, NKC])
    w_ffn_in = din("w_ffn_in", [NL, D, 2 * FF])
    w_ffn_out = din("w_ffn_out", [NL, FF, D])
    identf = din("identf", [128, 128])
    wq_d = din("wq", [NL, 256, 768])
    wqp_d = din("wqp", [NL, 256, 768])
    wk_d = din("wk", [NL, 128, 768])
    wv_d = din("wv", [NL, 128, 512])
    e96_d = din("e96", [32, 96])
    qkn_d = din("qkn", [NL, 128, 3])
    cos96_d = din("cos96", [96, 4096])
    sin96_d = din("sin96", [96, 4096])
    kcos_d = din("kcos", [32, 4096])
    ksin_d = din("ksin", [32, 4096])
    cckv_d = din("cckv", [NL, 512, 128])
    ckpe_d = din("ckpe", [NL, 512, 32])
    mconst_d = din("mconst", [64, 4, 64])
    mlng_d = din("mlng", [NL, 64, 256])
    stC_d = din("stC", [NL, 2, 4, 64, 64])
    stn_d = din("stn", [NL, 2, 4, 64])
    stm_d = din("stm", [NL, 2, 4])
    C_out = dout("C_out", [max(NPR, 1), NL, 2, 4, 64, 64])
    n_out = dout("n_out", [max(NPR, 1), NL, 2, 4, 64])
    m_out = dout("m_out", [max(NPR, 1), NL, 2, 4])
    HF = dscr("HF", [TT, 256])
    hconst_d = din("hconst", [32, 2, 32])
    hgng_d = din("hgng", [NL, 32, 256])
    lbl_d = din("lbl", [64, 4, 4])
    stS_d = din("stS", [NL, 2, 4, 64, 64])
    S_out = dout("S_out", [max(NPR, 1), NL, 2, 4, 64, 64])
    ckv_out = dout("ckv_out", [max(NPR, 1), NL, 256, 128])
    kpe_out = dout("kpe_out", [max(NPR, 1), NL, 256, 32])
    y_out = dout("y_out", [TT, D])

    XT = dscr("XT", [D, TT])
    MIXT = din("MIXT", [D, TT], BF16) if MIXIN else dscr("MIXT", [D, TT], BF16)
    UT = dscr("UT", [FF, TT], BF16)
    SCR = {k: dscr(k, [r, TT]) for k, r in FM_SCR.items()}
    TMV = dscr("TMV", [TT, NTM])

    with ExitStack() as st:
        WORDS = 47 * 1024
        big = st.enter_context(nc.sbuf_tensor("big", [128, WORDS], F32))
        psb = [st.enter_context(nc.psum_tensor(f"ps{i}", [128, 512], F32)) for i in range(8)]
        s = Sched(nc, st)
        mem = Mem(big, WORDS)
        psi = {}

        def bank(lo=0, hi=8):
            c = psi.get((lo, hi), 0)
            psi[(lo, hi)] = c + 1
            i = lo + c % (hi - lo)
            return psb[i], ('ps', i)

        idf, k_idf = mem.alloc([128, 128], F32, 'idf')
        idb, k_idb = mem.alloc([128, 128], BF16, 'idb')
        onesf, k_onesf = mem.alloc([128, 128], F32, 'onesf')
        modT, k_mod = mem.alloc([128, 48, 2], F32, 'modT')
        lnt, k_lnt = mem.alloc([128, 4, NKC], F32, 'lnt')
        s.dma('sp', idf, identf, writes=[k_idf], sk='idf')
        s.op('dve', lambda e: e.tensor_copy(out=idb, in_=idf), reads=[k_idf], writes=[k_idb])
        s.op('pool', lambda e: e.memset(onesf, 1.0), writes=[k_onesf])
        epsln, k_eps = mem.alloc([128, 2], F32, 'epsln')
        s.op('pool', lambda e: e.memset(epsln[:, 0:1], EPS_LN), writes=[k_eps])
        s.op('pool', lambda e: e.memset(epsln[:, 1:2], EPS), writes=[k_eps])
        base_mark = mem.mark()

        def load_w_bf16(dram2d, dst, kdst, nkc, ncols, stage):
            i = 0
            for kc in range(nkc):
                for c0 in range(0, ncols, 2048):
                    w = min(2048, ncols - c0)
                    sa, sk_ = stage[i % len(stage)]
                    i += 1
                    s.dma('sp', sa[:, 0:w], dram2d[kc * 128:(kc + 1) * 128, c0:c0 + w], writes=[sk_], sk=sk_)
                    s.op('pool', lambda e, kc=kc, c0=c0, w=w, sa=sa: e.tensor_copy(out=dst[:, kc, c0:c0 + w], in_=sa[:, 0:w]),
                         reads=[sk_], writes=[kdst])

        def fm_view(dram2d, t0, n):
            return dram2d.rearrange("(c p) t -> p c t", p=128)[:, :, t0:t0 + n]

        def phase_init():
            m0 = mem.mark()
            xin_t = [mem.alloc([128, 4, D], F32, 'xin_t') for _ in range(2)]
            xT_t = [mem.alloc([128, NKC, 512], F32, 'xT_t') for _ in range(2)]
            for i in range(NTILE):
                t0 = i * 512
                xa, kx = xin_t[i % 2]
                xo, ko = xT_t[i % 2]
                s.dma('sp', xa, xin[t0:t0 + 512, :].rearrange("(b p) c -> p b c", p=128), writes=[kx], sk=kx)
                for kc in range(NKC):
                    ps, kp = bank()
                    for b in range(4):
                        s.op('pe', lambda e, ps=ps, b=b, kc=kc, xa=xa: e.transpose(
                            out=ps[:, b * 128:(b + 1) * 128], in_=xa[:, b, kc * 128:(kc + 1) * 128], identity=idf),
                            reads=[kx, k_idf], writes=[kp])
                    eng = 'dve' if kc % 2 == 0 else 'act'
                    if eng == 'dve':
                        s.op('dve', lambda e, ps=ps, kc=kc, xo=xo: e.tensor_copy(out=xo[:, kc, :], in_=ps[:, :]),
                             reads=[kp], writes=[(ko, kc)])
                    else:
                        s.op('act', lambda e, ps=ps, kc=kc, xo=xo: e.copy(out=xo[:, kc, :], in_=ps[:, :]),
                             reads=[kp], writes=[(ko, kc)])
                s.dma('pool', fm_view(XT, t0, 512), xo, reads=[(ko, kc) for kc in range(NKC)], writes=[('XT', i)], sk=ko)
            s.barrier()
            mem.release(m0)

        def phase_final():
            m0 = mem.mark()
            xT_t = [mem.alloc([128, NKC, 512], F32, 'xT_f') for _ in range(2)]
            y_t = [mem.alloc([128, 4, D], F32, 'y_t') for _ in range(2)]
            for i in range(NTILE):
                t0 = i * 512
                xa, kx = xT_t[i % 2]
                ya, ky = y_t[i % 2]
                s.dma('sp', xa, fm_view(XT, t0, 512), reads=[('XT', i)], writes=[kx], sk=kx)
                for b in range(4):
                    for half in range(2):
                        ps, kp = bank()
                        for q in range(4):
                            kc = half * 4 + q
                            s.op('pe', lambda e, ps=ps, q=q, kc=kc, b=b, xa=xa: e.transpose(
                                out=ps[:, q * 128:(q + 1) * 128], in_=xa[:, kc, b * 128:(b + 1) * 128], identity=idf),
                                reads=[kx, k_idf], writes=[kp])
                        if half == 0:
                            s.op('dve', lambda e, ps=ps, b=b, ya=ya: e.tensor_copy(out=ya[:, b, 0:512], in_=ps[:, :]),
                                 reads=[kp], writes=[(ky, b, 0)])
                        else:
                            s.op('act', lambda e, ps=ps, b=b, ya=ya: e.copy(out=ya[:, b, 512:1024], in_=ps[:, :]),
                                 reads=[kp], writes=[(ky, b, 1)])
                s.dma('pool', y_out[t0:t0 + 512, :].rearrange("(b p) c -> p b c", p=128), ya,
                      reads=[(ky, b, h) for b in range(4) for h in range(2)], sk=ky)
            s.barrier()
            mem.release(m0)

        def phase_mod(l):
            m0 = mem.mark()
            cv, kcv = mem.alloc([128, NKC, 2], F32, 'cv')
            sil, ksil = mem.alloc([128, NKC, 2], F32, 'sil')
            bm, kbm = mem.alloc([128, 48], F32, 'bm')
            stage = [mem.alloc([128, 6 * D], F32, 'wm') for _ in range(2)]
            s.dma('sp', cv, cvT, writes=[kcv], sk=kcv)
            s.dma('sp', bm, b_modT[l], writes=[kbm], sk=kbm)
            s.dma('sp', lnt, lnp[l], writes=[k_lnt], sk=k_lnt)
            s.op('act', lambda e: e.activation(out=sil, in_=cv, func=AF.Silu), reads=[kcv], writes=[ksil])
            ps, kp = bank()
            for kc in range(NKC):
                sa, ks = stage[kc % 2]
                s.dma('sp', sa, w_mod[l, kc * 128:(kc + 1) * 128, :], writes=[ks], sk=ks)
                for g in range(48):
                    s.op('pe', lambda e, sa=sa, g=g, kc=kc: e.matmul(
                        ps[:, 2 * g:2 * g + 2], lhsT=sa[:, g * 128:(g + 1) * 128], rhs=sil[:, kc, :],
                        start=(kc == 0 and g == 0), stop=(kc == NKC - 1), skip_group_check=True),
                        reads=[ks, ksil], writes=[kp])
            s.op('dve', lambda e: e.tensor_tensor(
                out=modT, in0=ps[:, 0:96].rearrange("p (g v) -> p g v", v=2),
                in1=bm.unsqueeze(2).to_broadcast([128, 48, 2]), op=ALU.add), reads=[kp, kbm], writes=[k_mod])
            for lo, add in ((8, True), (16, False), (32, True), (40, False)):
                if add:
                    s.op('dve', lambda e, lo=lo: e.tensor_scalar_add(out=modT[:, lo:lo + 8, :], in0=modT[:, lo:lo + 8, :], scalar1=1.0),
                         reads=[k_mod], writes=[k_mod])
                else:
                    s.op('dve', lambda e, lo=lo: e.tensor_scalar_mul(out=modT[:, lo:lo + 8, :], in0=modT[:, lo:lo + 8, :], scalar1=1.0 / ALPHA),
                         reads=[k_mod], writes=[k_mod])
            s.barrier()
            mem.release(m0)

        SH1, SC1, G1, SH2, SC2, G2 = 0, 8, 16, 24, 32, 40

        def tile_v(i):
            return 0 if i < NST else 1

        def phase_p1(l):
            m0 = mem.mark()
            w1t, kw1 = mem.alloc([128, NKC, NW1], BF16, 'w1t')
            bfm, kbfm = mem.alloc([128, NG], F32, 'bfm')
            btm, kbtm = mem.alloc([128, NTM], F32, 'btm')
            stage = [mem.alloc([128, 2048], F32, 'wst') for _ in range(2)]
            xT_t = [mem.alloc([128, NKC, 512], F32, 'xT1') for _ in range(2)]
            hT_t = [mem.alloc([128, NKC, 512], BF16, 'hT') for _ in range(2)]
            fo_t = [mem.alloc([128, 512], F32, 'fo') for _ in range(4)]
            to_t = [mem.alloc([128, 4, NTM], F32, 'to') for _ in range(1)]
            s.dma('sp', bfm, b1fm[l], writes=[kbfm], sk=kbfm)
            s.dma('sp', btm, b1tm[l], writes=[kbtm], sk=kbtm)
            load_w_bf16(w1[l], w1t, kw1, NKC, NW1, stage)
            nfo = 0
            for i in range(NTILE):
                t0 = i * 512
                v = tile_v(i)
                xa, kx = xT_t[i % 2]
                ha, kh = hT_t[i % 2]
                s.dma('sp', xa, fm_view(XT, t0, 512), reads=[('XT', i)], writes=[kx], sk=kx)
                for kc in range(NKC):
                    s.op('dve', lambda e, kc=kc, xa=xa, ha=ha, v=v: e.tensor_scalar(
                        out=ha[:, kc, :], in0=xa[:, kc, :], scalar1=modT[:, SC1 + kc, v:v + 1], scalar2=modT[:, SH1 + kc, v:v + 1],
                        op0=ALU.mult, op1=ALU.add), reads=[kx, k_mod], writes=[(kh, kc)])
                for g, (c0, M, dst, r0) in enumerate(FM_GROUPS):
                    ps, kp = bank(0, 4)
                    for kc in range(NKC):
                        s.op('pe', lambda e, ps=ps, kc=kc, c0=c0, M=M, ha=ha: e.matmul(
                            ps[0:M, :], lhsT=w1t[:, kc, c0:c0 + M], rhs=ha[:, kc, :], start=(kc == 0), stop=(kc == NKC - 1)),
                            reads=[kw1, (kh, kc)], writes=[kp])
                    fo, kfo = fo_t[nfo % 4]
                    nfo += 1
                    s.op('act', lambda e, ps=ps, M=M, g=g, fo=fo: e.activation(
                        out=fo[0:M, :], in_=ps[0:M, :], func=AF.Identity, bias=bfm[0:M, g:g + 1], scale=1.0),
                        reads=[kp, kbfm], writes=[kfo])
                    s.dma('pool', SCR[dst][r0:r0 + M, t0:t0 + 512], fo[0:M, :], reads=[kfo], writes=[(dst, r0, i)], sk=kfo)
                ta, kt = to_t[0]
                for b in range(4):
                    for (c0, N) in TM_GROUPS:
                        ps, kp = bank(4, 8)
                        for kc in range(NKC):
                            s.op('pe', lambda e, ps=ps, kc=kc, c0=c0, N=N, b=b, ha=ha: e.matmul(
                                ps[:, 0:N], lhsT=ha[:, kc, b * 128:(b + 1) * 128], rhs=w1t[:, kc, NFM + c0:NFM + c0 + N],
                                start=(kc == 0), stop=(kc == NKC - 1)), reads=[kw1, (kh, kc)], writes=[kp])
                        s.op('dve', lambda e, ps=ps, c0=c0, N=N, b=b, ta=ta: e.tensor_tensor(
                            out=ta[:, b, c0:c0 + N], in0=ps[:, 0:N], in1=btm[:, c0:c0 + N], op=ALU.add),
                            reads=[kp, kbtm], writes=[(kt, b, c0)])
                s.dma('pool', TMV[t0:t0 + 512, :].rearrange("(b p) c -> p b c", p=128), ta,
                      reads=[(kt, b, c0) for b in range(4) for (c0, N) in TM_GROUPS], writes=[('TMV', i)], sk=kt)
            s.barrier()
            mem.release(m0)

        def ln_apply(ya, ky, xo, ko, l, gi, bi, tmp):
            (sq, ksq), (mean, kmean), (rstd, krstd), (msq, kmsq) = tmp
            p1, kp1 = bank(4, 6)
            p2, kp2 = bank(6, 8)
            for kc in range(NKC):
                s.op('pe', lambda e, kc=kc: e.matmul(p1[:, :], lhsT=onesf, rhs=ya[:, kc, :], start=(kc == 0), stop=(kc == NKC - 1)),
                     reads=[k_onesf, (ky, kc)], writes=[kp1])
            for kc in range(NKC):
                s.op('act', lambda e, kc=kc: e.activation(out=sq[:, kc % 2, :], in_=ya[:, kc, :], func=AF.Square),
                     reads=[(ky, kc)], writes=[(ksq, kc % 2)])
                s.op('pe', lambda e, kc=kc: e.matmul(p2[:, :], lhsT=onesf, rhs=sq[:, kc % 2, :], start=(kc == 0), stop=(kc == NKC - 1)),
                     reads=[k_onesf, (ksq, kc % 2)], writes=[kp2])
            s.op('dve', lambda e: e.tensor_scalar_mul(out=mean, in0=p1[:, :], scalar1=1.0 / D), reads=[kp1], writes=[kmean])
            s.op('dve', lambda e: e.tensor_tensor(out=msq, in0=mean, in1=mean, op=ALU.mult), reads=[kmean], writes=[kmsq])
            s.op('dve', lambda e: e.scalar_tensor_tensor(out=msq, in0=p2[:, :], scalar=1.0 / D, in1=msq, op0=ALU.mult, op1=ALU.subtract),
                 reads=[kp2, kmsq], writes=[kmsq])
            s.op('act', lambda e: e.activation(out=rstd, in_=msq, func=AF.Sqrt, bias=epsln[:, 0:1], scale=1.0), reads=[kmsq, k_eps], writes=[krstd])
            s.op('dve', lambda e: e.reciprocal(out=rstd, in_=rstd), reads=[krstd], writes=[krstd])
            for kc in range(NKC):
                s.op('dve', lambda e, kc=kc: e.tensor_tensor(out=ya[:, kc, :], in0=ya[:, kc, :], in1=mean, op=ALU.subtract),
                     reads=[(ky, kc), kmean], writes=[(ky, kc)])
                s.op('pool', lambda e, kc=kc: e.tensor_tensor(out=ya[:, kc, :], in0=ya[:, kc, :], in1=rstd, op=ALU.mult),
                     reads=[(ky, kc), krstd], writes=[(ky, kc)])
                s.op('dve', lambda e, kc=kc: e.tensor_scalar(out=xo[:, kc, :], in0=ya[:, kc, :], scalar1=lnt[:, gi, kc:kc + 1],
                                                             scalar2=lnt[:, bi, kc:kc + 1], op0=ALU.mult, op1=ALU.add),
                     reads=[(ky, kc), k_lnt], writes=[(ko, kc)])

        def phase_p3(l):
            m0 = mem.mark()
            wo, kwo = mem.alloc([128, NKC, D], BF16, 'wo')
            stage = [mem.alloc([128, 2048], F32, 'wst') for _ in range(2)]
            xT_t = [mem.alloc([128, NKC, 512], F32, 'xT3') for _ in range(2)]
            mx_t = [mem.alloc([128, NKC, 512], BF16, 'mx') for _ in range(2)]
            ya, ky = mem.alloc([128, NKC, 512], F32, 'ya')
            tmp = [mem.alloc([128, 2, 512], F32, 'sq'), mem.alloc([128, 512], F32, 'mean'), mem.alloc([128, 512], F32, 'rstd'),
                   mem.alloc([128, 512], F32, 'msq')]
            load_w_bf16(w_out[l], wo, kwo, NKC, D, stage)
            for i in range(NTILE):
                t0 = i * 512
                v = tile_v(i)
                xa, kx = xT_t[i % 2]
                ma, kmx = mx_t[i % 2]
                s.dma('sp', xa, fm_view(XT, t0, 512), reads=[('XT', i)], writes=[(kx, kc) for kc in range(NKC)], sk=kx)
                s.dma('sp', ma, fm_view(MIXT, t0, 512), reads=[('MIXT', i)], writes=[kmx], sk=kmx)
                for oc in range(NKC):
                    ps, kp = bank(0, 4)
                    for kc in range(NKC):
                        s.op('pe', lambda e, ps=ps, kc=kc, oc=oc, ma=ma: e.matmul(
                            ps[:, :], lhsT=wo[:, kc, oc * 128:(oc + 1) * 128], rhs=ma[:, kc, :], start=(kc == 0), stop=(kc == NKC - 1)),
                            reads=[kwo, kmx], writes=[kp])
                    s.op('dve', lambda e, ps=ps, oc=oc, xa=xa, v=v: e.scalar_tensor_tensor(
                        out=ya[:, oc, :], in0=ps[:, :], scalar=modT[:, G1 + oc, v:v + 1], in1=xa[:, oc, :], op0=ALU.mult, op1=ALU.add),
                        reads=[kp, (kx, oc), k_mod], writes=[(ky, oc)])
                ln_apply(ya, ky, xa, kx, l, 0, 1, tmp)
                s.dma('pool', fm_view(XT, t0, 512), xa, reads=[(kx, kc) for kc in range(NKC)], writes=[('XT', i)], sk=kx)
            s.barrier()
            mem.release(m0)

        def phase_p3b(l):
            m0 = mem.mark()
            wf, kwf = mem.alloc([128, NKC, 2 * FF], BF16, 'wf')
            stage = [mem.alloc([128, 2048], F32, 'wst') for _ in range(2)]
            xT_t = [mem.alloc([128, NKC, 512], F32, 'xT3b') for _ in range(1)]
            h2, kh2 = mem.alloc([128, NKC, 512], BF16, 'h2')
            sg_t = [mem.alloc([128, 512], F32, 'sg') for _ in range(2)]
            u_t = [mem.alloc([128, NFC, 512], BF16, 'u') for _ in range(2)]
            load_w_bf16(w_ffn_in[l], wf, kwf, NKC, 2 * FF, stage)
            for i in range(NTILE):
                t0 = i * 512
                v = tile_v(i)
                xa, kx = xT_t[0]
                s.dma('sp', xa, fm_view(XT, t0, 512), reads=[('XT', i)], writes=[kx], sk=kx)
                for kc in range(NKC):
                    s.op('dve', lambda e, kc=kc, xa=xa, v=v: e.tensor_scalar(
                        out=h2[:, kc, :], in0=xa[:, kc, :], scalar1=modT[:, SC2 + kc, v:v + 1], scalar2=modT[:, SH2 + kc, v:v + 1],
                        op0=ALU.mult, op1=ALU.add), reads=[kx, k_mod], writes=[(kh2, kc)])
                ua, ku = u_t[i % 2]
                for f in range(NFC):
                    pg, kpg = bank(0, 4)
                    pu, kpu = bank(4, 8)
                    for kc in range(NKC):
                        s.op('pe', lambda e, pg=pg, kc=kc, f=f: e.matmul(
                            pg[:, :], lhsT=wf[:, kc, f * 128:(f + 1) * 128], rhs=h2[:, kc, :], start=(kc == 0), stop=(kc == NKC - 1)),
                            reads=[kwf, (kh2, kc)], writes=[kpg])
                    for kc in range(NKC):
                        s.op('pe', lambda e, pu=pu, kc=kc, f=f: e.matmul(
                            pu[:, :], lhsT=wf[:, kc, FF + f * 128:FF + (f + 1) * 128], rhs=h2[:, kc, :], start=(kc == 0), stop=(kc == NKC - 1)),
                            reads=[kwf, (kh2, kc)], writes=[kpu])
                    sg, ksg = sg_t[f % 2]
                    s.op('act', lambda e, pg=pg, sg=sg: e.activation(out=sg, in_=pg[:, :], func=AF.Silu), reads=[kpg], writes=[ksg])
                    s.op('dve', lambda e, pu=pu, sg=sg, f=f, ua=ua: e.tensor_tensor(out=ua[:, f, :], in0=pu[:, :], in1=sg, op=ALU.mult),
                         reads=[kpu, ksg], writes=[(ku, f)])
                s.dma('pool', fm_view(UT, t0, 512), ua, reads=[(ku, f) for f in range(NFC)], writes=[('UT', i)], sk=ku)
            s.barrier()
            mem.release(m0)

        def phase_p4(l):
            m0 = mem.mark()
            w2, kw2 = mem.alloc([128, NFC, D], BF16, 'w2')
            stage = [mem.alloc([128, 2048], F32, 'wst') for _ in range(2)]
            xT_t = [mem.alloc([128, NKC, 512], F32, 'xT4') for _ in range(2)]
            u_t = [mem.alloc([128, NFC, 512], BF16, 'u4') for _ in range(2)]
            ya, ky = mem.alloc([128, NKC, 512], F32, 'ya4')
            tmp = [mem.alloc([128, 2, 512], F32, 'sq'), mem.alloc([128, 512], F32, 'mean'), mem.alloc([128, 512], F32, 'rstd'),
                   mem.alloc([128, 512], F32, 'msq')]
            load_w_bf16(w_ffn_out[l], w2, kw2, NFC, D, stage)
            for i in range(NTILE):
                t0 = i * 512
                v = tile_v(i)
                xa, kx = xT_t[i % 2]
                ua, ku = u_t[i % 2]
                s.dma('sp', xa, fm_view(XT, t0, 512), reads=[('XT', i)], writes=[(kx, kc) for kc in range(NKC)], sk=kx)
                s.dma('sp', ua, fm_view(UT, t0, 512), reads=[('UT', i)], writes=[ku], sk=ku)
                for oc in range(NKC):
                    ps, kp = bank(0, 4)
                    for f in range(NFC):
                        s.op('pe', lambda e, ps=ps, f=f, oc=oc, ua=ua: e.matmul(
                            ps[:, :], lhsT=w2[:, f, oc * 128:(oc + 1) * 128], rhs=ua[:, f, :], start=(f == 0), stop=(f == NFC - 1)),
                            reads=[kw2, ku], writes=[kp])
                    s.op('dve', lambda e, ps=ps, oc=oc, xa=xa, v=v: e.scalar_tensor_tensor(
                        out=ya[:, oc, :], in0=ps[:, :], scalar=modT[:, G2 + oc, v:v + 1], in1=xa[:, oc, :], op0=ALU.mult, op1=ALU.add),
                        reads=[kp, (kx, oc), k_mod], writes=[(ky, oc)])
                ln_apply(ya, ky, xa, kx, l, 2, 3, tmp)
                s.dma('pool', fm_view(XT, t0, 512), xa, reads=[(kx, kc) for kc in range(NKC)], writes=[('XT', i)], sk=kx)
            s.barrier()
            mem.release(m0)

        SEQS = ([dict(t0=0, T=TS, sample=True, j=-1)] if NST else []) + \
               [dict(t0=TS + 256 * j, T=256, sample=False, j=j) for j in range(NPR)]

        def small_w(dram2d, rows, cols, dt, name, stage):
            dst, kd = mem.alloc([rows, cols], dt, name)
            sa, sk_ = stage
            s.dma('sp', sa[0:rows, 0:cols], dram2d, writes=[sk_], sk=sk_)
            s.op('pool', lambda e: e.tensor_copy(out=dst, in_=sa[0:rows, 0:cols]), reads=[sk_], writes=[kd])
            return dst, kd

        def evac(i, out, in_, reads, writes):
            if i % 2 == 0:
                s.op('dve', lambda e: e.tensor_copy(out=out, in_=in_), reads=reads, writes=writes)
            else:
                s.op('act', lambda e: e.copy(out=out, in_=in_), reads=reads, writes=writes)

        def phase_mla(l):
            m0 = mem.mark()
            stage = mem.alloc([128, 1024], F32, 'wst')
            wq0, kwq0 = small_w(wq_d[l, 0:128, :], 128, 768, BF16, 'wq0', stage)
            wq1, kwq1 = small_w(wq_d[l, 128:256, :], 128, 768, BF16, 'wq1', stage)
            wp0, kwp0 = small_w(wqp_d[l, 0:128, :], 128, 768, BF16, 'wp0', stage)
            wp1, kwp1 = small_w(wqp_d[l, 128:256, :], 128, 768, BF16, 'wp1', stage)
            wqs, kwqs, wps, kwps = (wq0, wq1), (kwq0, kwq1), (wp0, wp1), (kwp0, kwp1)
            wk, kwk = small_w(wk_d[l], 128, 768, BF16, 'wk', stage)
            wv, kwv = small_w(wv_d[l], 128, 512, BF16, 'wv', stage)
            e96, ke96 = small_w(e96_d, 32, 96, BF16, 'e96', stage)
            qkn, kqkn = mem.alloc([128, 3], F32, 'qkn')
            s.dma('sp', qkn, qkn_d[l], writes=[kqkn], sk=kqkn)
            KMAX = 4608 if NST else 256
            KT, kKT = mem.alloc([96, 8, KMAX], BF16, 'KT')
            VA, kVA = mem.alloc([128, KMAX // 128, 8, 65], BF16, 'VA')
            s.op('pool', lambda e: e.memset(VA[:, :, :, 64:65], 1.0), writes=[(kVA, 'ones')])
            ckv, kckv = mem.alloc([128, 512], F32, 'ckv')
            sq, ksq = mem.alloc([128, 2, 512], F32, 'sqm')
            rstd, krstd = mem.alloc([128, 512], F32, 'rstdm')
            ckvn, kckvn = mem.alloc([128, 512], F32, 'ckvn')
            ckvb, kckvb = mem.alloc([128, 512], BF16, 'ckvb')
            kpe, kkpe = mem.alloc([32, 512], F32, 'kpe')
            kpp, kkpp = mem.alloc([32, 512], F32, 'kpp')
            kcs, kkcs = mem.alloc([32, 2, 512], F32, 'kcs')
            krb, kkrb = mem.alloc([32, 512], BF16, 'krb')
            ctm, kctm = mem.alloc([128, 4, 128], F32, 'ctm')
            ktm, kktm = mem.alloc([128, 4, 32], F32, 'ktm')
            cq, kcq = mem.alloc([128, 2, 512], F32, 'cq')
            cqn, kcqn = mem.alloc([128, 2, 512], BF16, 'cqn')
            cs96, kcs96 = mem.alloc([96, 2, 512], F32, 'cs96')
            crsr, kcrsr = mem.alloc([96, 2, 512], F32, 'crsr')
            t12, kt12 = mem.alloc([96, 2, 512], F32, 't12')
            qT_t = [mem.alloc([96, 512], BF16, 'qT') for _ in range(2)]
            PT_t = [mem.alloc([128, 512], BF16, 'PT') for _ in range(4)]
            rec, krec = mem.alloc([128, 4], F32, 'rec')
            att, katt = mem.alloc([128, 4, 512], BF16, 'att')
            mixo_t = [mem.alloc([128, 4, 512], BF16, 'mixo') for _ in range(2)]
            nev = [0]

            def rms_rstd(src2d_list, keys, n, div, extra_scale):
                ps, kp = bank(7, 8)
                for c, (a, ka) in enumerate(zip(src2d_list, keys)):
                    s.op('act', lambda e, a=a, c=c: e.activation(out=sq[:, c, 0:n], in_=a, func=AF.Square), reads=[ka], writes=[(ksq, c)])
                    s.op('pe', lambda e, c=c: e.matmul(ps[:, 0:n], lhsT=onesf, rhs=sq[:, c, 0:n], start=(c == 0), stop=(c == len(src2d_list) - 1)),
                         reads=[k_onesf, (ksq, c)], writes=[kp])
                s.op('act', lambda e: e.activation(out=rstd[:, 0:n], in_=ps[:, 0:n], func=AF.Sqrt, bias=epsln[:, 1:2], scale=1.0 / div),
                     reads=[kp, k_eps], writes=[krstd])
                s.op('dve', lambda e: e.reciprocal(out=rstd[:, 0:n], in_=rstd[:, 0:n]), reads=[krstd], writes=[krstd])
                if extra_scale != 1.0:
                    s.op('dve', lambda e: e.tensor_scalar_mul(out=rstd[:, 0:n], in0=rstd[:, 0:n], scalar1=extra_scale), reads=[krstd], writes=[krstd])

            def make_kv(k0, n):
                for h in range(8):
                    ps, kp = bank(0, 4)
                    s.op('pe', lambda e, ps=ps, h=h: e.matmul(ps[0:96, 0:n], lhsT=wk[:, h * 96:(h + 1) * 96], rhs=ckvb[:, 0:n], start=True, stop=False),
                         reads=[kwk, kckvb], writes=[kp])
                    s.op('pe', lambda e, ps=ps: e.matmul(ps[0:96, 0:n], lhsT=e96, rhs=krb[:, 0:n], start=False, stop=True),
                         reads=[ke96, kkrb], writes=[kp])
                    nev[0] += 1
                    evac(nev[0], KT[:, h, k0:k0 + n], ps[0:96, 0:n], [kp], [(kKT, h, k0)])
                for b in range(n // 128):
                    ps, kp = bank(0, 4)
                    s.op('pe', lambda e, ps=ps, b=b: e.matmul(ps[:, :], lhsT=ckvb[:, b * 128:(b + 1) * 128], rhs=wv, start=True, stop=True),
                         reads=[kwv, kckvb], writes=[kp])
                    kb = k0 // 128 + b
                    nev[0] += 1
                    evac(nev[0], VA[:, kb, :, 0:64], ps[:, :].rearrange("p (h e) -> p h e", e=64), [kp], [(kVA, kb)])

            for sq_ in SEQS:
                T, S0, samp, j = sq_['T'], sq_['t0'], sq_['sample'], sq_['j']
                nkeys = T + (512 if samp else 0)
                nkt = nkeys // 128
                for k0 in range(0, T, 512):
                    n = min(512, T - k0)
                    t0 = S0 + k0
                    s.dma('sp', ckv[:, 0:n], SCR['CKVT'][:, t0:t0 + n], reads=[('CKVT', 0, t0 // 512)], writes=[kckv], sk=kckv)
                    s.dma('sp', kpe[:, 0:n], SCR['KPET'][:, t0:t0 + n], reads=[('KPET', 0, t0 // 512)], writes=[kkpe], sk=kkpe)
                    rms_rstd([ckv[:, 0:n]], [kckv], n, 128.0, 1.0)
                    s.op('dve', lambda e, n=n: e.scalar_tensor_tensor(out=ckvn[:, 0:n], in0=ckv[:, 0:n], scalar=qkn[:, 2:3], in1=rstd[:, 0:n],
                                                                     op0=ALU.mult, op1=ALU.mult), reads=[kckv, kqkn, krstd], writes=[kckvn])
                    s.op('act', lambda e, n=n: e.copy(out=ckvb[:, 0:n], in_=ckvn[:, 0:n]), reads=[kckvn], writes=[kckvb])
                    if samp:
                        s.dma('sp', kpp[:, 0:n], SCR['KPEP'][:, t0:t0 + n], reads=[('KPEP', 0, t0 // 512)], writes=[kkpp], sk=kkpp)
                        s.dma_group('sp', [(kcs[:, 0, 0:n], kcos_d[:, k0:k0 + n]), (kcs[:, 1, 0:n], ksin_d[:, k0:k0 + n])], writes=[kkcs], sk=kkcs)
                        s.op('dve', lambda e, n=n: e.tensor_tensor(out=kpe[:, 0:n], in0=kpe[:, 0:n], in1=kcs[:, 0, 0:n], op=ALU.mult),
                             reads=[kkpe, kkcs], writes=[kkpe])
                        s.op('dve', lambda e, n=n: e.tensor_tensor(out=kpp[:, 0:n], in0=kpp[:, 0:n], in1=kcs[:, 1, 0:n], op=ALU.mult),
                             reads=[kkpp, kkcs], writes=[kkpp])
                        s.op('dve', lambda e, n=n: e.tensor_tensor(out=krb[:, 0:n], in0=kpe[:, 0:n], in1=kpp[:, 0:n], op=ALU.add),
                             reads=[kkpe, kkpp], writes=[kkrb])
                    else:
                        s.op('dve', lambda e, n=n: e.tensor_copy(out=krb[:, 0:n], in_=kpe[:, 0:n]), reads=[kkpe], writes=[kkrb])
                        for b in range(n // 128):
                            ps, kp = bank(4, 7)
                            s.op('pe', lambda e, ps=ps, b=b: e.transpose(out=ps[:, 0:128], in_=ckvn[:, b * 128:(b + 1) * 128], identity=idf),
                                 reads=[kckvn, k_idf], writes=[kp])
                            s.op('pe', lambda e, ps=ps, b=b: e.transpose(out=ps[:, 128:160], in_=kpe[0:32, b * 128:(b + 1) * 128], identity=idf[0:32, 0:32]),
                                 reads=[kkpe, k_idf], writes=[kp])
                            s.op('dve', lambda e, ps=ps, b=b: e.tensor_copy(out=ctm[:, b, :], in_=ps[:, 0:128]), reads=[kp], writes=[(kctm, b)])
                            s.op('act', lambda e, ps=ps, b=b: e.copy(out=ktm[:, b, :], in_=ps[:, 128:160]), reads=[kp], writes=[(kktm, b)])
                        nb = n // 128
                        s.dma('pool', ckv_out[j, l, k0:k0 + n, :].rearrange("(b p) c -> p b c", p=128), ctm[:, 0:nb, :],
                              reads=[(kctm, b) for b in range(nb)], sk=kctm)
                        s.dma('pool', kpe_out[j, l, k0:k0 + n, :].rearrange("(b p) c -> p b c", p=128), ktm[:, 0:nb, :],
                              reads=[(kktm, b) for b in range(nb)], sk=kktm)
                    make_kv(k0, n)
                if samp:
                    s.dma('sp', ctm, cckv_d[l].rearrange("(b p) c -> p b c", p=128), writes=[(kctm, b) for b in range(4)], sk=kctm)
                    s.dma('sp', ktm, ckpe_d[l].rearrange("(b p) c -> p b c", p=128), writes=[(kktm, b) for b in range(4)], sk=kktm)
                    ps, kp = bank(4, 7)
                    ps2, kp2 = bank(4, 7)
                    for b in range(4):
                        s.op('pe', lambda e, b=b, ps=ps: e.transpose(out=ps[:, b * 128:(b + 1) * 128], in_=ctm[:, b, :], identity=idf),
                             reads=[(kctm, b), k_idf], writes=[kp])
                        s.op('pe', lambda e, b=b, ps2=ps2: e.transpose(out=ps2[0:32, b * 128:(b + 1) * 128], in_=ktm[:, b, :], identity=idf),
                             reads=[(kktm, b), k_idf], writes=[kp2])
                    s.op('dve', lambda e, ps=ps: e.tensor_copy(out=ckvb, in_=ps[:, :]), reads=[kp], writes=[kckvb])
                    s.op('act', lambda e, ps2=ps2: e.copy(out=krb, in_=ps2[0:32, :]), reads=[kp2], writes=[kkrb])
                    make_kv(T, 512)
                for q0 in range(0, T, 512):
                    n = min(512, T - q0)
                    nqb = n // 128
                    t0 = S0 + q0
                    ti = t0 // 512
                    s.dma('sp', cq[:, :, 0:n], SCR['CQT'].rearrange("(c p) t -> p c t", p=128)[:, :, t0:t0 + n],
                          reads=[('CQT', 0, ti), ('CQT', 128, ti)], writes=[kcq], sk=kcq)
                    rms_rstd([cq[:, 0, 0:n], cq[:, 1, 0:n]], [kcq, kcq], n, 256.0, ATT_SCALE)
                    for kc in range(2):
                        s.op('dve', lambda e, kc=kc, n=n: e.tensor_scalar_mul(out=cqn[:, kc, 0:n], in0=cq[:, kc, 0:n], scalar1=qkn[:, kc:kc + 1]),
                             reads=[kcq, kqkn], writes=[(kcqn, kc)])
                    if samp:
                        s.dma_group('sp', [(cs96[:, 0, 0:n], cos96_d[:, q0:q0 + n]), (cs96[:, 1, 0:n], sin96_d[:, q0:q0 + n])], writes=[kcs96], sk=kcs96)
                        for c in range(2):
                            s.op('dve', lambda e, c=c, n=n: e.tensor_tensor(out=crsr[:, c, 0:n], in0=cs96[:, c, 0:n], in1=rstd[0:96, 0:n], op=ALU.mult),
                                 reads=[kcs96, krstd], writes=[(kcrsr, c)])
                    mo_, kmo = mixo_t[(t0 // 512) % 2]
                    def qproj(h):
                        qT, kqT = qT_t[h % 2]
                        psA, kpA = bank(0, 2)
                        for kc in range(2):
                            s.op('pe', lambda e, psA=psA, kc=kc, h=h, n=n: e.matmul(
                                psA[0:96, 0:n], lhsT=wqs[kc][:, h * 96:(h + 1) * 96], rhs=cqn[:, kc, 0:n], start=(kc == 0), stop=(kc == 1)),
                                reads=[kwqs[kc], (kcqn, kc)], writes=[kpA])
                        if samp:
                            psB, kpB = bank(0, 2)
                            for kc in range(2):
                                s.op('pe', lambda e, psB=psB, kc=kc, h=h, n=n: e.matmul(
                                    psB[0:96, 0:n], lhsT=wps[kc][:, h * 96:(h + 1) * 96], rhs=cqn[:, kc, 0:n], start=(kc == 0), stop=(kc == 1)),
                                    reads=[kwps[kc], (kcqn, kc)], writes=[kpB])
                            s.op('dve', lambda e, psA=psA, n=n: e.tensor_tensor(out=t12[:, 0, 0:n], in0=psA[0:96, 0:n], in1=crsr[:, 0, 0:n], op=ALU.mult),
                                 reads=[kpA, (kcrsr, 0)], writes=[(kt12, 0)])
                            s.op('dve', lambda e, psB=psB, n=n: e.tensor_tensor(out=t12[:, 1, 0:n], in0=psB[0:96, 0:n], in1=crsr[:, 1, 0:n], op=ALU.mult),
                                 reads=[kpB, (kcrsr, 1)], writes=[(kt12, 1)])
                            s.op('pool', lambda e, qT=qT, n=n: e.tensor_tensor(out=qT[:, 0:n], in0=t12[:, 0, 0:n], in1=t12[:, 1, 0:n], op=ALU.add),
                                 reads=[(kt12, 0), (kt12, 1)], writes=[kqT])
                        else:
                            s.op('dve', lambda e, psA=psA, qT=qT, n=n: e.tensor_tensor(out=qT[:, 0:n], in0=psA[0:96, 0:n], in1=rstd[0:96, 0:n], op=ALU.mult),
                                 reads=[kpA, krstd], writes=[kqT])

                    def attend(h):
                        qT, kqT = qT_t[h % 2]
                        psO, kpO = bank(5, 7)
                        LA = 2
                        pend = {}
                        for it in range(nkt + LA):
                            if it < nkt:
                                kt = it
                                psS, kpS = bank(2, 5)
                                kvk = [(kKT, h, (kt * 128) // 512 * 512 if kt * 128 < T else T)]
                                s.op('pe', lambda e, psS=psS, kt=kt, h=h, qT=qT, n=n: e.matmul(
                                    psS[:, 0:n], lhsT=KT[:, h, kt * 128:(kt + 1) * 128], rhs=qT[:, 0:n], start=True, stop=True),
                                    reads=kvk + [kqT], writes=[kpS])
                                PT, kPT = PT_t[kt % len(PT_t)]
                                s.op('act', lambda e, psS=psS, PT=PT, n=n: e.activation(out=PT[:, 0:n], in_=psS[:, 0:n], func=AF.Exp),
                                     reads=[kpS], writes=[kPT])
                                pend[kt] = (PT, kPT)
                            if it >= LA:
                                kt = it - LA
                                PT, kPT = pend.pop(kt)
                                for qb in range(nqb):
                                    s.op('pe', lambda e, psO=psO, PT=PT, qb=qb, kt=kt, h=h: e.matmul(
                                        psO[:, qb * 65:(qb + 1) * 65], lhsT=PT[:, qb * 128:(qb + 1) * 128], rhs=VA[:, kt, h, :],
                                        start=(kt == 0 and qb == 0), stop=(kt == nkt - 1), skip_group_check=True),
                                        reads=[kPT, (kVA, kt), (kVA, 'ones')], writes=[kpO])
                        pO3 = psO[:, 0:nqb * 65].rearrange("p (q e) -> p q e", e=65)
                        s.op('dve', lambda e, pO3=pO3, nqb=nqb: e.reciprocal(out=rec[:, 0:nqb], in_=pO3[:, :, 64]), reads=[kpO], writes=[krec])
                        s.op('dve', lambda e, pO3=pO3, nqb=nqb, h=h: e.tensor_tensor(
                            out=att[:, 0:nqb, h * 64:(h + 1) * 64], in0=pO3[:, :, 0:64], in1=rec[:, 0:nqb].unsqueeze(2).to_broadcast([128, nqb, 64]),
                            op=ALU.mult), reads=[kpO, krec], writes=[(katt, h)])

                    qproj(0)
                    for h in range(8):
                        if h + 1 < 8:
                            qproj(h + 1)
                        attend(h)
                    for qb in range(nqb):
                        ps, kp = bank(7, 8)
                        pbf = ps.bitcast(BF16)
                        for c in range(4):
                            s.op('pe', lambda e, pbf=pbf, c=c, qb=qb: e.transpose(out=pbf[:, c * 128:(c + 1) * 128], in_=att[:, qb, c * 128:(c + 1) * 128], identity=idb),
                                 reads=[(katt, 2 * c), (katt, 2 * c + 1), k_idb], writes=[kp])
                        nev[0] += 1
                        evac(nev[0], mo_[:, :, qb * 128:(qb + 1) * 128], pbf[:, 0:512].rearrange("p (c t) -> p c t", t=128), [kp], [(kmo, qb)])
                    s.dma('pool', MIXT.rearrange("(c p) t -> p c t", p=128)[:, 0:4, t0:t0 + n], mo_[:, :, 0:n],
                          reads=[(kmo, qb) for qb in range(nqb)], writes=[('MIXT_A', t0)], sk=kmo)
            s.barrier()
            mem.release(m0)

        def phase_mlstm(l):
            m0_ = mem.mark()
            mc, kmc = mem.alloc([64, 4, 64], F32, 'mconst')
            ng, kng = mem.alloc([64, 256], F32, 'mlng')
            s.dma('sp', mc, mconst_d, writes=[kmc], sk=kmc)
            s.dma('sp', ng, mlng_d[l], writes=[kng], sk=kng)
            rmask, krm = mem.alloc([64, 8, 64], F32, 'rmask')
            s.op('pool', lambda e: e.memset(rmask, 1.0), writes=[krm])
            s.op('pool', lambda e: e.memset(rmask[:, :, 0:1], 0.0), writes=[krm])
            A = lambda shape, dt=F32, name='m': mem.alloc(shape, dt, name)
            GI, kGI = A([64, 8, 64]); GF, kGF = A([64, 8, 64]); SP, kSP = A([64, 8, 64]); CS, kCS = A([64, 8, 64])
            NB, kNB = A([64, 8, 64]); Cc, kCc = A([64, 8, 64]); W, kW = A([64, 8, 64]); THR, kTHR = A([64, 8, 64])
            TOT, kTOT = A([64, 8]); CMAX, kCMAX = A([64, 8]); Gn, kGn = A([64, 8])
            ROW, kROW = A([4, 4, 64]); m0t, km0t = A([4, 2]); MN, kMN = A([4, 2, 64]); Rr, kRr = A([4, 2, 64])
            MP, kMP = A([4, 2, 64]); SCr, kSCr = A([4, 2, 64])
            TMPc, kTMPc = A([64, 16]); Rc, kRc = A([64, 8]); SCc, kSCc = A([64, 8])
            Wt, kWt = A([64, 8, 64]); THRt, kTHRt = A([64, 8, 64]); BD, kBD = A([64, 64, 8]); scB, kscB = A([64, 64, 8])
            Caug, kC = A([64, 4, 65]); Cs, kCs = A([64, 4, 65]); Csb, kCsb = A([64, 4, 65], BF16)
            q32_t = [A([64, 4, 512]) for _ in range(1)]
            k32_t = [A([64, 4, 512]) for _ in range(1)]
            qb_t = [A([64, 4, 512], BF16) for _ in range(2)]
            kb_t = [A([64, 4, 512], BF16) for _ in range(2)]
            vv_t = [A([64, 8, 256]) for _ in range(1)]
            kk_t = [A([64, 8, 256]) for _ in range(1)]
            kkb_t = [A([64, 8, 256], BF16) for _ in range(2)]
            va_t = [A([64, 8, 4, 65], BF16) for _ in range(2)]
            for va, kva in va_t:
                s.op('pool', lambda e, va=va: e.memset(va[:, :, :, 64:65], 1.0), writes=[(kva, 'ones')])
            vw_t = [A([64, 8, 4, 65], BF16) for _ in range(2)]
            PTg_t = [A([64, 8, 4, 64], BF16) for _ in range(2)]
            dC_t = [A([64, 8, 4, 65]) for _ in range(2)]
            hF_t = [A([64, 8, 256]) for _ in range(2)]
            mo_t = [A([64, 8, 256]) for _ in range(2)]
            hb_t = [A([64, 8, 256]) for _ in range(2)]
            den, kden = A([64, 2, 4]); rec, krec = A([64, 2, 4])
            s1, ks1 = A([64, 32]); s2, ks2 = A([64, 32]); s3, ks3 = A([64, 32])
            sqx, ksqx = q32_t[0]
            sqx = sqx.rearrange("p h (a b) -> p (h a) b", b=256)
            sig, ksig = k32_t[0]
            sig = sig.rearrange("p h (a b) -> p (h a) b", b=256)
            xo_t = [A([64, 8, 256], BF16) for _ in range(1)]
            mixo_t = [A([128, 2, 512], BF16) for _ in range(1)]
            maskF, maskB, J64, J4 = mc[:, 0, :], mc[:, 1, :], mc[:, 2, :], mc[0:4, 3, 0:4]
            gcount = [0]
            for sq_ in SEQS:
                T, S0, samp, j = sq_['T'], sq_['t0'], sq_['sample'], sq_['j']
                nch = T // 64
                Jn = J64 if nch == 64 else J4
                In = idf[0:nch, 0:nch]
                def gview(name):
                    return SCR[name][:, S0:S0 + T].rearrange("h (j t) -> j h t", t=64)
                rk = lambda nm: [(nm, 0, i) for i in range(S0 // 512, (S0 + T + 511) // 512)]
                s.dma_group('sp', [(GI[0:nch, 0:4, :], gview('MIF')), (GI[0:nch, 4:8, :], gview('MIB'))], reads=rk('MIF') + rk('MIB'), writes=[kGI], sk=kGI)
                s.dma_group('sp', [(GF[0:nch, 0:4, :], gview('MFF')), (GF[0:nch, 4:8, :], gview('MFB'))], reads=rk('MFF') + rk('MFB'), writes=[kGF], sk=kGF)
                s.op('act', lambda e, nch=nch: e.activation(out=SP[0:nch], in_=GF[0:nch], func=AF.Exp, scale=-1.0), reads=[kGF], writes=[kSP])
                s.op('act', lambda e, nch=nch: e.activation(out=SP[0:nch], in_=SP[0:nch], func=AF.Ln, bias=1.0), reads=[kSP], writes=[kSP])
                fl = lambda a, nch=nch: a[0:nch].rearrange("p a b -> p (a b)")
                s.op('dve', lambda e, nch=nch, fl=fl: e.tensor_tensor_scan(out=fl(CS), data0=fl(rmask), data1=fl(SP), initial=0.0, op0=ALU.mult, op1=ALU.add),
                     reads=[krm, kSP], writes=[kCS])
                s.op('dve', lambda e, nch=nch: e.tensor_copy(out=TOT[0:nch], in_=CS[0:nch, :, 63]), reads=[kCS], writes=[kTOT])
                s.op('dve', lambda e, nch=nch: e.tensor_copy(out=NB[0:nch, 0:4, :], in_=CS[0:nch, 0:4, :]), reads=[kCS], writes=[kNB])
                s.op('dve', lambda e, nch=nch: e.tensor_tensor(out=NB[0:nch, 4:8, :], in0=SP[0:nch, 4:8, :], in1=CS[0:nch, 4:8, :], op=ALU.subtract),
                     reads=[kCS, kSP, kNB], writes=[kNB])
                s.op('dve', lambda e, nch=nch: e.tensor_tensor(out=NB[0:nch, 4:8, :], in0=NB[0:nch, 4:8, :],
                                                              in1=TOT[0:nch, 4:8].unsqueeze(2).to_broadcast([nch, 4, 64]), op=ALU.add),
                     reads=[kNB, kTOT], writes=[kNB])
                s.op('dve', lambda e, nch=nch: e.tensor_tensor(out=Cc[0:nch], in0=GI[0:nch], in1=NB[0:nch], op=ALU.add), reads=[kGI, kNB], writes=[kCc])
                s.op('dve', lambda e, nch=nch: e.tensor_reduce(out=CMAX[0:nch], in_=Cc[0:nch], axis=AX.X, op=ALU.max), reads=[kCc], writes=[kCMAX])
                s.op('dve', lambda e, nch=nch: e.tensor_scalar_mul(out=Gn[0:nch], in0=TOT[0:nch], scalar1=-1.0), reads=[kTOT], writes=[kGn])
                ps, kp = bank(0, 8)
                for qi, (src_, ksrc, lo, mat) in enumerate(((CMAX, kCMAX, 0, In), (Gn, kGn, 0, In), (CMAX, kCMAX, 4, Jn), (Gn, kGn, 4, Jn))):
                    s.op('pe', lambda e, ps=ps, qi=qi, src_=src_, lo=lo, mat=mat, nch=nch: e.matmul(
                        ps[0:4, qi * nch:(qi + 1) * nch], lhsT=src_[0:nch, lo:lo + 4], rhs=mat, start=True, stop=True),
                        reads=[ksrc, k_idf, kmc], writes=[kp])
                s.op('dve', lambda e, ps=ps, nch=nch: e.tensor_copy(out=ROW[:, :, 0:nch], in_=ps[0:4, 0:4 * nch].rearrange("p (a b) -> p a b", b=nch)),
                     reads=[kp], writes=[kROW])
                if samp:
                    s.dma('sp', m0t, stm_d[l].rearrange("d h -> h d"), writes=[km0t], sk=km0t, allow_slow_non_contiguous=True)
                else:
                    s.op('dve', lambda e: e.memset(m0t, 0.0), writes=[km0t])
                for d in range(2):
                    s.op('dve', lambda e, d=d, nch=nch: e.tensor_tensor_scan(out=MN[:, d, 0:nch], data0=ROW[:, 2 * d, 0:nch], data1=ROW[:, 2 * d + 1, 0:nch],
                                                                            initial=m0t[:, d:d + 1], op0=ALU.max, op1=ALU.add),
                         reads=[kROW, km0t], writes=[(kMN, d)])
                    s.op('dve', lambda e, d=d, nch=nch: e.tensor_tensor(out=Rr[:, d, 0:nch], in0=MN[:, d, 0:nch], in1=ROW[:, 2 * d + 1, 0:nch], op=ALU.subtract),
                         reads=[(kMN, d), kROW], writes=[(kRr, d)])
                    s.op('dve', lambda e, d=d: e.tensor_copy(out=MP[:, d, 0:1], in_=m0t[:, d:d + 1]), reads=[km0t], writes=[(kMP, d, 0)])
                    s.op('dve', lambda e, d=d, nch=nch: e.tensor_copy(out=MP[:, d, 1:nch], in_=MN[:, d, 0:nch - 1]), reads=[(kMN, d)], writes=[(kMP, d, 1)])
                    s.op('dve', lambda e, d=d, nch=nch: e.tensor_tensor(out=SCr[:, d, 0:nch], in0=MP[:, d, 0:nch], in1=Rr[:, d, 0:nch], op=ALU.subtract),
                         reads=[(kMP, d, 0), (kMP, d, 1), (kRr, d)], writes=[(kSCr, d)])
                    s.op('act', lambda e, d=d, nch=nch: e.activation(out=SCr[:, d, 0:nch], in_=SCr[:, d, 0:nch], func=AF.Exp), reads=[(kSCr, d)], writes=[(kSCr, d)])
                    if not samp:
                        s.dma('pool', m_out[j, l, d, :].rearrange("(h o) -> h o", o=1), MN[:, d, nch - 1:nch], reads=[(kMN, d)], sk=(kMN, d))
                ps, kp = bank(0, 8)
                for qi, (src_, ksrc, d) in enumerate(((Rr, kRr, 0), (Rr, kRr, 1), (SCr, kSCr, 0), (SCr, kSCr, 1))):
                    s.op('pe', lambda e, ps=ps, qi=qi, src_=src_, d=d, nch=nch: e.matmul(
                        ps[0:nch, qi * 4:(qi + 1) * 4], lhsT=src_[:, d, 0:nch], rhs=idf[0:4, 0:4], start=True, stop=True),
                        reads=[(ksrc, d), k_idf], writes=[kp])
                s.op('dve', lambda e, ps=ps, nch=nch: e.tensor_copy(out=TMPc[0:nch], in_=ps[0:nch, 0:16]), reads=[kp], writes=[kTMPc])
                ps2, kp2 = bank(0, 8)
                for qi, lo in enumerate((4, 12)):
                    s.op('pe', lambda e, ps2=ps2, qi=qi, lo=lo, nch=nch, Jn=Jn: e.matmul(
                        ps2[0:nch, qi * 4:(qi + 1) * 4], lhsT=Jn, rhs=TMPc[0:nch, lo:lo + 4], start=True, stop=True),
                        reads=[kTMPc, kmc], writes=[kp2])
                s.op('dve', lambda e, nch=nch: e.tensor_copy(out=Rc[0:nch, 0:4], in_=TMPc[0:nch, 0:4]), reads=[kTMPc], writes=[(kRc, 0)])
                s.op('dve', lambda e, nch=nch, ps2=ps2: e.tensor_copy(out=Rc[0:nch, 4:8], in_=ps2[0:nch, 0:4]), reads=[kp2], writes=[(kRc, 1)])
                s.op('dve', lambda e, nch=nch: e.tensor_copy(out=SCc[0:nch, 0:4], in_=TMPc[0:nch, 8:12]), reads=[kTMPc], writes=[(kSCc, 0)])
                s.op('dve', lambda e, nch=nch, ps2=ps2: e.tensor_copy(out=SCc[0:nch, 4:8], in_=ps2[0:nch, 4:8]), reads=[kp2], writes=[(kSCc, 1)])
                rcb = lambda nch=nch: Rc[0:nch].unsqueeze(2).to_broadcast([nch, 8, 64])
                s.op('dve', lambda e, nch=nch, rcb=rcb: e.tensor_tensor(out=W[0:nch], in0=Cc[0:nch], in1=rcb(), op=ALU.subtract),
                     reads=[kCc, (kRc, 0), (kRc, 1)], writes=[kW])
                s.op('act', lambda e, nch=nch: e.activation(out=W[0:nch], in_=W[0:nch], func=AF.Exp), reads=[kW], writes=[kW])
                s.op('dve', lambda e, nch=nch, rcb=rcb: e.tensor_tensor(out=THR[0:nch], in0=NB[0:nch], in1=rcb(), op=ALU.subtract),
                     reads=[kNB, (kRc, 0), (kRc, 1)], writes=[kTHR])
                s.op('act', lambda e, nch=nch: e.activation(out=THR[0:nch], in_=THR[0:nch], func=AF.Exp), reads=[kTHR], writes=[kTHR])
                for src_, ksrc, dst, kdst in ((W, kW, Wt, kWt), (THR, kTHR, THRt, kTHRt)):
                    ps, kp = bank(0, 8)
                    for r in range(8):
                        s.op('pe', lambda e, ps=ps, r=r, src_=src_, nch=nch, In=In: e.transpose(
                            out=ps[0:64, r * nch:(r + 1) * nch], in_=src_[0:nch, r, :], identity=In), reads=[ksrc, k_idf], writes=[kp])
                    s.op('dve', lambda e, ps=ps, dst=dst, nch=nch: e.tensor_copy(out=dst[:, :, 0:nch], in_=ps[0:64, 0:8 * nch].rearrange("p (a b) -> p a b", b=nch)),
                         reads=[kp], writes=[kdst])
                s.op('dve', lambda e, nch=nch, In=In: e.tensor_tensor(
                    out=BD[0:nch, 0:nch, :], in0=In.unsqueeze(2).to_broadcast([nch, nch, 8]),
                    in1=SCc[0:nch].unsqueeze(1).to_broadcast([nch, nch, 8]), op=ALU.mult), reads=[k_idf, (kSCc, 0), (kSCc, 1)], writes=[kBD])
                ps, kp = bank(0, 8)
                s.op('pe', lambda e, ps=ps, nch=nch: e.matmul(ps[0:64, 0:nch * 8], lhsT=onesf[0:nch, 0:64],
                                                             rhs=BD[0:nch, 0:nch, :].rearrange("p a b -> p (a b)"), start=True, stop=True),
                     reads=[k_onesf, kBD], writes=[kp])
                s.op('dve', lambda e, ps=ps, nch=nch: e.tensor_copy(out=scB[:, 0:nch, :], in_=ps[0:64, 0:nch * 8].rearrange("p (a b) -> p a b", b=8)),
                     reads=[kp], writes=[kscB])
                G = min(8, nch)
                GT = 64 * G
                ngrp = nch // G
                for d in range(2):
                    if samp:
                        s.dma('sp', Caug[:, :, 0:64], stC_d[l, d].rearrange("h a b -> a h b"), writes=[kC], sk=kC)
                        s.dma('sp', Caug[:, :, 64], stn_d[l, d].rearrange("h a -> a h"), writes=[(kC, 'n')], sk=(kC, 'n'), allow_slow_non_contiguous=True)
                    else:
                        s.op('dve', lambda e: e.memset(Caug, 0.0), writes=[kC, (kC, 'n')])
                    mask = maskF if d == 0 else maskB
                    gorder = list(range(ngrp)) if d == 0 else list(range(ngrp - 1, -1, -1))
                    corder = list(range(G)) if d == 0 else list(range(G - 1, -1, -1))

                    def load_group(gi, d=d, G=G, GT=GT, S0=S0):
                        t0 = S0 + gi * GT
                        ti = t0 // 512
                        b = gcount[0] % 2
                        gcount[0] += 1
                        cx = dict(gi=gi, t0=t0, ti=ti, b=b)
                        q32, kq32 = q32_t[0]; k32, kk32 = k32_t[0]; vv, kvv = vv_t[0]; kk, kkk = kk_t[0]
                        qb, kqb = qb_t[b]; kb, kkb = kb_t[b]; va, kva = va_t[b]; kkb2, kkkb2 = kkb_t[b]
                        cx.update(qb=qb, kqb=kqb, kb=kb, kkb=kkb, va=va, kva=kva, kkb2=kkb2, kkkb2=kkkb2,
                                  vw=vw_t[b], PT=PTg_t[b], dC=dC_t[b], hb=hb_t[b])
                        s.dma('sp', q32[:, :, 0:GT], SCR['MQT'][:, t0:t0 + GT].rearrange("(h p) t -> p h t", p=64),
                              reads=[('MQT', 0, ti), ('MQT', 128, ti)], writes=[kq32], sk=kq32)
                        s.dma('sp', k32[:, :, 0:GT], SCR['MKT'][:, t0:t0 + GT].rearrange("(h p) t -> p h t", p=64),
                              reads=[('MKT', 0, ti), ('MKT', 128, ti)], writes=[kk32], sk=kk32)
                        s.dma('sp', vv[:, 0:G, :], TMV[t0:t0 + GT, 0:256].rearrange("(j t) c -> t j c", t=64), reads=[('TMV', ti)], writes=[kvv], sk=kvv)
                        s.dma('sp', kk[:, 0:G, :], TMV[t0:t0 + GT, 512:768].rearrange("(j t) c -> t j c", t=64), reads=[('TMV', ti)], writes=[kkk], sk=kkk)
                        s.op('act', lambda e: e.copy(out=qb[:, :, 0:GT], in_=q32[:, :, 0:GT]), reads=[kq32], writes=[kqb])
                        s.op('act', lambda e: e.mul(out=kb[:, :, 0:GT], in_=k32[:, :, 0:GT], mul=0.125), reads=[kk32], writes=[kkb])
                        s.op('pool', lambda e: e.tensor_copy(out=va[:, 0:G, :, 0:64], in_=vv[:, 0:G, :].rearrange("p g (h e) -> p g h e", e=64)),
                             reads=[kvv], writes=[kva])
                        s.op('act', lambda e: e.mul(out=kkb2[:, 0:G, :], in_=kk[:, 0:G, :], mul=0.125), reads=[kkk], writes=[kkkb2])
                        if d == 1:
                            hF, khF = hF_t[b]; mo_, kmo = mo_t[b]
                            cx.update(hF=hF, khF=khF, mo=mo_, kmo=kmo)
                            s.dma('sp', hF[:, 0:G, :], HF[t0:t0 + GT, :].rearrange("(j t) c -> t j c", t=64), reads=[('HF', t0)], writes=[khF], sk=khF)
                            s.dma('sp', mo_[:, 0:G, :], TMV[t0:t0 + GT, 256:512].rearrange("(j t) c -> t j c", t=64), reads=[('TMV', ti)], writes=[kmo], sk=kmo)
                        return cx

                    def stage1(cx, jl, d=d, mask=mask, G=G, GT=GT):
                        jg = cx['gi'] * G + jl
                        cs_ = slice(jl * 64, (jl + 1) * 64)
                        qb, kb, va, kkb2 = cx['qb'], cx['kb'], cx['va'], cx['kkb2']
                        (vw, kvw), (PT, kPT), (dC, kdC) = cx['vw'], cx['PT'], cx['dC']
                        psA, kpA = bank(0, 2)
                        for h in range(4):
                            s.op('pe', lambda e, h=h: e.matmul(psA[0:64, h * 64:(h + 1) * 64], lhsT=kb[:, h, cs_], rhs=qb[:, h, cs_], start=True, stop=True),
                                 reads=[cx['kkb'], cx['kqb']], writes=[kpA])
                        s.op('dve', lambda e: e.tensor_tensor(out=PT[:, jl], in0=psA[0:64, 0:256].rearrange("p (h t) -> p h t", t=64),
                                                              in1=mask.unsqueeze(1).to_broadcast([64, 4, 64]), op=ALU.mult),
                             reads=[kpA, kmc], writes=[(kPT, jl)])
                        s.op('pool', lambda e: e.tensor_tensor(out=vw[:, jl], in0=va[:, jl], in1=Wt[:, d * 4:d * 4 + 4, jg].unsqueeze(2).to_broadcast([64, 4, 65]), op=ALU.mult),
                             reads=[cx['kva'], (cx['kva'], 'ones'), kWt], writes=[(kvw, jl)])
                        psC, kpC = bank(2, 4)
                        for h in range(4):
                            s.op('pe', lambda e, h=h: e.matmul(psC[0:64, h * 65:(h + 1) * 65], lhsT=kkb2[:, jl, h * 64:(h + 1) * 64], rhs=vw[:, jl, h, :], start=True, stop=True),
                                 reads=[cx['kkkb2'], (kvw, jl)], writes=[kpC])
                        s.op('act', lambda e: e.copy(out=dC[:, jl], in_=psC[0:64, 0:260].rearrange("p (h e) -> p h e", e=65)), reads=[kpC], writes=[(kdC, jl)])

                    def stage2(cx, jl, d=d, G=G, GT=GT):
                        jg = cx['gi'] * G + jl
                        cs_ = slice(jl * 64, (jl + 1) * 64)
                        qb = cx['qb']
                        (vw, kvw), (PT, kPT), (dC, kdC), (hb, khb) = cx['vw'], cx['PT'], cx['dC'], cx['hb']
                        s.op('dve', lambda e: e.tensor_tensor(out=Cs, in0=Caug, in1=scB[:, jg, d * 4:d * 4 + 4].unsqueeze(2).to_broadcast([64, 4, 65]), op=ALU.mult),
                             reads=[kC, (kC, 'n'), kscB], writes=[kCs])
                        s.op('act', lambda e: e.copy(out=Csb, in_=Cs), reads=[kCs], writes=[kCsb])
                        s.op('dve', lambda e: e.tensor_tensor(out=Caug, in0=Cs, in1=dC[:, jl], op=ALU.add), reads=[kCs, (kdC, jl)], writes=[kC, (kC, 'n')])
                        psO, kpO = bank(4, 7)
                        for h in range(4):
                            s.op('pe', lambda e, h=h: e.matmul(psO[0:64, h * 65:(h + 1) * 65], lhsT=PT[:, jl, h, :], rhs=vw[:, jl, h, :], start=(h == 0), stop=False, skip_group_check=True),
                                 reads=[(kPT, jl), (kvw, jl)], writes=[kpO])
                            s.op('pe', lambda e, h=h: e.matmul(psO[0:64, h * 65:(h + 1) * 65], lhsT=qb[:, h, cs_], rhs=Csb[:, h, :], start=False, stop=True, skip_group_check=True),
                                 reads=[cx['kqb'], kCsb], writes=[kpO])
                        pO3 = psO[0:64, 0:260].rearrange("p (h e) -> p h e", e=65)
                        return lambda: stage2b(cx, jl, pO3, kpO)

                    def stage2b(cx, jl, pO3, kpO, d=d, G=G, GT=GT):
                        jg = cx['gi'] * G + jl
                        hb, khb = cx['hb']
                        s.op('act', lambda e: e.activation(out=den[:, jl % 2, :], in_=pO3[:, :, 64], func=AF.Abs), reads=[kpO], writes=[(kden, jl % 2)])
                        s.op('dve', lambda e: e.tensor_tensor(out=den[:, jl % 2, :], in0=den[:, jl % 2, :], in1=THRt[:, d * 4:d * 4 + 4, jg], op=ALU.max),
                             reads=[(kden, jl % 2), kTHRt], writes=[(kden, jl % 2)])
                        s.op('dve', lambda e: e.reciprocal(out=rec[:, jl % 2, :], in_=den[:, jl % 2, :]), reads=[(kden, jl % 2)], writes=[(krec, jl % 2)])
                        s.op('dve', lambda e: e.tensor_tensor(out=hb[:, jl, :].rearrange("p (h e) -> p h e", e=64), in0=pO3[:, :, 0:64],
                                                              in1=rec[:, jl % 2, :].unsqueeze(2).to_broadcast([64, 4, 64]), op=ALU.mult),
                             reads=[kpO, (krec, jl % 2)], writes=[(khb, jl)])

                    def finish_group(cx, d=d, G=G, GT=GT):
                        t0 = cx['t0']
                        hb, khb = cx['hb']
                        hk = [(khb, jl) for jl in range(G)]
                        if d == 0:
                            s.dma('pool', HF[t0:t0 + GT, :].rearrange("(j t) c -> t j c", t=64), hb[:, 0:G, :], reads=hk, writes=[('HF', t0)], sk=khb)
                            return
                        hF, khF, mo_, kmo = cx['hF'], cx['khF'], cx['mo'], cx['kmo']
                        s.op('pool', lambda e: e.tensor_tensor(out=hb[:, 0:G, :], in0=hb[:, 0:G, :], in1=hF[:, 0:G, :], op=ALU.add), reads=hk + [khF], writes=hk)
                        X4 = hb[:, 0:G, :].rearrange("p g (h e) -> p (g h) e", e=64)
                        n4 = G * 4
                        s.op('dve', lambda e: e.tensor_reduce(out=s1[:, 0:n4], in_=X4, axis=AX.X, op=ALU.add), reads=hk, writes=[ks1])
                        s.op('pool', lambda e: e.tensor_tensor(out=sqx[:, 0:G, :], in0=hb[:, 0:G, :], in1=hb[:, 0:G, :], op=ALU.mult), reads=hk, writes=[ksqx])
                        s.op('dve', lambda e: e.tensor_reduce(out=s2[:, 0:n4], in_=sqx[:, 0:G, :].rearrange("p g (h e) -> p (g h) e", e=64), axis=AX.X, op=ALU.add),
                             reads=[ksqx], writes=[ks2])
                        s.op('dve', lambda e: e.tensor_scalar_mul(out=s1[:, 0:n4], in0=s1[:, 0:n4], scalar1=1.0 / 64), reads=[ks1], writes=[ks1])
                        s.op('dve', lambda e: e.tensor_tensor(out=s3[:, 0:n4], in0=s1[:, 0:n4], in1=s1[:, 0:n4], op=ALU.mult), reads=[ks1], writes=[ks3])
                        s.op('dve', lambda e: e.scalar_tensor_tensor(out=s2[:, 0:n4], in0=s2[:, 0:n4], scalar=1.0 / 64, in1=s3[:, 0:n4], op0=ALU.mult, op1=ALU.subtract),
                             reads=[ks2, ks3], writes=[ks2])
                        s.op('act', lambda e: e.activation(out=s2[:, 0:n4], in_=s2[:, 0:n4], func=AF.Sqrt, bias=epsln[0:64, 1:2], scale=1.0), reads=[ks2, k_eps], writes=[ks2])
                        s.op('dve', lambda e: e.reciprocal(out=s2[:, 0:n4], in_=s2[:, 0:n4]), reads=[ks2], writes=[ks2])
                        s.op('dve', lambda e: e.tensor_tensor(out=X4, in0=X4, in1=s1[:, 0:n4].unsqueeze(2).to_broadcast([64, n4, 64]), op=ALU.subtract),
                             reads=hk + [ks1], writes=hk)
                        s.op('dve', lambda e: e.tensor_tensor(out=X4, in0=X4, in1=s2[:, 0:n4].unsqueeze(2).to_broadcast([64, n4, 64]), op=ALU.mult),
                             reads=hk + [ks2], writes=hk)
                        s.op('pool', lambda e: e.tensor_tensor(out=hb[:, 0:G, :], in0=hb[:, 0:G, :], in1=ng.unsqueeze(1).to_broadcast([64, G, 256]), op=ALU.mult),
                             reads=hk + [kng], writes=hk)
                        s.op('act', lambda e: e.activation(out=sig[:, 0:G, :], in_=mo_[:, 0:G, :], func=AF.Sigmoid), reads=[kmo], writes=[ksig])
                        xo, kxo = xo_t[0]
                        s.op('dve', lambda e: e.tensor_tensor(out=xo[:, 0:G, :], in0=hb[:, 0:G, :], in1=sig[:, 0:G, :], op=ALU.mult), reads=hk + [ksig], writes=[kxo])
                        mx, kmx = mixo_t[0]
                        ps, kp = bank(7, 8)
                        pbf = ps.bitcast(BF16)
                        for c in range(2):
                            for jl in range(G):
                                s.op('pe', lambda e, c=c, jl=jl: e.transpose(
                                    out=pbf[:, c * 512 + jl * 64:c * 512 + (jl + 1) * 64], in_=xo[:, jl, c * 128:(c + 1) * 128], identity=idb[0:64, 0:64]),
                                    reads=[kxo, k_idb], writes=[kp])
                        for c in range(2):
                            s.op('dve', lambda e, c=c: e.tensor_copy(out=mx[:, c, 0:GT], in_=pbf[:, c * 512:c * 512 + GT]), reads=[kp], writes=[(kmx, c)])
                        s.dma('pool', MIXT.rearrange("(c p) t -> p c t", p=128)[:, 4:6, t0:t0 + GT], mx[:, :, 0:GT],
                              reads=[(kmx, 0), (kmx, 1)], writes=[('MIXT_M', t0)], sk=kmx)

                    cur = load_group(gorder[0])
                    for jl in corder:
                        stage1(cur, jl)
                    for gidx, gi in enumerate(gorder):
                        nxt = load_group(gorder[gidx + 1]) if gidx + 1 < len(gorder) else None
                        pend = None
                        for jl in corder:
                            p2 = stage2(cur, jl)
                            if pend is not None:
                                pend()
                            pend = p2
                            if nxt is not None:
                                stage1(nxt, jl)
                        pend()
                        finish_group(cur)
                        cur = nxt
                    if not samp:
                        s.dma('pool', C_out[j, l, d].rearrange("h a b -> a h b"), Caug[:, :, 0:64], reads=[kC], sk=(kC, 0))
                        s.dma('pool', n_out[j, l, d].rearrange("h a -> a h"), Caug[:, :, 64], reads=[kC], sk=(kC, 1), allow_slow_non_contiguous=True)
            s.barrier()
            mem.release(m0_)

        def phase_hgrn(l):
            m0_ = mem.mark()
            A = lambda shape, dt=F32, name='g': mem.alloc(shape, dt, name)
            hc, khc = A([32, 2, 32]); ng, kng = A([32, 256]); lbl, klbl = A([64, 4, 4]); lbp, klbp = A([64, 4, 4])
            lbs, klbs = A([64, 4]); lbv, klbv = A([64, 3, 4])
            s.dma('sp', hc, hconst_d, writes=[khc], sk=khc)
            s.dma('sp', ng, hgng_d[l], writes=[kng], sk=kng)
            s.dma('sp', lbl, lbl_d, writes=[klbl], sk=klbl)
            s.op('act', lambda e: e.activation(out=lbp, in_=lbl, func=AF.Exp), reads=[klbl], writes=[klbp])
            s.op('dve', lambda e: e.tensor_reduce(out=lbs, in_=lbp, axis=AX.X, op=ALU.add), reads=[klbp], writes=[klbs])
            s.op('dve', lambda e: e.reciprocal(out=lbs, in_=lbs), reads=[klbs], writes=[klbs])
            s.op('dve', lambda e: e.tensor_tensor(out=lbp, in0=lbp, in1=lbs.unsqueeze(2).to_broadcast([64, 4, 4]), op=ALU.mult), reads=[klbp, klbs], writes=[klbp])
            if l == 0:
                s.op('dve', lambda e: e.memset(lbv[:, 0, :], 0.0), writes=[klbv])
            else:
                s.op('dve', lambda e: e.tensor_reduce(out=lbv[:, 0, :], in_=lbp[:, :, 1:l + 1], axis=AX.X, op=ALU.add), reads=[klbp], writes=[klbv])
            s.op('dve', lambda e: e.tensor_scalar(out=lbv[:, 1, :], in0=lbv[:, 0, :], scalar1=-1.0, scalar2=1.0, op0=ALU.mult, op1=ALU.add), reads=[klbv], writes=[klbv])
            s.op('dve', lambda e: e.tensor_scalar_mul(out=lbv[:, 2, :], in0=lbv[:, 1, :], scalar1=-1.0), reads=[klbv], writes=[klbv])
            rmask, krm = A([64, 512])
            s.op('pool', lambda e: e.memset(rmask, 1.0), writes=[krm])
            s.op('pool', lambda e: e.memset(rmask.rearrange("p (j t) -> p j t", t=32)[:, :, 0:1], 0.0), writes=[krm])
            GTM = 256
            gq32, kgq = A([64, 4, GTM]); gf32, kgf = A([64, 4, GTM]); qf, kqf = gq32, kgq
            sg, ksg = A([64, 4, GTM]); lg, klg = A([64, 4, GTM]); Bc, kBc = A([64, 4, GTM]); kf, kkf = A([64, 4, GTM])
            tot, ktot = A([64, 4, 8])
            egl_t = [A([64, 4, 8]) for _ in range(2)]
            qs_t = [A([64, 4, GTM], BF16) for _ in range(2)]
            ks_t = [A([64, 4, GTM], BF16) for _ in range(2)]
            v32, kv32 = A([32, 8, 256]); vb_t = [A([32, 8, 256], BF16) for _ in range(2)]
            PTg_t = [A([32, 8, 4, 32], BF16) for _ in range(2)]
            dS_t = [A([64, 8, 4, 64]) for _ in range(2)]
            oF, koF = A([32, 8, 256]); gg32, kgg = A([32, 8, 256]); ob_t = [A([32, 8, 256]) for _ in range(2)]
            S, kS = A([64, 4, 64]); Sb, kSb = A([64, 4, 64], BF16); St, kSt = A([64, 4, 64])
            kst_t = [A([32, 256], BF16) for _ in range(2)]
            s2, ks2 = A([32, 32]); sqx, ksqx = A([32, 8, 256]); xo, kxo = A([32, 8, 256], BF16)
            mx, kmx = A([128, 2, GTM], BF16)
            gcount = [0]
            for sq_ in SEQS:
                T, S0, samp, j = sq_['T'], sq_['t0'], sq_['sample'], sq_['j']
                GT = min(GTM, T)
                G = GT // 32
                ngrp = T // GT
                for d in range(2):
                    sview = lambda a: a.rearrange("h c e -> c h e")
                    if samp:
                        s.dma('sp', S, sview(stS_d[l, d]), writes=[kS], sk=kS)
                    else:
                        s.op('dve', lambda e: e.memset(S, 0.0), writes=[kS])
                    s.op('act', lambda e: e.copy(out=Sb, in_=S), reads=[kS], writes=[kSb])
                    mask = hc[:, d, :]
                    gorder = list(range(ngrp)) if d == 0 else list(range(ngrp - 1, -1, -1))
                    corder = list(range(G)) if d == 0 else list(range(G - 1, -1, -1))

                    def load_group(gi, d=d, G=G, GT=GT, S0=S0):
                        t0 = S0 + gi * GT
                        ti = t0 // 512
                        b = gcount[0] % 2
                        gcount[0] += 1
                        qs, kqs = qs_t[b]; ks, kks = ks_t[b]; vb, kvb = vb_t[b]; egl, kegl = egl_t[b]
                        cx = dict(gi=gi, t0=t0, ti=ti, b=b, qs=qs, kqs=kqs, ks=ks, kks=kks, vb=vb, kvb=kvb, egl=egl, kegl=kegl,
                                  PT=PTg_t[b], dS=dS_t[b], ob=ob_t[b])
                        fsrc = 'GFF' if d == 0 else 'GFB'
                        s.dma('sp', gq32[:, :, 0:GT], SCR['GQT'].rearrange("(c p) t -> p c t", p=64)[:, :, t0:t0 + GT],
                              reads=[('GQT', 0, ti), ('GQT', 128, ti)], writes=[kgq], sk=kgq)
                        s.dma('sp', gf32[:, :, 0:GT], SCR[fsrc].rearrange("(c p) t -> p c t", p=64)[:, :, t0:t0 + GT],
                              reads=[(fsrc, 0, ti), (fsrc, 128, ti)], writes=[kgf], sk=kgf)
                        s.dma('sp', v32[:, 0:G, :], TMV[t0:t0 + GT, 768:1024].rearrange("(j t) c -> t j c", t=32), reads=[('TMV', ti)], writes=[kv32], sk=kv32)
                        s.op('act', lambda e: e.copy(out=vb[:, 0:G, :], in_=v32[:, 0:G, :]), reads=[kv32], writes=[kvb])
                        W_ = slice(0, GT)
                        klgs = [(klg, i_) for i_ in range(4)]
                        kkfs = [(kkf, i_) for i_ in range(4)]
                        s.op('act', lambda e: e.activation(out=qf[:, :, W_], in_=gq32[:, :, W_], func=AF.Silu), reads=[kgq], writes=[kgq])
                        s.op('act', lambda e: e.activation(out=sg[:, :, W_], in_=gf32[:, :, W_], func=AF.Sigmoid), reads=[kgf], writes=[ksg])
                        for cc in range(4):
                            s.op('dve', lambda e, cc=cc: e.tensor_scalar(out=lg[:, cc, W_], in0=sg[:, cc, W_], scalar1=lbv[:, 1, cc:cc + 1], scalar2=lbv[:, 0, cc:cc + 1],
                                                                        op0=ALU.mult, op1=ALU.add), reads=[ksg, klbv], writes=[(klg, cc)])
                            s.op('pool', lambda e, cc=cc: e.tensor_scalar(out=kf[:, cc, W_], in0=sg[:, cc, W_], scalar1=lbv[:, 2, cc:cc + 1], scalar2=lbv[:, 1, cc:cc + 1],
                                                                         op0=ALU.mult, op1=ALU.add), reads=[ksg, klbv], writes=[(kkf, cc)])
                        s.op('act', lambda e: e.activation(out=lg[:, :, W_], in_=lg[:, :, W_], func=AF.Ln), reads=klgs, writes=klgs)
                        for cc in range(4):
                            s.op('dve', lambda e, cc=cc: e.tensor_tensor_scan(out=Bc[:, cc, W_], data0=rmask[:, W_], data1=lg[:, cc, W_], initial=0.0,
                                                                             op0=ALU.mult, op1=ALU.add), reads=[krm, (klg, cc)], writes=[(kBc, cc)])
                        Bc4 = Bc[:, :, W_].rearrange("p c (j t) -> p c j t", t=32)
                        lg4 = lg[:, :, W_].rearrange("p c (j t) -> p c j t", t=32)
                        kBcs = [(kBc, i_) for i_ in range(4)]
                        s.op('dve', lambda e: e.tensor_copy(out=tot[:, :, 0:G], in_=Bc4[:, :, :, 31]), reads=kBcs, writes=[ktot])
                        if d == 1:
                            s.op('dve', lambda e: e.tensor_tensor(out=Bc4, in0=lg4, in1=Bc4, op=ALU.subtract), reads=kBcs + klgs, writes=kBcs)
                            s.op('dve', lambda e: e.tensor_tensor(out=Bc4, in0=Bc4, in1=tot[:, :, 0:G].unsqueeze(3).to_broadcast([64, 4, G, 32]), op=ALU.add),
                                 reads=kBcs + [ktot], writes=kBcs)
                        s.op('act', lambda e: e.activation(out=egl[:, :, 0:G], in_=tot[:, :, 0:G], func=AF.Exp), reads=[ktot], writes=[kegl])
                        s.op('act', lambda e: e.activation(out=sg[:, :, W_], in_=Bc[:, :, W_], func=AF.Exp), reads=kBcs + [ksg] + kkfs + klgs, writes=[ksg])
                        s.op('dve', lambda e: e.tensor_tensor(out=qs[:, :, W_], in0=qf[:, :, W_], in1=sg[:, :, W_], op=ALU.mult), reads=[kgq, ksg], writes=[kqs])
                        s.op('dve', lambda e: e.tensor_scalar_max(out=Bc[:, :, W_], in0=Bc[:, :, W_], scalar1=-80.0), reads=kBcs + [ksg], writes=kBcs)
                        s.op('act', lambda e: e.activation(out=lg[:, :, W_], in_=Bc[:, :, W_], func=AF.Exp, scale=-1.0), reads=kBcs + klgs, writes=klgs)
                        s.op('dve', lambda e: e.tensor_tensor(out=ks[:, :, W_], in0=kf[:, :, W_], in1=lg[:, :, W_], op=ALU.mult), reads=kkfs + klgs, writes=[kks])
                        return cx

                    def stage1(cx, jl, d=d, mask=mask, G=G, GT=GT):
                        cs_ = slice(jl * 32, (jl + 1) * 32)
                        qs, ks, vb, egl = cx['qs'], cx['ks'], cx['vb'], cx['egl']
                        (PT, kPT), (dS, kdS) = cx['PT'], cx['dS']
                        psA, kpA = bank(0, 2)
                        for h in range(4):
                            s.op('pe', lambda e, h=h: e.matmul(psA[0:32, h * 32:(h + 1) * 32], lhsT=ks[:, h, cs_], rhs=qs[:, h, cs_], start=True, stop=True),
                                 reads=[cx['kks'], cx['kqs']], writes=[kpA])
                        psT, kpT = bank(2, 4)
                        pbf = psT.bitcast(BF16)
                        for cc in range(4):
                            s.op('pe', lambda e, cc=cc: e.transpose(out=pbf[0:32, cc * 64:(cc + 1) * 64], in_=ks[:, cc, cs_], identity=idb[0:64, 0:64]),
                                 reads=[cx['kks'], k_idb], writes=[kpT])
                        kst, kkst = kst_t[jl % 2]
                        s.op('dve', lambda e: e.tensor_tensor(out=PT[:, jl], in0=psA[0:32, 0:128].rearrange("p (h t) -> p h t", t=32),
                                                              in1=mask.unsqueeze(1).to_broadcast([32, 4, 32]), op=ALU.mult),
                             reads=[kpA, khc], writes=[(kPT, jl)])
                        s.op('act', lambda e: e.copy(out=kst, in_=pbf[0:32, 0:256]), reads=[kpT], writes=[kkst])
                        psS, kpS = bank(4, 6)
                        for h in range(4):
                            s.op('pe', lambda e, h=h: e.matmul(psS[0:64, h * 64:(h + 1) * 64], lhsT=kst[:, h * 64:(h + 1) * 64], rhs=vb[:, jl, h * 64:(h + 1) * 64], start=True, stop=True),
                                 reads=[kkst, cx['kvb']], writes=[kpS])
                        s.op('dve', lambda e: e.tensor_tensor(out=dS[:, jl], in0=psS[0:64, 0:256].rearrange("p (c e) -> p c e", e=64),
                                                              in1=egl[:, :, jl].unsqueeze(2).to_broadcast([64, 4, 64]), op=ALU.mult),
                             reads=[kpS, cx['kegl']], writes=[(kdS, jl)])

                    def stage2(cx, jl, d=d, G=G, GT=GT):
                        cs_ = slice(jl * 32, (jl + 1) * 32)
                        qs, vb, egl = cx['qs'], cx['vb'], cx['egl']
                        (PT, kPT), (dS, kdS), (ob, kob) = cx['PT'], cx['dS'], cx['ob']
                        psO, kpO = bank(6, 8)
                        for h in range(4):
                            s.op('pe', lambda e, h=h: e.matmul(psO[0:32, h * 64:(h + 1) * 64], lhsT=PT[:, jl, h, :], rhs=vb[:, jl, h * 64:(h + 1) * 64],
                                                               start=(h == 0), stop=False, skip_group_check=True), reads=[(kPT, jl), cx['kvb']], writes=[kpO])
                            s.op('pe', lambda e, h=h: e.matmul(psO[0:32, h * 64:(h + 1) * 64], lhsT=qs[:, h, cs_], rhs=Sb[:, h, :],
                                                               start=False, stop=True, skip_group_check=True), reads=[cx['kqs'], kSb], writes=[kpO])
                        s.op('dve', lambda e: e.tensor_tensor(out=St, in0=S, in1=egl[:, :, jl].unsqueeze(2).to_broadcast([64, 4, 64]), op=ALU.mult),
                             reads=[kS, cx['kegl']], writes=[kSt])
                        s.op('dve', lambda e: e.tensor_tensor(out=S, in0=St, in1=dS[:, jl], op=ALU.add), reads=[kSt, (kdS, jl)], writes=[kS])
                        s.op('act', lambda e: e.copy(out=Sb, in_=S), reads=[kS], writes=[kSb])
                        return lambda: s.op('act', lambda e: e.copy(out=ob[:, jl, :], in_=psO[0:32, 0:256]), reads=[kpO], writes=[(kob, jl)])

                    def finish_group(cx, d=d, G=G, GT=GT):
                        t0, ti = cx['t0'], cx['ti']
                        ob, kob = cx['ob']
                        ok_ = [(kob, jl) for jl in range(G)]
                        if d == 0:
                            s.dma('pool', HF[t0:t0 + GT, :].rearrange("(j t) c -> t j c", t=32), ob[:, 0:G, :], reads=ok_, writes=[('HF', t0)], sk=kob)
                            return
                        s.dma('sp', oF[:, 0:G, :], HF[t0:t0 + GT, :].rearrange("(j t) c -> t j c", t=32), reads=[('HF', t0)], writes=[koF], sk=koF)
                        s.dma('sp', gg32[:, 0:G, :], TMV[t0:t0 + GT, 1024:1280].rearrange("(j t) c -> t j c", t=32), reads=[('TMV', ti)], writes=[kgg], sk=kgg)
                        s.op('pool', lambda e: e.tensor_tensor(out=ob[:, 0:G, :], in0=ob[:, 0:G, :], in1=oF[:, 0:G, :], op=ALU.add), reads=ok_ + [koF], writes=ok_)
                        n4 = G * 4
                        s.op('pool', lambda e: e.tensor_tensor(out=sqx[:, 0:G, :], in0=ob[:, 0:G, :], in1=ob[:, 0:G, :], op=ALU.mult), reads=ok_, writes=[ksqx])
                        s.op('dve', lambda e: e.tensor_reduce(out=s2[:, 0:n4], in_=sqx[:, 0:G, :].rearrange("p g (h e) -> p (g h) e", e=64), axis=AX.X, op=ALU.add),
                             reads=[ksqx], writes=[ks2])
                        s.op('act', lambda e: e.activation(out=s2[:, 0:n4], in_=s2[:, 0:n4], func=AF.Sqrt, bias=epsln[0:32, 1:2], scale=1.0 / 64), reads=[ks2, k_eps], writes=[ks2])
                        s.op('dve', lambda e: e.reciprocal(out=s2[:, 0:n4], in_=s2[:, 0:n4]), reads=[ks2], writes=[ks2])
                        X4 = ob[:, 0:G, :].rearrange("p g (h e) -> p (g h) e", e=64)
                        s.op('dve', lambda e: e.tensor_tensor(out=X4, in0=X4, in1=s2[:, 0:n4].unsqueeze(2).to_broadcast([32, n4, 64]), op=ALU.mult),
                             reads=ok_ + [ks2], writes=ok_)
                        s.op('pool', lambda e: e.tensor_tensor(out=ob[:, 0:G, :], in0=ob[:, 0:G, :], in1=ng.unsqueeze(1).to_broadcast([32, G, 256]), op=ALU.mult),
                             reads=ok_ + [kng], writes=ok_)
                        s.op('act', lambda e: e.activation(out=sqx[:, 0:G, :], in_=gg32[:, 0:G, :], func=AF.Silu), reads=[kgg, ksqx, ks2], writes=[ksqx])
                        s.op('dve', lambda e: e.tensor_tensor(out=xo[:, 0:G, :], in0=ob[:, 0:G, :], in1=sqx[:, 0:G, :], op=ALU.mult), reads=ok_ + [ksqx], writes=[kxo])
                        ps, kp = bank(0, 2)
                        pbf = ps.bitcast(BF16)
                        for c in range(2):
                            for jl in range(G):
                                s.op('pe', lambda e, c=c, jl=jl: e.transpose(
                                    out=pbf[:, c * 512 + jl * 32:c * 512 + (jl + 1) * 32], in_=xo[:, jl, c * 128:(c + 1) * 128], identity=idb[0:32, 0:32]),
                                    reads=[kxo, k_idb], writes=[kp])
                        for c in range(2):
                            s.op('dve', lambda e, c=c: e.tensor_copy(out=mx[:, c, 0:GT], in_=pbf[:, c * 512:c * 512 + GT]), reads=[kp], writes=[(kmx, c)])
                        s.dma('pool', MIXT.rearrange("(c p) t -> p c t", p=128)[:, 6:8, t0:t0 + GT], mx[:, :, 0:GT],
                              reads=[(kmx, 0), (kmx, 1)], writes=[('MIXT_G', t0)], sk=kmx)

                    cur = load_group(gorder[0])
                    for jl in corder:
                        stage1(cur, jl)
                    for gidx, gi in enumerate(gorder):
                        nxt = load_group(gorder[gidx + 1]) if gidx + 1 < len(gorder) else None
                        pend = None
                        for jl in corder:
                            p2 = stage2(cur, jl)
                            if pend is not None:
                                pend()
                            pend = p2
                            if nxt is not None and not HG_NOINT:
                                stage1(nxt, jl)
                        pend()
                        if nxt is not None and HG_NOINT:
                            for jl in corder:
                                stage1(nxt, jl)
                        finish_group(cur)
                        cur = nxt
                    if not samp:
                        s.dma('pool', sview(S_out[j, l, d]), S, reads=[kS], sk=(kS, 'o'))
            s.barrier()
            mem.release(m0_)

        phase_init()
        for l in range(NL):
            phase_mod(l)
            if 'p1' in PH:
                phase_p1(l)
            if 'mla' in PH:
                phase_mla(l)
            if 'mlstm' in PH:
                phase_mlstm(l)
            if 'hgrn' in PH:
                phase_hgrn(l)
            if 'dense' in PH:
                phase_p3(l)
                phase_p3b(l)
                phase_p4(l)
        phase_final()
        s.emit()
    return nc


def _bf(x):
    return np.ascontiguousarray(x, dtype=np.float32)


def make_in_maps(inputs, cfg):
    NL = cfg.get('n_layers', DEPTH)
    NST = cfg.get('n_sample_tiles', 8)
    NPR = cfg.get('n_prompts', 4)
    cores = cfg.get('cores', list(range(8)))
    g = {k: np.asarray(v) for k, v in inputs.items()}
    w1 = _bf(g['w_in'][:NL][:, :, W1_COLS])
    b1 = g['b_in'][:NL][:, W1_COLS]
    b1fm = np.zeros((NL, 128, NG), np.float32)
    for gi, (c0, M, _, _) in enumerate(FM_GROUPS):
        b1fm[:, :M, gi] = b1[:, c0:c0 + M]
    b1tm = _bf(np.broadcast_to(b1[:, None, NFM:], (NL, 128, NTM)))
    b_modT = _bf(g['b_mod'][:NL].reshape(NL, 48, 128).transpose(0, 2, 1))
    lnp = _bf(np.stack([g[k][:NL].reshape(NL, NKC, 128).transpose(0, 2, 1) for k in ('ln1_g', 'ln1_b', 'ln2_g', 'ln2_b')], axis=2))
    shared = dict(w_mod=_bf(g['w_mod'][:NL]), b_modT=b_modT, w1=w1, b1fm=b1fm, b1tm=b1tm, w_out=_bf(g['w_out'][:NL]),
                  lnp=lnp, w_ffn_in=_bf(g['w_ffn_in'][:NL]), w_ffn_out=_bf(g['w_ffn_out'][:NL]),
                  identf=np.eye(128, dtype=np.float32))
    wuq = g['w_uq'][:NL]
    wqp = np.zeros_like(wuq)
    for h in range(8):
        wqp[:, :, h * 96 + 64:h * 96 + 96] = wuq[:, :, h * 96 + 64 + _PERM]
    wukv = g['w_ukv'][:NL]
    wk = np.zeros((NL, 128, 768), np.float32)
    wv = np.zeros((NL, 128, 512), np.float32)
    for h in range(8):
        wk[:, :, h * 96:h * 96 + 64] = wukv[:, :, h * 128:h * 128 + 64]
        wv[:, :, h * 64:(h + 1) * 64] = wukv[:, :, h * 128 + 64:h * 128 + 128]
    e96 = np.zeros((32, 96), np.float32)
    e96[np.arange(32), 64 + np.arange(32)] = 1.0
    qkn = _bf(np.stack([g['mla_q_norm'][:NL, :128], g['mla_q_norm'][:NL, 128:], g['mla_kv_norm'][:NL]], axis=-1))
    pos = np.arange(4096)
    freqs = 10000.0 ** (-np.arange(8, dtype=np.float64) * 0.125)
    ar = (pos // 64)[:, None] * freqs
    ac = (pos % 64)[:, None] * freqs
    ang = np.concatenate([ar, ar, ac, ac], -1)
    cosT = np.cos(ang).T.astype(np.float32)
    sinT = (np.sin(ang) * _ROT_SIGN).T.astype(np.float32)
    cos96 = np.ones((96, 4096), np.float32)
    sin96 = np.zeros((96, 4096), np.float32)
    cos96[64:] = cosT
    sin96[64:] = sinT
    shared.update(wq=_bf(wuq), wqp=_bf(wqp), wk=wk, wv=wv, e96=e96, qkn=qkn, cos96=cos96, sin96=sin96,
                  kcos=_bf(cosT), ksin=_bf(sinT))
    mconst = np.zeros((64, 4, 64), np.float32)
    ii = np.arange(64)
    mconst[:, 0, :] = (ii[None, :] >= ii[:, None])
    mconst[:, 1, :] = (ii[:, None] >= ii[None, :])
    mconst[:, 2, :] = np.eye(64)[::-1]
    mconst[0:4, 3, 0:4] = np.eye(4)[::-1]
    shared.update(mconst=mconst, mlng=_bf(np.broadcast_to(g['mlstm_norm'][:NL, None, :], (NL, 64, 256))))
    hconst = np.zeros((32, 2, 32), np.float32)
    i32 = np.arange(32)
    hconst[:, 0, :] = (i32[None, :] >= i32[:, None])
    hconst[:, 1, :] = (i32[:, None] >= i32[None, :])
    lbl = _bf(g['hgrn_lb_logits'].reshape(4, 4, 64).transpose(2, 1, 0))
    shared.update(hconst=hconst, lbl=lbl, hgng=_bf(np.broadcast_to(g['hgrn_norm'][:NL, None, :], (NL, 32, 256))))
    maps = []
    for ci in cores:
        b = ci // 2
        parts = []
        if NST:
            parts.append(g['x_sample'][b, :NST * 512])
        for j in range(NPR):
            parts.append(g['x_prompt'][4 * ci + j])
        xin = _bf(np.concatenate(parts, axis=0))
        cv = np.stack([g['c'][b], g['c_ctx']], axis=-1)
        cvT = _bf(cv.reshape(NKC, 128, 2).transpose(1, 0, 2))
        m = dict(shared)
        m.update(stC=_bf(g['state_mlstm_C'][b, :NL]), stn=_bf(g['state_mlstm_n'][b, :NL]), stm=_bf(g['state_mlstm_m'][b, :NL]))
        m.update(stS=_bf(g['state_hgrn_S'][b, :NL]))
        m.update(xin=xin, cvT=cvT, cckv=_bf(g['cache_mla_ckv'][b, :NL]), ckpe=_bf(g['cache_mla_kpe'][b, :NL]))
        maps.append(m)
    return maps


def kernel(**inputs):
    cfg = {}
    nc = build_program(cfg)
    maps = make_in_maps(inputs, cfg)
    res = run_bass_kernel_spmd(nc, maps, core_ids=list(range(8)))
    r = res.results
    cat = lambda k: np.ascontiguousarray(np.concatenate([r[i][k] for i in range(8)], axis=0), dtype=np.float32)
    y_prompt = np.concatenate([r[i]['y_out'][4096:].reshape(4, 256, D) for i in range(8)], axis=0).astype(np.float32)
    y_sample = np.stack([r[2 * b]['y_out'][:4096] for b in range(4)], axis=0).astype(np.float32)
    return (y_prompt, y_sample, cat('ckv_out'), cat('kpe_out'), cat('C_out'), cat('n_out'), cat('m_out'), cat('S_out'))
```

```python
import math
from contextlib import ExitStack
import numpy as np
import concourse.bass as bass
import concourse.mybir as mybir
from concourse.bass_utils import run_bass_kernel_spmd

F32 = mybir.dt.float32
BF16 = mybir.dt.bfloat16
AF = mybir.ActivationFunctionType
ALU = mybir.AluOpType
AX = mybir.AxisListType

D = 1024
DEPTH = 4
NKC = 8
FF = 2816
NFC = 22
ALPHA = (2 * DEPTH) ** 0.25
EPS = 1e-6
EPS_LN = EPS / (ALPHA * ALPHA)
ATT_SCALE = 96 ** -0.5

_IN_SIZES = (256, 128, 32, 256, 256, 256, 256, 4, 4, 4, 4, 256, 256, 256, 256, 256)
_OFF = np.cumsum((0,) + _IN_SIZES)
_NAMES = ['cq', 'ckv', 'kpe', 'mq', 'mk', 'mv', 'mo', 'mi_f', 'mi_b', 'mf_f', 'mf_b', 'gq', 'gf_f', 'gf_b', 'gi', 'gg']
_COL = {n: np.arange(_OFF[i], _OFF[i + 1]) for i, n in enumerate(_NAMES)}
_PERM = np.concatenate([np.arange(8, 16), np.arange(0, 8), np.arange(24, 32), np.arange(16, 24)])
_ROT_SIGN = np.concatenate([-np.ones(8), np.ones(8), -np.ones(8), np.ones(8)]).astype(np.float32)
FM_GROUPS = []
_fm_cols = []


def _add_fm(cols, dst, r0):
    c0 = sum(len(c) for c in _fm_cols)
    FM_GROUPS.append((c0, len(cols), dst, r0))
    _fm_cols.append(np.asarray(cols))


_add_fm(_COL['cq'][:128], 'CQT', 0)
_add_fm(_COL['cq'][128:], 'CQT', 128)
_add_fm(_COL['ckv'], 'CKVT', 0)
_add_fm(_COL['kpe'], 'KPET', 0)
_add_fm(_COL['kpe'][_PERM], 'KPEP', 0)
for _n, _d in (('mq', 'MQT'), ('mk', 'MKT'), ('gq', 'GQT'), ('gf_f', 'GFF'), ('gf_b', 'GFB')):
    _add_fm(_COL[_n][:128], _d, 0)
    _add_fm(_COL[_n][128:], _d, 128)
for _n, _d in (('mi_f', 'MIF'), ('mi_b', 'MIB'), ('mf_f', 'MFF'), ('mf_b', 'MFB')):
    _add_fm(_COL[_n], _d, 0)
NFM = sum(len(c) for c in _fm_cols)
_tm_cols = [_COL['mv'], _COL['mo'], _COL['mk'], _COL['gi'], _COL['gg']]
NTM = 1280
TM_GROUPS = [(0, 512), (512, 512), (1024, 256)]
W1_COLS = np.concatenate(_fm_cols + _tm_cols)
NW1 = len(W1_COLS)
NG = len(FM_GROUPS)
FM_SCR = {'CQT': 256, 'CKVT': 128, 'KPET': 32, 'KPEP': 32, 'MQT': 256, 'MKT': 256, 'GQT': 256, 'GFF': 256,
          'GFB': 256, 'MIF': 4, 'MIB': 4, 'MFF': 4, 'MFB': 4}


class Sched:
    EPOCH = 40000

    def __init__(self, nc, stack):
        self.nc = nc
        self.stack = stack
        self.eng = {'pe': nc.tensor, 'act': nc.scalar, 'dve': nc.vector, 'pool': nc.gpsimd, 'sp': nc.sync}
        self.prog = {k: [] for k in self.eng}
        self.seq = {k: 0 for k in self.eng}
        self.esems = {k: [] for k in self.eng}
        self.dsem = {}
        self.free_dsems = {}
        self.lastw = {}
        self.readers = {}
        self.waited = {}
        self.nsem = 0

    def _newsem(self, name):
        self.nsem += 1
        return self.stack.enter_context(self.nc.semaphore(name))

    def _deps(self, engine, reads, writes):
        deps = {}

        def add(ev):
            k = id(ev[0])
            if k not in deps or deps[k][1] < ev[1]:
                deps[k] = ev
        for k in reads:
            if k in self.lastw:
                add(self.lastw[k])
        for k in writes:
            if k in self.lastw:
                add(self.lastw[k])
            for ev in self.readers.get(k, {}).values():
                add(ev)
        waits = []
        own = set(id(s) for s in self.esems[engine]) if engine == 'pe' else set()
        for k, (sem, val) in deps.items():
            if k in own:
                continue
            wk = (engine, k)
            if self.waited.get(wk, 0) < val:
                self.waited[wk] = val
                waits.append((sem, val))
        return waits

    def _commit(self, ev, reads, writes):
        for k in writes:
            self.lastw[k] = ev
            self.readers[k] = {}
        for k in reads:
            self.readers.setdefault(k, {})[id(ev[0])] = ev

    def op(self, engine, fn, reads=(), writes=()):
        waits = self._deps(engine, reads, writes)
        n = self.seq[engine]
        e = n // self.EPOCH
        while len(self.esems[engine]) <= e:
            self.esems[engine].append(self._newsem(f"e_{engine}_{len(self.esems[engine])}"))
        sem = self.esems[engine][e]
        val = n % self.EPOCH + 1
        self.seq[engine] = n + 1
        self.prog[engine].append((waits, fn, sem, 1))
        self._commit((sem, val), reads, writes)

    def dma(self, queue, out, in_, reads=(), writes=(), sk=None, **kw):
        self.dma_group(queue, [(out, in_)], reads, writes, sk, **kw)

    def dma_group(self, queue, pairs, reads=(), writes=(), sk=None, **kw):
        assert sk is not None
        sk = (queue, sk)
        if sk not in self.dsem:
            fl = self.free_dsems.setdefault(queue, [])
            self.dsem[sk] = fl.pop() if fl else [self._newsem(f"d_{self.nsem}"), 0]
        ent = self.dsem[sk]
        waits = self._deps(queue, reads, writes)
        for i, (o, a) in enumerate(pairs):
            ent[1] += 16
            self.prog[queue].append((waits if i == 0 else [],
                                     (lambda eng, o=o, a=a: eng.dma_start(out=o, in_=a, **kw)), ent[0], 16))
        self._commit((ent[0], ent[1]), reads, writes)

    def barrier(self):
        evs = []
        for e, sems in self.esems.items():
            n = self.seq[e]
            if n > 0:
                evs.append((sems[(n - 1) // self.EPOCH], (n - 1) % self.EPOCH + 1))
        for sk, (sem, cnt) in self.dsem.items():
            if cnt > 0:
                evs.append((sem, cnt))
        for fl in self.free_dsems.values():
            for sem, cnt in fl:
                if cnt > 0:
                    evs.append((sem, cnt))
        for engine in self.eng:
            waits = []
            for sem, val in evs:
                wk = (engine, id(sem))
                if self.waited.get(wk, 0) < val:
                    self.waited[wk] = val
                    waits.append((sem, val))
            if waits:
                self.prog[engine].append((waits, None, None, 0))
        for (q, _), ent in self.dsem.items():
            self.free_dsems.setdefault(q, []).append(ent)
        self.dsem = {}
        self.lastw = {}
        self.readers = {}

    def emit(self):
        self.barrier()
        nc = self.nc
        with nc.Block() as block:
            def run(name):
                def f(eng):
                    for waits, fn, sem, amt in self.prog[name]:
                        for ws, wv in waits:
                            eng.wait_ge(ws, wv)
                        if fn is not None:
                            fn(eng).then_inc(sem, amt)
                return f
            block.sync(run('sp'))
            block.tensor(run('pe'))
            block.scalar(run('act'))
            block.vector(run('dve'))
            block.gpsimd(run('pool'))


class Mem:
    def __init__(self, big, words):
        self.big = big
        self.words = words
        self.top = 0
        self.n = 0

    def mark(self):
        return self.top

    def release(self, m):
        self.top = m

    def alloc(self, shape, dt=F32, name='t'):
        P = shape[0]
        n = int(np.prod(shape[1:]))
        w = n if dt == F32 else (n + 1) // 2
        w = (w + 15) // 16 * 16
        assert self.top + w <= self.words, f"SBUF overflow allocating {name} {shape}"
        a = self.big[0:P, self.top:self.top + w]
        self.top += w
        if dt != F32:
            a = a.bitcast(dt)
        a = a[:, 0:n]
        if len(shape) == 3:
            a = a.rearrange("p (a b) -> p a b", a=shape[1], b=shape[2])
        elif len(shape) == 4:
            a = a.rearrange("p (a b c) -> p a b c", a=shape[1], b=shape[2], c=shape[3])
        self.n += 1
        return a, f"{name}#{self.n}"


def build_program(cfg):
    NL = cfg.get('n_layers', DEPTH)
    NST = cfg.get('n_sample_tiles', 8)
    NPR = cfg.get('n_prompts', 4)
    DBG = cfg.get('debug', [])
    PH = cfg.get('phases', ['p1', 'mla', 'mlstm', 'hgrn', 'dense'])
    MIXIN = cfg.get('mix_input', False)
    HG_NOINT = cfg.get('hg_noint', False)
    TS = NST * 512
    TT = TS + NPR * 256
    NTILE = TT // 512
    nc = bass.Bass("TRN2", target_bir_lowering=False)

    def din(name, shape, dt=F32):
        return nc.dram_tensor(name, list(shape), dt, kind="ExternalInput").ap()

    def dout(name, shape, dt=F32):
        return nc.dram_tensor(name, list(shape), dt, kind="ExternalOutput").ap()

    def dscr(name, shape, dt=F32):
        kind = "ExternalOutput" if name in DBG else "Internal"
        return nc.dram_tensor(name, list(shape), dt, kind=kind).ap()

    xin = din("xin", [TT, D])
    cvT = din("cvT", [128, NKC, 2])
    w_mod = din("w_mod", [NL, D, 6 * D])
    b_modT = din("b_modT", [NL, 128, 48])
    w1 = din("w1", [NL, D, NW1])
    b1fm = din("b1fm", [NL, 128, NG])
    b1tm = din("b1tm", [NL, 128, NTM])
    w_out = din("w_out", [NL, D, D])
    lnp = din("lnp", [NL, 128, 4, NKC])
    w_ffn_in = din("w_ffn_in", [NL, D, 2 * FF])
    w_ffn_out = din("w_ffn_out", [NL, FF, D])
    identf = din("identf", [128, 128])
    wq_d = din("wq", [NL, 256, 768])
    wqp_d = din("wqp", [NL, 256, 768])
    wk_d = din("wk", [NL, 128, 768])
    wv_d = din("wv", [NL, 128, 512])
    e96_d = din("e96", [32, 96])
    qkn_d = din("qkn", [NL, 128, 3])
    cos96_d = din("cos96", [96, 4096])
    sin96_d = din("sin96", [96, 4096])
    kcos_d = din("kcos", [32, 4096])
    ksin_d = din("ksin", [32, 4096])
    cckv_d = din("cckv", [NL, 512, 128])
    ckpe_d = din("ckpe", [NL, 512, 32])
    mconst_d = din("mconst", [64, 4, 64])
    mlng_d = din("mlng", [NL, 64, 256])
    stC_d = din("stC", [NL, 2, 4, 64, 64])
    stn_d = din("stn", [NL, 2, 4, 64])
    stm_d = din("stm", [NL, 2, 4])
    C_out = dout("C_out", [max(NPR, 1), NL, 2, 4, 64, 64])
    n_out = dout("n_out", [max(NPR, 1), NL, 2, 4, 64])
    m_out = dout("m_out", [max(NPR, 1), NL, 2, 4])
    HF = dscr("HF", [TT, 256])
    hconst_d = din("hconst", [32, 2, 32])
    hgng_d = din("hgng", [NL, 32, 256])
    lbl_d = din("lbl", [64, 4, 4])
    stS_d = din("stS", [NL, 2, 4, 64, 64])
    S_out = dout("S_out", [max(NPR, 1), NL, 2, 4, 64, 64])
    ckv_out = dout("ckv_out", [max(NPR, 1), NL, 256, 128])
    kpe_out = dout("kpe_out", [max(NPR, 1), NL, 256, 32])
    y_out = dout("y_out", [TT, D])

    XT = dscr("XT", [D, TT])
    MIXT = din("MIXT", [D, TT], BF16) if MIXIN else dscr("MIXT", [D, TT], BF16)
    UT = dscr("UT", [FF, TT], BF16)
    SCR = {k: dscr(k, [r, TT]) for k, r in FM_SCR.items()}
    TMV = dscr("TMV", [TT, NTM])

    with ExitStack() as st:
        WORDS = 47 * 1024
        big = st.enter_context(nc.sbuf_tensor("big", [128, WORDS], F32))
        psb = [st.enter_context(nc.psum_tensor(f"ps{i}", [128, 512], F32)) for i in range(8)]
        s = Sched(nc, st)
        mem = Mem(big, WORDS)
        psi = {}

        def bank(lo=0, hi=8):
            c = psi.get((lo, hi), 0)
            psi[(lo, hi)] = c + 1
            i = lo + c % (hi - lo)
            return psb[i], ('ps', i)

        idf, k_idf = mem.alloc([128, 128], F32, 'idf')
        idb, k_idb = mem.alloc([128, 128], BF16, 'idb')
        onesf, k_onesf = mem.alloc([128, 128], F32, 'onesf')
        modT, k_mod = mem.alloc([128, 48, 2], F32, 'modT')
        lnt, k_lnt = mem.alloc([128, 4, NKC], F32, 'lnt')
        s.dma('sp', idf, identf, writes=[k_idf], sk='idf')
        s.op('dve', lambda e: e.tensor_copy(out=idb, in_=idf), reads=[k_idf], writes=[k_idb])
        s.op('pool', lambda e: e.memset(onesf, 1.0), writes=[k_onesf])
        epsln, k_eps = mem.alloc([128, 2], F32, 'epsln')
        s.op('pool', lambda e: e.memset(epsln[:, 0:1], EPS_LN), writes=[k_eps])
        s.op('pool', lambda e: e.memset(epsln[:, 1:2], EPS), writes=[k_eps])
        base_mark = mem.mark()

        def load_w_bf16(dram2d, dst, kdst, nkc, ncols, stage):
            i = 0
            cw = stage[0][0].shape[1]
            for kc in range(nkc):
                for c0 in range(0, ncols, cw):
                    w = min(cw, ncols - c0)
                    sa, sk_ = stage[i % len(stage)]
                    i += 1
                    s.dma('sp', sa[:, 0:w], dram2d[kc * 128:(kc + 1) * 128, c0:c0 + w], writes=[sk_], sk=sk_)
                    ce = ('pool', 'dve', 'act')[i % 3]
                    if ce == 'act':
                        s.op('act', lambda e, kc=kc, c0=c0, w=w, sa=sa: e.copy(out=dst[:, kc, c0:c0 + w], in_=sa[:, 0:w]), reads=[sk_], writes=[kdst])
                    else:
                        s.op(ce, lambda e, kc=kc, c0=c0, w=w, sa=sa: e.tensor_copy(out=dst[:, kc, c0:c0 + w], in_=sa[:, 0:w]), reads=[sk_], writes=[kdst])

        def fm_view(dram2d, t0, n):
            return dram2d.rearrange("(c p) t -> p c t", p=128)[:, :, t0:t0 + n]

        def phase_init():
            m0 = mem.mark()
            xin_t = [mem.alloc([128, 4, D], F32, 'xin_t') for _ in range(2)]
            xT_t = [mem.alloc([128, NKC, 512], F32, 'xT_t') for _ in range(2)]
            for i in range(NTILE):
                t0 = i * 512
                xa, kx = xin_t[i % 2]
                xo, ko = xT_t[i % 2]
                s.dma('sp', xa, xin[t0:t0 + 512, :].rearrange("(b p) c -> p b c", p=128), writes=[kx], sk=kx)
                for kc in range(NKC):
                    ps, kp = bank()
                    for b in range(4):
                        s.op('pe', lambda e, ps=ps, b=b, kc=kc, xa=xa: e.transpose(
                            out=ps[:, b * 128:(b + 1) * 128], in_=xa[:, b, kc * 128:(kc + 1) * 128], identity=idf),
                            reads=[kx, k_idf], writes=[kp])
                    eng = 'dve' if kc % 2 == 0 else 'act'
                    if eng == 'dve':
                        s.op('dve', lambda e, ps=ps, kc=kc, xo=xo: e.tensor_copy(out=xo[:, kc, :], in_=ps[:, :]),
                             reads=[kp], writes=[(ko, kc)])
                    else:
                        s.op('act', lambda e, ps=ps, kc=kc, xo=xo: e.copy(out=xo[:, kc, :], in_=ps[:, :]),
                             reads=[kp], writes=[(ko, kc)])
                s.dma('pool', fm_view(XT, t0, 512), xo, reads=[(ko, kc) for kc in range(NKC)], writes=[('XT', i)], sk=ko)
            s.barrier()
            mem.release(m0)

        def phase_final():
            m0 = mem.mark()
            xT_t = [mem.alloc([128, NKC, 512], F32, 'xT_f') for _ in range(2)]
            y_t = [mem.alloc([128, 4, D], F32, 'y_t') for _ in range(2)]
            for i in range(NTILE):
                t0 = i * 512
                xa, kx = xT_t[i % 2]
                ya, ky = y_t[i % 2]
                s.dma('sp', xa, fm_view(XT, t0, 512), reads=[('XT', i)], writes=[kx], sk=kx)
                for b in range(4):
                    for half in range(2):
                        ps, kp = bank()
                        for q in range(4):
                            kc = half * 4 + q
                            s.op('pe', lambda e, ps=ps, q=q, kc=kc, b=b, xa=xa: e.transpose(
                                out=ps[:, q * 128:(q + 1) * 128], in_=xa[:, kc, b * 128:(b + 1) * 128], identity=idf),
                                reads=[kx, k_idf], writes=[kp])
                        if half == 0:
                            s.op('dve', lambda e, ps=ps, b=b, ya=ya: e.tensor_copy(out=ya[:, b, 0:512], in_=ps[:, :]),
                                 reads=[kp], writes=[(ky, b, 0)])
                        else:
                            s.op('act', lambda e, ps=ps, b=b, ya=ya: e.copy(out=ya[:, b, 512:1024], in_=ps[:, :]),
                                 reads=[kp], writes=[(ky, b, 1)])
                s.dma('pool', y_out[t0:t0 + 512, :].rearrange("(b p) c -> p b c", p=128), ya,
                      reads=[(ky, b, h) for b in range(4) for h in range(2)], sk=ky)
            s.barrier()
            mem.release(m0)

        def phase_mod(l):
            m0 = mem.mark()
            cv, kcv = mem.alloc([128, NKC, 2], F32, 'cv')
            sil, ksil = mem.alloc([128, NKC, 2], F32, 'sil')
            bm, kbm = mem.alloc([128, 48], F32, 'bm')
            stage = [mem.alloc([128, 6 * D], F32, 'wm') for _ in range(2)]
            s.dma('sp', cv, cvT, writes=[kcv], sk=kcv)
            s.dma('sp', bm, b_modT[l], writes=[kbm], sk=kbm)
            s.dma('sp', lnt, lnp[l], writes=[k_lnt], sk=k_lnt)
            s.op('act', lambda e: e.activation(out=sil, in_=cv, func=AF.Silu), reads=[kcv], writes=[ksil])
            ps, kp = bank()
            for kc in range(NKC):
                sa, ks = stage[kc % 2]
                s.dma('sp', sa, w_mod[l, kc * 128:(kc + 1) * 128, :], writes=[ks], sk=ks)
                for g in range(48):
                    s.op('pe', lambda e, sa=sa, g=g, kc=kc: e.matmul(
                        ps[:, 2 * g:2 * g + 2], lhsT=sa[:, g * 128:(g + 1) * 128], rhs=sil[:, kc, :],
                        start=(kc == 0 and g == 0), stop=(kc == NKC - 1), skip_group_check=True),
                        reads=[ks, ksil], writes=[kp])
            s.op('dve', lambda e: e.tensor_tensor(
                out=modT, in0=ps[:, 0:96].rearrange("p (g v) -> p g v", v=2),
                in1=bm.unsqueeze(2).to_broadcast([128, 48, 2]), op=ALU.add), reads=[kp, kbm], writes=[k_mod])
            for lo, add in ((8, True), (16, False), (32, True), (40, False)):
                if add:
                    s.op('dve', lambda e, lo=lo: e.tensor_scalar_add(out=modT[:, lo:lo + 8, :], in0=modT[:, lo:lo + 8, :], scalar1=1.0),
                         reads=[k_mod], writes=[k_mod])
                else:
                    s.op('dve', lambda e, lo=lo: e.tensor_scalar_mul(out=modT[:, lo:lo + 8, :], in0=modT[:, lo:lo + 8, :], scalar1=1.0 / ALPHA),
                         reads=[k_mod], writes=[k_mod])
            s.barrier()
            mem.release(m0)

        SH1, SC1, G1, SH2, SC2, G2 = 0, 8, 16, 24, 32, 40

        def tile_v(i):
            return 0 if i < NST else 1

        def phase_p1(l):
            m0 = mem.mark()
            w1t, kw1 = mem.alloc([128, NKC, NW1], BF16, 'w1t')
            bfm, kbfm = mem.alloc([128, NG], F32, 'bfm')
            btm, kbtm = mem.alloc([128, NTM], F32, 'btm')
            stage = [mem.alloc([128, 2048], F32, 'wst') for _ in range(3)]
            xT_t = [mem.alloc([128, NKC, 512], F32, 'xT1') for _ in range(2)]
            hT_t = [mem.alloc([128, NKC, 512], BF16, 'hT') for _ in range(2)]
            fo_t = [mem.alloc([128, 512], F32, 'fo') for _ in range(4)]
            to_t = [mem.alloc([128, 4, NTM], F32, 'to') for _ in range(1)]
            s.dma('sp', bfm, b1fm[l], writes=[kbfm], sk=kbfm)
            s.dma('sp', btm, b1tm[l], writes=[kbtm], sk=kbtm)
            load_w_bf16(w1[l], w1t, kw1, NKC, NW1, stage)
            nfo = 0
            for i in range(NTILE):
                t0 = i * 512
                v = tile_v(i)
                xa, kx = xT_t[i % 2]
                ha, kh = hT_t[i % 2]
                s.dma('sp', xa, fm_view(XT, t0, 512), reads=[('XT', i)], writes=[kx], sk=kx)
                for kc in range(NKC):
                    s.op('dve', lambda e, kc=kc, xa=xa, ha=ha, v=v: e.tensor_scalar(
                        out=ha[:, kc, :], in0=xa[:, kc, :], scalar1=modT[:, SC1 + kc, v:v + 1], scalar2=modT[:, SH1 + kc, v:v + 1],
                        op0=ALU.mult, op1=ALU.add), reads=[kx, k_mod], writes=[(kh, kc)])
                for g, (c0, M, dst, r0) in enumerate(FM_GROUPS):
                    ps, kp = bank(0, 4)
                    for kc in range(NKC):
                        s.op('pe', lambda e, ps=ps, kc=kc, c0=c0, M=M, ha=ha: e.matmul(
                            ps[0:M, :], lhsT=w1t[:, kc, c0:c0 + M], rhs=ha[:, kc, :], start=(kc == 0), stop=(kc == NKC - 1)),
                            reads=[kw1, (kh, kc)], writes=[kp])
                    fo, kfo = fo_t[nfo % 4]
                    nfo += 1
                    s.op('act', lambda e, ps=ps, M=M, g=g, fo=fo: e.activation(
                        out=fo[0:M, :], in_=ps[0:M, :], func=AF.Identity, bias=bfm[0:M, g:g + 1], scale=1.0),
                        reads=[kp, kbfm], writes=[kfo])
                    s.dma('pool', SCR[dst][r0:r0 + M, t0:t0 + 512], fo[0:M, :], reads=[kfo], writes=[(dst, r0, i)], sk=kfo)
                ta, kt = to_t[0]
                for b in range(4):
                    for (c0, N) in TM_GROUPS:
                        ps, kp = bank(4, 8)
                        for kc in range(NKC):
                            s.op('pe', lambda e, ps=ps, kc=kc, c0=c0, N=N, b=b, ha=ha: e.matmul(
                                ps[:, 0:N], lhsT=ha[:, kc, b * 128:(b + 1) * 128], rhs=w1t[:, kc, NFM + c0:NFM + c0 + N],
                                start=(kc == 0), stop=(kc == NKC - 1)), reads=[kw1, (kh, kc)], writes=[kp])
                        s.op('dve', lambda e, ps=ps, c0=c0, N=N, b=b, ta=ta: e.tensor_tensor(
                            out=ta[:, b, c0:c0 + N], in0=ps[:, 0:N], in1=btm[:, c0:c0 + N], op=ALU.add),
                            reads=[kp, kbtm], writes=[(kt, b, c0)])
                s.dma('pool', TMV[t0:t0 + 512, :].rearrange("(b p) c -> p b c", p=128), ta,
                      reads=[(kt, b, c0) for b in range(4) for (c0, N) in TM_GROUPS], writes=[('TMV', i)], sk=kt)
            s.barrier()
            mem.release(m0)

        def ln_apply(ya, ky, xo, ko, l, gi, bi, tmp):
            (sq, ksq), (mean, kmean), (rstd, krstd), (msq, kmsq) = tmp
            p1, kp1 = bank(4, 6)
            p2, kp2 = bank(6, 8)
            for kc in range(NKC):
                s.op('pe', lambda e, kc=kc: e.matmul(p1[:, :], lhsT=onesf, rhs=ya[:, kc, :], start=(kc == 0), stop=(kc == NKC - 1)),
                     reads=[k_onesf, (ky, kc)], writes=[kp1])
            for kc in range(NKC):
                s.op('act', lambda e, kc=kc: e.activation(out=sq[:, kc % 2, :], in_=ya[:, kc, :], func=AF.Square),
                     reads=[(ky, kc)], writes=[(ksq, kc % 2)])
                s.op('pe', lambda e, kc=kc: e.matmul(p2[:, :], lhsT=onesf, rhs=sq[:, kc % 2, :], start=(kc == 0), stop=(kc == NKC - 1)),
                     reads=[k_onesf, (ksq, kc % 2)], writes=[kp2])
            s.op('dve', lambda e: e.tensor_scalar_mul(out=mean, in0=p1[:, :], scalar1=1.0 / D), reads=[kp1], writes=[kmean])
            s.op('dve', lambda e: e.tensor_tensor(out=msq, in0=mean, in1=mean, op=ALU.mult), reads=[kmean], writes=[kmsq])
            s.op('dve', lambda e: e.scalar_tensor_tensor(out=msq, in0=p2[:, :], scalar=1.0 / D, in1=msq, op0=ALU.mult, op1=ALU.subtract),
                 reads=[kp2, kmsq], writes=[kmsq])
            s.op('act', lambda e: e.activation(out=rstd, in_=msq, func=AF.Sqrt, bias=epsln[:, 0:1], scale=1.0), reads=[kmsq, k_eps], writes=[krstd])
            s.op('dve', lambda e: e.reciprocal(out=rstd, in_=rstd), reads=[krstd], writes=[krstd])
            for kc in range(NKC):
                s.op('dve', lambda e, kc=kc: e.tensor_tensor(out=ya[:, kc, :], in0=ya[:, kc, :], in1=mean, op=ALU.subtract),
                     reads=[(ky, kc), kmean], writes=[(ky, kc)])
                s.op('pool', lambda e, kc=kc: e.tensor_tensor(out=ya[:, kc, :], in0=ya[:, kc, :], in1=rstd, op=ALU.mult),
                     reads=[(ky, kc), krstd], writes=[(ky, kc)])
                s.op('dve', lambda e, kc=kc: e.tensor_scalar(out=xo[:, kc, :], in0=ya[:, kc, :], scalar1=lnt[:, gi, kc:kc + 1],
                                                             scalar2=lnt[:, bi, kc:kc + 1], op0=ALU.mult, op1=ALU.add),
                     reads=[(ky, kc), k_lnt], writes=[(ko, kc)])

        def phase_p3(l):
            m0 = mem.mark()
            wo, kwo = mem.alloc([128, NKC, D], BF16, 'wo')
            stage = [mem.alloc([128, 2048], F32, 'wst') for _ in range(3)]
            xT_t = [mem.alloc([128, NKC, 512], F32, 'xT3') for _ in range(2)]
            mx_t = [mem.alloc([128, NKC, 512], BF16, 'mx') for _ in range(2)]
            ya_t = [mem.alloc([128, NKC, 512], F32, 'ya') for _ in range(2)]
            tmp = [mem.alloc([128, 2, 512], F32, 'sq'), mem.alloc([128, 512], F32, 'mean'), mem.alloc([128, 512], F32, 'rstd'),
                   mem.alloc([128, 512], F32, 'msq')]
            load_w_bf16(w_out[l], wo, kwo, NKC, D, stage)
            for i in range(NTILE):
                t0 = i * 512
                v = tile_v(i)
                xa, kx = xT_t[i % 2]
                ma, kmx = mx_t[i % 2]
                ya, ky = ya_t[i % 2]
                s.dma('sp', xa, fm_view(XT, t0, 512), reads=[('XT', i)], writes=[(kx, kc) for kc in range(NKC)], sk=kx)
                s.dma('sp', ma, fm_view(MIXT, t0, 512), reads=[('MIXT', i)], writes=[kmx], sk=kmx)
                for oc in range(NKC):
                    ps, kp = bank(0, 4)
                    for kc in range(NKC):
                        s.op('pe', lambda e, ps=ps, kc=kc, oc=oc, ma=ma: e.matmul(
                            ps[:, :], lhsT=wo[:, kc, oc * 128:(oc + 1) * 128], rhs=ma[:, kc, :], start=(kc == 0), stop=(kc == NKC - 1)),
                            reads=[kwo, kmx], writes=[kp])
                    s.op('dve', lambda e, ps=ps, oc=oc, xa=xa, v=v, ya=ya: e.scalar_tensor_tensor(
                        out=ya[:, oc, :], in0=ps[:, :], scalar=modT[:, G1 + oc, v:v + 1], in1=xa[:, oc, :], op0=ALU.mult, op1=ALU.add),
                        reads=[kp, (kx, oc), k_mod], writes=[(ky, oc)])
                ln_apply(ya, ky, xa, kx, l, 0, 1, tmp)
                s.dma('pool', fm_view(XT, t0, 512), xa, reads=[(kx, kc) for kc in range(NKC)], writes=[('XT', i)], sk=kx)
            s.barrier()
            mem.release(m0)

        def phase_p3b(l):
            m0 = mem.mark()
            wf, kwf = mem.alloc([128, NKC, 2 * FF], BF16, 'wf')
            stage = [mem.alloc([128, 1024], F32, 'wst') for _ in range(3)]
            xT_t = [mem.alloc([128, NKC, 512], F32, 'xT3b') for _ in range(1)]
            h2, kh2 = mem.alloc([128, NKC, 512], BF16, 'h2')
            sg_t = [mem.alloc([128, 512], F32, 'sg') for _ in range(2)]
            u_t = [mem.alloc([128, NFC, 512], BF16, 'u') for _ in range(2)]
            load_w_bf16(w_ffn_in[l], wf, kwf, NKC, 2 * FF, stage)
            for i in range(NTILE):
                t0 = i * 512
                v = tile_v(i)
                xa, kx = xT_t[0]
                s.dma('sp', xa, fm_view(XT, t0, 512), reads=[('XT', i)], writes=[kx], sk=kx)
                for kc in range(NKC):
                    s.op('dve', lambda e, kc=kc, xa=xa, v=v: e.tensor_scalar(
                        out=h2[:, kc, :], in0=xa[:, kc, :], scalar1=modT[:, SC2 + kc, v:v + 1], scalar2=modT[:, SH2 + kc, v:v + 1],
                        op0=ALU.mult, op1=ALU.add), reads=[kx, k_mod], writes=[(kh2, kc)])
                ua, ku = u_t[i % 2]
                for f in range(NFC):
                    pg, kpg = bank(0, 4)
                    pu, kpu = bank(4, 8)
                    for kc in range(NKC):
                        s.op('pe', lambda e, pg=pg, kc=kc, f=f: e.matmul(
                            pg[:, :], lhsT=wf[:, kc, f * 128:(f + 1) * 128], rhs=h2[:, kc, :], start=(kc == 0), stop=(kc == NKC - 1)),
                            reads=[kwf, (kh2, kc)], writes=[kpg])
                    for kc in range(NKC):
                        s.op('pe', lambda e, pu=pu, kc=kc, f=f: e.matmul(
                            pu[:, :], lhsT=wf[:, kc, FF + f * 128:FF + (f + 1) * 128], rhs=h2[:, kc, :], start=(kc == 0), stop=(kc == NKC - 1)),
                            reads=[kwf, (kh2, kc)], writes=[kpu])
                    sg, ksg = sg_t[f % 2]
                    s.op('act', lambda e, pg=pg, sg=sg: e.activation(out=sg, in_=pg[:, :], func=AF.Silu), reads=[kpg], writes=[ksg])
                    s.op('dve', lambda e, pu=pu, sg=sg, f=f, ua=ua: e.tensor_tensor(out=ua[:, f, :], in0=pu[:, :], in1=sg, op=ALU.mult),
                         reads=[kpu, ksg], writes=[(ku, f)])
                s.dma('pool', fm_view(UT, t0, 512), ua, reads=[(ku, f) for f in range(NFC)], writes=[('UT', i)], sk=ku)
            s.barrier()
            mem.release(m0)

        def phase_p4(l):
            m0 = mem.mark()
            w2, kw2 = mem.alloc([128, NFC, D], BF16, 'w2')
            stage = [mem.alloc([128, 2048], F32, 'wst') for _ in range(3)]
            xT_t = [mem.alloc([128, NKC, 512], F32, 'xT4') for _ in range(2)]
            u_t = [mem.alloc([128, NFC, 512], BF16, 'u4') for _ in range(2)]
            ya_t = [mem.alloc([128, NKC, 512], F32, 'ya4') for _ in range(2)]
            tmp = [mem.alloc([128, 2, 512], F32, 'sq'), mem.alloc([128, 512], F32, 'mean'), mem.alloc([128, 512], F32, 'rstd'),
                   mem.alloc([128, 512], F32, 'msq')]
            load_w_bf16(w_ffn_out[l], w2, kw2, NFC, D, stage)
            for i in range(NTILE):
                t0 = i * 512
                v = tile_v(i)
                xa, kx = xT_t[i % 2]
                ua, ku = u_t[i % 2]
                ya, ky = ya_t[i % 2]
                s.dma('sp', xa, fm_view(XT, t0, 512), reads=[('XT', i)], writes=[(kx, kc) for kc in range(NKC)], sk=kx)
                s.dma('sp', ua, fm_view(UT, t0, 512), reads=[('UT', i)], writes=[ku], sk=ku)
                for oc in range(NKC):
                    ps, kp = bank(0, 4)
                    for f in range(NFC):
                        s.op('pe', lambda e, ps=ps, f=f, oc=oc, ua=ua: e.matmul(
                            ps[:, :], lhsT=w2[:, f, oc * 128:(oc + 1) * 128], rhs=ua[:, f, :], start=(f == 0), stop=(f == NFC - 1)),
                            reads=[kw2, ku], writes=[kp])
                    s.op('dve', lambda e, ps=ps, oc=oc, xa=xa, v=v, ya=ya: e.scalar_tensor_tensor(
                        out=ya[:, oc, :], in0=ps[:, :], scalar=modT[:, G2 + oc, v:v + 1], in1=xa[:, oc, :], op0=ALU.mult, op1=ALU.add),
                        reads=[kp, (kx, oc), k_mod], writes=[(ky, oc)])
                ln_apply(ya, ky, xa, kx, l, 2, 3, tmp)
                s.dma('pool', fm_view(XT, t0, 512), xa, reads=[(kx, kc) for kc in range(NKC)], writes=[('XT', i)], sk=kx)
            s.barrier()
            mem.release(m0)

        SEQS = ([dict(t0=0, T=TS, sample=True, j=-1)] if NST else []) + \
               [dict(t0=TS + 256 * j, T=256, sample=False, j=j) for j in range(NPR)]

        def small_w(dram2d, rows, cols, dt, name, stage):
            dst, kd = mem.alloc([rows, cols], dt, name)
            sa, sk_ = stage
            s.dma('sp', sa[0:rows, 0:cols], dram2d, writes=[sk_], sk=sk_)
            s.op('pool', lambda e: e.tensor_copy(out=dst, in_=sa[0:rows, 0:cols]), reads=[sk_], writes=[kd])
            return dst, kd

        def evac(i, out, in_, reads, writes):
            if i % 2 == 0:
                s.op('dve', lambda e: e.tensor_copy(out=out, in_=in_), reads=reads, writes=writes)
            else:
                s.op('act', lambda e: e.copy(out=out, in_=in_), reads=reads, writes=writes)

        def phase_mla(l):
            m0 = mem.mark()
            stage = mem.alloc([128, 1024], F32, 'wst')
            wq0, kwq0 = small_w(wq_d[l, 0:128, :], 128, 768, BF16, 'wq0', stage)
            wq1, kwq1 = small_w(wq_d[l, 128:256, :], 128, 768, BF16, 'wq1', stage)
            wp0, kwp0 = small_w(wqp_d[l, 0:128, :], 128, 768, BF16, 'wp0', stage)
            wp1, kwp1 = small_w(wqp_d[l, 128:256, :], 128, 768, BF16, 'wp1', stage)
            wqs, kwqs, wps, kwps = (wq0, wq1), (kwq0, kwq1), (wp0, wp1), (kwp0, kwp1)
            wk, kwk = small_w(wk_d[l], 128, 768, BF16, 'wk', stage)
            wv, kwv = small_w(wv_d[l], 128, 512, BF16, 'wv', stage)
            e96, ke96 = small_w(e96_d, 32, 96, BF16, 'e96', stage)
            qkn, kqkn = mem.alloc([128, 3], F32, 'qkn')
            s.dma('sp', qkn, qkn_d[l], writes=[kqkn], sk=kqkn)
            KMAX = 4608 if NST else 256
            KT, kKT = mem.alloc([96, 8, KMAX], BF16, 'KT')
            VA, kVA = mem.alloc([128, KMAX // 128, 8, 65], BF16, 'VA')
            s.op('pool', lambda e: e.memset(VA[:, :, :, 64:65], 1.0), writes=[(kVA, 'ones')])
            ckv, kckv = mem.alloc([128, 512], F32, 'ckv')
            sq, ksq = mem.alloc([128, 2, 512], F32, 'sqm')
            rstd, krstd = mem.alloc([128, 512], F32, 'rstdm')
            ckvn, kckvn = mem.alloc([128, 512], F32, 'ckvn')
            ckvb, kckvb = mem.alloc([128, 512], BF16, 'ckvb')
            kpe, kkpe = mem.alloc([32, 512], F32, 'kpe')
            kpp, kkpp = mem.alloc([32, 512], F32, 'kpp')
            kcs, kkcs = mem.alloc([32, 2, 512], F32, 'kcs')
            krb, kkrb = mem.alloc([32, 512], BF16, 'krb')
            ctm, kctm = mem.alloc([128, 4, 128], F32, 'ctm')
            ktm, kktm = mem.alloc([128, 4, 32], F32, 'ktm')
            cq, kcq = mem.alloc([128, 2, 512], F32, 'cq')
            cqn, kcqn = mem.alloc([128, 2, 512], BF16, 'cqn')
            cs96, kcs96 = mem.alloc([96, 2, 512], F32, 'cs96')
            crsr, kcrsr = mem.alloc([96, 2, 512], F32, 'crsr')
            t12, kt12 = mem.alloc([96, 2, 512], F32, 't12')
            qT_t = [mem.alloc([96, 512], BF16, 'qT') for _ in range(2)]
            PT_t = [mem.alloc([128, 512], BF16, 'PT') for _ in range(4)]
            rec, krec = mem.alloc([128, 4], F32, 'rec')
            att, katt = mem.alloc([128, 4, 512], BF16, 'att')
            mixo_t = [mem.alloc([128, 4, 512], BF16, 'mixo') for _ in range(2)]
            nev = [0]

            def rms_rstd(src2d_list, keys, n, div, extra_scale):
                ps, kp = bank(7, 8)
                for c, (a, ka) in enumerate(zip(src2d_list, keys)):
                    s.op('act', lambda e, a=a, c=c: e.activation(out=sq[:, c, 0:n], in_=a, func=AF.Square), reads=[ka], writes=[(ksq, c)])
                    s.op('pe', lambda e, c=c: e.matmul(ps[:, 0:n], lhsT=onesf, rhs=sq[:, c, 0:n], start=(c == 0), stop=(c == len(src2d_list) - 1)),
                         reads=[k_onesf, (ksq, c)], writes=[kp])
                s.op('act', lambda e: e.activation(out=rstd[:, 0:n], in_=ps[:, 0:n], func=AF.Sqrt, bias=epsln[:, 1:2], scale=1.0 / div),
                     reads=[kp, k_eps], writes=[krstd])
                s.op('dve', lambda e: e.reciprocal(out=rstd[:, 0:n], in_=rstd[:, 0:n]), reads=[krstd], writes=[krstd])
                if extra_scale != 1.0:
                    s.op('dve', lambda e: e.tensor_scalar_mul(out=rstd[:, 0:n], in0=rstd[:, 0:n], scalar1=extra_scale), reads=[krstd], writes=[krstd])

            def make_kv(k0, n):
                for h in range(8):
                    ps, kp = bank(0, 4)
                    s.op('pe', lambda e, ps=ps, h=h: e.matmul(ps[0:96, 0:n], lhsT=wk[:, h * 96:(h + 1) * 96], rhs=ckvb[:, 0:n], start=True, stop=False),
                         reads=[kwk, kckvb], writes=[kp])
                    s.op('pe', lambda e, ps=ps: e.matmul(ps[0:96, 0:n], lhsT=e96, rhs=krb[:, 0:n], start=False, stop=True),
                         reads=[ke96, kkrb], writes=[kp])
                    nev[0] += 1
                    evac(nev[0], KT[:, h, k0:k0 + n], ps[0:96, 0:n], [kp], [(kKT, h, k0)])
                for b in range(n // 128):
                    ps, kp = bank(0, 4)
                    s.op('pe', lambda e, ps=ps, b=b: e.matmul(ps[:, :], lhsT=ckvb[:, b * 128:(b + 1) * 128], rhs=wv, start=True, stop=True),
                         reads=[kwv, kckvb], writes=[kp])
                    kb = k0 // 128 + b
                    nev[0] += 1
                    evac(nev[0], VA[:, kb, :, 0:64], ps[:, :].rearrange("p (h e) -> p h e", e=64), [kp], [(kVA, kb)])

            for sq_ in SEQS:
                T, S0, samp, j = sq_['T'], sq_['t0'], sq_['sample'], sq_['j']
                nkeys = T + (512 if samp else 0)
                nkt = nkeys // 128
                for k0 in range(0, T, 512):
                    n = min(512, T - k0)
                    t0 = S0 + k0
                    s.dma('sp', ckv[:, 0:n], SCR['CKVT'][:, t0:t0 + n], reads=[('CKVT', 0, t0 // 512)], writes=[kckv], sk=kckv)
                    s.dma('sp', kpe[:, 0:n], SCR['KPET'][:, t0:t0 + n], reads=[('KPET', 0, t0 // 512)], writes=[kkpe], sk=kkpe)
                    rms_rstd([ckv[:, 0:n]], [kckv], n, 128.0, 1.0)
                    s.op('dve', lambda e, n=n: e.scalar_tensor_tensor(out=ckvn[:, 0:n], in0=ckv[:, 0:n], scalar=qkn[:, 2:3], in1=rstd[:, 0:n],
                                                                     op0=ALU.mult, op1=ALU.mult), reads=[kckv, kqkn, krstd], writes=[kckvn])
                    s.op('act', lambda e, n=n: e.copy(out=ckvb[:, 0:n], in_=ckvn[:, 0:n]), reads=[kckvn], writes=[kckvb])
                    if samp:
                        s.dma('sp', kpp[:, 0:n], SCR['KPEP'][:, t0:t0 + n], reads=[('KPEP', 0, t0 // 512)], writes=[kkpp], sk=kkpp)
                        s.dma_group('sp', [(kcs[:, 0, 0:n], kcos_d[:, k0:k0 + n]), (kcs[:, 1, 0:n], ksin_d[:, k0:k0 + n])], writes=[kkcs], sk=kkcs)
                        s.op('dve', lambda e, n=n: e.tensor_tensor(out=kpe[:, 0:n], in0=kpe[:, 0:n], in1=kcs[:, 0, 0:n], op=ALU.mult),
                             reads=[kkpe, kkcs], writes=[kkpe])
                        s.op('dve', lambda e, n=n: e.tensor_tensor(out=kpp[:, 0:n], in0=kpp[:, 0:n], in1=kcs[:, 1, 0:n], op=ALU.mult),
                             reads=[kkpp, kkcs], writes=[kkpp])
                        s.op('dve', lambda e, n=n: e.tensor_tensor(out=krb[:, 0:n], in0=kpe[:, 0:n], in1=kpp[:, 0:n], op=ALU.add),
                             reads=[kkpe, kkpp], writes=[kkrb])
                    else:
                        s.op('dve', lambda e, n=n: e.tensor_copy(out=krb[:, 0:n], in_=kpe[:, 0:n]), reads=[kkpe], writes=[kkrb])
                        for b in range(n // 128):
                            ps, kp = bank(4, 7)
                            s.op('pe', lambda e, ps=ps, b=b: e.transpose(out=ps[:, 0:128], in_=ckvn[:, b * 128:(b + 1) * 128], identity=idf),
                                 reads=[kckvn, k_idf], writes=[kp])
                            s.op('pe', lambda e, ps=ps, b=b: e.transpose(out=ps[:, 128:160], in_=kpe[0:32, b * 128:(b + 1) * 128], identity=idf[0:32, 0:32]),
                                 reads=[kkpe, k_idf], writes=[kp])
                            s.op('dve', lambda e, ps=ps, b=b: e.tensor_copy(out=ctm[:, b, :], in_=ps[:, 0:128]), reads=[kp], writes=[(kctm, b)])
                            s.op('act', lambda e, ps=ps, b=b: e.copy(out=ktm[:, b, :], in_=ps[:, 128:160]), reads=[kp], writes=[(kktm, b)])
                        nb = n // 128
                        s.dma('pool', ckv_out[j, l, k0:k0 + n, :].rearrange("(b p) c -> p b c", p=128), ctm[:, 0:nb, :],
                              reads=[(kctm, b) for b in range(nb)], sk=kctm)
                        s.dma('pool', kpe_out[j, l, k0:k0 + n, :].rearrange("(b p) c -> p b c", p=128), ktm[:, 0:nb, :],
                              reads=[(kktm, b) for b in range(nb)], sk=kktm)
                    make_kv(k0, n)
                if samp:
                    s.dma('sp', ctm, cckv_d[l].rearrange("(b p) c -> p b c", p=128), writes=[(kctm, b) for b in range(4)], sk=kctm)
                    s.dma('sp', ktm, ckpe_d[l].rearrange("(b p) c -> p b c", p=128), writes=[(kktm, b) for b in range(4)], sk=kktm)
                    ps, kp = bank(4, 7)
                    ps2, kp2 = bank(4, 7)
                    for b in range(4):
                        s.op('pe', lambda e, b=b, ps=ps: e.transpose(out=ps[:, b * 128:(b + 1) * 128], in_=ctm[:, b, :], identity=idf),
                             reads=[(kctm, b), k_idf], writes=[kp])
                        s.op('pe', lambda e, b=b, ps2=ps2: e.transpose(out=ps2[0:32, b * 128:(b + 1) * 128], in_=ktm[:, b, :], identity=idf),
                             reads=[(kktm, b), k_idf], writes=[kp2])
                    s.op('dve', lambda e, ps=ps: e.tensor_copy(out=ckvb, in_=ps[:, :]), reads=[kp], writes=[kckvb])
                    s.op('act', lambda e, ps2=ps2: e.copy(out=krb, in_=ps2[0:32, :]), reads=[kp2], writes=[kkrb])
                    make_kv(T, 512)
                for q0 in range(0, T, 512):
                    n = min(512, T - q0)
                    nqb = n // 128
                    t0 = S0 + q0
                    ti = t0 // 512
                    s.dma('sp', cq[:, :, 0:n], SCR['CQT'].rearrange("(c p) t -> p c t", p=128)[:, :, t0:t0 + n],
                          reads=[('CQT', 0, ti), ('CQT', 128, ti)], writes=[kcq], sk=kcq)
                    rms_rstd([cq[:, 0, 0:n], cq[:, 1, 0:n]], [kcq, kcq], n, 256.0, ATT_SCALE)
                    for kc in range(2):
                        s.op('dve', lambda e, kc=kc, n=n: e.tensor_scalar_mul(out=cqn[:, kc, 0:n], in0=cq[:, kc, 0:n], scalar1=qkn[:, kc:kc + 1]),
                             reads=[kcq, kqkn], writes=[(kcqn, kc)])
                    if samp:
                        s.dma_group('sp', [(cs96[:, 0, 0:n], cos96_d[:, q0:q0 + n]), (cs96[:, 1, 0:n], sin96_d[:, q0:q0 + n])], writes=[kcs96], sk=kcs96)
                        for c in range(2):
                            s.op('dve', lambda e, c=c, n=n: e.tensor_tensor(out=crsr[:, c, 0:n], in0=cs96[:, c, 0:n], in1=rstd[0:96, 0:n], op=ALU.mult),
                                 reads=[kcs96, krstd], writes=[(kcrsr, c)])
                    mo_, kmo = mixo_t[(t0 // 512) % 2]
                    def qproj(h):
                        qT, kqT = qT_t[h % 2]
                        psA, kpA = bank(0, 2)
                        for kc in range(2):
                            s.op('pe', lambda e, psA=psA, kc=kc, h=h, n=n: e.matmul(
                                psA[0:96, 0:n], lhsT=wqs[kc][:, h * 96:(h + 1) * 96], rhs=cqn[:, kc, 0:n], start=(kc == 0), stop=(kc == 1)),
                                reads=[kwqs[kc], (kcqn, kc)], writes=[kpA])
                        if samp:
                            psB, kpB = bank(0, 2)
                            for kc in range(2):
                                s.op('pe', lambda e, psB=psB, kc=kc, h=h, n=n: e.matmul(
                                    psB[0:96, 0:n], lhsT=wps[kc][:, h * 96:(h + 1) * 96], rhs=cqn[:, kc, 0:n], start=(kc == 0), stop=(kc == 1)),
                                    reads=[kwps[kc], (kcqn, kc)], writes=[kpB])
                            s.op('dve', lambda e, psA=psA, n=n: e.tensor_tensor(out=t12[:, 0, 0:n], in0=psA[0:96, 0:n], in1=crsr[:, 0, 0:n], op=ALU.mult),
                                 reads=[kpA, (kcrsr, 0)], writes=[(kt12, 0)])
                            s.op('dve', lambda e, psB=psB, n=n: e.tensor_tensor(out=t12[:, 1, 0:n], in0=psB[0:96, 0:n], in1=crsr[:, 1, 0:n], op=ALU.mult),
                                 reads=[kpB, (kcrsr, 1)], writes=[(kt12, 1)])
                            s.op('pool', lambda e, qT=qT, n=n: e.tensor_tensor(out=qT[:, 0:n], in0=t12[:, 0, 0:n], in1=t12[:, 1, 0:n], op=ALU.add),
                                 reads=[(kt12, 0), (kt12, 1)], writes=[kqT])
                        else:
                            s.op('dve', lambda e, psA=psA, qT=qT, n=n: e.tensor_tensor(out=qT[:, 0:n], in0=psA[0:96, 0:n], in1=rstd[0:96, 0:n], op=ALU.mult),
                                 reads=[kpA, krstd], writes=[kqT])

                    def attend(h):
                        qT, kqT = qT_t[h % 2]
                        psO, kpO = bank(5, 7)
                        LA = 2
                        pend = {}
                        for it in range(nkt + LA):
                            if it < nkt:
                                kt = it
                                psS, kpS = bank(2, 5)
                                kvk = [(kKT, h, (kt * 128) // 512 * 512 if kt * 128 < T else T)]
                                s.op('pe', lambda e, psS=psS, kt=kt, h=h, qT=qT, n=n: e.matmul(
                                    psS[:, 0:n], lhsT=KT[:, h, kt * 128:(kt + 1) * 128], rhs=qT[:, 0:n], start=True, stop=True),
                                    reads=kvk + [kqT], writes=[kpS])
                                PT, kPT = PT_t[kt % len(PT_t)]
                                s.op('act', lambda e, psS=psS, PT=PT, n=n: e.activation(out=PT[:, 0:n], in_=psS[:, 0:n], func=AF.Exp),
                                     reads=[kpS], writes=[kPT])
                                pend[kt] = (PT, kPT)
                            if it >= LA:
                                kt = it - LA
                                PT, kPT = pend.pop(kt)
                                for qb in range(nqb):
                                    s.op('pe', lambda e, psO=psO, PT=PT, qb=qb, kt=kt, h=h: e.matmul(
                                        psO[:, qb * 65:(qb + 1) * 65], lhsT=PT[:, qb * 128:(qb + 1) * 128], rhs=VA[:, kt, h, :],
                                        start=(kt == 0 and qb == 0), stop=(kt == nkt - 1), skip_group_check=True),
                                        reads=[kPT, (kVA, kt), (kVA, 'ones')], writes=[kpO])
                        pO3 = psO[:, 0:nqb * 65].rearrange("p (q e) -> p q e", e=65)
                        s.op('dve', lambda e, pO3=pO3, nqb=nqb: e.reciprocal(out=rec[:, 0:nqb], in_=pO3[:, :, 64]), reads=[kpO], writes=[krec])
                        s.op('dve', lambda e, pO3=pO3, nqb=nqb, h=h: e.tensor_tensor(
                            out=att[:, 0:nqb, h * 64:(h + 1) * 64], in0=pO3[:, :, 0:64], in1=rec[:, 0:nqb].unsqueeze(2).to_broadcast([128, nqb, 64]),
                            op=ALU.mult), reads=[kpO, krec], writes=[(katt, h)])

                    qproj(0)
                    for h in range(8):
                        if h + 1 < 8:
                            qproj(h + 1)
                        attend(h)
                    for qb in range(nqb):
                        ps, kp = bank(7, 8)
                        pbf = ps.bitcast(BF16)
                        for c in range(4):
                            s.op('pe', lambda e, pbf=pbf, c=c, qb=qb: e.transpose(out=pbf[:, c * 128:(c + 1) * 128], in_=att[:, qb, c * 128:(c + 1) * 128], identity=idb),
                                 reads=[(katt, 2 * c), (katt, 2 * c + 1), k_idb], writes=[kp])
                        nev[0] += 1
                        evac(nev[0], mo_[:, :, qb * 128:(qb + 1) * 128], pbf[:, 0:512].rearrange("p (c t) -> p c t", t=128), [kp], [(kmo, qb)])
                    s.dma('pool', MIXT.rearrange("(c p) t -> p c t", p=128)[:, 0:4, t0:t0 + n], mo_[:, :, 0:n],
                          reads=[(kmo, qb) for qb in range(nqb)], writes=[('MIXT_A', t0)], sk=kmo)
            s.barrier()
            mem.release(m0)

        def phase_mlstm(l):
            m0_ = mem.mark()
            mc, kmc = mem.alloc([64, 4, 64], F32, 'mconst')
            ng, kng = mem.alloc([64, 256], F32, 'mlng')
            s.dma('sp', mc, mconst_d, writes=[kmc], sk=kmc)
            s.dma('sp', ng, mlng_d[l], writes=[kng], sk=kng)
            rmask, krm = mem.alloc([64, 8, 64], F32, 'rmask')
            s.op('pool', lambda e: e.memset(rmask, 1.0), writes=[krm])
            s.op('pool', lambda e: e.memset(rmask[:, :, 0:1], 0.0), writes=[krm])
            A = lambda shape, dt=F32, name='m': mem.alloc(shape, dt, name)
            GI, kGI = A([64, 8, 64]); GF, kGF = A([64, 8, 64]); SP, kSP = A([64, 8, 64]); CS, kCS = A([64, 8, 64])
            NB, kNB = A([64, 8, 64]); Cc, kCc = A([64, 8, 64]); W, kW = A([64, 8, 64]); THR, kTHR = A([64, 8, 64])
            TOT, kTOT = A([64, 8]); CMAX, kCMAX = A([64, 8]); Gn, kGn = A([64, 8])
            ROW, kROW = A([4, 4, 64]); m0t, km0t = A([4, 2]); MN, kMN = A([4, 2, 64]); Rr, kRr = A([4, 2, 64])
            MP, kMP = A([4, 2, 64]); SCr, kSCr = A([4, 2, 64])
            TMPc, kTMPc = A([64, 16]); Rc, kRc = A([64, 8]); SCc, kSCc = A([64, 8])
            Wt, kWt = A([64, 8, 64]); THRt, kTHRt = A([64, 8, 64]); BD, kBD = A([64, 64, 8]); scB, kscB = A([64, 64, 8])
            Caug, kC = A([64, 4, 65]); Cs, kCs = A([64, 4, 65]); Csb, kCsb = A([64, 4, 65], BF16)
            q32_t = [A([64, 4, 512]) for _ in range(1)]
            k32_t = [A([64, 4, 512]) for _ in range(1)]
            qb_t = [A([64, 4, 512], BF16) for _ in range(2)]
            kb_t = [A([64, 4, 512], BF16) for _ in range(2)]
            vv_t = [A([64, 8, 256]) for _ in range(1)]
            kk_t = [A([64, 8, 256]) for _ in range(1)]
            kkb_t = [A([64, 8, 256], BF16) for _ in range(2)]
            va_t = [A([64, 8, 4, 65], BF16) for _ in range(2)]
            for va, kva in va_t:
                s.op('pool', lambda e, va=va: e.memset(va[:, :, :, 64:65], 1.0), writes=[(kva, 'ones')])
            vw_t = [A([64, 8, 4, 65], BF16) for _ in range(2)]
            PTg_t = [A([64, 8, 4, 64], BF16) for _ in range(2)]
            dC_t = [A([64, 8, 4, 65]) for _ in range(2)]
            hF_t = [A([64, 8, 256]) for _ in range(2)]
            mo_t = [A([64, 8, 256]) for _ in range(2)]
            hb_t = [A([64, 8, 256]) for _ in range(2)]
            den, kden = A([64, 2, 4]); rec, krec = A([64, 2, 4])
            s1, ks1 = A([64, 32]); s2, ks2 = A([64, 32]); s3, ks3 = A([64, 32])
            sqx, ksqx = q32_t[0]
            sqx = sqx.rearrange("p h (a b) -> p (h a) b", b=256)
            sig, ksig = k32_t[0]
            sig = sig.rearrange("p h (a b) -> p (h a) b", b=256)
            xo_t = [A([64, 8, 256], BF16) for _ in range(1)]
            mixo_t = [A([128, 2, 512], BF16) for _ in range(1)]
            maskF, maskB, J64, J4 = mc[:, 0, :], mc[:, 1, :], mc[:, 2, :], mc[0:4, 3, 0:4]
            gcount = [0]
            for sq_ in SEQS:
                T, S0, samp, j = sq_['T'], sq_['t0'], sq_['sample'], sq_['j']
                nch = T // 64
                Jn = J64 if nch == 64 else J4
                In = idf[0:nch, 0:nch]
                def gview(name):
                    return SCR[name][:, S0:S0 + T].rearrange("h (j t) -> j h t", t=64)
                rk = lambda nm: [(nm, 0, i) for i in range(S0 // 512, (S0 + T + 511) // 512)]
                s.dma_group('sp', [(GI[0:nch, 0:4, :], gview('MIF')), (GI[0:nch, 4:8, :], gview('MIB'))], reads=rk('MIF') + rk('MIB'), writes=[kGI], sk=kGI)
                s.dma_group('sp', [(GF[0:nch, 0:4, :], gview('MFF')), (GF[0:nch, 4:8, :], gview('MFB'))], reads=rk('MFF') + rk('MFB'), writes=[kGF], sk=kGF)
                s.op('act', lambda e, nch=nch: e.activation(out=SP[0:nch], in_=GF[0:nch], func=AF.Exp, scale=-1.0), reads=[kGF], writes=[kSP])
                s.op('act', lambda e, nch=nch: e.activation(out=SP[0:nch], in_=SP[0:nch], func=AF.Ln, bias=1.0), reads=[kSP], writes=[kSP])
                fl = lambda a, nch=nch: a[0:nch].rearrange("p a b -> p (a b)")
                s.op('dve', lambda e, nch=nch, fl=fl: e.tensor_tensor_scan(out=fl(CS), data0=fl(rmask), data1=fl(SP), initial=0.0, op0=ALU.mult, op1=ALU.add),
                     reads=[krm, kSP], writes=[kCS])
                s.op('dve', lambda e, nch=nch: e.tensor_copy(out=TOT[0:nch], in_=CS[0:nch, :, 63]), reads=[kCS], writes=[kTOT])
                s.op('dve', lambda e, nch=nch: e.tensor_copy(out=NB[0:nch, 0:4, :], in_=CS[0:nch, 0:4, :]), reads=[kCS], writes=[kNB])
                s.op('dve', lambda e, nch=nch: e.tensor_tensor(out=NB[0:nch, 4:8, :], in0=SP[0:nch, 4:8, :], in1=CS[0:nch, 4:8, :], op=ALU.subtract),
                     reads=[kCS, kSP, kNB], writes=[kNB])
                s.op('dve', lambda e, nch=nch: e.tensor_tensor(out=NB[0:nch, 4:8, :], in0=NB[0:nch, 4:8, :],
                                                              in1=TOT[0:nch, 4:8].unsqueeze(2).to_broadcast([nch, 4, 64]), op=ALU.add),
                     reads=[kNB, kTOT], writes=[kNB])
                s.op('dve', lambda e, nch=nch: e.tensor_tensor(out=Cc[0:nch], in0=GI[0:nch], in1=NB[0:nch], op=ALU.add), reads=[kGI, kNB], writes=[kCc])
                s.op('dve', lambda e, nch=nch: e.tensor_reduce(out=CMAX[0:nch], in_=Cc[0:nch], axis=AX.X, op=ALU.max), reads=[kCc], writes=[kCMAX])
                s.op('dve', lambda e, nch=nch: e.tensor_scalar_mul(out=Gn[0:nch], in0=TOT[0:nch], scalar1=-1.0), reads=[kTOT], writes=[kGn])
                ps, kp = bank(0, 8)
                for qi, (src_, ksrc, lo, mat) in enumerate(((CMAX, kCMAX, 0, In), (Gn, kGn, 0, In), (CMAX, kCMAX, 4, Jn), (Gn, kGn, 4, Jn))):
                    s.op('pe', lambda e, ps=ps, qi=qi, src_=src_, lo=lo, mat=mat, nch=nch: e.matmul(
                        ps[0:4, qi * nch:(qi + 1) * nch], lhsT=src_[0:nch, lo:lo + 4], rhs=mat, start=True, stop=True),
                        reads=[ksrc, k_idf, kmc], writes=[kp])
                s.op('dve', lambda e, ps=ps, nch=nch: e.tensor_copy(out=ROW[:, :, 0:nch], in_=ps[0:4, 0:4 * nch].rearrange("p (a b) -> p a b", b=nch)),
                     reads=[kp], writes=[kROW])
                if samp:
                    s.dma('sp', m0t, stm_d[l].rearrange("d h -> h d"), writes=[km0t], sk=km0t, allow_slow_non_contiguous=True)
                else:
                    s.op('dve', lambda e: e.memset(m0t, 0.0), writes=[km0t])
                for d in range(2):
                    s.op('dve', lambda e, d=d, nch=nch: e.tensor_tensor_scan(out=MN[:, d, 0:nch], data0=ROW[:, 2 * d, 0:nch], data1=ROW[:, 2 * d + 1, 0:nch],
                                                                            initial=m0t[:, d:d + 1], op0=ALU.max, op1=ALU.add),
                         reads=[kROW, km0t], writes=[(kMN, d)])
                    s.op('dve', lambda e, d=d, nch=nch: e.tensor_tensor(out=Rr[:, d, 0:nch], in0=MN[:, d, 0:nch], in1=ROW[:, 2 * d + 1, 0:nch], op=ALU.subtract),
                         reads=[(kMN, d), kROW], writes=[(kRr, d)])
                    s.op('dve', lambda e, d=d: e.tensor_copy(out=MP[:, d, 0:1], in_=m0t[:, d:d + 1]), reads=[km0t], writes=[(kMP, d, 0)])
                    s.op('dve', lambda e, d=d, nch=nch: e.tensor_copy(out=MP[:, d, 1:nch], in_=MN[:, d, 0:nch - 1]), reads=[(kMN, d)], writes=[(kMP, d, 1)])
                    s.op('dve', lambda e, d=d, nch=nch: e.tensor_tensor(out=SCr[:, d, 0:nch], in0=MP[:, d, 0:nch], in1=Rr[:, d, 0:nch], op=ALU.subtract),
                         reads=[(kMP, d, 0), (kMP, d, 1), (kRr, d)], writes=[(kSCr, d)])
                    s.op('act', lambda e, d=d, nch=nch: e.activation(out=SCr[:, d, 0:nch], in_=SCr[:, d, 0:nch], func=AF.Exp), reads=[(kSCr, d)], writes=[(kSCr, d)])
                    if not samp:
                        s.dma('pool', m_out[j, l, d, :].rearrange("(h o) -> h o", o=1), MN[:, d, nch - 1:nch], reads=[(kMN, d)], sk=(kMN, d))
                ps, kp = bank(0, 8)
                for qi, (src_, ksrc, d) in enumerate(((Rr, kRr, 0), (Rr, kRr, 1), (SCr, kSCr, 0), (SCr, kSCr, 1))):
                    s.op('pe', lambda e, ps=ps, qi=qi, src_=src_, d=d, nch=nch: e.matmul(
                        ps[0:nch, qi * 4:(qi + 1) * 4], lhsT=src_[:, d, 0:nch], rhs=idf[0:4, 0:4], start=True, stop=True),
                        reads=[(ksrc, d), k_idf], writes=[kp])
                s.op('dve', lambda e, ps=ps, nch=nch: e.tensor_copy(out=TMPc[0:nch], in_=ps[0:nch, 0:16]), reads=[kp], writes=[kTMPc])
                ps2, kp2 = bank(0, 8)
                for qi, lo in enumerate((4, 12)):
                    s.op('pe', lambda e, ps2=ps2, qi=qi, lo=lo, nch=nch, Jn=Jn: e.matmul(
                        ps2[0:nch, qi * 4:(qi + 1) * 4], lhsT=Jn, rhs=TMPc[0:nch, lo:lo + 4], start=True, stop=True),
                        reads=[kTMPc, kmc], writes=[kp2])
                s.op('dve', lambda e, nch=nch: e.tensor_copy(out=Rc[0:nch, 0:4], in_=TMPc[0:nch, 0:4]), reads=[kTMPc], writes=[(kRc, 0)])
                s.op('dve', lambda e, nch=nch, ps2=ps2: e.tensor_copy(out=Rc[0:nch, 4:8], in_=ps2[0:nch, 0:4]), reads=[kp2], writes=[(kRc, 1)])
                s.op('dve', lambda e, nch=nch: e.tensor_copy(out=SCc[0:nch, 0:4], in_=TMPc[0:nch, 8:12]), reads=[kTMPc], writes=[(kSCc, 0)])
                s.op('dve', lambda e, nch=nch, ps2=ps2: e.tensor_copy(out=SCc[0:nch, 4:8], in_=ps2[0:nch, 4:8]), reads=[kp2], writes=[(kSCc, 1)])
                rcb = lambda nch=nch: Rc[0:nch].unsqueeze(2).to_broadcast([nch, 8, 64])
                s.op('dve', lambda e, nch=nch, rcb=rcb: e.tensor_tensor(out=W[0:nch], in0=Cc[0:nch], in1=rcb(), op=ALU.subtract),
                     reads=[kCc, (kRc, 0), (kRc, 1)], writes=[kW])
                s.op('act', lambda e, nch=nch: e.activation(out=W[0:nch], in_=W[0:nch], func=AF.Exp), reads=[kW], writes=[kW])
                s.op('dve', lambda e, nch=nch, rcb=rcb: e.tensor_tensor(out=THR[0:nch], in0=NB[0:nch], in1=rcb(), op=ALU.subtract),
                     reads=[kNB, (kRc, 0), (kRc, 1)], writes=[kTHR])
                s.op('act', lambda e, nch=nch: e.activation(out=THR[0:nch], in_=THR[0:nch], func=AF.Exp), reads=[kTHR], writes=[kTHR])
                for src_, ksrc, dst, kdst in ((W, kW, Wt, kWt), (THR, kTHR, THRt, kTHRt)):
                    ps, kp = bank(0, 8)
                    for r in range(8):
                        s.op('pe', lambda e, ps=ps, r=r, src_=src_, nch=nch, In=In: e.transpose(
                            out=ps[0:64, r * nch:(r + 1) * nch], in_=src_[0:nch, r, :], identity=In), reads=[ksrc, k_idf], writes=[kp])
                    s.op('dve', lambda e, ps=ps, dst=dst, nch=nch: e.tensor_copy(out=dst[:, :, 0:nch], in_=ps[0:64, 0:8 * nch].rearrange("p (a b) -> p a b", b=nch)),
                         reads=[kp], writes=[kdst])
                s.op('dve', lambda e, nch=nch, In=In: e.tensor_tensor(
                    out=BD[0:nch, 0:nch, :], in0=In.unsqueeze(2).to_broadcast([nch, nch, 8]),
                    in1=SCc[0:nch].unsqueeze(1).to_broadcast([nch, nch, 8]), op=ALU.mult), reads=[k_idf, (kSCc, 0), (kSCc, 1)], writes=[kBD])
                ps, kp = bank(0, 8)
                s.op('pe', lambda e, ps=ps, nch=nch: e.matmul(ps[0:64, 0:nch * 8], lhsT=onesf[0:nch, 0:64],
                                                             rhs=BD[0:nch, 0:nch, :].rearrange("p a b -> p (a b)"), start=True, stop=True),
                     reads=[k_onesf, kBD], writes=[kp])
                s.op('dve', lambda e, ps=ps, nch=nch: e.tensor_copy(out=scB[:, 0:nch, :], in_=ps[0:64, 0:nch * 8].rearrange("p (a b) -> p a b", b=8)),
                     reads=[kp], writes=[kscB])
                G = min(8, nch)
                GT = 64 * G
                ngrp = nch // G
                for d in range(2):
                    if samp:
                        s.dma('sp', Caug[:, :, 0:64], stC_d[l, d].rearrange("h a b -> a h b"), writes=[kC], sk=kC)
                        s.dma('sp', Caug[:, :, 64], stn_d[l, d].rearrange("h a -> a h"), writes=[(kC, 'n')], sk=(kC, 'n'), allow_slow_non_contiguous=True)
                    else:
                        s.op('dve', lambda e: e.memset(Caug, 0.0), writes=[kC, (kC, 'n')])
                    mask = maskF if d == 0 else maskB
                    gorder = list(range(ngrp)) if d == 0 else list(range(ngrp - 1, -1, -1))
                    corder = list(range(G)) if d == 0 else list(range(G - 1, -1, -1))

                    def load_group(gi, d=d, G=G, GT=GT, S0=S0):
                        t0 = S0 + gi * GT
                        ti = t0 // 512
                        b = gcount[0] % 2
                        gcount[0] += 1
                        cx = dict(gi=gi, t0=t0, ti=ti, b=b)
                        q32, kq32 = q32_t[0]; k32, kk32 = k32_t[0]; vv, kvv = vv_t[0]; kk, kkk = kk_t[0]
                        qb, kqb = qb_t[b]; kb, kkb = kb_t[b]; va, kva = va_t[b]; kkb2, kkkb2 = kkb_t[b]
                        cx.update(qb=qb, kqb=kqb, kb=kb, kkb=kkb, va=va, kva=kva, kkb2=kkb2, kkkb2=kkkb2,
                                  vw=vw_t[b], PT=PTg_t[b], dC=dC_t[b], hb=hb_t[b])
                        s.dma('sp', q32[:, :, 0:GT], SCR['MQT'][:, t0:t0 + GT].rearrange("(h p) t -> p h t", p=64),
                              reads=[('MQT', 0, ti), ('MQT', 128, ti)], writes=[kq32], sk=kq32)
                        s.dma('sp', k32[:, :, 0:GT], SCR['MKT'][:, t0:t0 + GT].rearrange("(h p) t -> p h t", p=64),
                              reads=[('MKT', 0, ti), ('MKT', 128, ti)], writes=[kk32], sk=kk32)
                        s.dma('sp', vv[:, 0:G, :], TMV[t0:t0 + GT, 0:256].rearrange("(j t) c -> t j c", t=64), reads=[('TMV', ti)], writes=[kvv], sk=kvv)
                        s.dma('sp', kk[:, 0:G, :], TMV[t0:t0 + GT, 512:768].rearrange("(j t) c -> t j c", t=64), reads=[('TMV', ti)], writes=[kkk], sk=kkk)
                        s.op('act', lambda e: e.copy(out=qb[:, :, 0:GT], in_=q32[:, :, 0:GT]), reads=[kq32], writes=[kqb])
                        s.op('act', lambda e: e.mul(out=kb[:, :, 0:GT], in_=k32[:, :, 0:GT], mul=0.125), reads=[kk32], writes=[kkb])
                        s.op('pool', lambda e: e.tensor_copy(out=va[:, 0:G, :, 0:64], in_=vv[:, 0:G, :].rearrange("p g (h e) -> p g h e", e=64)),
                             reads=[kvv], writes=[kva])
                        s.op('act', lambda e: e.mul(out=kkb2[:, 0:G, :], in_=kk[:, 0:G, :], mul=0.125), reads=[kkk], writes=[kkkb2])
                        if d == 1:
                            hF, khF = hF_t[b]; mo_, kmo = mo_t[b]
                            cx.update(hF=hF, khF=khF, mo=mo_, kmo=kmo)
                            s.dma('sp', hF[:, 0:G, :], HF[t0:t0 + GT, :].rearrange("(j t) c -> t j c", t=64), reads=[('HF', t0)], writes=[khF], sk=khF)
                            s.dma('sp', mo_[:, 0:G, :], TMV[t0:t0 + GT, 256:512].rearrange("(j t) c -> t j c", t=64), reads=[('TMV', ti)], writes=[kmo], sk=kmo)
                        return cx

                    def stage1(cx, jl, d=d, mask=mask, G=G, GT=GT):
                        jg = cx['gi'] * G + jl
                        cs_ = slice(jl * 64, (jl + 1) * 64)
                        qb, kb, va, kkb2 = cx['qb'], cx['kb'], cx['va'], cx['kkb2']
                        (vw, kvw), (PT, kPT), (dC, kdC) = cx['vw'], cx['PT'], cx['dC']
                        psA, kpA = bank(0, 2)
                        for h in range(4):
                            s.op('pe', lambda e, h=h: e.matmul(psA[0:64, h * 64:(h + 1) * 64], lhsT=kb[:, h, cs_], rhs=qb[:, h, cs_], start=True, stop=True),
                                 reads=[cx['kkb'], cx['kqb']], writes=[kpA])
                        s.op('dve', lambda e: e.tensor_tensor(out=PT[:, jl], in0=psA[0:64, 0:256].rearrange("p (h t) -> p h t", t=64),
                                                              in1=mask.unsqueeze(1).to_broadcast([64, 4, 64]), op=ALU.mult),
                             reads=[kpA, kmc], writes=[(kPT, jl)])
                        s.op('pool', lambda e: e.tensor_tensor(out=vw[:, jl], in0=va[:, jl], in1=Wt[:, d * 4:d * 4 + 4, jg].unsqueeze(2).to_broadcast([64, 4, 65]), op=ALU.mult),
                             reads=[cx['kva'], (cx['kva'], 'ones'), kWt], writes=[(kvw, jl)])
                        psC, kpC = bank(2, 4)
                        for h in range(4):
                            s.op('pe', lambda e, h=h: e.matmul(psC[0:64, h * 65:(h + 1) * 65], lhsT=kkb2[:, jl, h * 64:(h + 1) * 64], rhs=vw[:, jl, h, :], start=True, stop=True),
                                 reads=[cx['kkkb2'], (kvw, jl)], writes=[kpC])
                        s.op('act', lambda e: e.copy(out=dC[:, jl], in_=psC[0:64, 0:260].rearrange("p (h e) -> p h e", e=65)), reads=[kpC], writes=[(kdC, jl)])

                    def stage2(cx, jl, d=d, G=G, GT=GT):
                        jg = cx['gi'] * G + jl
                        cs_ = slice(jl * 64, (jl + 1) * 64)
                        qb = cx['qb']
                        (vw, kvw), (PT, kPT), (dC, kdC), (hb, khb) = cx['vw'], cx['PT'], cx['dC'], cx['hb']
                        s.op('dve', lambda e: e.tensor_tensor(out=Cs, in0=Caug, in1=scB[:, jg, d * 4:d * 4 + 4].unsqueeze(2).to_broadcast([64, 4, 65]), op=ALU.mult),
                             reads=[kC, (kC, 'n'), kscB], writes=[kCs])
                        s.op('act', lambda e: e.copy(out=Csb, in_=Cs), reads=[kCs], writes=[kCsb])
                        s.op('dve', lambda e: e.tensor_tensor(out=Caug, in0=Cs, in1=dC[:, jl], op=ALU.add), reads=[kCs, (kdC, jl)], writes=[kC, (kC, 'n')])
                        psO, kpO = bank(4, 7)
                        for h in range(4):
                            s.op('pe', lambda e, h=h: e.matmul(psO[0:64, h * 65:(h + 1) * 65], lhsT=PT[:, jl, h, :], rhs=vw[:, jl, h, :], start=(h == 0), stop=False, skip_group_check=True),
                                 reads=[(kPT, jl), (kvw, jl)], writes=[kpO])
                            s.op('pe', lambda e, h=h: e.matmul(psO[0:64, h * 65:(h + 1) * 65], lhsT=qb[:, h, cs_], rhs=Csb[:, h, :], start=False, stop=True, skip_group_check=True),
                                 reads=[cx['kqb'], kCsb], writes=[kpO])
                        pO3 = psO[0:64, 0:260].rearrange("p (h e) -> p h e", e=65)
                        return lambda: stage2b(cx, jl, pO3, kpO)

                    def stage2b(cx, jl, pO3, kpO, d=d, G=G, GT=GT):
                        jg = cx['gi'] * G + jl
                        hb, khb = cx['hb']
                        s.op('act', lambda e: e.activation(out=den[:, jl % 2, :], in_=pO3[:, :, 64], func=AF.Abs), reads=[kpO], writes=[(kden, jl % 2)])
                        s.op('dve', lambda e: e.tensor_tensor(out=den[:, jl % 2, :], in0=den[:, jl % 2, :], in1=THRt[:, d * 4:d * 4 + 4, jg], op=ALU.max),
                             reads=[(kden, jl % 2), kTHRt], writes=[(kden, jl % 2)])
                        s.op('dve', lambda e: e.reciprocal(out=rec[:, jl % 2, :], in_=den[:, jl % 2, :]), reads=[(kden, jl % 2)], writes=[(krec, jl % 2)])
                        s.op('dve', lambda e: e.tensor_tensor(out=hb[:, jl, :].rearrange("p (h e) -> p h e", e=64), in0=pO3[:, :, 0:64],
                                                              in1=rec[:, jl % 2, :].unsqueeze(2).to_broadcast([64, 4, 64]), op=ALU.mult),
                             reads=[kpO, (krec, jl % 2)], writes=[(khb, jl)])

                    def finish_group(cx, d=d, G=G, GT=GT):
                        t0 = cx['t0']
                        hb, khb = cx['hb']
                        hk = [(khb, jl) for jl in range(G)]
                        if d == 0:
                            s.dma('pool', HF[t0:t0 + GT, :].rearrange("(j t) c -> t j c", t=64), hb[:, 0:G, :], reads=hk, writes=[('HF', t0)], sk=khb)
                            return
                        hF, khF, mo_, kmo = cx['hF'], cx['khF'], cx['mo'], cx['kmo']
                        s.op('pool', lambda e: e.tensor_tensor(out=hb[:, 0:G, :], in0=hb[:, 0:G, :], in1=hF[:, 0:G, :], op=ALU.add), reads=hk + [khF], writes=hk)
                        X4 = hb[:, 0:G, :].rearrange("p g (h e) -> p (g h) e", e=64)
                        n4 = G * 4
                        s.op('dve', lambda e: e.tensor_reduce(out=s1[:, 0:n4], in_=X4, axis=AX.X, op=ALU.add), reads=hk, writes=[ks1])
                        s.op('pool', lambda e: e.tensor_tensor(out=sqx[:, 0:G, :], in0=hb[:, 0:G, :], in1=hb[:, 0:G, :], op=ALU.mult), reads=hk, writes=[ksqx])
                        s.op('dve', lambda e: e.tensor_reduce(out=s2[:, 0:n4], in_=sqx[:, 0:G, :].rearrange("p g (h e) -> p (g h) e", e=64), axis=AX.X, op=ALU.add),
                             reads=[ksqx], writes=[ks2])
                        s.op('dve', lambda e: e.tensor_scalar_mul(out=s1[:, 0:n4], in0=s1[:, 0:n4], scalar1=1.0 / 64), reads=[ks1], writes=[ks1])
                        s.op('dve', lambda e: e.tensor_tensor(out=s3[:, 0:n4], in0=s1[:, 0:n4], in1=s1[:, 0:n4], op=ALU.mult), reads=[ks1], writes=[ks3])
                        s.op('dve', lambda e: e.scalar_tensor_tensor(out=s2[:, 0:n4], in0=s2[:, 0:n4], scalar=1.0 / 64, in1=s3[:, 0:n4], op0=ALU.mult, op1=ALU.subtract),
                             reads=[ks2, ks3], writes=[ks2])
                        s.op('act', lambda e: e.activation(out=s2[:, 0:n4], in_=s2[:, 0:n4], func=AF.Sqrt, bias=epsln[0:64, 1:2], scale=1.0), reads=[ks2, k_eps], writes=[ks2])
                        s.op('dve', lambda e: e.reciprocal(out=s2[:, 0:n4], in_=s2[:, 0:n4]), reads=[ks2], writes=[ks2])
                        s.op('dve', lambda e: e.tensor_tensor(out=X4, in0=X4, in1=s1[:, 0:n4].unsqueeze(2).to_broadcast([64, n4, 64]), op=ALU.subtract),
                             reads=hk + [ks1], writes=hk)
                        s.op('dve', lambda e: e.tensor_tensor(out=X4, in0=X4, in1=s2[:, 0:n4].unsqueeze(2).to_broadcast([64, n4, 64]), op=ALU.mult),
                             reads=hk + [ks2], writes=hk)
                        s.op('pool', lambda e: e.tensor_tensor(out=hb[:, 0:G, :], in0=hb[:, 0:G, :], in1=ng.unsqueeze(1).to_broadcast([64, G, 256]), op=ALU.mult),
                             reads=hk + [kng], writes=hk)
                        s.op('act', lambda e: e.activation(out=sig[:, 0:G, :], in_=mo_[:, 0:G, :], func=AF.Sigmoid), reads=[kmo], writes=[ksig])
                        xo, kxo = xo_t[0]
                        s.op('dve', lambda e: e.tensor_tensor(out=xo[:, 0:G, :], in0=hb[:, 0:G, :], in1=sig[:, 0:G, :], op=ALU.mult), reads=hk + [ksig], writes=[kxo])
                        mx, kmx = mixo_t[0]
                        ps, kp = bank(7, 8)
                        pbf = ps.bitcast(BF16)
                        for c in range(2):
                            for jl in range(G):
                                s.op('pe', lambda e, c=c, jl=jl: e.transpose(
                                    out=pbf[:, c * 512 + jl * 64:c * 512 + (jl + 1) * 64], in_=xo[:, jl, c * 128:(c + 1) * 128], identity=idb[0:64, 0:64]),
                                    reads=[kxo, k_idb], writes=[kp])
                        for c in range(2):
                            s.op('dve', lambda e, c=c: e.tensor_copy(out=mx[:, c, 0:GT], in_=pbf[:, c * 512:c * 512 + GT]), reads=[kp], writes=[(kmx, c)])
                        s.dma('pool', MIXT.rearrange("(c p) t -> p c t", p=128)[:, 4:6, t0:t0 + GT], mx[:, :, 0:GT],
                              reads=[(kmx, 0), (kmx, 1)], writes=[('MIXT_M', t0)], sk=kmx)

                    cur = load_group(gorder[0])
                    for jl in corder:
                        stage1(cur, jl)
                    for gidx, gi in enumerate(gorder):
                        nxt = load_group(gorder[gidx + 1]) if gidx + 1 < len(gorder) else None
                        pend = None
                        for jl in corder:
                            p2 = stage2(cur, jl)
                            if pend is not None:
                                pend()
                            pend = p2
                            if nxt is not None:
                                stage1(nxt, jl)
                        pend()
                        finish_group(cur)
                        cur = nxt
                    if not samp:
                        s.dma('pool', C_out[j, l, d].rearrange("h a b -> a h b"), Caug[:, :, 0:64], reads=[kC], sk=(kC, 0))
                        s.dma('pool', n_out[j, l, d].rearrange("h a -> a h"), Caug[:, :, 64], reads=[kC], sk=(kC, 1), allow_slow_non_contiguous=True)
            s.barrier()
            mem.release(m0_)

        def phase_hgrn(l):
            m0_ = mem.mark()
            A = lambda shape, dt=F32, name='g': mem.alloc(shape, dt, name)
            hc, khc = A([32, 2, 32]); ng, kng = A([32, 256]); lbl, klbl = A([64, 4, 4]); lbp, klbp = A([64, 4, 4])
            lbs, klbs = A([64, 4]); lbv, klbv = A([64, 3, 4])
            s.dma('sp', hc, hconst_d, writes=[khc], sk=khc)
            s.dma('sp', ng, hgng_d[l], writes=[kng], sk=kng)
            s.dma('sp', lbl, lbl_d, writes=[klbl], sk=klbl)
            s.op('act', lambda e: e.activation(out=lbp, in_=lbl, func=AF.Exp), reads=[klbl], writes=[klbp])
            s.op('dve', lambda e: e.tensor_reduce(out=lbs, in_=lbp, axis=AX.X, op=ALU.add), reads=[klbp], writes=[klbs])
            s.op('dve', lambda e: e.reciprocal(out=lbs, in_=lbs), reads=[klbs], writes=[klbs])
            s.op('dve', lambda e: e.tensor_tensor(out=lbp, in0=lbp, in1=lbs.unsqueeze(2).to_broadcast([64, 4, 4]), op=ALU.mult), reads=[klbp, klbs], writes=[klbp])
            if l == 0:
                s.op('dve', lambda e: e.memset(lbv[:, 0, :], 0.0), writes=[klbv])
            else:
                s.op('dve', lambda e: e.tensor_reduce(out=lbv[:, 0, :], in_=lbp[:, :, 1:l + 1], axis=AX.X, op=ALU.add), reads=[klbp], writes=[klbv])
            s.op('dve', lambda e: e.tensor_scalar(out=lbv[:, 1, :], in0=lbv[:, 0, :], scalar1=-1.0, scalar2=1.0, op0=ALU.mult, op1=ALU.add), reads=[klbv], writes=[klbv])
            s.op('dve', lambda e: e.tensor_scalar_mul(out=lbv[:, 2, :], in0=lbv[:, 1, :], scalar1=-1.0), reads=[klbv], writes=[klbv])
            rmask, krm = A([64, 512])
            s.op('pool', lambda e: e.memset(rmask, 1.0), writes=[krm])
            s.op('pool', lambda e: e.memset(rmask.rearrange("p (j t) -> p j t", t=32)[:, :, 0:1], 0.0), writes=[krm])
            GTM = 256
            gq32, kgq = A([64, 4, GTM]); gf32, kgf = A([64, 4, GTM]); qf, kqf = gq32, kgq
            sg, ksg = A([64, 4, GTM]); lg, klg = A([64, 4, GTM]); Bc, kBc = A([64, 4, GTM]); kf, kkf = A([64, 4, GTM])
            tot, ktot = A([64, 4, 8])
            egl_t = [A([64, 4, 8]) for _ in range(2)]
            qs_t = [A([64, 4, GTM], BF16) for _ in range(2)]
            ks_t = [A([64, 4, GTM], BF16) for _ in range(2)]
            v32, kv32 = A([32, 8, 256]); vb_t = [A([32, 8, 256], BF16) for _ in range(2)]
            PTg_t = [A([32, 8, 4, 32], BF16) for _ in range(2)]
            dS_t = [A([64, 8, 4, 64]) for _ in range(2)]
            oF, koF = A([32, 8, 256]); gg32, kgg = A([32, 8, 256]); ob_t = [A([32, 8, 256]) for _ in range(2)]
            S, kS = A([64, 4, 64]); Sb, kSb = A([64, 4, 64], BF16); St, kSt = A([64, 4, 64])
            kst_t = [A([32, 256], BF16) for _ in range(2)]
            s2, ks2 = A([32, 32]); sqx, ksqx = A([32, 8, 256]); xo, kxo = A([32, 8, 256], BF16)
            mx, kmx = A([128, 2, GTM], BF16)
            gcount = [0]
            for sq_ in SEQS:
                T, S0, samp, j = sq_['T'], sq_['t0'], sq_['sample'], sq_['j']
                GT = min(GTM, T)
                G = GT // 32
                ngrp = T // GT
                for d in range(2):
                    sview = lambda a: a.rearrange("h c e -> c h e")
                    if samp:
                        s.dma('sp', S, sview(stS_d[l, d]), writes=[kS], sk=kS)
                    else:
                        s.op('dve', lambda e: e.memset(S, 0.0), writes=[kS])
                    s.op('act', lambda e: e.copy(out=Sb, in_=S), reads=[kS], writes=[kSb])
                    mask = hc[:, d, :]
                    gorder = list(range(ngrp)) if d == 0 else list(range(ngrp - 1, -1, -1))
                    corder = list(range(G)) if d == 0 else list(range(G - 1, -1, -1))

                    def load_group(gi, d=d, G=G, GT=GT, S0=S0):
                        t0 = S0 + gi * GT
                        ti = t0 // 512
                        b = gcount[0] % 2
                        gcount[0] += 1
                        qs, kqs = qs_t[b]; ks, kks = ks_t[b]; vb, kvb = vb_t[b]; egl, kegl = egl_t[b]
                        cx = dict(gi=gi, t0=t0, ti=ti, b=b, qs=qs, kqs=kqs, ks=ks, kks=kks, vb=vb, kvb=kvb, egl=egl, kegl=kegl,
                                  PT=PTg_t[b], dS=dS_t[b], ob=ob_t[b])
                        fsrc = 'GFF' if d == 0 else 'GFB'
                        s.dma('sp', gq32[:, :, 0:GT], SCR['GQT'].rearrange("(c p) t -> p c t", p=64)[:, :, t0:t0 + GT],
                              reads=[('GQT', 0, ti), ('GQT', 128, ti)], writes=[kgq], sk=kgq)
                        s.dma('sp', gf32[:, :, 0:GT], SCR[fsrc].rearrange("(c p) t -> p c t", p=64)[:, :, t0:t0 + GT],
                              reads=[(fsrc, 0, ti), (fsrc, 128, ti)], writes=[kgf], sk=kgf)
                        s.dma('sp', v32[:, 0:G, :], TMV[t0:t0 + GT, 768:1024].rearrange("(j t) c -> t j c", t=32), reads=[('TMV', ti)], writes=[kv32], sk=kv32)
                        s.op('act', lambda e: e.copy(out=vb[:, 0:G, :], in_=v32[:, 0:G, :]), reads=[kv32], writes=[kvb])
                        W_ = slice(0, GT)
                        klgs = [(klg, i_) for i_ in range(4)]
                        kkfs = [(kkf, i_) for i_ in range(4)]
                        s.op('act', lambda e: e.activation(out=qf[:, :, W_], in_=gq32[:, :, W_], func=AF.Silu), reads=[kgq], writes=[kgq])
                        s.op('act', lambda e: e.activation(out=sg[:, :, W_], in_=gf32[:, :, W_], func=AF.Sigmoid), reads=[kgf], writes=[ksg])
                        for cc in range(4):
                            s.op('dve', lambda e, cc=cc: e.tensor_scalar(out=lg[:, cc, W_], in0=sg[:, cc, W_], scalar1=lbv[:, 1, cc:cc + 1], scalar2=lbv[:, 0, cc:cc + 1],
                                                                        op0=ALU.mult, op1=ALU.add), reads=[ksg, klbv], writes=[(klg, cc)])
                            s.op('pool', lambda e, cc=cc: e.tensor_scalar(out=kf[:, cc, W_], in0=sg[:, cc, W_], scalar1=lbv[:, 2, cc:cc + 1], scalar2=lbv[:, 1, cc:cc + 1],
                                                                         op0=ALU.mult, op1=ALU.add), reads=[ksg, klbv], writes=[(kkf, cc)])
                        s.op('act', lambda e: e.activation(out=lg[:, :, W_], in_=lg[:, :, W_], func=AF.Ln), reads=klgs, writes=klgs)
                        for cc in range(4):
                            s.op('dve', lambda e, cc=cc: e.tensor_tensor_scan(out=Bc[:, cc, W_], data0=rmask[:, W_], data1=lg[:, cc, W_], initial=0.0,
                                                                             op0=ALU.mult, op1=ALU.add), reads=[krm, (klg, cc)], writes=[(kBc, cc)])
                        Bc4 = Bc[:, :, W_].rearrange("p c (j t) -> p c j t", t=32)
                        lg4 = lg[:, :, W_].rearrange("p c (j t) -> p c j t", t=32)
                        kBcs = [(kBc, i_) for i_ in range(4)]
                        s.op('dve', lambda e: e.tensor_copy(out=tot[:, :, 0:G], in_=Bc4[:, :, :, 31]), reads=kBcs, writes=[ktot])
                        if d == 1:
                            s.op('dve', lambda e: e.tensor_tensor(out=Bc4, in0=lg4, in1=Bc4, op=ALU.subtract), reads=kBcs + klgs, writes=kBcs)
                            s.op('dve', lambda e: e.tensor_tensor(out=Bc4, in0=Bc4, in1=tot[:, :, 0:G].unsqueeze(3).to_broadcast([64, 4, G, 32]), op=ALU.add),
                                 reads=kBcs + [ktot], writes=kBcs)
                        s.op('act', lambda e: e.activation(out=egl[:, :, 0:G], in_=tot[:, :, 0:G], func=AF.Exp), reads=[ktot], writes=[kegl])
                        s.op('act', lambda e: e.activation(out=sg[:, :, W_], in_=Bc[:, :, W_], func=AF.Exp), reads=kBcs + [ksg] + kkfs + klgs, writes=[ksg])
                        s.op('dve', lambda e: e.tensor_tensor(out=qs[:, :, W_], in0=qf[:, :, W_], in1=sg[:, :, W_], op=ALU.mult), reads=[kgq, ksg], writes=[kqs])
                        s.op('dve', lambda e: e.tensor_scalar_max(out=Bc[:, :, W_], in0=Bc[:, :, W_], scalar1=-80.0), reads=kBcs + [ksg], writes=kBcs)
                        s.op('act', lambda e: e.activation(out=lg[:, :, W_], in_=Bc[:, :, W_], func=AF.Exp, scale=-1.0), reads=kBcs + klgs, writes=klgs)
                        s.op('dve', lambda e: e.tensor_tensor(out=ks[:, :, W_], in0=kf[:, :, W_], in1=lg[:, :, W_], op=ALU.mult), reads=kkfs + klgs, writes=[kks])
                        return cx

                    def stage1(cx, jl, d=d, mask=mask, G=G, GT=GT):
                        cs_ = slice(jl * 32, (jl + 1) * 32)
                        qs, ks, vb, egl = cx['qs'], cx['ks'], cx['vb'], cx['egl']
                        (PT, kPT), (dS, kdS) = cx['PT'], cx['dS']
                        psA, kpA = bank(0, 2)
                        for h in range(4):
                            s.op('pe', lambda e, h=h: e.matmul(psA[0:32, h * 32:(h + 1) * 32], lhsT=ks[:, h, cs_], rhs=qs[:, h, cs_], start=True, stop=True),
                                 reads=[cx['kks'], cx['kqs']], writes=[kpA])
                        psT, kpT = bank(2, 4)
                        pbf = psT.bitcast(BF16)
                        for cc in range(4):
                            s.op('pe', lambda e, cc=cc: e.transpose(out=pbf[0:32, cc * 64:(cc + 1) * 64], in_=ks[:, cc, cs_], identity=idb[0:64, 0:64]),
                                 reads=[cx['kks'], k_idb], writes=[kpT])
                        kst, kkst = kst_t[jl % 2]
                        s.op('dve', lambda e: e.tensor_tensor(out=PT[:, jl], in0=psA[0:32, 0:128].rearrange("p (h t) -> p h t", t=32),
                                                              in1=mask.unsqueeze(1).to_broadcast([32, 4, 32]), op=ALU.mult),
                             reads=[kpA, khc], writes=[(kPT, jl)])
                        s.op('act', lambda e: e.copy(out=kst, in_=pbf[0:32, 0:256]), reads=[kpT], writes=[kkst])
                        psS, kpS = bank(4, 6)
                        for h in range(4):
                            s.op('pe', lambda e, h=h: e.matmul(psS[0:64, h * 64:(h + 1) * 64], lhsT=kst[:, h * 64:(h + 1) * 64], rhs=vb[:, jl, h * 64:(h + 1) * 64], start=True, stop=True),
                                 reads=[kkst, cx['kvb']], writes=[kpS])
                        s.op('dve', lambda e: e.tensor_tensor(out=dS[:, jl], in0=psS[0:64, 0:256].rearrange("p (c e) -> p c e", e=64),
                                                              in1=egl[:, :, jl].unsqueeze(2).to_broadcast([64, 4, 64]), op=ALU.mult),
                             reads=[kpS, cx['kegl']], writes=[(kdS, jl)])

                    def stage2(cx, jl, d=d, G=G, GT=GT):
                        cs_ = slice(jl * 32, (jl + 1) * 32)
                        qs, vb, egl = cx['qs'], cx['vb'], cx['egl']
                        (PT, kPT), (dS, kdS), (ob, kob) = cx['PT'], cx['dS'], cx['ob']
                        psO, kpO = bank(6, 8)
                        for h in range(4):
                            s.op('pe', lambda e, h=h: e.matmul(psO[0:32, h * 64:(h + 1) * 64], lhsT=PT[:, jl, h, :], rhs=vb[:, jl, h * 64:(h + 1) * 64],
                                                               start=(h == 0), stop=False, skip_group_check=True), reads=[(kPT, jl), cx['kvb']], writes=[kpO])
                            s.op('pe', lambda e, h=h: e.matmul(psO[0:32, h * 64:(h + 1) * 64], lhsT=qs[:, h, cs_], rhs=Sb[:, h, :],
                                                               start=False, stop=True, skip_group_check=True), reads=[cx['kqs'], kSb], writes=[kpO])
                        s.op('dve', lambda e: e.tensor_tensor(out=St, in0=S, in1=egl[:, :, jl].unsqueeze(2).to_broadcast([64, 4, 64]), op=ALU.mult),
                             reads=[kS, cx['kegl']], writes=[kSt])
                        s.op('dve', lambda e: e.tensor_tensor(out=S, in0=St, in1=dS[:, jl], op=ALU.add), reads=[kSt, (kdS, jl)], writes=[kS])
                        s.op('act', lambda e: e.copy(out=Sb, in_=S), reads=[kS], writes=[kSb])
                        return lambda: s.op('act', lambda e: e.copy(out=ob[:, jl, :], in_=psO[0:32, 0:256]), reads=[kpO], writes=[(kob, jl)])

                    def finish_group(cx, d=d, G=G, GT=GT):
                        t0, ti = cx['t0'], cx['ti']
                        ob, kob = cx['ob']
                        ok_ = [(kob, jl) for jl in range(G)]
                        if d == 0:
                            s.dma('pool', HF[t0:t0 + GT, :].rearrange("(j t) c -> t j c", t=32), ob[:, 0:G, :], reads=ok_, writes=[('HF', t0)], sk=kob)
                            return
                        s.dma('sp', oF[:, 0:G, :], HF[t0:t0 + GT, :].rearrange("(j t) c -> t j c", t=32), reads=[('HF', t0)], writes=[koF], sk=koF)
                        s.dma('sp', gg32[:, 0:G, :], TMV[t0:t0 + GT, 1024:1280].rearrange("(j t) c -> t j c", t=32), reads=[('TMV', ti)], writes=[kgg], sk=kgg)
                        s.op('pool', lambda e: e.tensor_tensor(out=ob[:, 0:G, :], in0=ob[:, 0:G, :], in1=oF[:, 0:G, :], op=ALU.add), reads=ok_ + [koF], writes=ok_)
                        n4 = G * 4
                        s.op('pool', lambda e: e.tensor_tensor(out=sqx[:, 0:G, :], in0=ob[:, 0:G, :], in1=ob[:, 0:G, :], op=ALU.mult), reads=ok_, writes=[ksqx])
                        s.op('dve', lambda e: e.tensor_reduce(out=s2[:, 0:n4], in_=sqx[:, 0:G, :].rearrange("p g (h e) -> p (g h) e", e=64), axis=AX.X, op=ALU.add),
                             reads=[ksqx], writes=[ks2])
                        s.op('act', lambda e: e.activation(out=s2[:, 0:n4], in_=s2[:, 0:n4], func=AF.Sqrt, bias=epsln[0:32, 1:2], scale=1.0 / 64), reads=[ks2, k_eps], writes=[ks2])
                        s.op('dve', lambda e: e.reciprocal(out=s2[:, 0:n4], in_=s2[:, 0:n4]), reads=[ks2], writes=[ks2])
                        X4 = ob[:, 0:G, :].rearrange("p g (h e) -> p (g h) e", e=64)
                        s.op('dve', lambda e: e.tensor_tensor(out=X4, in0=X4, in1=s2[:, 0:n4].unsqueeze(2).to_broadcast([32, n4, 64]), op=ALU.mult),
                             reads=ok_ + [ks2], writes=ok_)
                        s.op('pool', lambda e: e.tensor_tensor(out=ob[:, 0:G, :], in0=ob[:, 0:G, :], in1=ng.unsqueeze(1).to_broadcast([32, G, 256]), op=ALU.mult),
                             reads=ok_ + [kng], writes=ok_)
                        s.op('act', lambda e: e.activation(out=sqx[:, 0:G, :], in_=gg32[:, 0:G, :], func=AF.Silu), reads=[kgg, ksqx, ks2], writes=[ksqx])
                        s.op('dve', lambda e: e.tensor_tensor(out=xo[:, 0:G, :], in0=ob[:, 0:G, :], in1=sqx[:, 0:G, :], op=ALU.mult), reads=ok_ + [ksqx], writes=[kxo])
                        ps, kp = bank(0, 2)
                        pbf = ps.bitcast(BF16)
                        for c in range(2):
                            for jl in range(G):
                                s.op('pe', lambda e, c=c, jl=jl: e.transpose(
                                    out=pbf[:, c * 512 + jl * 32:c * 512 + (jl + 1) * 32], in_=xo[:, jl, c * 128:(c + 1) * 128], identity=idb[0:32, 0:32]),
                                    reads=[kxo, k_idb], writes=[kp])
                        for c in range(2):
                            s.op('dve', lambda e, c=c: e.tensor_copy(out=mx[:, c, 0:GT], in_=pbf[:, c * 512:c * 512 + GT]), reads=[kp], writes=[(kmx, c)])
                        s.dma('pool', MIXT.rearrange("(c p) t -> p c t", p=128)[:, 6:8, t0:t0 + GT], mx[:, :, 0:GT],
                              reads=[(kmx, 0), (kmx, 1)], writes=[('MIXT_G', t0)], sk=kmx)

                    cur = load_group(gorder[0])
                    for jl in corder:
                        stage1(cur, jl)
                    for gidx, gi in enumerate(gorder):
                        nxt = load_group(gorder[gidx + 1]) if gidx + 1 < len(gorder) else None
                        pend = None
                        for jl in corder:
                            p2 = stage2(cur, jl)
                            if pend is not None:
                                pend()
                            pend = p2
                            if nxt is not None and not HG_NOINT:
                                stage1(nxt, jl)
                        pend()
                        if nxt is not None and HG_NOINT:
                            for jl in corder:
                                stage1(nxt, jl)
                        finish_group(cur)
                        cur = nxt
                    if not samp:
                        s.dma('pool', sview(S_out[j, l, d]), S, reads=[kS], sk=(kS, 'o'))
            s.barrier()
            mem.release(m0_)

        phase_init()
        for l in range(NL):
            phase_mod(l)
            if 'p1' in PH:
                phase_p1(l)
            if 'mla' in PH:
                phase_mla(l)
            if 'mlstm' in PH:
                phase_mlstm(l)
            if 'hgrn' in PH:
                phase_hgrn(l)
            if 'dense' in PH:
                phase_p3(l)
                phase_p3b(l)
                phase_p4(l)
        phase_final()
        s.emit()
    return nc


def _bf(x):
    return np.ascontiguousarray(x, dtype=np.float32)


def make_in_maps(inputs, cfg):
    NL = cfg.get('n_layers', DEPTH)
    NST = cfg.get('n_sample_tiles', 8)
    NPR = cfg.get('n_prompts', 4)
    cores = cfg.get('cores', list(range(8)))
    g = {k: np.asarray(v) for k, v in inputs.items()}
    w1 = _bf(g['w_in'][:NL][:, :, W1_COLS])
    b1 = g['b_in'][:NL][:, W1_COLS]
    b1fm = np.zeros((NL, 128, NG), np.float32)
    for gi, (c0, M, _, _) in enumerate(FM_GROUPS):
        b1fm[:, :M, gi] = b1[:, c0:c0 + M]
    b1tm = _bf(np.broadcast_to(b1[:, None, NFM:], (NL, 128, NTM)))
    b_modT = _bf(g['b_mod'][:NL].reshape(NL, 48, 128).transpose(0, 2, 1))
    lnp = _bf(np.stack([g[k][:NL].reshape(NL, NKC, 128).transpose(0, 2, 1) for k in ('ln1_g', 'ln1_b', 'ln2_g', 'ln2_b')], axis=2))
    shared = dict(w_mod=_bf(g['w_mod'][:NL]), b_modT=b_modT, w1=w1, b1fm=b1fm, b1tm=b1tm, w_out=_bf(g['w_out'][:NL]),
                  lnp=lnp, w_ffn_in=_bf(g['w_ffn_in'][:NL]), w_ffn_out=_bf(g['w_ffn_out'][:NL]),
                  identf=np.eye(128, dtype=np.float32))
    wuq = g['w_uq'][:NL]
    wqp = np.zeros_like(wuq)
    for h in range(8):
        wqp[:, :, h * 96 + 64:h * 96 + 96] = wuq[:, :, h * 96 + 64 + _PERM]
    wukv = g['w_ukv'][:NL]
    wk = np.zeros((NL, 128, 768), np.float32)
    wv = np.zeros((NL, 128, 512), np.float32)
    for h in range(8):
        wk[:, :, h * 96:h * 96 + 64] = wukv[:, :, h * 128:h * 128 + 64]
        wv[:, :, h * 64:(h + 1) * 64] = wukv[:, :, h * 128 + 64:h * 128 + 128]
    e96 = np.zeros((32, 96), np.float32)
    e96[np.arange(32), 64 + np.arange(32)] = 1.0
    qkn = _bf(np.stack([g['mla_q_norm'][:NL, :128], g['mla_q_norm'][:NL, 128:], g['mla_kv_norm'][:NL]], axis=-1))
    pos = np.arange(4096)
    freqs = 10000.0 ** (-np.arange(8, dtype=np.float64) * 0.125)
    ar = (pos // 64)[:, None] * freqs
    ac = (pos % 64)[:, None] * freqs
    ang = np.concatenate([ar, ar, ac, ac], -1)
    cosT = np.cos(ang).T.astype(np.float32)
    sinT = (np.sin(ang) * _ROT_SIGN).T.astype(np.float32)
    cos96 = np.ones((96, 4096), np.float32)
    sin96 = np.zeros((96, 4096), np.float32)
    cos96[64:] = cosT
    sin96[64:] = sinT
    shared.update(wq=_bf(wuq), wqp=_bf(wqp), wk=wk, wv=wv, e96=e96, qkn=qkn, cos96=cos96, sin96=sin96,
                  kcos=_bf(cosT), ksin=_bf(sinT))
    mconst = np.zeros((64, 4, 64), np.float32)
    ii = np.arange(64)
    mconst[:, 0, :] = (ii[None, :] >= ii[:, None])
    mconst[:, 1, :] = (ii[:, None] >= ii[None, :])
    mconst[:, 2, :] = np.eye(64)[::-1]
    mconst[0:4, 3, 0:4] = np.eye(4)[::-1]
    shared.update(mconst=mconst, mlng=_bf(np.broadcast_to(g['mlstm_norm'][:NL, None, :], (NL, 64, 256))))
    hconst = np.zeros((32, 2, 32), np.float32)
    i32 = np.arange(32)
    hconst[:, 0, :] = (i32[None, :] >= i32[:, None])
    hconst[:, 1, :] = (i32[:, None] >= i32[None, :])
    lbl = _bf(g['hgrn_lb_logits'].reshape(4, 4, 64).transpose(2, 1, 0))
    shared.update(hconst=hconst, lbl=lbl, hgng=_bf(np.broadcast_to(g['hgrn_norm'][:NL, None, :], (NL, 32, 256))))
    maps = []
    for ci in cores:
        b = ci // 2
        parts = []
        if NST:
            parts.append(g['x_sample'][b, :NST * 512])
        for j in range(NPR):
            parts.append(g['x_prompt'][4 * ci + j])
        xin = _bf(np.concatenate(parts, axis=0))
        cv = np.stack([g['c'][b], g['c_ctx']], axis=-1)
        cvT = _bf(cv.reshape(NKC, 128, 2).transpose(1, 0, 2))
        m = dict(shared)
        m.update(stC=_bf(g['state_mlstm_C'][b, :NL]), stn=_bf(g['state_mlstm_n'][b, :NL]), stm=_bf(g['state_mlstm_m'][b, :NL]))
        m.update(stS=_bf(g['state_hgrn_S'][b, :NL]))
        m.update(xin=xin, cvT=cvT, cckv=_bf(g['cache_mla_ckv'][b, :NL]), ckpe=_bf(g['cache_mla_kpe'][b, :NL]))
        maps.append(m)
    return maps


def kernel(**inputs):
    cfg = {}
    nc = build_program(cfg)
    maps = make_in_maps(inputs, cfg)
    res = run_bass_kernel_spmd(nc, maps, core_ids=list(range(8)))
    r = res.results
    cat = lambda k: np.ascontiguousarray(np.concatenate([r[i][k] for i in range(8)], axis=0), dtype=np.float32)
    y_prompt = np.concatenate([r[i]['y_out'][4096:].reshape(4, 256, D) for i in range(8)], axis=0).astype(np.float32)
    y_sample = np.stack([r[2 * b]['y_out'][:4096] for b in range(4)], axis=0).astype(np.float32)
    return (y_prompt, y_sample, cat('ckv_out'), cat('kpe_out'), cat('C_out'), cat('n_out'), cat('m_out'), cat('S_out'))
```

```python
import math
from contextlib import ExitStack
import numpy as np
import concourse.bass as bass
import concourse.mybir as mybir
from concourse.bass_utils import run_bass_kernel_spmd

F32 = mybir.dt.float32
BF16 = mybir.dt.bfloat16
AF = mybir.ActivationFunctionType
ALU = mybir.AluOpType
AX = mybir.AxisListType

D = 1024
DEPTH = 4
NKC = 8
FF = 2816
NFC = 22
ALPHA = (2 * DEPTH) ** 0.25
EPS = 1e-6
EPS_LN = EPS / (ALPHA * ALPHA)
ATT_SCALE = 96 ** -0.5

_IN_SIZES = (256, 128, 32, 256, 256, 256, 256, 4, 4, 4, 4, 256, 256, 256, 256, 256)
_OFF = np.cumsum((0,) + _IN_SIZES)
_NAMES = ['cq', 'ckv', 'kpe', 'mq', 'mk', 'mv', 'mo', 'mi_f', 'mi_b', 'mf_f', 'mf_b', 'gq', 'gf_f', 'gf_b', 'gi', 'gg']
_COL = {n: np.arange(_OFF[i], _OFF[i + 1]) for i, n in enumerate(_NAMES)}
_PERM = np.concatenate([np.arange(8, 16), np.arange(0, 8), np.arange(24, 32), np.arange(16, 24)])
_ROT_SIGN = np.concatenate([-np.ones(8), np.ones(8), -np.ones(8), np.ones(8)]).astype(np.float32)
FM_GROUPS = []
_fm_cols = []


def _add_fm(cols, dst, r0):
    c0 = sum(len(c) for c in _fm_cols)
    FM_GROUPS.append((c0, len(cols), dst, r0))
    _fm_cols.append(np.asarray(cols))


_add_fm(_COL['cq'][:128], 'CQT', 0)
_add_fm(_COL['cq'][128:], 'CQT', 128)
_add_fm(_COL['ckv'], 'CKVT', 0)
_add_fm(_COL['kpe'], 'KPET', 0)
_add_fm(_COL['kpe'][_PERM], 'KPEP', 0)
for _n, _d in (('mq', 'MQT'), ('mk', 'MKT'), ('gq', 'GQT'), ('gf_f', 'GFF'), ('gf_b', 'GFB')):
    _add_fm(_COL[_n][:128], _d, 0)
    _add_fm(_COL[_n][128:], _d, 128)
for _n, _d in (('mi_f', 'MIF'), ('mi_b', 'MIB'), ('mf_f', 'MFF'), ('mf_b', 'MFB')):
    _add_fm(_COL[_n], _d, 0)
NFM = sum(len(c) for c in _fm_cols)
_tm_cols = [_COL['mv'], _COL['mo'], _COL['mk'], _COL['gi'], _COL['gg']]
NTM = 1280
TM_GROUPS = [(0, 512), (512, 512), (1024, 256)]
W1_COLS = np.concatenate(_fm_cols + _tm_cols)
NW1 = len(W1_COLS)
NG = len(FM_GROUPS)
FM_SCR = {'CQT': 256, 'CKVT': 128, 'KPET': 32, 'KPEP': 32, 'MQT': 256, 'MKT': 256, 'GQT': 256, 'GFF': 256,
          'GFB': 256, 'MIF': 4, 'MIB': 4, 'MFF': 4, 'MFB': 4}


class Sched:
    EPOCH = 40000

    def __init__(self, nc, stack):
        self.nc = nc
        self.stack = stack
        self.eng = {'pe': nc.tensor, 'act': nc.scalar, 'dve': nc.vector, 'pool': nc.gpsimd, 'sp': nc.sync}
        self.prog = {k: [] for k in self.eng}
        self.seq = {k: 0 for k in self.eng}
        self.esems = {k: [] for k in self.eng}
        self.dsem = {}
        self.free_dsems = {}
        self.lastw = {}
        self.readers = {}
        self.waited = {}
        self.nsem = 0

    def _newsem(self, name):
        self.nsem += 1
        return self.stack.enter_context(self.nc.semaphore(name))

    def _deps(self, engine, reads, writes):
        deps = {}

        def add(ev):
            k = id(ev[0])
            if k not in deps or deps[k][1] < ev[1]:
                deps[k] = ev
        for k in reads:
            if k in self.lastw:
                add(self.lastw[k])
        for k in writes:
            if k in self.lastw:
                add(self.lastw[k])
            for ev in self.readers.get(k, {}).values():
                add(ev)
        waits = []
        own = set(id(s) for s in self.esems[engine]) if engine == 'pe' else set()
        for k, (sem, val) in deps.items():
            if k in own:
                continue
            wk = (engine, k)
            if self.waited.get(wk, 0) < val:
                self.waited[wk] = val
                waits.append((sem, val))
        return waits

    def _commit(self, ev, reads, writes):
        for k in writes:
            self.lastw[k] = ev
            self.readers[k] = {}
        for k in reads:
            self.readers.setdefault(k, {})[id(ev[0])] = ev

    def op(self, engine, fn, reads=(), writes=()):
        waits = self._deps(engine, reads, writes)
        n = self.seq[engine]
        e = n // self.EPOCH
        while len(self.esems[engine]) <= e:
            self.esems[engine].append(self._newsem(f"e_{engine}_{len(self.esems[engine])}"))
        sem = self.esems[engine][e]
        val = n % self.EPOCH + 1
        self.seq[engine] = n + 1
        self.prog[engine].append((waits, fn, sem, 1))
        self._commit((sem, val), reads, writes)

    def dma(self, queue, out, in_, reads=(), writes=(), sk=None, **kw):
        self.dma_group(queue, [(out, in_)], reads, writes, sk, **kw)

    def dma_group(self, queue, pairs, reads=(), writes=(), sk=None, **kw):
        assert sk is not None
        sk = (queue, sk)
        if sk not in self.dsem:
            fl = self.free_dsems.setdefault(queue, [])
            self.dsem[sk] = fl.pop() if fl else [self._newsem(f"d_{self.nsem}"), 0]
        ent = self.dsem[sk]
        waits = self._deps(queue, reads, writes)
        for i, (o, a) in enumerate(pairs):
            ent[1] += 16
            self.prog[queue].append((waits if i == 0 else [],
                                     (lambda eng, o=o, a=a: eng.dma_start(out=o, in_=a, **kw)), ent[0], 16))
        self._commit((ent[0], ent[1]), reads, writes)

    def barrier(self):
        evs = []
        for e, sems in self.esems.items():
            n = self.seq[e]
            if n > 0:
                evs.append((sems[(n - 1) // self.EPOCH], (n - 1) % self.EPOCH + 1))
        for sk, (sem, cnt) in self.dsem.items():
            if cnt > 0:
                evs.append((sem, cnt))
        for fl in self.free_dsems.values():
            for sem, cnt in fl:
                if cnt > 0:
                    evs.append((sem, cnt))
        for engine in self.eng:
            waits = []
            for sem, val in evs:
                wk = (engine, id(sem))
                if self.waited.get(wk, 0) < val:
                    self.waited[wk] = val
                    waits.append((sem, val))
            if waits:
                self.prog[engine].append((waits, None, None, 0))
        for (q, _), ent in self.dsem.items():
            self.free_dsems.setdefault(q, []).append(ent)
        self.dsem = {}
        self.lastw = {}
        self.readers = {}

    def emit(self):
        self.barrier()
        nc = self.nc
        with nc.Block() as block:
            def run(name):
                def f(eng):
                    for waits, fn, sem, amt in self.prog[name]:
                        for ws, wv in waits:
                            eng.wait_ge(ws, wv)
                        if fn is not None:
                            fn(eng).then_inc(sem, amt)
                return f
            block.sync(run('sp'))
            block.tensor(run('pe'))
            block.scalar(run('act'))
            block.vector(run('dve'))
            block.gpsimd(run('pool'))


class Mem:
    def __init__(self, big, words):
        self.big = big
        self.words = words
        self.top = 0
        self.n = 0

    def mark(self):
        return self.top

    def release(self, m):
        self.top = m

    def alloc(self, shape, dt=F32, name='t'):
        P = shape[0]
        n = int(np.prod(shape[1:]))
        w = n if dt == F32 else (n + 1) // 2
        w = (w + 15) // 16 * 16
        assert self.top + w <= self.words, f"SBUF overflow allocating {name} {shape}"
        a = self.big[0:P, self.top:self.top + w]
        self.top += w
        if dt != F32:
            a = a.bitcast(dt)
        a = a[:, 0:n]
        if len(shape) == 3:
            a = a.rearrange("p (a b) -> p a b", a=shape[1], b=shape[2])
        elif len(shape) == 4:
            a = a.rearrange("p (a b c) -> p a b c", a=shape[1], b=shape[2], c=shape[3])
        self.n += 1
        return a, f"{name}#{self.n}"


def build_program(cfg):
    NL = cfg.get('n_layers', DEPTH)
    NST = cfg.get('n_sample_tiles', 8)
    NPR = cfg.get('n_prompts', 4)
    DBG = cfg.get('debug', [])
    PH = cfg.get('phases', ['p1', 'mla', 'mlstm', 'hgrn', 'dense'])
    MIXIN = cfg.get('mix_input', False)
    HG_NOINT = cfg.get('hg_noint', False)
    TS = NST * 512
    TT = TS + NPR * 256
    NTILE = TT // 512
    nc = bass.Bass("TRN2", target_bir_lowering=False)

    def din(name, shape, dt=F32):
        return nc.dram_tensor(name, list(shape), dt, kind="ExternalInput").ap()

    def dout(name, shape, dt=F32):
        return nc.dram_tensor(name, list(shape), dt, kind="ExternalOutput").ap()

    def dscr(name, shape, dt=F32):
        kind = "ExternalOutput" if name in DBG else "Internal"
        return nc.dram_tensor(name, list(shape), dt, kind=kind).ap()

    xin = din("xin", [TT, D])
    cvT = din("cvT", [128, NKC, 2])
    w_mod = din("w_mod", [NL, D, 6 * D])
    b_modT = din("b_modT", [NL, 128, 48])
    w1 = din("w1", [NL, D, NW1])
    b1fm = din("b1fm", [NL, 128, NG])
    b1tm = din("b1tm", [NL, 128, NTM])
    w_out = din("w_out", [NL, D, D])
    lnp = din("lnp", [NL, 128, 4, NKC])
    w_ffn_in = din("w_ffn_in", [NL, D, 2 * FF])
    w_ffn_out = din("w_ffn_out", [NL, FF, D])
    identf = din("identf", [128, 128])
    wq_d = din("wq", [NL, 256, 768])
    wqp_d = din("wqp", [NL, 256, 768])
    wk_d = din("wk", [NL, 128, 768])
    wv_d = din("wv", [NL, 128, 512])
    e96_d = din("e96", [32, 96])
    qkn_d = din("qkn", [NL, 128, 3])
    cos96_d = din("cos96", [96, 4096])
    sin96_d = din("sin96", [96, 4096])
    kcos_d = din("kcos", [32, 4096])
    ksin_d = din("ksin", [32, 4096])
    cckv_d = din("cckv", [NL, 512, 128])
    ckpe_d = din("ckpe", [NL, 512, 32])
    mconst_d = din("mconst", [64, 4, 64])
    mlng_d = din("mlng", [NL, 64, 256])
    stC_d = din("stC", [NL, 2, 4, 64, 64])
    stn_d = din("stn", [NL, 2, 4, 64])
    stm_d = din("stm", [NL, 2, 4])
    C_out = dout("C_out", [max(NPR, 1), NL, 2, 4, 64, 64])
    n_out = dout("n_out", [max(NPR, 1), NL, 2, 4, 64])
    m_out = dout("m_out", [max(NPR, 1), NL, 2, 4])
    HF = dscr("HF", [TT, 256])
    hconst_d = din("hconst", [32, 2, 32])
    hgng_d = din("hgng", [NL, 32, 256])
    lbl_d = din("lbl", [64, 4, 4])
    stS_d = din("stS", [NL, 2, 4, 64, 64])
    S_out = dout("S_out", [max(NPR, 1), NL, 2, 4, 64, 64])
    ckv_out = dout("ckv_out", [max(NPR, 1), NL, 256, 128])
    kpe_out = dout("kpe_out", [max(NPR, 1), NL, 256, 32])
    y_out = dout("y_out", [TT, D])

    XT = dscr("XT", [D, TT])
    MIXT = din("MIXT", [D, TT], BF16) if MIXIN else dscr("MIXT", [D, TT], BF16)
    UT = dscr("UT", [FF, TT], BF16)
    SCR = {k: dscr(k, [r, TT]) for k, r in FM_SCR.items()}
    TMV = dscr("TMV", [TT, NTM])

    with ExitStack() as st:
        WORDS = 47 * 1024
        big = st.enter_context(nc.sbuf_tensor("big", [128, WORDS], F32))
        psb = [st.enter_context(nc.psum_tensor(f"ps{i}", [128, 512], F32)) for i in range(8)]
        s = Sched(nc, st)
        mem = Mem(big, WORDS)
        psi = {}

        def bank(lo=0, hi=8):
            c = psi.get((lo, hi), 0)
            psi[(lo, hi)] = c + 1
            i = lo + c % (hi - lo)
            return psb[i], ('ps', i)

        idf, k_idf = mem.alloc([128, 128], F32, 'idf')
        idb, k_idb = mem.alloc([128, 128], BF16, 'idb')
        onesf, k_onesf = mem.alloc([128, 128], F32, 'onesf')
        modT, k_mod = mem.alloc([128, 48, 2], F32, 'modT')
        lnt, k_lnt = mem.alloc([128, 4, NKC], F32, 'lnt')
        s.dma('sp', idf, identf, writes=[k_idf], sk='idf')
        s.op('dve', lambda e: e.tensor_copy(out=idb, in_=idf), reads=[k_idf], writes=[k_idb])
        s.op('pool', lambda e: e.memset(onesf, 1.0), writes=[k_onesf])
        epsln, k_eps = mem.alloc([128, 2], F32, 'epsln')
        s.op('pool', lambda e: e.memset(epsln[:, 0:1], EPS_LN), writes=[k_eps])
        s.op('pool', lambda e: e.memset(epsln[:, 1:2], EPS), writes=[k_eps])
        base_mark = mem.mark()

        def load_w_bf16(dram2d, dst, kdst, nkc, ncols, stage):
            i = 0
            cw = stage[0][0].shape[1]
            for kc in range(nkc):
                for c0 in range(0, ncols, cw):
                    w = min(cw, ncols - c0)
                    sa, sk_ = stage[i % len(stage)]
                    i += 1
                    s.dma('sp', sa[:, 0:w], dram2d[kc * 128:(kc + 1) * 128, c0:c0 + w], writes=[sk_], sk=sk_)
                    ce = ('pool', 'dve', 'act')[i % 3]
                    if ce == 'act':
                        s.op('act', lambda e, kc=kc, c0=c0, w=w, sa=sa: e.copy(out=dst[:, kc, c0:c0 + w], in_=sa[:, 0:w]), reads=[sk_], writes=[kdst])
                    else:
                        s.op(ce, lambda e, kc=kc, c0=c0, w=w, sa=sa: e.tensor_copy(out=dst[:, kc, c0:c0 + w], in_=sa[:, 0:w]), reads=[sk_], writes=[kdst])

        def fm_view(dram2d, t0, n):
            return dram2d.rearrange("(c p) t -> p c t", p=128)[:, :, t0:t0 + n]

        def phase_init():
            m0 = mem.mark()
            xin_t = [mem.alloc([128, 4, D], F32, 'xin_t') for _ in range(2)]
            xT_t = [mem.alloc([128, NKC, 512], F32, 'xT_t') for _ in range(2)]
            for i in range(NTILE):
                t0 = i * 512
                xa, kx = xin_t[i % 2]
                xo, ko = xT_t[i % 2]
                s.dma('sp', xa, xin[t0:t0 + 512, :].rearrange("(b p) c -> p b c", p=128), writes=[kx], sk=kx)
                for kc in range(NKC):
                    ps, kp = bank()
                    for b in range(4):
                        s.op('pe', lambda e, ps=ps, b=b, kc=kc, xa=xa: e.transpose(
                            out=ps[:, b * 128:(b + 1) * 128], in_=xa[:, b, kc * 128:(kc + 1) * 128], identity=idf),
                            reads=[kx, k_idf], writes=[kp])
                    eng = 'dve' if kc % 2 == 0 else 'act'
                    if eng == 'dve':
                        s.op('dve', lambda e, ps=ps, kc=kc, xo=xo: e.tensor_copy(out=xo[:, kc, :], in_=ps[:, :]),
                             reads=[kp], writes=[(ko, kc)])
                    else:
                        s.op('act', lambda e, ps=ps, kc=kc, xo=xo: e.copy(out=xo[:, kc, :], in_=ps[:, :]),
                             reads=[kp], writes=[(ko, kc)])
                s.dma('pool', fm_view(XT, t0, 512), xo, reads=[(ko, kc) for kc in range(NKC)], writes=[('XT', i)], sk=ko)
            s.barrier()
            mem.release(m0)

        def phase_final():
            m0 = mem.mark()
            xT_t = [mem.alloc([128, NKC, 512], F32, 'xT_f') for _ in range(2)]
            y_t = [mem.alloc([128, 4, D], F32, 'y_t') for _ in range(2)]
            for i in range(NTILE):
                t0 = i * 512
                xa, kx = xT_t[i % 2]
                ya, ky = y_t[i % 2]
                s.dma('sp', xa, fm_view(XT, t0, 512), reads=[('XT', i)], writes=[kx], sk=kx)
                for b in range(4):
                    for half in range(2):
                        ps, kp = bank()
                        for q in range(4):
                            kc = half * 4 + q
                            s.op('pe', lambda e, ps=ps, q=q, kc=kc, b=b, xa=xa: e.transpose(
                                out=ps[:, q * 128:(q + 1) * 128], in_=xa[:, kc, b * 128:(b + 1) * 128], identity=idf),
                                reads=[kx, k_idf], writes=[kp])
                        if half == 0:
                            s.op('dve', lambda e, ps=ps, b=b, ya=ya: e.tensor_copy(out=ya[:, b, 0:512], in_=ps[:, :]),
                                 reads=[kp], writes=[(ky, b, 0)])
                        else:
                            s.op('act', lambda e, ps=ps, b=b, ya=ya: e.copy(out=ya[:, b, 512:1024], in_=ps[:, :]),
                                 reads=[kp], writes=[(ky, b, 1)])
                s.dma('pool', y_out[t0:t0 + 512, :].rearrange("(b p) c -> p b c", p=128), ya,
                      reads=[(ky, b, h) for b in range(4) for h in range(2)], sk=ky)
            s.barrier()
            mem.release(m0)

        def phase_mod(l):
            m0 = mem.mark()
            cv, kcv = mem.alloc([128, NKC, 2], F32, 'cv')
            sil, ksil = mem.alloc([128, NKC, 2], F32, 'sil')
            bm, kbm = mem.alloc([128, 48], F32, 'bm')
            stage = [mem.alloc([128, 6 * D], F32, 'wm') for _ in range(2)]
            s.dma('sp', cv, cvT, writes=[kcv], sk=kcv)
            s.dma('sp', bm, b_modT[l], writes=[kbm], sk=kbm)
            s.dma('sp', lnt, lnp[l], writes=[k_lnt], sk=k_lnt)
            s.op('act', lambda e: e.activation(out=sil, in_=cv, func=AF.Silu), reads=[kcv], writes=[ksil])
            ps, kp = bank()
            for kc in range(NKC):
                sa, ks = stage[kc % 2]
                s.dma('sp', sa, w_mod[l, kc * 128:(kc + 1) * 128, :], writes=[ks], sk=ks)
                for g in range(48):
                    s.op('pe', lambda e, sa=sa, g=g, kc=kc: e.matmul(
                        ps[:, 2 * g:2 * g + 2], lhsT=sa[:, g * 128:(g + 1) * 128], rhs=sil[:, kc, :],
                        start=(kc == 0 and g == 0), stop=(kc == NKC - 1), skip_group_check=True),
                        reads=[ks, ksil], writes=[kp])
            s.op('dve', lambda e: e.tensor_tensor(
                out=modT, in0=ps[:, 0:96].rearrange("p (g v) -> p g v", v=2),
                in1=bm.unsqueeze(2).to_broadcast([128, 48, 2]), op=ALU.add), reads=[kp, kbm], writes=[k_mod])
            for lo, add in ((8, True), (16, False), (32, True), (40, False)):
                if add:
                    s.op('dve', lambda e, lo=lo: e.tensor_scalar_add(out=modT[:, lo:lo + 8, :], in0=modT[:, lo:lo + 8, :], scalar1=1.0),
                         reads=[k_mod], writes=[k_mod])
                else:
                    s.op('dve', lambda e, lo=lo: e.tensor_scalar_mul(out=modT[:, lo:lo + 8, :], in0=modT[:, lo:lo + 8, :], scalar1=1.0 / ALPHA),
                         reads=[k_mod], writes=[k_mod])
            s.barrier()
            mem.release(m0)

        SH1, SC1, G1, SH2, SC2, G2 = 0, 8, 16, 24, 32, 40

        def tile_v(i):
            return 0 if i < NST else 1

        def phase_p1(l):
            m0 = mem.mark()
            w1t, kw1 = mem.alloc([128, NKC, NW1], BF16, 'w1t')
            bfm, kbfm = mem.alloc([128, NG], F32, 'bfm')
            btm, kbtm = mem.alloc([128, NTM], F32, 'btm')
            stage = [mem.alloc([128, 2048], F32, 'wst') for _ in range(3)]
            xT_t = [mem.alloc([128, NKC, 512], F32, 'xT1') for _ in range(2)]
            hT_t = [mem.alloc([128, NKC, 512], BF16, 'hT') for _ in range(2)]
            fo_t = [mem.alloc([128, 512], F32, 'fo') for _ in range(4)]
            to_t = [mem.alloc([128, 4, NTM], F32, 'to') for _ in range(1)]
            s.dma('sp', bfm, b1fm[l], writes=[kbfm], sk=kbfm)
            s.dma('sp', btm, b1tm[l], writes=[kbtm], sk=kbtm)
            load_w_bf16(w1[l], w1t, kw1, NKC, NW1, stage)
            nfo = 0
            for i in range(NTILE):
                t0 = i * 512
                v = tile_v(i)
                xa, kx = xT_t[i % 2]
                ha, kh = hT_t[i % 2]
                s.dma('sp', xa, fm_view(XT, t0, 512), reads=[('XT', i)], writes=[kx], sk=kx)
                for kc in range(NKC):
                    s.op('dve', lambda e, kc=kc, xa=xa, ha=ha, v=v: e.tensor_scalar(
                        out=ha[:, kc, :], in0=xa[:, kc, :], scalar1=modT[:, SC1 + kc, v:v + 1], scalar2=modT[:, SH1 + kc, v:v + 1],
                        op0=ALU.mult, op1=ALU.add), reads=[kx, k_mod], writes=[(kh, kc)])
                for g, (c0, M, dst, r0) in enumerate(FM_GROUPS):
                    ps, kp = bank(0, 4)
                    for kc in range(NKC):
                        s.op('pe', lambda e, ps=ps, kc=kc, c0=c0, M=M, ha=ha: e.matmul(
                            ps[0:M, :], lhsT=w1t[:, kc, c0:c0 + M], rhs=ha[:, kc, :], start=(kc == 0), stop=(kc == NKC - 1)),
                            reads=[kw1, (kh, kc)], writes=[kp])
                    fo, kfo = fo_t[nfo % 4]
                    nfo += 1
                    s.op('act', lambda e, ps=ps, M=M, g=g, fo=fo: e.activation(
                        out=fo[0:M, :], in_=ps[0:M, :], func=AF.Identity, bias=bfm[0:M, g:g + 1], scale=1.0),
                        reads=[kp, kbfm], writes=[kfo])
                    s.dma('pool', SCR[dst][r0:r0 + M, t0:t0 + 512], fo[0:M, :], reads=[kfo], writes=[(dst, r0, i)], sk=kfo)
                ta, kt = to_t[0]
                for b in range(4):
                    for (c0, N) in TM_GROUPS:
                        ps, kp = bank(4, 8)
                        for kc in range(NKC):
                            s.op('pe', lambda e, ps=ps, kc=kc, c0=c0, N=N, b=b, ha=ha: e.matmul(
                                ps[:, 0:N], lhsT=ha[:, kc, b * 128:(b + 1) * 128], rhs=w1t[:, kc, NFM + c0:NFM + c0 + N],
                                start=(kc == 0), stop=(kc == NKC - 1)), reads=[kw1, (kh, kc)], writes=[kp])
                        s.op('dve', lambda e, ps=ps, c0=c0, N=N, b=b, ta=ta: e.tensor_tensor(
                            out=ta[:, b, c0:c0 + N], in0=ps[:, 0:N], in1=btm[:, c0:c0 + N], op=ALU.add),
                            reads=[kp, kbtm], writes=[(kt, b, c0)])
                s.dma('pool', TMV[t0:t0 + 512, :].rearrange("(b p) c -> p b c", p=128), ta,
                      reads=[(kt, b, c0) for b in range(4) for (c0, N) in TM_GROUPS], writes=[('TMV', i)], sk=kt)
            s.barrier()
            mem.release(m0)

        def ln_apply(ya, ky, xo, ko, l, gi, bi, tmp):
            (sq, ksq), (mean, kmean), (rstd, krstd), (msq, kmsq) = tmp
            p1, kp1 = bank(4, 6)
            p2, kp2 = bank(6, 8)
            for kc in range(NKC):
                s.op('pe', lambda e, kc=kc: e.matmul(p1[:, :], lhsT=onesf, rhs=ya[:, kc, :], start=(kc == 0), stop=(kc == NKC - 1)),
                     reads=[k_onesf, (ky, kc)], writes=[kp1])
            for kc in range(NKC):
                s.op('act', lambda e, kc=kc: e.activation(out=sq[:, kc % 2, :], in_=ya[:, kc, :], func=AF.Square),
                     reads=[(ky, kc)], writes=[(ksq, kc % 2)])
                s.op('pe', lambda e, kc=kc: e.matmul(p2[:, :], lhsT=onesf, rhs=sq[:, kc % 2, :], start=(kc == 0), stop=(kc == NKC - 1)),
                     reads=[k_onesf, (ksq, kc % 2)], writes=[kp2])
            s.op('dve', lambda e: e.tensor_scalar_mul(out=mean, in0=p1[:, :], scalar1=1.0 / D), reads=[kp1], writes=[kmean])
            s.op('dve', lambda e: e.tensor_tensor(out=msq, in0=mean, in1=mean, op=ALU.mult), reads=[kmean], writes=[kmsq])
            s.op('dve', lambda e: e.scalar_tensor_tensor(out=msq, in0=p2[:, :], scalar=1.0 / D, in1=msq, op0=ALU.mult, op1=ALU.subtract),
                 reads=[kp2, kmsq], writes=[kmsq])
            s.op('act', lambda e: e.activation(out=rstd, in_=msq, func=AF.Sqrt, bias=epsln[:, 0:1], scale=1.0), reads=[kmsq, k_eps], writes=[krstd])
            s.op('dve', lambda e: e.reciprocal(out=rstd, in_=rstd), reads=[krstd], writes=[krstd])
            for kc in range(NKC):
                s.op('dve', lambda e, kc=kc: e.tensor_tensor(out=ya[:, kc, :], in0=ya[:, kc, :], in1=mean, op=ALU.subtract),
                     reads=[(ky, kc), kmean], writes=[(ky, kc)])
                s.op('pool', lambda e, kc=kc: e.tensor_tensor(out=ya[:, kc, :], in0=ya[:, kc, :], in1=rstd, op=ALU.mult),
                     reads=[(ky, kc), krstd], writes=[(ky, kc)])
                s.op('dve', lambda e, kc=kc: e.tensor_scalar(out=xo[:, kc, :], in0=ya[:, kc, :], scalar1=lnt[:, gi, kc:kc + 1],
                                                             scalar2=lnt[:, bi, kc:kc + 1], op0=ALU.mult, op1=ALU.add),
                     reads=[(ky, kc), k_lnt], writes=[(ko, kc)])

        def phase_p3(l):
            m0 = mem.mark()
            wo, kwo = mem.alloc([128, NKC, D], BF16, 'wo')
            stage = [mem.alloc([128, 2048], F32, 'wst') for _ in range(3)]
            xT_t = [mem.alloc([128, NKC, 512], F32, 'xT3') for _ in range(2)]
            mx_t = [mem.alloc([128, NKC, 512], BF16, 'mx') for _ in range(2)]
            ya_t = [mem.alloc([128, NKC, 512], F32, 'ya') for _ in range(2)]
            tmp = [mem.alloc([128, 2, 512], F32, 'sq'), mem.alloc([128, 512], F32, 'mean'), mem.alloc([128, 512], F32, 'rstd'),
                   mem.alloc([128, 512], F32, 'msq')]
            load_w_bf16(w_out[l], wo, kwo, NKC, D, stage)
            for i in range(NTILE):
                t0 = i * 512
                v = tile_v(i)
                xa, kx = xT_t[i % 2]
                ma, kmx = mx_t[i % 2]
                ya, ky = ya_t[i % 2]
                s.dma('sp', xa, fm_view(XT, t0, 512), reads=[('XT', i)], writes=[(kx, kc) for kc in range(NKC)], sk=kx)
                s.dma('sp', ma, fm_view(MIXT, t0, 512), reads=[('MIXT', i)], writes=[kmx], sk=kmx)
                for oc in range(NKC):
                    ps, kp = bank(0, 4)
                    for kc in range(NKC):
                        s.op('pe', lambda e, ps=ps, kc=kc, oc=oc, ma=ma: e.matmul(
                            ps[:, :], lhsT=wo[:, kc, oc * 128:(oc + 1) * 128], rhs=ma[:, kc, :], start=(kc == 0), stop=(kc == NKC - 1)),
                            reads=[kwo, kmx], writes=[kp])
                    s.op('dve', lambda e, ps=ps, oc=oc, xa=xa, v=v, ya=ya: e.scalar_tensor_tensor(
                        out=ya[:, oc, :], in0=ps[:, :], scalar=modT[:, G1 + oc, v:v + 1], in1=xa[:, oc, :], op0=ALU.mult, op1=ALU.add),
                        reads=[kp, (kx, oc), k_mod], writes=[(ky, oc)])
                ln_apply(ya, ky, xa, kx, l, 0, 1, tmp)
                s.dma('pool', fm_view(XT, t0, 512), xa, reads=[(kx, kc) for kc in range(NKC)], writes=[('XT', i)], sk=kx)
            s.barrier()
            mem.release(m0)

        def phase_p3b(l):
            m0 = mem.mark()
            wf, kwf = mem.alloc([128, NKC, 2 * FF], BF16, 'wf')
            stage = [mem.alloc([128, 1024], F32, 'wst') for _ in range(3)]
            xT_t = [mem.alloc([128, NKC, 512], F32, 'xT3b') for _ in range(1)]
            h2, kh2 = mem.alloc([128, NKC, 512], BF16, 'h2')
            sg_t = [mem.alloc([128, 512], F32, 'sg') for _ in range(2)]
            u_t = [mem.alloc([128, NFC, 512], BF16, 'u') for _ in range(2)]
            load_w_bf16(w_ffn_in[l], wf, kwf, NKC, 2 * FF, stage)
            for i in range(NTILE):
                t0 = i * 512
                v = tile_v(i)
                xa, kx = xT_t[0]
                s.dma('sp', xa, fm_view(XT, t0, 512), reads=[('XT', i)], writes=[kx], sk=kx)
                for kc in range(NKC):
                    s.op('dve', lambda e, kc=kc, xa=xa, v=v: e.tensor_scalar(
                        out=h2[:, kc, :], in0=xa[:, kc, :], scalar1=modT[:, SC2 + kc, v:v + 1], scalar2=modT[:, SH2 + kc, v:v + 1],
                        op0=ALU.mult, op1=ALU.add), reads=[kx, k_mod], writes=[(kh2, kc)])
                ua, ku = u_t[i % 2]
                for f in range(NFC):
                    pg, kpg = bank(0, 4)
                    pu, kpu = bank(4, 8)
                    for kc in range(NKC):
                        s.op('pe', lambda e, pg=pg, kc=kc, f=f: e.matmul(
                            pg[:, :], lhsT=wf[:, kc, f * 128:(f + 1) * 128], rhs=h2[:, kc, :], start=(kc == 0), stop=(kc == NKC - 1)),
                            reads=[kwf, (kh2, kc)], writes=[kpg])
                    for kc in range(NKC):
                        s.op('pe', lambda e, pu=pu, kc=kc, f=f: e.matmul(
                            pu[:, :], lhsT=wf[:, kc, FF + f * 128:FF + (f + 1) * 128], rhs=h2[:, kc, :], start=(kc == 0), stop=(kc == NKC - 1)),
                            reads=[kwf, (kh2, kc)], writes=[kpu])
                    sg, ksg = sg_t[f % 2]
                    s.op('act', lambda e, pg=pg, sg=sg: e.activation(out=sg, in_=pg[:, :], func=AF.Silu), reads=[kpg], writes=[ksg])
                    s.op('dve', lambda e, pu=pu, sg=sg, f=f, ua=ua: e.tensor_tensor(out=ua[:, f, :], in0=pu[:, :], in1=sg, op=ALU.mult),
                         reads=[kpu, ksg], writes=[(ku, f)])
                s.dma('pool', fm_view(UT, t0, 512), ua, reads=[(ku, f) for f in range(NFC)], writes=[('UT', i)], sk=ku)
            s.barrier()
            mem.release(m0)

        def phase_p4(l):
            m0 = mem.mark()
            w2, kw2 = mem.alloc([128, NFC, D], BF16, 'w2')
            stage = [mem.alloc([128, 2048], F32, 'wst') for _ in range(3)]
            xT_t = [mem.alloc([128, NKC, 512], F32, 'xT4') for _ in range(2)]
            u_t = [mem.alloc([128, NFC, 512], BF16, 'u4') for _ in range(2)]
            ya_t = [mem.alloc([128, NKC, 512], F32, 'ya4') for _ in range(2)]
            tmp = [mem.alloc([128, 2, 512], F32, 'sq'), mem.alloc([128, 512], F32, 'mean'), mem.alloc([128, 512], F32, 'rstd'),
                   mem.alloc([128, 512], F32, 'msq')]
            load_w_bf16(w_ffn_out[l], w2, kw2, NFC, D, stage)
            for i in range(NTILE):
                t0 = i * 512
                v = tile_v(i)
                xa, kx = xT_t[i % 2]
                ua, ku = u_t[i % 2]
                ya, ky = ya_t[i % 2]
                s.dma('sp', xa, fm_view(XT, t0, 512), reads=[('XT', i)], writes=[(kx, kc) for kc in range(NKC)], sk=kx)
                s.dma('sp', ua, fm_view(UT, t0, 512), reads=[('UT', i)], writes=[ku], sk=ku)
                for oc in range(NKC):
                    ps, kp = bank(0, 4)
                    for f in range(NFC):
                        s.op('pe', lambda e, ps=ps, f=f, oc=oc, ua=ua: e.matmul(
                            ps[:, :], lhsT=w2[:, f, oc * 128:(oc + 1) * 128], rhs=ua[:, f, :], start=(f == 0), stop=(f == NFC - 1)),
                            reads=[kw2, ku], writes=[kp])
                    s.op('dve', lambda e, ps=ps, oc=oc, xa=xa, v=v, ya=ya: e.scalar_tensor_tensor(
                        out=ya[:, oc, :], in0=ps[:, :], scalar=modT[:, G2 + oc, v:v + 1], in1=xa[:, oc, :], op0=ALU.mult, op1=ALU.add),
                        reads=[kp, (kx, oc), k_mod], writes=[(ky, oc)])
                ln_apply(ya, ky, xa, kx, l, 2, 3, tmp)
                s.dma('pool', fm_view(XT, t0, 512), xa, reads=[(kx, kc) for kc in range(NKC)], writes=[('XT', i)], sk=kx)
            s.barrier()
            mem.release(m0)

        SEQS = ([dict(t0=0, T=TS, sample=True, j=-1)] if NST else []) + \
               [dict(t0=TS + 256 * j, T=256, sample=False, j=j) for j in range(NPR)]

        def small_w(dram2d, rows, cols, dt, name, stage):
            dst, kd = mem.alloc([rows, cols], dt, name)
            sa, sk_ = stage
            s.dma('sp', sa[0:rows, 0:cols], dram2d, writes=[sk_], sk=sk_)
            s.op('pool', lambda e: e.tensor_copy(out=dst, in_=sa[0:rows, 0:cols]), reads=[sk_], writes=[kd])
            return dst, kd

        def evac(i, out, in_, reads, writes):
            if i % 2 == 0:
                s.op('dve', lambda e: e.tensor_copy(out=out, in_=in_), reads=reads, writes=writes)
            else:
                s.op('act', lambda e: e.copy(out=out, in_=in_), reads=reads, writes=writes)

        def phase_mla(l):
            m0 = mem.mark()
            stage = mem.alloc([128, 1024], F32, 'wst')
            wq0, kwq0 = small_w(wq_d[l, 0:128, :], 128, 768, BF16, 'wq0', stage)
            wq1, kwq1 = small_w(wq_d[l, 128:256, :], 128, 768, BF16, 'wq1', stage)
            wp0, kwp0 = small_w(wqp_d[l, 0:128, :], 128, 768, BF16, 'wp0', stage)
            wp1, kwp1 = small_w(wqp_d[l, 128:256, :], 128, 768, BF16, 'wp1', stage)
            wqs, kwqs, wps, kwps = (wq0, wq1), (kwq0, kwq1), (wp0, wp1), (kwp0, kwp1)
            wk, kwk = small_w(wk_d[l], 128, 768, BF16, 'wk', stage)
            wv, kwv = small_w(wv_d[l], 128, 512, BF16, 'wv', stage)
            e96, ke96 = small_w(e96_d, 32, 96, BF16, 'e96', stage)
            qkn, kqkn = mem.alloc([128, 3], F32, 'qkn')
            s.dma('sp', qkn, qkn_d[l], writes=[kqkn], sk=kqkn)
            KMAX = 4608 if NST else 256
            KT, kKT = mem.alloc([96, 8, KMAX], BF16, 'KT')
            VA, kVA = mem.alloc([128, KMAX // 128, 8, 65], BF16, 'VA')
            s.op('pool', lambda e: e.memset(VA[:, :, :, 64:65], 1.0), writes=[(kVA, 'ones')])
            ckv, kckv = mem.alloc([128, 512], F32, 'ckv')
            sq, ksq = mem.alloc([128, 2, 512], F32, 'sqm')
            rstd, krstd = mem.alloc([128, 512], F32, 'rstdm')
            ckvn, kckvn = mem.alloc([128, 512], F32, 'ckvn')
            ckvb, kckvb = mem.alloc([128, 512], BF16, 'ckvb')
            kpe, kkpe = mem.alloc([32, 512], F32, 'kpe')
            kpp, kkpp = mem.alloc([32, 512], F32, 'kpp')
            kcs, kkcs = mem.alloc([32, 2, 512], F32, 'kcs')
            krb, kkrb = mem.alloc([32, 512], BF16, 'krb')
            ctm, kctm = mem.alloc([128, 4, 128], F32, 'ctm')
            ktm, kktm = mem.alloc([128, 4, 32], F32, 'ktm')
            cq, kcq = mem.alloc([128, 2, 512], F32, 'cq')
            cqn, kcqn = mem.alloc([128, 2, 512], BF16, 'cqn')
            cs96, kcs96 = mem.alloc([96, 2, 512], F32, 'cs96')
            crsr, kcrsr = mem.alloc([96, 2, 512], F32, 'crsr')
            t12, kt12 = mem.alloc([96, 2, 512], F32, 't12')
            qT_t = [mem.alloc([96, 512], BF16, 'qT') for _ in range(2)]
            PT_t = [mem.alloc([128, 512], BF16, 'PT') for _ in range(4)]
            rec, krec = mem.alloc([128, 4], F32, 'rec')
            att, katt = mem.alloc([128, 4, 512], BF16, 'att')
            mixo_t = [mem.alloc([128, 4, 512], BF16, 'mixo') for _ in range(2)]
            nev = [0]

            def rms_rstd(src2d_list, keys, n, div, extra_scale):
                ps, kp = bank(7, 8)
                for c, (a, ka) in enumerate(zip(src2d_list, keys)):
                    s.op('act', lambda e, a=a, c=c: e.activation(out=sq[:, c, 0:n], in_=a, func=AF.Square), reads=[ka], writes=[(ksq, c)])
                    s.op('pe', lambda e, c=c: e.matmul(ps[:, 0:n], lhsT=onesf, rhs=sq[:, c, 0:n], start=(c == 0), stop=(c == len(src2d_list) - 1)),
                         reads=[k_onesf, (ksq, c)], writes=[kp])
                s.op('act', lambda e: e.activation(out=rstd[:, 0:n], in_=ps[:, 0:n], func=AF.Sqrt, bias=epsln[:, 1:2], scale=1.0 / div),
                     reads=[kp, k_eps], writes=[krstd])
                s.op('dve', lambda e: e.reciprocal(out=rstd[:, 0:n], in_=rstd[:, 0:n]), reads=[krstd], writes=[krstd])
                if extra_scale != 1.0:
                    s.op('dve', lambda e: e.tensor_scalar_mul(out=rstd[:, 0:n], in0=rstd[:, 0:n], scalar1=extra_scale), reads=[krstd], writes=[krstd])

            def make_kv(k0, n):
                for h in range(8):
                    ps, kp = bank(0, 4)
                    s.op('pe', lambda e, ps=ps, h=h: e.matmul(ps[0:96, 0:n], lhsT=wk[:, h * 96:(h + 1) * 96], rhs=ckvb[:, 0:n], start=True, stop=False),
                         reads=[kwk, kckvb], writes=[kp])
                    s.op('pe', lambda e, ps=ps: e.matmul(ps[0:96, 0:n], lhsT=e96, rhs=krb[:, 0:n], start=False, stop=True),
                         reads=[ke96, kkrb], writes=[kp])
                    nev[0] += 1
                    evac(nev[0], KT[:, h, k0:k0 + n], ps[0:96, 0:n], [kp], [(kKT, h, k0)])
                for b in range(n // 128):
                    ps, kp = bank(0, 4)
                    s.op('pe', lambda e, ps=ps, b=b: e.matmul(ps[:, :], lhsT=ckvb[:, b * 128:(b + 1) * 128], rhs=wv, start=True, stop=True),
                         reads=[kwv, kckvb], writes=[kp])
                    kb = k0 // 128 + b
                    nev[0] += 1
                    evac(nev[0], VA[:, kb, :, 0:64], ps[:, :].rearrange("p (h e) -> p h e", e=64), [kp], [(kVA, kb)])

            for sq_ in SEQS:
                T, S0, samp, j = sq_['T'], sq_['t0'], sq_['sample'], sq_['j']
                nkeys = T + (512 if samp else 0)
                nkt = nkeys // 128
                for k0 in range(0, T, 512):
                    n = min(512, T - k0)
                    t0 = S0 + k0
                    s.dma('sp', ckv[:, 0:n], SCR['CKVT'][:, t0:t0 + n], reads=[('CKVT', 0, t0 // 512)], writes=[kckv], sk=kckv)
                    s.dma('sp', kpe[:, 0:n], SCR['KPET'][:, t0:t0 + n], reads=[('KPET', 0, t0 // 512)], writes=[kkpe], sk=kkpe)
                    rms_rstd([ckv[:, 0:n]], [kckv], n, 128.0, 1.0)
                    s.op('dve', lambda e, n=n: e.scalar_tensor_tensor(out=ckvn[:, 0:n], in0=ckv[:, 0:n], scalar=qkn[:, 2:3], in1=rstd[:, 0:n],
                                                                     op0=ALU.mult, op1=ALU.mult), reads=[kckv, kqkn, krstd], writes=[kckvn])
                    s.op('act', lambda e, n=n: e.copy(out=ckvb[:, 0:n], in_=ckvn[:, 0:n]), reads=[kckvn], writes=[kckvb])
                    if samp:
                        s.dma('sp', kpp[:, 0:n], SCR['KPEP'][:, t0:t0 + n], reads=[('KPEP', 0, t0 // 512)], writes=[kkpp], sk=kkpp)
                        s.dma_group('sp', [(kcs[:, 0, 0:n], kcos_d[:, k0:k0 + n]), (kcs[:, 1, 0:n], ksin_d[:, k0:k0 + n])], writes=[kkcs], sk=kkcs)
                        s.op('dve', lambda e, n=n: e.tensor_tensor(out=kpe[:, 0:n], in0=kpe[:, 0:n], in1=kcs[:, 0, 0:n], op=ALU.mult),
                             reads=[kkpe, kkcs], writes=[kkpe])
                        s.op('dve', lambda e, n=n: e.tensor_tensor(out=kpp[:, 0:n], in0=kpp[:, 0:n], in1=kcs[:, 1, 0:n], op=ALU.mult),
                             reads=[kkpp, kkcs], writes=[kkpp])
                        s.op('dve', lambda e, n=n: e.tensor_tensor(out=krb[:, 0:n], in0=kpe[:, 0:n], in1=kpp[:, 0:n], op=ALU.add),
                             reads=[kkpe, kkpp], writes=[kkrb])
                    else:
                        s.op('dve', lambda e, n=n: e.tensor_copy(out=krb[:, 0:n], in_=kpe[:, 0:n]), reads=[kkpe], writes=[kkrb])
                        for b in range(n // 128):
                            ps, kp = bank(4, 7)
                            s.op('pe', lambda e, ps=ps, b=b: e.transpose(out=ps[:, 0:128], in_=ckvn[:, b * 128:(b + 1) * 128], identity=idf),
                                 reads=[kckvn, k_idf], writes=[kp])
                            s.op('pe', lambda e, ps=ps, b=b: e.transpose(out=ps[:, 128:160], in_=kpe[0:32, b * 128:(b + 1) * 128], identity=idf[0:32, 0:32]),
                                 reads=[kkpe, k_idf], writes=[kp])
                            s.op('dve', lambda e, ps=ps, b=b: e.tensor_copy(out=ctm[:, b, :], in_=ps[:, 0:128]), reads=[kp], writes=[(kctm, b)])
                            s.op('act', lambda e, ps=ps, b=b: e.copy(out=ktm[:, b, :], in_=ps[:, 128:160]), reads=[kp], writes=[(kktm, b)])
                        nb = n // 128
                        s.dma('pool', ckv_out[j, l, k0:k0 + n, :].rearrange("(b p) c -> p b c", p=128), ctm[:, 0:nb, :],
                              reads=[(kctm, b) for b in range(nb)], sk=kctm)
                        s.dma('pool', kpe_out[j, l, k0:k0 + n, :].rearrange("(b p) c -> p b c", p=128), ktm[:, 0:nb, :],
                              reads=[(kktm, b) for b in range(nb)], sk=kktm)
                    make_kv(k0, n)
                if samp:
                    s.dma('sp', ctm, cckv_d[l].rearrange("(b p) c -> p b c", p=128), writes=[(kctm, b) for b in range(4)], sk=kctm)
                    s.dma('sp', ktm, ckpe_d[l].rearrange("(b p) c -> p b c", p=128), writes=[(kktm, b) for b in range(4)], sk=kktm)
                    ps, kp = bank(4, 7)
                    ps2, kp2 = bank(4, 7)
                    for b in range(4):
                        s.op('pe', lambda e, b=b, ps=ps: e.transpose(out=ps[:, b * 128:(b + 1) * 128], in_=ctm[:, b, :], identity=idf),
                             reads=[(kctm, b), k_idf], writes=[kp])
                        s.op('pe', lambda e, b=b, ps2=ps2: e.transpose(out=ps2[0:32, b * 128:(b + 1) * 128], in_=ktm[:, b, :], identity=idf),
                             reads=[(kktm, b), k_idf], writes=[kp2])
                    s.op('dve', lambda e, ps=ps: e.tensor_copy(out=ckvb, in_=ps[:, :]), reads=[kp], writes=[kckvb])
                    s.op('act', lambda e, ps2=ps2: e.copy(out=krb, in_=ps2[0:32, :]), reads=[kp2], writes=[kkrb])
                    make_kv(T, 512)
                for q0 in range(0, T, 512):
                    n = min(512, T - q0)
                    nqb = n // 128
                    t0 = S0 + q0
                    ti = t0 // 512
                    s.dma('sp', cq[:, :, 0:n], SCR['CQT'].rearrange("(c p) t -> p c t", p=128)[:, :, t0:t0 + n],
                          reads=[('CQT', 0, ti), ('CQT', 128, ti)], writes=[kcq], sk=kcq)
                    rms_rstd([cq[:, 0, 0:n], cq[:, 1, 0:n]], [kcq, kcq], n, 256.0, ATT_SCALE)
                    for kc in range(2):
                        s.op('dve', lambda e, kc=kc, n=n: e.tensor_scalar_mul(out=cqn[:, kc, 0:n], in0=cq[:, kc, 0:n], scalar1=qkn[:, kc:kc + 1]),
                             reads=[kcq, kqkn], writes=[(kcqn, kc)])
                    if samp:
                        s.dma_group('sp', [(cs96[:, 0, 0:n], cos96_d[:, q0:q0 + n]), (cs96[:, 1, 0:n], sin96_d[:, q0:q0 + n])], writes=[kcs96], sk=kcs96)
                        for c in range(2):
                            s.op('dve', lambda e, c=c, n=n: e.tensor_tensor(out=crsr[:, c, 0:n], in0=cs96[:, c, 0:n], in1=rstd[0:96, 0:n], op=ALU.mult),
                                 reads=[kcs96, krstd], writes=[(kcrsr, c)])
                    mo_, kmo = mixo_t[(t0 // 512) % 2]
                    def qproj(h):
                        qT, kqT = qT_t[h % 2]
                        psA, kpA = bank(0, 2)
                        for kc in range(2):
                            s.op('pe', lambda e, psA=psA, kc=kc, h=h, n=n: e.matmul(
                                psA[0:96, 0:n], lhsT=wqs[kc][:, h * 96:(h + 1) * 96], rhs=cqn[:, kc, 0:n], start=(kc == 0), stop=(kc == 1)),
                                reads=[kwqs[kc], (kcqn, kc)], writes=[kpA])
                        if samp:
                            psB, kpB = bank(0, 2)
                            for kc in range(2):
                                s.op('pe', lambda e, psB=psB, kc=kc, h=h, n=n: e.matmul(
                                    psB[0:96, 0:n], lhsT=wps[kc][:, h * 96:(h + 1) * 96], rhs=cqn[:, kc, 0:n], start=(kc == 0), stop=(kc == 1)),
                                    reads=[kwps[kc], (kcqn, kc)], writes=[kpB])
                            s.op('dve', lambda e, psA=psA, n=n: e.tensor_tensor(out=t12[:, 0, 0:n], in0=psA[0:96, 0:n], in1=crsr[:, 0, 0:n], op=ALU.mult),
                                 reads=[kpA, (kcrsr, 0)], writes=[(kt12, 0)])
                            s.op('dve', lambda e, psB=psB, n=n: e.tensor_tensor(out=t12[:, 1, 0:n], in0=psB[0:96, 0:n], in1=crsr[:, 1, 0:n], op=ALU.mult),
                                 reads=[kpB, (kcrsr, 1)], writes=[(kt12, 1)])
                            s.op('pool', lambda e, qT=qT, n=n: e.tensor_tensor(out=qT[:, 0:n], in0=t12[:, 0, 0:n], in1=t12[:, 1, 0:n], op=ALU.add),
                                 reads=[(kt12, 0), (kt12, 1)], writes=[kqT])
                        else:
                            s.op('dve', lambda e, psA=psA, qT=qT, n=n: e.tensor_tensor(out=qT[:, 0:n], in0=psA[0:96, 0:n], in1=rstd[0:96, 0:n], op=ALU.mult),
                                 reads=[kpA, krstd], writes=[kqT])

                    def attend(h):
                        qT, kqT = qT_t[h % 2]
                        psO, kpO = bank(5, 7)
                        LA = 2
                        pend = {}
                        for it in range(nkt + LA):
                            if it < nkt:
                                kt = it
                                psS, kpS = bank(2, 5)
                                kvk = [(kKT, h, (kt * 128) // 512 * 512 if kt * 128 < T else T)]
                                s.op('pe', lambda e, psS=psS, kt=kt, h=h, qT=qT, n=n: e.matmul(
                                    psS[:, 0:n], lhsT=KT[:, h, kt * 128:(kt + 1) * 128], rhs=qT[:, 0:n], start=True, stop=True),
                                    reads=kvk + [kqT], writes=[kpS])
                                PT, kPT = PT_t[kt % len(PT_t)]
                                s.op('act', lambda e, psS=psS, PT=PT, n=n: e.activation(out=PT[:, 0:n], in_=psS[:, 0:n], func=AF.Exp),
                                     reads=[kpS], writes=[kPT])
                                pend[kt] = (PT, kPT)
                            if it >= LA:
                                kt = it - LA
                                PT, kPT = pend.pop(kt)
                                for qb in range(nqb):
                                    s.op('pe', lambda e, psO=psO, PT=PT, qb=qb, kt=kt, h=h: e.matmul(
                                        psO[:, qb * 65:(qb + 1) * 65], lhsT=PT[:, qb * 128:(qb + 1) * 128], rhs=VA[:, kt, h, :],
                                        start=(kt == 0 and qb == 0), stop=(kt == nkt - 1), skip_group_check=True),
                                        reads=[kPT, (kVA, kt), (kVA, 'ones')], writes=[kpO])
                        pO3 = psO[:, 0:nqb * 65].rearrange("p (q e) -> p q e", e=65)
                        s.op('dve', lambda e, pO3=pO3, nqb=nqb: e.reciprocal(out=rec[:, 0:nqb], in_=pO3[:, :, 64]), reads=[kpO], writes=[krec])
                        s.op('dve', lambda e, pO3=pO3, nqb=nqb, h=h: e.tensor_tensor(
                            out=att[:, 0:nqb, h * 64:(h + 1) * 64], in0=pO3[:, :, 0:64], in1=rec[:, 0:nqb].unsqueeze(2).to_broadcast([128, nqb, 64]),
                            op=ALU.mult), reads=[kpO, krec], writes=[(katt, h)])

                    qproj(0)
                    for h in range(8):
                        if h + 1 < 8:
                            qproj(h + 1)
                        attend(h)
                    for qb in range(nqb):
                        ps, kp = bank(7, 8)
                        pbf = ps.bitcast(BF16)
                        for c in range(4):
                            s.op('pe', lambda e, pbf=pbf, c=c, qb=qb: e.transpose(out=pbf[:, c * 128:(c + 1) * 128], in_=att[:, qb, c * 128:(c + 1) * 128], identity=idb),
                                 reads=[(katt, 2 * c), (katt, 2 * c + 1), k_idb], writes=[kp])
                        nev[0] += 1
                        evac(nev[0], mo_[:, :, qb * 128:(qb + 1) * 128], pbf[:, 0:512].rearrange("p (c t) -> p c t", t=128), [kp], [(kmo, qb)])
                    s.dma('pool', MIXT.rearrange("(c p) t -> p c t", p=128)[:, 0:4, t0:t0 + n], mo_[:, :, 0:n],
                          reads=[(kmo, qb) for qb in range(nqb)], writes=[('MIXT_A', t0)], sk=kmo)
            s.barrier()
            mem.release(m0)

        def phase_mlstm(l):
            m0_ = mem.mark()
            mc, kmc = mem.alloc([64, 4, 64], F32, 'mconst')
            ng, kng = mem.alloc([64, 256], F32, 'mlng')
            s.dma('sp', mc, mconst_d, writes=[kmc], sk=kmc)
            s.dma('sp', ng, mlng_d[l], writes=[kng], sk=kng)
            rmask, krm = mem.alloc([64, 8, 64], F32, 'rmask')
            s.op('pool', lambda e: e.memset(rmask, 1.0), writes=[krm])
            s.op('pool', lambda e: e.memset(rmask[:, :, 0:1], 0.0), writes=[krm])
            A = lambda shape, dt=F32, name='m': mem.alloc(shape, dt, name)
            GI, kGI = A([64, 8, 64]); GF, kGF = A([64, 8, 64]); SP, kSP = A([64, 8, 64]); CS, kCS = A([64, 8, 64])
            NB, kNB = A([64, 8, 64]); Cc, kCc = A([64, 8, 64]); W, kW = A([64, 8, 64]); THR, kTHR = A([64, 8, 64])
            TOT, kTOT = A([64, 8]); CMAX, kCMAX = A([64, 8]); Gn, kGn = A([64, 8])
            ROW, kROW = A([4, 4, 64]); m0t, km0t = A([4, 2]); MN, kMN = A([4, 2, 64]); Rr, kRr = A([4, 2, 64])
            MP, kMP = A([4, 2, 64]); SCr, kSCr = A([4, 2, 64])
            TMPc, kTMPc = A([64, 16]); Rc, kRc = A([64, 8]); SCc, kSCc = A([64, 8])
            Wt, kWt = A([64, 8, 64]); THRt, kTHRt = A([64, 8, 64]); BD, kBD = A([64, 64, 8]); scB, kscB = A([64, 64, 8])
            Caug, kC = A([64, 4, 65]); Cs, kCs = A([64, 4, 65]); Csb, kCsb = A([64, 4, 65], BF16)
            q32_t = [A([64, 4, 512]) for _ in range(1)]
            k32_t = [A([64, 4, 512]) for _ in range(1)]
            qb_t = [A([64, 4, 512], BF16) for _ in range(2)]
            kb_t = [A([64, 4, 512], BF16) for _ in range(2)]
            vv_t = [A([64, 8, 256]) for _ in range(1)]
            kk_t = [A([64, 8, 256]) for _ in range(1)]
            kkb_t = [A([64, 8, 256], BF16) for _ in range(2)]
            va_t = [A([64, 8, 4, 65], BF16) for _ in range(2)]
            for va, kva in va_t:
                s.op('pool', lambda e, va=va: e.memset(va[:, :, :, 64:65], 1.0), writes=[(kva, 'ones')])
            vw_t = [A([64, 8, 4, 65], BF16) for _ in range(2)]
            PTg_t = [A([64, 8, 4, 64], BF16) for _ in range(2)]
            dC_t = [A([64, 8, 4, 65]) for _ in range(2)]
            hF_t = [A([64, 8, 256]) for _ in range(2)]
            mo_t = [A([64, 8, 256]) for _ in range(2)]
            hb_t = [A([64, 8, 256]) for _ in range(2)]
            den, kden = A([64, 2, 4]); rec, krec = A([64, 2, 4])
            s1, ks1 = A([64, 32]); s2, ks2 = A([64, 32]); s3, ks3 = A([64, 32])
            sqx, ksqx = q32_t[0]
            sqx = sqx.rearrange("p h (a b) -> p (h a) b", b=256)
            sig, ksig = k32_t[0]
            sig = sig.rearrange("p h (a b) -> p (h a) b", b=256)
            xo_t = [A([64, 8, 256], BF16) for _ in range(1)]
            mixo_t = [A([128, 2, 512], BF16) for _ in range(1)]
            maskF, maskB, J64, J4 = mc[:, 0, :], mc[:, 1, :], mc[:, 2, :], mc[0:4, 3, 0:4]
            gcount = [0]
            for sq_ in SEQS:
                T, S0, samp, j = sq_['T'], sq_['t0'], sq_['sample'], sq_['j']
                nch = T // 64
                Jn = J64 if nch == 64 else J4
                In = idf[0:nch, 0:nch]
                def gview(name):
                    return SCR[name][:, S0:S0 + T].rearrange("h (j t) -> j h t", t=64)
                rk = lambda nm: [(nm, 0, i) for i in range(S0 // 512, (S0 + T + 511) // 512)]
                s.dma_group('sp', [(GI[0:nch, 0:4, :], gview('MIF')), (GI[0:nch, 4:8, :], gview('MIB'))], reads=rk('MIF') + rk('MIB'), writes=[kGI], sk=kGI)
                s.dma_group('sp', [(GF[0:nch, 0:4, :], gview('MFF')), (GF[0:nch, 4:8, :], gview('MFB'))], reads=rk('MFF') + rk('MFB'), writes=[kGF], sk=kGF)
                s.op('act', lambda e, nch=nch: e.activation(out=SP[0:nch], in_=GF[0:nch], func=AF.Exp, scale=-1.0), reads=[kGF], writes=[kSP])
                s.op('act', lambda e, nch=nch: e.activation(out=SP[0:nch], in_=SP[0:nch], func=AF.Ln, bias=1.0), reads=[kSP], writes=[kSP])
                fl = lambda a, nch=nch: a[0:nch].rearrange("p a b -> p (a b)")
                s.op('dve', lambda e, nch=nch, fl=fl: e.tensor_tensor_scan(out=fl(CS), data0=fl(rmask), data1=fl(SP), initial=0.0, op0=ALU.mult, op1=ALU.add),
                     reads=[krm, kSP], writes=[kCS])
                s.op('dve', lambda e, nch=nch: e.tensor_copy(out=TOT[0:nch], in_=CS[0:nch, :, 63]), reads=[kCS], writes=[kTOT])
                s.op('dve', lambda e, nch=nch: e.tensor_copy(out=NB[0:nch, 0:4, :], in_=CS[0:nch, 0:4, :]), reads=[kCS], writes=[kNB])
                s.op('dve', lambda e, nch=nch: e.tensor_tensor(out=NB[0:nch, 4:8, :], in0=SP[0:nch, 4:8, :], in1=CS[0:nch, 4:8, :], op=ALU.subtract),
                     reads=[kCS, kSP, kNB], writes=[kNB])
                s.op('dve', lambda e, nch=nch: e.tensor_tensor(out=NB[0:nch, 4:8, :], in0=NB[0:nch, 4:8, :],
                                                              in1=TOT[0:nch, 4:8].unsqueeze(2).to_broadcast([nch, 4, 64]), op=ALU.add),
                     reads=[kNB, kTOT], writes=[kNB])
                s.op('dve', lambda e, nch=nch: e.tensor_tensor(out=Cc[0:nch], in0=GI[0:nch], in1=NB[0:nch], op=ALU.add), reads=[kGI, kNB], writes=[kCc])
                s.op('dve', lambda e, nch=nch: e.tensor_reduce(out=CMAX[0:nch], in_=Cc[0:nch], axis=AX.X, op=ALU.max), reads=[kCc], writes=[kCMAX])
                s.op('dve', lambda e, nch=nch: e.tensor_scalar_mul(out=Gn[0:nch], in0=TOT[0:nch], scalar1=-1.0), reads=[kTOT], writes=[kGn])
                ps, kp = bank(0, 8)
                for qi, (src_, ksrc, lo, mat) in enumerate(((CMAX, kCMAX, 0, In), (Gn, kGn, 0, In), (CMAX, kCMAX, 4, Jn), (Gn, kGn, 4, Jn))):
                    s.op('pe', lambda e, ps=ps, qi=qi, src_=src_, lo=lo, mat=mat, nch=nch: e.matmul(
                        ps[0:4, qi * nch:(qi + 1) * nch], lhsT=src_[0:nch, lo:lo + 4], rhs=mat, start=True, stop=True),
                        reads=[ksrc, k_idf, kmc], writes=[kp])
                s.op('dve', lambda e, ps=ps, nch=nch: e.tensor_copy(out=ROW[:, :, 0:nch], in_=ps[0:4, 0:4 * nch].rearrange("p (a b) -> p a b", b=nch)),
                     reads=[kp], writes=[kROW])
                if samp:
                    s.dma('sp', m0t, stm_d[l].rearrange("d h -> h d"), writes=[km0t], sk=km0t, allow_slow_non_contiguous=True)
                else:
                    s.op('dve', lambda e: e.memset(m0t, 0.0), writes=[km0t])
                for d in range(2):
                    s.op('dve', lambda e, d=d, nch=nch: e.tensor_tensor_scan(out=MN[:, d, 0:nch], data0=ROW[:, 2 * d, 0:nch], data1=ROW[:, 2 * d + 1, 0:nch],
                                                                            initial=m0t[:, d:d + 1], op0=ALU.max, op1=ALU.add),
                         reads=[kROW, km0t], writes=[(kMN, d)])
                    s.op('dve', lambda e, d=d, nch=nch: e.tensor_tensor(out=Rr[:, d, 0:nch], in0=MN[:, d, 0:nch], in1=ROW[:, 2 * d + 1, 0:nch], op=ALU.subtract),
                         reads=[(kMN, d), kROW], writes=[(kRr, d)])
                    s.op('dve', lambda e, d=d: e.tensor_copy(out=MP[:, d, 0:1], in_=m0t[:, d:d + 1]), reads=[km0t], writes=[(kMP, d, 0)])
                    s.op('dve', lambda e, d=d, nch=nch: e.tensor_copy(out=MP[:, d, 1:nch], in_=MN[:, d, 0:nch - 1]), reads=[(kMN, d)], writes=[(kMP, d, 1)])
                    s.op('dve', lambda e, d=d, nch=nch: e.tensor_tensor(out=SCr[:, d, 0:nch], in0=MP[:, d, 0:nch], in1=Rr[:, d, 0:nch], op=ALU.subtract),
                         reads=[(kMP, d, 0), (kMP, d, 1), (kRr, d)], writes=[(kSCr, d)])
                    s.op('act', lambda e, d=d, nch=nch: e.activation(out=SCr[:, d, 0:nch], in_=SCr[:, d, 0:nch], func=AF.Exp), reads=[(kSCr, d)], writes=[(kSCr, d)])
                    if not samp:
                        s.dma('pool', m_out[j, l, d, :].rearrange("(h o) -> h o", o=1), MN[:, d, nch - 1:nch], reads=[(kMN, d)], sk=(kMN, d))
                ps, kp = bank(0, 8)
                for qi, (src_, ksrc, d) in enumerate(((Rr, kRr, 0), (Rr, kRr, 1), (SCr, kSCr, 0), (SCr, kSCr, 1))):
                    s.op('pe', lambda e, ps=ps, qi=qi, src_=src_, d=d, nch=nch: e.matmul(
                        ps[0:nch, qi * 4:(qi + 1) * 4], lhsT=src_[:, d, 0:nch], rhs=idf[0:4, 0:4], start=True, stop=True),
                        reads=[(ksrc, d), k_idf], writes=[kp])
                s.op('dve', lambda e, ps=ps, nch=nch: e.tensor_copy(out=TMPc[0:nch], in_=ps[0:nch, 0:16]), reads=[kp], writes=[kTMPc])
                ps2, kp2 = bank(0, 8)
                for qi, lo in enumerate((4, 12)):
                    s.op('pe', lambda e, ps2=ps2, qi=qi, lo=lo, nch=nch, Jn=Jn: e.matmul(
                        ps2[0:nch, qi * 4:(qi + 1) * 4], lhsT=Jn, rhs=TMPc[0:nch, lo:lo + 4], start=True, stop=True),
                        reads=[kTMPc, kmc], writes=[kp2])
                s.op('dve', lambda e, nch=nch: e.tensor_copy(out=Rc[0:nch, 0:4], in_=TMPc[0:nch, 0:4]), reads=[kTMPc], writes=[(kRc, 0)])
                s.op('dve', lambda e, nch=nch, ps2=ps2: e.tensor_copy(out=Rc[0:nch, 4:8], in_=ps2[0:nch, 0:4]), reads=[kp2], writes=[(kRc, 1)])
                s.op('dve', lambda e, nch=nch: e.tensor_copy(out=SCc[0:nch, 0:4], in_=TMPc[0:nch, 8:12]), reads=[kTMPc], writes=[(kSCc, 0)])
                s.op('dve', lambda e, nch=nch, ps2=ps2: e.tensor_copy(out=SCc[0:nch, 4:8], in_=ps2[0:nch, 4:8]), reads=[kp2], writes=[(kSCc, 1)])
                rcb = lambda nch=nch: Rc[0:nch].unsqueeze(2).to_broadcast([nch, 8, 64])
                s.op('dve', lambda e, nch=nch, rcb=rcb: e.tensor_tensor(out=W[0:nch], in0=Cc[0:nch], in1=rcb(), op=ALU.subtract),
                     reads=[kCc, (kRc, 0), (kRc, 1)], writes=[kW])
                s.op('act', lambda e, nch=nch: e.activation(out=W[0:nch], in_=W[0:nch], func=AF.Exp), reads=[kW], writes=[kW])
                s.op('dve', lambda e, nch=nch, rcb=rcb: e.tensor_tensor(out=THR[0:nch], in0=NB[0:nch], in1=rcb(), op=ALU.subtract),
                     reads=[kNB, (kRc, 0), (kRc, 1)], writes=[kTHR])
                s.op('act', lambda e, nch=nch: e.activation(out=THR[0:nch], in_=THR[0:nch], func=AF.Exp), reads=[kTHR], writes=[kTHR])
                for src_, ksrc, dst, kdst in ((W, kW, Wt, kWt), (THR, kTHR, THRt, kTHRt)):
                    ps, kp = bank(0, 8)
                    for r in range(8):
                        s.op('pe', lambda e, ps=ps, r=r, src_=src_, nch=nch, In=In: e.transpose(
                            out=ps[0:64, r * nch:(r + 1) * nch], in_=src_[0:nch, r, :], identity=In), reads=[ksrc, k_idf], writes=[kp])
                    s.op('dve', lambda e, ps=ps, dst=dst, nch=nch: e.tensor_copy(out=dst[:, :, 0:nch], in_=ps[0:64, 0:8 * nch].rearrange("p (a b) -> p a b", b=nch)),
                         reads=[kp], writes=[kdst])
                s.op('dve', lambda e, nch=nch, In=In: e.tensor_tensor(
                    out=BD[0:nch, 0:nch, :], in0=In.unsqueeze(2).to_broadcast([nch, nch, 8]),
                    in1=SCc[0:nch].unsqueeze(1).to_broadcast([nch, nch, 8]), op=ALU.mult), reads=[k_idf, (kSCc, 0), (kSCc, 1)], writes=[kBD])
                ps, kp = bank(0, 8)
                s.op('pe', lambda e, ps=ps, nch=nch: e.matmul(ps[0:64, 0:nch * 8], lhsT=onesf[0:nch, 0:64],
                                                             rhs=BD[0:nch, 0:nch, :].rearrange("p a b -> p (a b)"), start=True, stop=True),
                     reads=[k_onesf, kBD], writes=[kp])
                s.op('dve', lambda e, ps=ps, nch=nch: e.tensor_copy(out=scB[:, 0:nch, :], in_=ps[0:64, 0:nch * 8].rearrange("p (a b) -> p a b", b=8)),
                     reads=[kp], writes=[kscB])
                G = min(8, nch)
                GT = 64 * G
                ngrp = nch // G
                for d in range(2):
                    if samp:
                        s.dma('sp', Caug[:, :, 0:64], stC_d[l, d].rearrange("h a b -> a h b"), writes=[kC], sk=kC)
                        s.dma('sp', Caug[:, :, 64], stn_d[l, d].rearrange("h a -> a h"), writes=[(kC, 'n')], sk=(kC, 'n'), allow_slow_non_contiguous=True)
                    else:
                        s.op('dve', lambda e: e.memset(Caug, 0.0), writes=[kC, (kC, 'n')])
                    mask = maskF if d == 0 else maskB
                    gorder = list(range(ngrp)) if d == 0 else list(range(ngrp - 1, -1, -1))
                    corder = list(range(G)) if d == 0 else list(range(G - 1, -1, -1))

                    def load_group(gi, d=d, G=G, GT=GT, S0=S0):
                        t0 = S0 + gi * GT
                        ti = t0 // 512
                        b = gcount[0] % 2
                        gcount[0] += 1
                        cx = dict(gi=gi, t0=t0, ti=ti, b=b)
                        q32, kq32 = q32_t[0]; k32, kk32 = k32_t[0]; vv, kvv = vv_t[0]; kk, kkk = kk_t[0]
                        qb, kqb = qb_t[b]; kb, kkb = kb_t[b]; va, kva = va_t[b]; kkb2, kkkb2 = kkb_t[b]
                        cx.update(qb=qb, kqb=kqb, kb=kb, kkb=kkb, va=va, kva=kva, kkb2=kkb2, kkkb2=kkkb2,
                                  vw=vw_t[b], PT=PTg_t[b], dC=dC_t[b], hb=hb_t[b])
                        s.dma('sp', q32[:, :, 0:GT], SCR['MQT'][:, t0:t0 + GT].rearrange("(h p) t -> p h t", p=64),
                              reads=[('MQT', 0, ti), ('MQT', 128, ti)], writes=[kq32], sk=kq32)
                        s.dma('sp', k32[:, :, 0:GT], SCR['MKT'][:, t0:t0 + GT].rearrange("(h p) t -> p h t", p=64),
                              reads=[('MKT', 0, ti), ('MKT', 128, ti)], writes=[kk32], sk=kk32)
                        s.dma('sp', vv[:, 0:G, :], TMV[t0:t0 + GT, 0:256].rearrange("(j t) c -> t j c", t=64), reads=[('TMV', ti)], writes=[kvv], sk=kvv)
                        s.dma('sp', kk[:, 0:G, :], TMV[t0:t0 + GT, 512:768].rearrange("(j t) c -> t j c", t=64), reads=[('TMV', ti)], writes=[kkk], sk=kkk)
                        s.op('act', lambda e: e.copy(out=qb[:, :, 0:GT], in_=q32[:, :, 0:GT]), reads=[kq32], writes=[kqb])
                        s.op('act', lambda e: e.mul(out=kb[:, :, 0:GT], in_=k32[:, :, 0:GT], mul=0.125), reads=[kk32], writes=[kkb])
                        s.op('pool', lambda e: e.tensor_copy(out=va[:, 0:G, :, 0:64], in_=vv[:, 0:G, :].rearrange("p g (h e) -> p g h e", e=64)),
                             reads=[kvv], writes=[kva])
                        s.op('act', lambda e: e.mul(out=kkb2[:, 0:G, :], in_=kk[:, 0:G, :], mul=0.125), reads=[kkk], writes=[kkkb2])
                        if d == 1:
                            hF, khF = hF_t[b]; mo_, kmo = mo_t[b]
                            cx.update(hF=hF, khF=khF, mo=mo_, kmo=kmo)
                            s.dma('sp', hF[:, 0:G, :], HF[t0:t0 + GT, :].rearrange("(j t) c -> t j c", t=64), reads=[('HF', t0)], writes=[khF], sk=khF)
                            s.dma('sp', mo_[:, 0:G, :], TMV[t0:t0 + GT, 256:512].rearrange("(j t) c -> t j c", t=64), reads=[('TMV', ti)], writes=[kmo], sk=kmo)
                        return cx

                    def stage1(cx, jl, d=d, mask=mask, G=G, GT=GT):
                        jg = cx['gi'] * G + jl
                        cs_ = slice(jl * 64, (jl + 1) * 64)
                        qb, kb, va, kkb2 = cx['qb'], cx['kb'], cx['va'], cx['kkb2']
                        (vw, kvw), (PT, kPT), (dC, kdC) = cx['vw'], cx['PT'], cx['dC']
                        psA, kpA = bank(0, 2)
                        for h in range(4):
                            s.op('pe', lambda e, h=h: e.matmul(psA[0:64, h * 64:(h + 1) * 64], lhsT=kb[:, h, cs_], rhs=qb[:, h, cs_], start=True, stop=True),
                                 reads=[cx['kkb'], cx['kqb']], writes=[kpA])
                        s.op('dve', lambda e: e.tensor_tensor(out=PT[:, jl], in0=psA[0:64, 0:256].rearrange("p (h t) -> p h t", t=64),
                                                              in1=mask.unsqueeze(1).to_broadcast([64, 4, 64]), op=ALU.mult),
                             reads=[kpA, kmc], writes=[(kPT, jl)])
                        s.op('pool', lambda e: e.tensor_tensor(out=vw[:, jl], in0=va[:, jl], in1=Wt[:, d * 4:d * 4 + 4, jg].unsqueeze(2).to_broadcast([64, 4, 65]), op=ALU.mult),
                             reads=[cx['kva'], (cx['kva'], 'ones'), kWt], writes=[(kvw, jl)])
                        psC, kpC = bank(2, 4)
                        for h in range(4):
                            s.op('pe', lambda e, h=h: e.matmul(psC[0:64, h * 65:(h + 1) * 65], lhsT=kkb2[:, jl, h * 64:(h + 1) * 64], rhs=vw[:, jl, h, :], start=True, stop=True),
                                 reads=[cx['kkkb2'], (kvw, jl)], writes=[kpC])
                        s.op('act', lambda e: e.copy(out=dC[:, jl], in_=psC[0:64, 0:260].rearrange("p (h e) -> p h e", e=65)), reads=[kpC], writes=[(kdC, jl)])

                    def stage2(cx, jl, d=d, G=G, GT=GT):
                        jg = cx['gi'] * G + jl
                        cs_ = slice(jl * 64, (jl + 1) * 64)
                        qb = cx['qb']
                        (vw, kvw), (PT, kPT), (dC, kdC), (hb, khb) = cx['vw'], cx['PT'], cx['dC'], cx['hb']
                        s.op('dve', lambda e: e.tensor_tensor(out=Cs, in0=Caug, in1=scB[:, jg, d * 4:d * 4 + 4].unsqueeze(2).to_broadcast([64, 4, 65]), op=ALU.mult),
                             reads=[kC, (kC, 'n'), kscB], writes=[kCs])
                        s.op('act', lambda e: e.copy(out=Csb, in_=Cs), reads=[kCs], writes=[kCsb])
                        s.op('dve', lambda e: e.tensor_tensor(out=Caug, in0=Cs, in1=dC[:, jl], op=ALU.add), reads=[kCs, (kdC, jl)], writes=[kC, (kC, 'n')])
                        psO, kpO = bank(4, 7)
                        for h in range(4):
                            s.op('pe', lambda e, h=h: e.matmul(psO[0:64, h * 65:(h + 1) * 65], lhsT=PT[:, jl, h, :], rhs=vw[:, jl, h, :], start=(h == 0), stop=False, skip_group_check=True),
                                 reads=[(kPT, jl), (kvw, jl)], writes=[kpO])
                            s.op('pe', lambda e, h=h: e.matmul(psO[0:64, h * 65:(h + 1) * 65], lhsT=qb[:, h, cs_], rhs=Csb[:, h, :], start=False, stop=True, skip_group_check=True),
                                 reads=[cx['kqb'], kCsb], writes=[kpO])
                        pO3 = psO[0:64, 0:260].rearrange("p (h e) -> p h e", e=65)
                        return lambda: stage2b(cx, jl, pO3, kpO)

                    def stage2b(cx, jl, pO3, kpO, d=d, G=G, GT=GT):
                        jg = cx['gi'] * G + jl
                        hb, khb = cx['hb']
                        s.op('act', lambda e: e.activation(out=den[:, jl % 2, :], in_=pO3[:, :, 64], func=AF.Abs), reads=[kpO], writes=[(kden, jl % 2)])
                        s.op('dve', lambda e: e.tensor_tensor(out=den[:, jl % 2, :], in0=den[:, jl % 2, :], in1=THRt[:, d * 4:d * 4 + 4, jg], op=ALU.max),
                             reads=[(kden, jl % 2), kTHRt], writes=[(kden, jl % 2)])
                        s.op('dve', lambda e: e.reciprocal(out=rec[:, jl % 2, :], in_=den[:, jl % 2, :]), reads=[(kden, jl % 2)], writes=[(krec, jl % 2)])
                        s.op('dve', lambda e: e.tensor_tensor(out=hb[:, jl, :].rearrange("p (h e) -> p h e", e=64), in0=pO3[:, :, 0:64],
                                                              in1=rec[:, jl % 2, :].unsqueeze(2).to_broadcast([64, 4, 64]), op=ALU.mult),
                             reads=[kpO, (krec, jl % 2)], writes=[(khb, jl)])

                    def finish_group(cx, d=d, G=G, GT=GT):
                        t0 = cx['t0']
                        hb, khb = cx['hb']
                        hk = [(khb, jl) for jl in range(G)]
                        if d == 0:
                            s.dma('pool', HF[t0:t0 + GT, :].rearrange("(j t) c -> t j c", t=64), hb[:, 0:G, :], reads=hk, writes=[('HF', t0)], sk=khb)
                            return
                        hF, khF, mo_, kmo = cx['hF'], cx['khF'], cx['mo'], cx['kmo']
                        s.op('pool', lambda e: e.tensor_tensor(out=hb[:, 0:G, :], in0=hb[:, 0:G, :], in1=hF[:, 0:G, :], op=ALU.add), reads=hk + [khF], writes=hk)
                        X4 = hb[:, 0:G, :].rearrange("p g (h e) -> p (g h) e", e=64)
                        n4 = G * 4
                        s.op('dve', lambda e: e.tensor_reduce(out=s1[:, 0:n4], in_=X4, axis=AX.X, op=ALU.add), reads=hk, writes=[ks1])
                        s.op('pool', lambda e: e.tensor_tensor(out=sqx[:, 0:G, :], in0=hb[:, 0:G, :], in1=hb[:, 0:G, :], op=ALU.mult), reads=hk, writes=[ksqx])
                        s.op('dve', lambda e: e.tensor_reduce(out=s2[:, 0:n4], in_=sqx[:, 0:G, :].rearrange("p g (h e) -> p (g h) e", e=64), axis=AX.X, op=ALU.add),
                             reads=[ksqx], writes=[ks2])
                        s.op('dve', lambda e: e.tensor_scalar_mul(out=s1[:, 0:n4], in0=s1[:, 0:n4], scalar1=1.0 / 64), reads=[ks1], writes=[ks1])
                        s.op('dve', lambda e: e.tensor_tensor(out=s3[:, 0:n4], in0=s1[:, 0:n4], in1=s1[:, 0:n4], op=ALU.mult), reads=[ks1], writes=[ks3])
                        s.op('dve', lambda e: e.scalar_tensor_tensor(out=s2[:, 0:n4], in0=s2[:, 0:n4], scalar=1.0 / 64, in1=s3[:, 0:n4], op0=ALU.mult, op1=ALU.subtract),
                             reads=[ks2, ks3], writes=[ks2])
                        s.op('act', lambda e: e.activation(out=s2[:, 0:n4], in_=s2[:, 0:n4], func=AF.Sqrt, bias=epsln[0:64, 1:2], scale=1.0), reads=[ks2, k_eps], writes=[ks2])
                        s.op('dve', lambda e: e.reciprocal(out=s2[:, 0:n4], in_=s2[:, 0:n4]), reads=[ks2], writes=[ks2])
                        s.op('dve', lambda e: e.tensor_tensor(out=X4, in0=X4, in1=s1[:, 0:n4].unsqueeze(2).to_broadcast([64, n4, 64]), op=ALU.subtract),
                             reads=hk + [ks1], writes=hk)
                        s.op('dve', lambda e: e.tensor_tensor(out=X4, in0=X4, in1=s2[:, 0:n4].unsqueeze(2).to_broadcast([64, n4, 64]), op=ALU.mult),
                             reads=hk + [ks2], writes=hk)
                        s.op('pool', lambda e: e.tensor_tensor(out=hb[:, 0:G, :], in0=hb[:, 0:G, :], in1=ng.unsqueeze(1).to_broadcast([64, G, 256]), op=ALU.mult),
                             reads=hk + [kng], writes=hk)
                        s.op('act', lambda e: e.activation(out=sig[:, 0:G, :], in_=mo_[:, 0:G, :], func=AF.Sigmoid), reads=[kmo], writes=[ksig])
                        xo, kxo = xo_t[0]
                        s.op('dve', lambda e: e.tensor_tensor(out=xo[:, 0:G, :], in0=hb[:, 0:G, :], in1=sig[:, 0:G, :], op=ALU.mult), reads=hk + [ksig], writes=[kxo])
                        mx, kmx = mixo_t[0]
                        ps, kp = bank(7, 8)
                        pbf = ps.bitcast(BF16)
                        for c in range(2):
                            for jl in range(G):
                                s.op('pe', lambda e, c=c, jl=jl: e.transpose(
                                    out=pbf[:, c * 512 + jl * 64:c * 512 + (jl + 1) * 64], in_=xo[:, jl, c * 128:(c + 1) * 128], identity=idb[0:64, 0:64]),
                                    reads=[kxo, k_idb], writes=[kp])
                        for c in range(2):
                            s.op('dve', lambda e, c=c: e.tensor_copy(out=mx[:, c, 0:GT], in_=pbf[:, c * 512:c * 512 + GT]), reads=[kp], writes=[(kmx, c)])
                        s.dma('pool', MIXT.rearrange("(c p) t -> p c t", p=128)[:, 4:6, t0:t0 + GT], mx[:, :, 0:GT],
                              reads=[(kmx, 0), (kmx, 1)], writes=[('MIXT_M', t0)], sk=kmx)

                    cur = load_group(gorder[0])
                    for jl in corder:
                        stage1(cur, jl)
                    for gidx, gi in enumerate(gorder):
                        nxt = load_group(gorder[gidx + 1]) if gidx + 1 < len(gorder) else None
                        pend = None
                        for jl in corder:
                            p2 = stage2(cur, jl)
                            if pend is not None:
                                pend()
                            pend = p2
                            if nxt is not None:
                                stage1(nxt, jl)
                        pend()
                        finish_group(cur)
                        cur = nxt
                    if not samp:
                        s.dma('pool', C_out[j, l, d].rearrange("h a b -> a h b"), Caug[:, :, 0:64], reads=[kC], sk=(kC, 0))
                        s.dma('pool', n_out[j, l, d].rearrange("h a -> a h"), Caug[:, :, 64], reads=[kC], sk=(kC, 1), allow_slow_non_contiguous=True)
            s.barrier()
            mem.release(m0_)

        def phase_hgrn(l):
            m0_ = mem.mark()
            A = lambda shape, dt=F32, name='g': mem.alloc(shape, dt, name)
            hc, khc = A([32, 2, 32]); ng, kng = A([32, 256]); lbl, klbl = A([64, 4, 4]); lbp, klbp = A([64, 4, 4])
            lbs, klbs = A([64, 4]); lbv, klbv = A([64, 3, 4])
            s.dma('sp', hc, hconst_d, writes=[khc], sk=khc)
            s.dma('sp', ng, hgng_d[l], writes=[kng], sk=kng)
            s.dma('sp', lbl, lbl_d, writes=[klbl], sk=klbl)
            s.op('act', lambda e: e.activation(out=lbp, in_=lbl, func=AF.Exp), reads=[klbl], writes=[klbp])
            s.op('dve', lambda e: e.tensor_reduce(out=lbs, in_=lbp, axis=AX.X, op=ALU.add), reads=[klbp], writes=[klbs])
            s.op('dve', lambda e: e.reciprocal(out=lbs, in_=lbs), reads=[klbs], writes=[klbs])
            s.op('dve', lambda e: e.tensor_tensor(out=lbp, in0=lbp, in1=lbs.unsqueeze(2).to_broadcast([64, 4, 4]), op=ALU.mult), reads=[klbp, klbs], writes=[klbp])
            if l == 0:
                s.op('dve', lambda e: e.memset(lbv[:, 0, :], 0.0), writes=[klbv])
            else:
                s.op('dve', lambda e: e.tensor_reduce(out=lbv[:, 0, :], in_=lbp[:, :, 1:l + 1], axis=AX.X, op=ALU.add), reads=[klbp], writes=[klbv])
            s.op('dve', lambda e: e.tensor_scalar(out=lbv[:, 1, :], in0=lbv[:, 0, :], scalar1=-1.0, scalar2=1.0, op0=ALU.mult, op1=ALU.add), reads=[klbv], writes=[klbv])
            s.op('dve', lambda e: e.tensor_scalar_mul(out=lbv[:, 2, :], in0=lbv[:, 1, :], scalar1=-1.0), reads=[klbv], writes=[klbv])
            rmask, krm = A([64, 512])
            s.op('pool', lambda e: e.memset(rmask, 1.0), writes=[krm])
            s.op('pool', lambda e: e.memset(rmask.rearrange("p (j t) -> p j t", t=32)[:, :, 0:1], 0.0), writes=[krm])
            GTM = 256
            gq32, kgq = A([64, 4, GTM]); gf32, kgf = A([64, 4, GTM]); qf, kqf = gq32, kgq
            sg, ksg = A([64, 4, GTM]); lg, klg = A([64, 4, GTM]); Bc, kBc = A([64, 4, GTM]); kf, kkf = A([64, 4, GTM])
            tot, ktot = A([64, 4, 8])
            egl_t = [A([64, 4, 8]) for _ in range(2)]
            qs_t = [A([96, 4, GTM], BF16) for _ in range(2)]
            RS_t = [A([96, 8, 4, 64], BF16) for _ in range(2)]
            v96, kv96 = A([96, 8, 256])
            hc96, khc96 = A([96, 2, 32])
            s.dma('sp', hc96[64:96], hconst_d, writes=[khc96], sk=khc96)
            ks_t = [A([64, 4, GTM], BF16) for _ in range(2)]
            v32, kv32 = A([32, 8, 256]); vb_t = [A([32, 8, 256], BF16) for _ in range(2)]
            PTg_t = [A([32, 8, 4, 32], BF16) for _ in range(2)]
            dS_t = [A([64, 8, 4, 64]) for _ in range(2)]
            oF, koF = A([32, 8, 256]); gg32, kgg = A([32, 8, 256]); ob_t = [A([32, 8, 256]) for _ in range(2)]
            S, kS = A([64, 4, 64]); Sb, kSb = A([64, 4, 64], BF16); St, kSt = A([64, 4, 64])
            kst_t = [A([32, 256], BF16) for _ in range(2)]
            s2, ks2 = A([32, 32]); sqx, ksqx = A([32, 8, 256]); xo, kxo = A([32, 8, 256], BF16)
            mx, kmx = A([128, 2, GTM], BF16)
            gcount = [0]
            for sq_ in SEQS:
                T, S0, samp, j = sq_['T'], sq_['t0'], sq_['sample'], sq_['j']
                GT = min(GTM, T)
                G = GT // 32
                ngrp = T // GT
                for d in range(2):
                    sview = lambda a: a.rearrange("h c e -> c h e")
                    if samp:
                        s.dma('sp', S, sview(stS_d[l, d]), writes=[kS], sk=kS)
                    else:
                        s.op('dve', lambda e: e.memset(S, 0.0), writes=[kS])
                    mask = hc[:, d, :]
                    gorder = list(range(ngrp)) if d == 0 else list(range(ngrp - 1, -1, -1))
                    corder = list(range(G)) if d == 0 else list(range(G - 1, -1, -1))

                    def load_group(gi, d=d, G=G, GT=GT, S0=S0):
                        t0 = S0 + gi * GT
                        ti = t0 // 512
                        b = gcount[0] % 2
                        gcount[0] += 1
                        LS, kqs = qs_t[b]; qs = LS[0:64]; ks, kks = ks_t[b]; vb, kvb = vb_t[b]; egl, kegl = egl_t[b]
                        RS, kRS = RS_t[b]
                        cx = dict(gi=gi, t0=t0, ti=ti, b=b, qs=qs, kqs=kqs, ks=ks, kks=kks, vb=vb, kvb=kvb, egl=egl, kegl=kegl,
                                  PT=PTg_t[b], dS=dS_t[b], ob=ob_t[b], LS=LS, RS=RS, kRS=kRS)
                        s.dma('sp', v96[64:96, 0:G, :], TMV[t0:t0 + GT, 768:1024].rearrange("(j t) c -> t j c", t=32), reads=[('TMV', ti)], writes=[kv96], sk=kv96)
                        s.op('act', lambda e: e.copy(out=RS[64:96, 0:G], in_=v96[64:96, 0:G, :].rearrange("p g (h e) -> p g h e", e=64)), reads=[kv96], writes=[(kRS, 'v')])
                        fsrc = 'GFF' if d == 0 else 'GFB'
                        s.dma('sp', gq32[:, :, 0:GT], SCR['GQT'].rearrange("(c p) t -> p c t", p=64)[:, :, t0:t0 + GT],
                              reads=[('GQT', 0, ti), ('GQT', 128, ti)], writes=[kgq], sk=kgq)
                        s.dma('sp', gf32[:, :, 0:GT], SCR[fsrc].rearrange("(c p) t -> p c t", p=64)[:, :, t0:t0 + GT],
                              reads=[(fsrc, 0, ti), (fsrc, 128, ti)], writes=[kgf], sk=kgf)
                        s.dma('sp', v32[:, 0:G, :], TMV[t0:t0 + GT, 768:1024].rearrange("(j t) c -> t j c", t=32), reads=[('TMV', ti)], writes=[kv32], sk=kv32)
                        s.op('act', lambda e: e.copy(out=vb[:, 0:G, :], in_=v32[:, 0:G, :]), reads=[kv32], writes=[kvb])
                        W_ = slice(0, GT)
                        klgs = [(klg, i_) for i_ in range(4)]
                        kkfs = [(kkf, i_) for i_ in range(4)]
                        s.op('act', lambda e: e.activation(out=qf[:, :, W_], in_=gq32[:, :, W_], func=AF.Silu), reads=[kgq], writes=[kgq])
                        s.op('act', lambda e: e.activation(out=sg[:, :, W_], in_=gf32[:, :, W_], func=AF.Sigmoid), reads=[kgf], writes=[ksg])
                        for cc in range(4):
                            s.op('dve', lambda e, cc=cc: e.tensor_scalar(out=lg[:, cc, W_], in0=sg[:, cc, W_], scalar1=lbv[:, 1, cc:cc + 1], scalar2=lbv[:, 0, cc:cc + 1],
                                                                        op0=ALU.mult, op1=ALU.add), reads=[ksg, klbv], writes=[(klg, cc)])
                            s.op('pool', lambda e, cc=cc: e.tensor_scalar(out=kf[:, cc, W_], in0=sg[:, cc, W_], scalar1=lbv[:, 2, cc:cc + 1], scalar2=lbv[:, 1, cc:cc + 1],
                                                                         op0=ALU.mult, op1=ALU.add), reads=[ksg, klbv], writes=[(kkf, cc)])
                        s.op('act', lambda e: e.activation(out=lg[:, :, W_], in_=lg[:, :, W_], func=AF.Ln), reads=klgs, writes=klgs)
                        for cc in range(4):
                            s.op('dve', lambda e, cc=cc: e.tensor_tensor_scan(out=Bc[:, cc, W_], data0=rmask[:, W_], data1=lg[:, cc, W_], initial=0.0,
                                                                             op0=ALU.mult, op1=ALU.add), reads=[krm, (klg, cc)], writes=[(kBc, cc)])
                        Bc4 = Bc[:, :, W_].rearrange("p c (j t) -> p c j t", t=32)
                        lg4 = lg[:, :, W_].rearrange("p c (j t) -> p c j t", t=32)
                        kBcs = [(kBc, i_) for i_ in range(4)]
                        s.op('dve', lambda e: e.tensor_copy(out=tot[:, :, 0:G], in_=Bc4[:, :, :, 31]), reads=kBcs, writes=[ktot])
                        if d == 1:
                            s.op('dve', lambda e: e.tensor_tensor(out=Bc4, in0=lg4, in1=Bc4, op=ALU.subtract), reads=kBcs + klgs, writes=kBcs)
                            s.op('dve', lambda e: e.tensor_tensor(out=Bc4, in0=Bc4, in1=tot[:, :, 0:G].unsqueeze(3).to_broadcast([64, 4, G, 32]), op=ALU.add),
                                 reads=kBcs + [ktot], writes=kBcs)
                        s.op('act', lambda e: e.activation(out=egl[:, :, 0:G], in_=tot[:, :, 0:G], func=AF.Exp), reads=[ktot], writes=[kegl])
                        s.op('act', lambda e: e.activation(out=sg[:, :, W_], in_=Bc[:, :, W_], func=AF.Exp), reads=kBcs + [ksg] + kkfs + klgs, writes=[ksg])
                        s.op('dve', lambda e: e.tensor_tensor(out=qs[:, :, W_], in0=qf[:, :, W_], in1=sg[:, :, W_], op=ALU.mult), reads=[kgq, ksg], writes=[kqs])
                        s.op('dve', lambda e: e.tensor_scalar_max(out=Bc[:, :, W_], in0=Bc[:, :, W_], scalar1=-80.0), reads=kBcs + [ksg], writes=kBcs)
                        s.op('act', lambda e: e.activation(out=lg[:, :, W_], in_=Bc[:, :, W_], func=AF.Exp, scale=-1.0), reads=kBcs + klgs, writes=klgs)
                        s.op('dve', lambda e: e.tensor_tensor(out=ks[:, :, W_], in0=kf[:, :, W_], in1=lg[:, :, W_], op=ALU.mult), reads=kkfs + klgs, writes=[kks])
                        return cx

                    def stage1(cx, jl, d=d, mask=mask, G=G, GT=GT):
                        cs_ = slice(jl * 32, (jl + 1) * 32)
                        qs, ks, vb, egl = cx['qs'], cx['ks'], cx['vb'], cx['egl']
                        (PT, kPT), (dS, kdS) = cx['PT'], cx['dS']
                        psA, kpA = bank(0, 2)
                        for h in range(4):
                            s.op('pe', lambda e, h=h: e.matmul(psA[64:96, h * 32:(h + 1) * 32], lhsT=ks[:, h, cs_], rhs=qs[:, h, cs_], start=True, stop=True),
                                 reads=[cx['kks'], cx['kqs']], writes=[kpA])
                        psT, kpT = bank(2, 4)
                        pbf = psT.bitcast(BF16)
                        for cc in range(4):
                            s.op('pe', lambda e, cc=cc: e.transpose(out=pbf[0:32, cc * 64:(cc + 1) * 64], in_=ks[:, cc, cs_], identity=idb[0:64, 0:64]),
                                 reads=[cx['kks'], k_idb], writes=[kpT])
                        kst, kkst = kst_t[jl % 2]
                        LS = cx['LS']
                        s.op('dve', lambda e: e.tensor_tensor(out=LS[64:96, :, cs_], in0=psA[64:96, 0:128].rearrange("p (h t) -> p h t", t=32),
                                                              in1=hc96[64:96, d, :].unsqueeze(1).to_broadcast([32, 4, 32]), op=ALU.mult),
                             reads=[kpA, khc96], writes=[(kPT, jl)])
                        s.op('act', lambda e: e.copy(out=kst, in_=pbf[0:32, 0:256]), reads=[kpT], writes=[kkst])
                        psS, kpS = bank(4, 6)
                        for h in range(4):
                            s.op('pe', lambda e, h=h: e.matmul(psS[0:64, h * 64:(h + 1) * 64], lhsT=kst[:, h * 64:(h + 1) * 64], rhs=vb[:, jl, h * 64:(h + 1) * 64], start=True, stop=True),
                                 reads=[kkst, cx['kvb']], writes=[kpS])
                        s.op('dve', lambda e: e.tensor_tensor(out=dS[:, jl], in0=psS[0:64, 0:256].rearrange("p (c e) -> p c e", e=64),
                                                              in1=egl[:, :, jl].unsqueeze(2).to_broadcast([64, 4, 64]), op=ALU.mult),
                             reads=[kpS, cx['kegl']], writes=[(kdS, jl)])

                    def stage2(cx, jl, tgt, d=d, G=G, GT=GT):
                        cs_ = slice(jl * 32, (jl + 1) * 32)
                        egl, LS, RS, kRS = cx['egl'], cx['LS'], cx['RS'], cx['kRS']
                        (PT, kPT), (dS, kdS), (ob, kob) = cx['PT'], cx['dS'], cx['ob']
                        psO, kpO = bank(6, 8)
                        for h in range(4):
                            s.op('pe', lambda e, h=h: e.matmul(psO[0:32, h * 64:(h + 1) * 64], lhsT=LS[0:96, h, cs_], rhs=RS[0:96, jl, h, :], start=True, stop=True),
                                 reads=[(kPT, jl), cx['kqs'], (kRS, 'v'), (kRS, 'S', jl)], writes=[kpO])
                        s.op('dve', lambda e: e.tensor_tensor(out=St, in0=S, in1=egl[:, :, jl].unsqueeze(2).to_broadcast([64, 4, 64]), op=ALU.mult),
                             reads=[kS, cx['kegl']], writes=[kSt])
                        s.op('dve', lambda e: e.tensor_tensor(out=S, in0=St, in1=dS[:, jl], op=ALU.add), reads=[kSt, (kdS, jl)], writes=[kS])
                        if tgt is not None:
                            tcx, tjl = tgt
                            tRS, tkRS = tcx['RS'], tcx['kRS']
                            s.op('act', lambda e: e.copy(out=tRS[0:64, tjl], in_=S), reads=[kS], writes=[(tkRS, 'S', tjl)])
                        return lambda: s.op('act', lambda e: e.copy(out=ob[:, jl, :], in_=psO[0:32, 0:256]), reads=[kpO], writes=[(kob, jl)])

                    def finish_group(cx, d=d, G=G, GT=GT):
                        t0, ti = cx['t0'], cx['ti']
                        ob, kob = cx['ob']
                        ok_ = [(kob, jl) for jl in range(G)]
                        if d == 0:
                            s.dma('pool', HF[t0:t0 + GT, :].rearrange("(j t) c -> t j c", t=32), ob[:, 0:G, :], reads=ok_, writes=[('HF', t0)], sk=kob)
                            return
                        s.dma('sp', oF[:, 0:G, :], HF[t0:t0 + GT, :].rearrange("(j t) c -> t j c", t=32), reads=[('HF', t0)], writes=[koF], sk=koF)
                        s.dma('sp', gg32[:, 0:G, :], TMV[t0:t0 + GT, 1024:1280].rearrange("(j t) c -> t j c", t=32), reads=[('TMV', ti)], writes=[kgg], sk=kgg)
                        s.op('pool', lambda e: e.tensor_tensor(out=ob[:, 0:G, :], in0=ob[:, 0:G, :], in1=oF[:, 0:G, :], op=ALU.add), reads=ok_ + [koF], writes=ok_)
                        n4 = G * 4
                        s.op('pool', lambda e: e.tensor_tensor(out=sqx[:, 0:G, :], in0=ob[:, 0:G, :], in1=ob[:, 0:G, :], op=ALU.mult), reads=ok_, writes=[ksqx])
                        s.op('dve', lambda e: e.tensor_reduce(out=s2[:, 0:n4], in_=sqx[:, 0:G, :].rearrange("p g (h e) -> p (g h) e", e=64), axis=AX.X, op=ALU.add),
                             reads=[ksqx], writes=[ks2])
                        s.op('act', lambda e: e.activation(out=s2[:, 0:n4], in_=s2[:, 0:n4], func=AF.Sqrt, bias=epsln[0:32, 1:2], scale=1.0 / 64), reads=[ks2, k_eps], writes=[ks2])
                        s.op('dve', lambda e: e.reciprocal(out=s2[:, 0:n4], in_=s2[:, 0:n4]), reads=[ks2], writes=[ks2])
                        X4 = ob[:, 0:G, :].rearrange("p g (h e) -> p (g h) e", e=64)
                        s.op('dve', lambda e: e.tensor_tensor(out=X4, in0=X4, in1=s2[:, 0:n4].unsqueeze(2).to_broadcast([32, n4, 64]), op=ALU.mult),
                             reads=ok_ + [ks2], writes=ok_)
                        s.op('pool', lambda e: e.tensor_tensor(out=ob[:, 0:G, :], in0=ob[:, 0:G, :], in1=ng.unsqueeze(1).to_broadcast([32, G, 256]), op=ALU.mult),
                             reads=ok_ + [kng], writes=ok_)
                        s.op('act', lambda e: e.activation(out=sqx[:, 0:G, :], in_=gg32[:, 0:G, :], func=AF.Silu), reads=[kgg, ksqx, ks2], writes=[ksqx])
                        s.op('dve', lambda e: e.tensor_tensor(out=xo[:, 0:G, :], in0=ob[:, 0:G, :], in1=sqx[:, 0:G, :], op=ALU.mult), reads=ok_ + [ksqx], writes=[kxo])
                        ps, kp = bank(0, 2)
                        pbf = ps.bitcast(BF16)
                        for c in range(2):
                            for jl in range(G):
                                s.op('pe', lambda e, c=c, jl=jl: e.transpose(
                                    out=pbf[:, c * 512 + jl * 32:c * 512 + (jl + 1) * 32], in_=xo[:, jl, c * 128:(c + 1) * 128], identity=idb[0:32, 0:32]),
                                    reads=[kxo, k_idb], writes=[kp])
                        for c in range(2):
                            s.op('dve', lambda e, c=c: e.tensor_copy(out=mx[:, c, 0:GT], in_=pbf[:, c * 512:c * 512 + GT]), reads=[kp], writes=[(kmx, c)])
                        s.dma('pool', MIXT.rearrange("(c p) t -> p c t", p=128)[:, 6:8, t0:t0 + GT], mx[:, :, 0:GT],
                              reads=[(kmx, 0), (kmx, 1)], writes=[('MIXT_G', t0)], sk=kmx)

                    cur = load_group(gorder[0])
                    for jl in corder:
                        stage1(cur, jl)
                    s.op('act', lambda e, cur=cur, j0=corder[0]: e.copy(out=cur['RS'][0:64, j0], in_=S), reads=[kS], writes=[(cur['kRS'], 'S', corder[0])])
                    for gidx, gi in enumerate(gorder):
                        nxt = load_group(gorder[gidx + 1]) if gidx + 1 < len(gorder) else None
                        pend = None
                        for ci, jl in enumerate(corder):
                            tgt = (cur, corder[ci + 1]) if ci + 1 < len(corder) else ((nxt, corder[0]) if nxt is not None else None)
                            p2 = stage2(cur, jl, tgt)
                            if pend is not None:
                                pend()
                            pend = p2
                            if nxt is not None and not HG_NOINT:
                                stage1(nxt, jl)
                        pend()
                        if nxt is not None and HG_NOINT:
                            for jl in corder:
                                stage1(nxt, jl)
                        finish_group(cur)
                        cur = nxt
                    if not samp:
                        s.dma('pool', sview(S_out[j, l, d]), S, reads=[kS], sk=(kS, 'o'))
            s.barrier()
            mem.release(m0_)

        phase_init()
        for l in range(NL):
            phase_mod(l)
            if 'p1' in PH:
                phase_p1(l)
            if 'mla' in PH:
                phase_mla(l)
            if 'mlstm' in PH:
                phase_mlstm(l)
            if 'hgrn' in PH:
                phase_hgrn(l)
            if 'dense' in PH:
                phase_p3(l)
                phase_p3b(l)
                phase_p4(l)
        phase_final()
        s.emit()
    return nc


def _bf(x):
    return np.ascontiguousarray(x, dtype=np.float32)


def make_in_maps(inputs, cfg):
    NL = cfg.get('n_layers', DEPTH)
    NST = cfg.get('n_sample_tiles', 8)
    NPR = cfg.get('n_prompts', 4)
    cores = cfg.get('cores', list(range(8)))
    g = {k: np.asarray(v) for k, v in inputs.items()}
    w1 = _bf(g['w_in'][:NL][:, :, W1_COLS])
    b1 = g['b_in'][:NL][:, W1_COLS]
    b1fm = np.zeros((NL, 128, NG), np.float32)
    for gi, (c0, M, _, _) in enumerate(FM_GROUPS):
        b1fm[:, :M, gi] = b1[:, c0:c0 + M]
    b1tm = _bf(np.broadcast_to(b1[:, None, NFM:], (NL, 128, NTM)))
    b_modT = _bf(g['b_mod'][:NL].reshape(NL, 48, 128).transpose(0, 2, 1))
    lnp = _bf(np.stack([g[k][:NL].reshape(NL, NKC, 128).transpose(0, 2, 1) for k in ('ln1_g', 'ln1_b', 'ln2_g', 'ln2_b')], axis=2))
    shared = dict(w_mod=_bf(g['w_mod'][:NL]), b_modT=b_modT, w1=w1, b1fm=b1fm, b1tm=b1tm, w_out=_bf(g['w_out'][:NL]),
                  lnp=lnp, w_ffn_in=_bf(g['w_ffn_in'][:NL]), w_ffn_out=_bf(g['w_ffn_out'][:NL]),
                  identf=np.eye(128, dtype=np.float32))
    wuq = g['w_uq'][:NL]
    wqp = np.zeros_like(wuq)
    for h in range(8):
        wqp[:, :, h * 96 + 64:h * 96 + 96] = wuq[:, :, h * 96 + 64 + _PERM]
    wukv = g['w_ukv'][:NL]
    wk = np.zeros((NL, 128, 768), np.float32)
    wv = np.zeros((NL, 128, 512), np.float32)
    for h in range(8):
        wk[:, :, h * 96:h * 96 + 64] = wukv[:, :, h * 128:h * 128 + 64]
        wv[:, :, h * 64:(h + 1) * 64] = wukv[:, :, h * 128 + 64:h * 128 + 128]
    e96 = np.zeros((32, 96), np.float32)
    e96[np.arange(32), 64 + np.arange(32)] = 1.0
    qkn = _bf(np.stack([g['mla_q_norm'][:NL, :128], g['mla_q_norm'][:NL, 128:], g['mla_kv_norm'][:NL]], axis=-1))
    pos = np.arange(4096)
    freqs = 10000.0 ** (-np.arange(8, dtype=np.float64) * 0.125)
    ar = (pos // 64)[:, None] * freqs
    ac = (pos % 64)[:, None] * freqs
    ang = np.concatenate([ar, ar, ac, ac], -1)
    cosT = np.cos(ang).T.astype(np.float32)
    sinT = (np.sin(ang) * _ROT_SIGN).T.astype(np.float32)
    cos96 = np.ones((96, 4096), np.float32)
    sin96 = np.zeros((96, 4096), np.float32)
    cos96[64:] = cosT
    sin96[64:] = sinT
    shared.update(wq=_bf(wuq), wqp=_bf(wqp), wk=wk, wv=wv, e96=e96, qkn=qkn, cos96=cos96, sin96=sin96,
                  kcos=_bf(cosT), ksin=_bf(sinT))
    mconst = np.zeros((64, 4, 64), np.float32)
    ii = np.arange(64)
    mconst[:, 0, :] = (ii[None, :] >= ii[:, None])
    mconst[:, 1, :] = (ii[:, None] >= ii[None, :])
    mconst[:, 2, :] = np.eye(64)[::-1]
    mconst[0:4, 3, 0:4] = np.eye(4)[::-1]
    shared.update(mconst=mconst, mlng=_bf(np.broadcast_to(g['mlstm_norm'][:NL, None, :], (NL, 64, 256))))
    hconst = np.zeros((32, 2, 32), np.float32)
    i32 = np.arange(32)
    hconst[:, 0, :] = (i32[None, :] >= i32[:, None])
    hconst[:, 1, :] = (i32[:, None] >= i32[None, :])
    lbl = _bf(g['hgrn_lb_logits'].reshape(4, 4, 64).transpose(2, 1, 0))
    shared.update(hconst=hconst, lbl=lbl, hgng=_bf(np.broadcast_to(g['hgrn_norm'][:NL, None, :], (NL, 32, 256))))
    maps = []
    for ci in cores:
        b = ci // 2
        parts = []
        if NST:
            parts.append(g['x_sample'][b, :NST * 512])
        for j in range(NPR):
            parts.append(g['x_prompt'][4 * ci + j])
        xin = _bf(np.concatenate(parts, axis=0))
        cv = np.stack([g['c'][b], g['c_ctx']], axis=-1)
        cvT = _bf(cv.reshape(NKC, 128, 2).transpose(1, 0, 2))
        m = dict(shared)
        m.update(stC=_bf(g['state_mlstm_C'][b, :NL]), stn=_bf(g['state_mlstm_n'][b, :NL]), stm=_bf(g['state_mlstm_m'][b, :NL]))
        m.update(stS=_bf(g['state_hgrn_S'][b, :NL]))
        m.update(xin=xin, cvT=cvT, cckv=_bf(g['cache_mla_ckv'][b, :NL]), ckpe=_bf(g['cache_mla_kpe'][b, :NL]))
        maps.append(m)
    return maps


def kernel(**inputs):
    cfg = {}
    nc = build_program(cfg)
    maps = make_in_maps(inputs, cfg)
    res = run_bass_kernel_spmd(nc, maps, core_ids=list(range(8)))
    r = res.results
    cat = lambda k: np.ascontiguousarray(np.concatenate([r[i][k] for i in range(8)], axis=0), dtype=np.float32)
    y_prompt = np.concatenate([r[i]['y_out'][4096:].reshape(4, 256, D) for i in range(8)], axis=0).astype(np.float32)
    y_sample = np.stack([r[2 * b]['y_out'][:4096] for b in range(4)], axis=0).astype(np.float32)
    return (y_prompt, y_sample, cat('ckv_out'), cat('kpe_out'), cat('C_out'), cat('n_out'), cat('m_out'), cat('S_out'))
```

```python
import math
from contextlib import ExitStack
import numpy as np
import concourse.bass as bass
import concourse.mybir as mybir
from concourse.bass_utils import run_bass_kernel_spmd

F32 = mybir.dt.float32
BF16 = mybir.dt.bfloat16
AF = mybir.ActivationFunctionType
ALU = mybir.AluOpType
AX = mybir.AxisListType

D = 1024
DEPTH = 4
NKC = 8
FF = 2816
NFC = 22
ALPHA = (2 * DEPTH) ** 0.25
EPS = 1e-6
EPS_LN = EPS / (ALPHA * ALPHA)
ATT_SCALE = 96 ** -0.5

_IN_SIZES = (256, 128, 32, 256, 256, 256, 256, 4, 4, 4, 4, 256, 256, 256, 256, 256)
_OFF = np.cumsum((0,) + _IN_SIZES)
_NAMES = ['cq', 'ckv', 'kpe', 'mq', 'mk', 'mv', 'mo', 'mi_f', 'mi_b', 'mf_f', 'mf_b', 'gq', 'gf_f', 'gf_b', 'gi', 'gg']
_COL = {n: np.arange(_OFF[i], _OFF[i + 1]) for i, n in enumerate(_NAMES)}
_PERM = np.concatenate([np.arange(8, 16), np.arange(0, 8), np.arange(24, 32), np.arange(16, 24)])
_ROT_SIGN = np.concatenate([-np.ones(8), np.ones(8), -np.ones(8), np.ones(8)]).astype(np.float32)
FM_GROUPS = []
_fm_cols = []


def _add_fm(cols, dst, r0):
    c0 = sum(len(c) for c in _fm_cols)
    FM_GROUPS.append((c0, len(cols), dst, r0))
    _fm_cols.append(np.asarray(cols))


_add_fm(_COL['cq'][:128], 'CQT', 0)
_add_fm(_COL['cq'][128:], 'CQT', 128)
_add_fm(_COL['ckv'], 'CKVT', 0)
_add_fm(_COL['kpe'], 'KPET', 0)
_add_fm(_COL['kpe'][_PERM], 'KPEP', 0)
for _n, _d in (('mq', 'MQT'), ('mk', 'MKT'), ('gq', 'GQT'), ('gf_f', 'GFF'), ('gf_b', 'GFB')):
    _add_fm(_COL[_n][:128], _d, 0)
    _add_fm(_COL[_n][128:], _d, 128)
for _n, _d in (('mi_f', 'MIF'), ('mi_b', 'MIB'), ('mf_f', 'MFF'), ('mf_b', 'MFB')):
    _add_fm(_COL[_n], _d, 0)
NFM = sum(len(c) for c in _fm_cols)
_tm_cols = [_COL['mv'], _COL['mo'], _COL['mk'], _COL['gi'], _COL['gg']]
NTM = 1280
TM_GROUPS = [(0, 512), (512, 512), (1024, 256)]
W1_COLS = np.concatenate(_fm_cols + _tm_cols)
NW1 = len(W1_COLS)
NG = len(FM_GROUPS)
FM_SCR = {'CQT': 256, 'CKVT': 128, 'KPET': 32, 'KPEP': 32, 'MQT': 256, 'MKT': 256, 'GQT': 256, 'GFF': 256,
          'GFB': 256, 'MIF': 4, 'MIB': 4, 'MFF': 4, 'MFB': 4}


class Sched:
    EPOCH = 40000

    def __init__(self, nc, stack):
        self.nc = nc
        self.stack = stack
        self.eng = {'pe': nc.tensor, 'act': nc.scalar, 'dve': nc.vector, 'pool': nc.gpsimd, 'sp': nc.sync}
        self.prog = {k: [] for k in self.eng}
        self.seq = {k: 0 for k in self.eng}
        self.esems = {k: [] for k in self.eng}
        self.dsem = {}
        self.free_dsems = {}
        self.lastw = {}
        self.readers = {}
        self.waited = {}
        self.nsem = 0

    def _newsem(self, name):
        self.nsem += 1
        return self.stack.enter_context(self.nc.semaphore(name))

    def _deps(self, engine, reads, writes):
        deps = {}

        def add(ev):
            k = id(ev[0])
            if k not in deps or deps[k][1] < ev[1]:
                deps[k] = ev
        for k in reads:
            if k in self.lastw:
                add(self.lastw[k])
        for k in writes:
            if k in self.lastw:
                add(self.lastw[k])
            for ev in self.readers.get(k, {}).values():
                add(ev)
        waits = []
        own = set(id(s) for s in self.esems[engine]) if engine == 'pe' else set()
        for k, (sem, val) in deps.items():
            if k in own:
                continue
            wk = (engine, k)
            if self.waited.get(wk, 0) < val:
                self.waited[wk] = val
                waits.append((sem, val))
        return waits

    def _commit(self, ev, reads, writes):
        for k in writes:
            self.lastw[k] = ev
            self.readers[k] = {}
        for k in reads:
            self.readers.setdefault(k, {})[id(ev[0])] = ev

    def op(self, engine, fn, reads=(), writes=()):
        waits = self._deps(engine, reads, writes)
        n = self.seq[engine]
        e = n // self.EPOCH
        while len(self.esems[engine]) <= e:
            self.esems[engine].append(self._newsem(f"e_{engine}_{len(self.esems[engine])}"))
        sem = self.esems[engine][e]
        val = n % self.EPOCH + 1
        self.seq[engine] = n + 1
        self.prog[engine].append((waits, fn, sem, 1))
        self._commit((sem, val), reads, writes)

    def dma(self, queue, out, in_, reads=(), writes=(), sk=None, **kw):
        self.dma_group(queue, [(out, in_)], reads, writes, sk, **kw)

    def dma_group(self, queue, pairs, reads=(), writes=(), sk=None, **kw):
        assert sk is not None
        sk = (queue, sk)
        if sk not in self.dsem:
            fl = self.free_dsems.setdefault(queue, [])
            self.dsem[sk] = fl.pop() if fl else [self._newsem(f"d_{self.nsem}"), 0]
        ent = self.dsem[sk]
        waits = self._deps(queue, reads, writes)
        for i, (o, a) in enumerate(pairs):
            ent[1] += 16
            self.prog[queue].append((waits if i == 0 else [],
                                     (lambda eng, o=o, a=a: eng.dma_start(out=o, in_=a, **kw)), ent[0], 16))
        self._commit((ent[0], ent[1]), reads, writes)

    def barrier(self):
        evs = []
        for e, sems in self.esems.items():
            n = self.seq[e]
            if n > 0:
                evs.append((sems[(n - 1) // self.EPOCH], (n - 1) % self.EPOCH + 1))
        for sk, (sem, cnt) in self.dsem.items():
            if cnt > 0:
                evs.append((sem, cnt))
        for fl in self.free_dsems.values():
            for sem, cnt in fl:
                if cnt > 0:
                    evs.append((sem, cnt))
        for engine in self.eng:
            waits = []
            for sem, val in evs:
                wk = (engine, id(sem))
                if self.waited.get(wk, 0) < val:
                    self.waited[wk] = val
                    waits.append((sem, val))
            if waits:
                self.prog[engine].append((waits, None, None, 0))
        for (q, _), ent in self.dsem.items():
            self.free_dsems.setdefault(q, []).append(ent)
        self.dsem = {}
        self.lastw = {}
        self.readers = {}

    def emit(self):
        self.barrier()
        nc = self.nc
        with nc.Block() as block:
            def run(name):
                def f(eng):
                    for waits, fn, sem, amt in self.prog[name]:
                        for ws, wv in waits:
                            eng.wait_ge(ws, wv)
                        if fn is not None:
                            fn(eng).then_inc(sem, amt)
                return f
            block.sync(run('sp'))
            block.tensor(run('pe'))
            block.scalar(run('act'))
            block.vector(run('dve'))
            block.gpsimd(run('pool'))


class Mem:
    def __init__(self, big, words):
        self.big = big
        self.words = words
        self.top = 0
        self.n = 0

    def mark(self):
        return self.top

    def release(self, m):
        self.top = m

    def alloc(self, shape, dt=F32, name='t'):
        P = shape[0]
        n = int(np.prod(shape[1:]))
        w = n if dt == F32 else (n + 1) // 2
        w = (w + 15) // 16 * 16
        assert self.top + w <= self.words, f"SBUF overflow allocating {name} {shape}"
        a = self.big[0:P, self.top:self.top + w]
        self.top += w
        if dt != F32:
            a = a.bitcast(dt)
        a = a[:, 0:n]
        if len(shape) == 3:
            a = a.rearrange("p (a b) -> p a b", a=shape[1], b=shape[2])
        elif len(shape) == 4:
            a = a.rearrange("p (a b c) -> p a b c", a=shape[1], b=shape[2], c=shape[3])
        self.n += 1
        return a, f"{name}#{self.n}"


def build_program(cfg):
    NL = cfg.get('n_layers', DEPTH)
    NST = cfg.get('n_sample_tiles', 8)
    NPR = cfg.get('n_prompts', 4)
    DBG = cfg.get('debug', [])
    PH = cfg.get('phases', ['p1', 'mla', 'mlstm', 'hgrn', 'dense'])
    MIXIN = cfg.get('mix_input', False)
    HG_NOINT = cfg.get('hg_noint', False)
    TS = NST * 512
    TT = TS + NPR * 256
    NTILE = TT // 512
    nc = bass.Bass("TRN2", target_bir_lowering=False)

    def din(name, shape, dt=F32):
        return nc.dram_tensor(name, list(shape), dt, kind="ExternalInput").ap()

    def dout(name, shape, dt=F32):
        return nc.dram_tensor(name, list(shape), dt, kind="ExternalOutput").ap()

    def dscr(name, shape, dt=F32):
        kind = "ExternalOutput" if name in DBG else "Internal"
        return nc.dram_tensor(name, list(shape), dt, kind=kind).ap()

    xin = din("xin", [TT, D])
    cvT = din("cvT", [128, NKC, 2])
    w_mod = din("w_mod", [NL, D, 6 * D])
    b_modT = din("b_modT", [NL, 128, 48])
    w1 = din("w1", [NL, D, NW1])
    b1fm = din("b1fm", [NL, 128, NG])
    b1tm = din("b1tm", [NL, 128, NTM])
    w_out = din("w_out", [NL, D, D])
    lnp = din("lnp", [NL, 128, 4, NKC])
    w_ffn_in = din("w_ffn_in", [NL, D, 2 * FF])
    w_ffn_out = din("w_ffn_out", [NL, FF, D])
    identf = din("identf", [128, 128])
    wq_d = din("wq", [NL, 256, 768])
    wqp_d = din("wqp", [NL, 256, 768])
    wk_d = din("wk", [NL, 128, 768])
    wv_d = din("wv", [NL, 128, 512])
    e96_d = din("e96", [32, 96])
    qkn_d = din("qkn", [NL, 128, 3])
    cos96_d = din("cos96", [96, 4096])
    sin96_d = din("sin96", [96, 4096])
    kcos_d = din("kcos", [32, 4096])
    ksin_d = din("ksin", [32, 4096])
    cckv_d = din("cckv", [NL, 512, 128])
    ckpe_d = din("ckpe", [NL, 512, 32])
    mconst_d = din("mconst", [64, 4, 64])
    mlng_d = din("mlng", [NL, 64, 256])
    stC_d = din("stC", [NL, 2, 4, 64, 64])
    stn_d = din("stn", [NL, 2, 4, 64])
    stm_d = din("stm", [NL, 2, 4])
    C_out = dout("C_out", [max(NPR, 1), NL, 2, 4, 64, 64])
    n_out = dout("n_out", [max(NPR, 1), NL, 2, 4, 64])
    m_out = dout("m_out", [max(NPR, 1), NL, 2, 4])
    HF = dscr("HF", [TT, 256])
    hconst_d = din("hconst", [32, 2, 32])
    hgng_d = din("hgng", [NL, 32, 256])
    lbl_d = din("lbl", [64, 4, 4])
    stS_d = din("stS", [NL, 2, 4, 64, 64])
    S_out = dout("S_out", [max(NPR, 1), NL, 2, 4, 64, 64])
    ckv_out = dout("ckv_out", [max(NPR, 1), NL, 256, 128])
    kpe_out = dout("kpe_out", [max(NPR, 1), NL, 256, 32])
    y_out = dout("y_out", [TT, D])

    XT = dscr("XT", [D, TT])
    MIXT = din("MIXT", [D, TT], BF16) if MIXIN else dscr("MIXT", [D, TT], BF16)
    UT = dscr("UT", [FF, TT], BF16)
    SCR = {k: dscr(k, [r, TT]) for k, r in FM_SCR.items()}
    TMV = dscr("TMV", [TT, NTM])

    with ExitStack() as st:
        WORDS = 47 * 1024
        big = st.enter_context(nc.sbuf_tensor("big", [128, WORDS], F32))
        psb = [st.enter_context(nc.psum_tensor(f"ps{i}", [128, 512], F32)) for i in range(8)]
        s = Sched(nc, st)
        mem = Mem(big, WORDS)
        psi = {}

        def bank(lo=0, hi=8):
            c = psi.get((lo, hi), 0)
            psi[(lo, hi)] = c + 1
            i = lo + c % (hi - lo)
            return psb[i], ('ps', i)

        idf, k_idf = mem.alloc([128, 128], F32, 'idf')
        idb, k_idb = mem.alloc([128, 128], BF16, 'idb')
        onesf, k_onesf = mem.alloc([128, 128], F32, 'onesf')
        modT, k_mod = mem.alloc([128, 48, 2], F32, 'modT')
        lnt, k_lnt = mem.alloc([128, 4, NKC], F32, 'lnt')
        s.dma('sp', idf, identf, writes=[k_idf], sk='idf')
        s.op('dve', lambda e: e.tensor_copy(out=idb, in_=idf), reads=[k_idf], writes=[k_idb])
        s.op('pool', lambda e: e.memset(onesf, 1.0), writes=[k_onesf])
        onesb, k_onesb = mem.alloc([128, 128], BF16, 'onesb')
        s.op('pool', lambda e: e.memset(onesb, 1.0), writes=[k_onesb])
        epsln, k_eps = mem.alloc([128, 2], F32, 'epsln')
        s.op('pool', lambda e: e.memset(epsln[:, 0:1], EPS_LN), writes=[k_eps])
        s.op('pool', lambda e: e.memset(epsln[:, 1:2], EPS), writes=[k_eps])
        base_mark = mem.mark()

        def load_w_bf16(dram2d, dst, kdst, nkc, ncols, stage):
            i = 0
            cw = stage[0][0].shape[1]
            for kc in range(nkc):
                for c0 in range(0, ncols, cw):
                    w = min(cw, ncols - c0)
                    sa, sk_ = stage[i % len(stage)]
                    i += 1
                    s.dma('sp', sa[:, 0:w], dram2d[kc * 128:(kc + 1) * 128, c0:c0 + w], writes=[sk_], sk=sk_)
                    ce = ('pool', 'dve', 'act')[i % 3]
                    if ce == 'act':
                        s.op('act', lambda e, kc=kc, c0=c0, w=w, sa=sa: e.copy(out=dst[:, kc, c0:c0 + w], in_=sa[:, 0:w]), reads=[sk_], writes=[kdst])
                    else:
                        s.op(ce, lambda e, kc=kc, c0=c0, w=w, sa=sa: e.tensor_copy(out=dst[:, kc, c0:c0 + w], in_=sa[:, 0:w]), reads=[sk_], writes=[kdst])

        def fm_view(dram2d, t0, n):
            return dram2d.rearrange("(c p) t -> p c t", p=128)[:, :, t0:t0 + n]

        def phase_init():
            m0 = mem.mark()
            xin_t = [mem.alloc([128, 4, D], F32, 'xin_t') for _ in range(2)]
            xT_t = [mem.alloc([128, NKC, 512], F32, 'xT_t') for _ in range(2)]
            for i in range(NTILE):
                t0 = i * 512
                xa, kx = xin_t[i % 2]
                xo, ko = xT_t[i % 2]
                s.dma('sp', xa, xin[t0:t0 + 512, :].rearrange("(b p) c -> p b c", p=128), writes=[kx], sk=kx)
                for kc in range(NKC):
                    ps, kp = bank()
                    for b in range(4):
                        s.op('pe', lambda e, ps=ps, b=b, kc=kc, xa=xa: e.transpose(
                            out=ps[:, b * 128:(b + 1) * 128], in_=xa[:, b, kc * 128:(kc + 1) * 128], identity=idf),
                            reads=[kx, k_idf], writes=[kp])
                    eng = 'dve' if kc % 2 == 0 else 'act'
                    if eng == 'dve':
                        s.op('dve', lambda e, ps=ps, kc=kc, xo=xo: e.tensor_copy(out=xo[:, kc, :], in_=ps[:, :]),
                             reads=[kp], writes=[(ko, kc)])
                    else:
                        s.op('act', lambda e, ps=ps, kc=kc, xo=xo: e.copy(out=xo[:, kc, :], in_=ps[:, :]),
                             reads=[kp], writes=[(ko, kc)])
                s.dma('pool', fm_view(XT, t0, 512), xo, reads=[(ko, kc) for kc in range(NKC)], writes=[('XT', i)], sk=ko)
            s.barrier()
            mem.release(m0)

        def phase_final():
            m0 = mem.mark()
            xT_t = [mem.alloc([128, NKC, 512], F32, 'xT_f') for _ in range(2)]
            y_t = [mem.alloc([128, 4, D], F32, 'y_t') for _ in range(2)]
            for i in range(NTILE):
                t0 = i * 512
                xa, kx = xT_t[i % 2]
                ya, ky = y_t[i % 2]
                s.dma('sp', xa, fm_view(XT, t0, 512), reads=[('XT', i)], writes=[kx], sk=kx)
                for b in range(4):
                    for half in range(2):
                        ps, kp = bank()
                        for q in range(4):
                            kc = half * 4 + q
                            s.op('pe', lambda e, ps=ps, q=q, kc=kc, b=b, xa=xa: e.transpose(
                                out=ps[:, q * 128:(q + 1) * 128], in_=xa[:, kc, b * 128:(b + 1) * 128], identity=idf),
                                reads=[kx, k_idf], writes=[kp])
                        if half == 0:
                            s.op('dve', lambda e, ps=ps, b=b, ya=ya: e.tensor_copy(out=ya[:, b, 0:512], in_=ps[:, :]),
                                 reads=[kp], writes=[(ky, b, 0)])
                        else:
                            s.op('act', lambda e, ps=ps, b=b, ya=ya: e.copy(out=ya[:, b, 512:1024], in_=ps[:, :]),
                                 reads=[kp], writes=[(ky, b, 1)])
                s.dma('pool', y_out[t0:t0 + 512, :].rearrange("(b p) c -> p b c", p=128), ya,
                      reads=[(ky, b, h) for b in range(4) for h in range(2)], sk=ky)
            s.barrier()
            mem.release(m0)

        def phase_mod(l):
            m0 = mem.mark()
            cv, kcv = mem.alloc([128, NKC, 2], F32, 'cv')
            sil, ksil = mem.alloc([128, NKC, 2], F32, 'sil')
            bm, kbm = mem.alloc([128, 48], F32, 'bm')
            stage = [mem.alloc([128, 6 * D], F32, 'wm') for _ in range(2)]
            s.dma('sp', cv, cvT, writes=[kcv], sk=kcv)
            s.dma('sp', bm, b_modT[l], writes=[kbm], sk=kbm)
            s.dma('sp', lnt, lnp[l], writes=[k_lnt], sk=k_lnt)
            s.op('act', lambda e: e.activation(out=sil, in_=cv, func=AF.Silu), reads=[kcv], writes=[ksil])
            ps, kp = bank()
            for kc in range(NKC):
                sa, ks = stage[kc % 2]
                s.dma('sp', sa, w_mod[l, kc * 128:(kc + 1) * 128, :], writes=[ks], sk=ks)
                for g in range(48):
                    s.op('pe', lambda e, sa=sa, g=g, kc=kc: e.matmul(
                        ps[:, 2 * g:2 * g + 2], lhsT=sa[:, g * 128:(g + 1) * 128], rhs=sil[:, kc, :],
                        start=(kc == 0 and g == 0), stop=(kc == NKC - 1), skip_group_check=True),
                        reads=[ks, ksil], writes=[kp])
            s.op('dve', lambda e: e.tensor_tensor(
                out=modT, in0=ps[:, 0:96].rearrange("p (g v) -> p g v", v=2),
                in1=bm.unsqueeze(2).to_broadcast([128, 48, 2]), op=ALU.add), reads=[kp, kbm], writes=[k_mod])
            for lo, add in ((8, True), (16, False), (32, True), (40, False)):
                if add:
                    s.op('dve', lambda e, lo=lo: e.tensor_scalar_add(out=modT[:, lo:lo + 8, :], in0=modT[:, lo:lo + 8, :], scalar1=1.0),
                         reads=[k_mod], writes=[k_mod])
                else:
                    s.op('dve', lambda e, lo=lo: e.tensor_scalar_mul(out=modT[:, lo:lo + 8, :], in0=modT[:, lo:lo + 8, :], scalar1=1.0 / ALPHA),
                         reads=[k_mod], writes=[k_mod])
            s.barrier()
            mem.release(m0)

        SH1, SC1, G1, SH2, SC2, G2 = 0, 8, 16, 24, 32, 40

        def tile_v(i):
            return 0 if i < NST else 1

        def phase_p1(l):
            m0 = mem.mark()
            w1t, kw1 = mem.alloc([128, NKC, NW1], BF16, 'w1t')
            bfm, kbfm = mem.alloc([128, NG], F32, 'bfm')
            btm, kbtm = mem.alloc([128, NTM], F32, 'btm')
            stage = [mem.alloc([128, 2048], F32, 'wst') for _ in range(3)]
            xT_t = [mem.alloc([128, NKC, 512], F32, 'xT1') for _ in range(2)]
            hT_t = [mem.alloc([128, NKC, 512], BF16, 'hT') for _ in range(2)]
            fo_t = [mem.alloc([128, 512], F32, 'fo') for _ in range(4)]
            to_t = [mem.alloc([128, 4, NTM], F32, 'to') for _ in range(1)]
            s.dma('sp', bfm, b1fm[l], writes=[kbfm], sk=kbfm)
            s.dma('sp', btm, b1tm[l], writes=[kbtm], sk=kbtm)
            load_w_bf16(w1[l], w1t, kw1, NKC, NW1, stage)
            nfo = 0
            for i in range(NTILE):
                t0 = i * 512
                v = tile_v(i)
                xa, kx = xT_t[i % 2]
                ha, kh = hT_t[i % 2]
                s.dma('sp', xa, fm_view(XT, t0, 512), reads=[('XT', i)], writes=[kx], sk=kx)
                for kc in range(NKC):
                    s.op('dve', lambda e, kc=kc, xa=xa, ha=ha, v=v: e.tensor_scalar(
                        out=ha[:, kc, :], in0=xa[:, kc, :], scalar1=modT[:, SC1 + kc, v:v + 1], scalar2=modT[:, SH1 + kc, v:v + 1],
                        op0=ALU.mult, op1=ALU.add), reads=[kx, k_mod], writes=[(kh, kc)])
                for g, (c0, M, dst, r0) in enumerate(FM_GROUPS):
                    ps, kp = bank(0, 4)
                    for kc in range(NKC):
                        s.op('pe', lambda e, ps=ps, kc=kc, c0=c0, M=M, ha=ha: e.matmul(
                            ps[0:M, :], lhsT=w1t[:, kc, c0:c0 + M], rhs=ha[:, kc, :], start=(kc == 0), stop=(kc == NKC - 1)),
                            reads=[kw1, (kh, kc)], writes=[kp])
                    fo, kfo = fo_t[nfo % 4]
                    nfo += 1
                    s.op('act', lambda e, ps=ps, M=M, g=g, fo=fo: e.activation(
                        out=fo[0:M, :], in_=ps[0:M, :], func=AF.Identity, bias=bfm[0:M, g:g + 1], scale=1.0),
                        reads=[kp, kbfm], writes=[kfo])
                    s.dma('pool', SCR[dst][r0:r0 + M, t0:t0 + 512], fo[0:M, :], reads=[kfo], writes=[(dst, r0, i)], sk=kfo)
                ta, kt = to_t[0]
                for b in range(4):
                    for (c0, N) in TM_GROUPS:
                        ps, kp = bank(4, 8)
                        for kc in range(NKC):
                            s.op('pe', lambda e, ps=ps, kc=kc, c0=c0, N=N, b=b, ha=ha: e.matmul(
                                ps[:, 0:N], lhsT=ha[:, kc, b * 128:(b + 1) * 128], rhs=w1t[:, kc, NFM + c0:NFM + c0 + N],
                                start=(kc == 0), stop=(kc == NKC - 1)), reads=[kw1, (kh, kc)], writes=[kp])
                        s.op('dve', lambda e, ps=ps, c0=c0, N=N, b=b, ta=ta: e.tensor_tensor(
                            out=ta[:, b, c0:c0 + N], in0=ps[:, 0:N], in1=btm[:, c0:c0 + N], op=ALU.add),
                            reads=[kp, kbtm], writes=[(kt, b, c0)])
                s.dma('pool', TMV[t0:t0 + 512, :].rearrange("(b p) c -> p b c", p=128), ta,
                      reads=[(kt, b, c0) for b in range(4) for (c0, N) in TM_GROUPS], writes=[('TMV', i)], sk=kt)
            s.barrier()
            mem.release(m0)

        def ln_apply(ya, ky, xo, ko, l, gi, bi, tmp):
            (sq, ksq), (mean, kmean), (rstd, krstd), (msq, kmsq), (ybf, kybf) = tmp
            p1, kp1 = bank(4, 6)
            p2, kp2 = bank(6, 8)
            for kc in range(NKC):
                s.op('act', lambda e, kc=kc: e.copy(out=ybf[:, kc % 2, :], in_=ya[:, kc, :]), reads=[(ky, kc)], writes=[(kybf, kc % 2)])
                s.op('pe', lambda e, kc=kc: e.matmul(p1[:, :], lhsT=onesb, rhs=ybf[:, kc % 2, :], start=(kc == 0), stop=(kc == NKC - 1)),
                     reads=[k_onesb, (kybf, kc % 2)], writes=[kp1])
                s.op('act', lambda e, kc=kc: e.activation(out=sq[:, kc % 2, :], in_=ya[:, kc, :], func=AF.Square),
                     reads=[(ky, kc)], writes=[(ksq, kc % 2)])
                s.op('pe', lambda e, kc=kc: e.matmul(p2[:, :], lhsT=onesb, rhs=sq[:, kc % 2, :], start=(kc == 0), stop=(kc == NKC - 1)),
                     reads=[k_onesb, (ksq, kc % 2)], writes=[kp2])
            s.op('dve', lambda e: e.tensor_scalar_mul(out=mean, in0=p1[:, :], scalar1=1.0 / D), reads=[kp1], writes=[kmean])
            s.op('dve', lambda e: e.tensor_tensor(out=msq, in0=mean, in1=mean, op=ALU.mult), reads=[kmean], writes=[kmsq])
            s.op('dve', lambda e: e.scalar_tensor_tensor(out=msq, in0=p2[:, :], scalar=1.0 / D, in1=msq, op0=ALU.mult, op1=ALU.subtract),
                 reads=[kp2, kmsq], writes=[kmsq])
            s.op('act', lambda e: e.activation(out=rstd, in_=msq, func=AF.Sqrt, bias=epsln[:, 0:1], scale=1.0), reads=[kmsq, k_eps], writes=[krstd])
            s.op('dve', lambda e: e.reciprocal(out=rstd, in_=rstd), reads=[krstd], writes=[krstd])
            for kc in range(NKC):
                s.op('dve', lambda e, kc=kc: e.tensor_tensor(out=ya[:, kc, :], in0=ya[:, kc, :], in1=mean, op=ALU.subtract),
                     reads=[(ky, kc), kmean], writes=[(ky, kc)])
                s.op('pool', lambda e, kc=kc: e.tensor_tensor(out=ya[:, kc, :], in0=ya[:, kc, :], in1=rstd, op=ALU.mult),
                     reads=[(ky, kc), krstd], writes=[(ky, kc)])
                s.op('dve', lambda e, kc=kc: e.tensor_scalar(out=xo[:, kc, :], in0=ya[:, kc, :], scalar1=lnt[:, gi, kc:kc + 1],
                                                             scalar2=lnt[:, bi, kc:kc + 1], op0=ALU.mult, op1=ALU.add),
                     reads=[(ky, kc), k_lnt], writes=[(ko, kc)])

        def phase_p3(l):
            m0 = mem.mark()
            wo, kwo = mem.alloc([128, NKC, D], BF16, 'wo')
            stage = [mem.alloc([128, 2048], F32, 'wst') for _ in range(3)]
            xT_t = [mem.alloc([128, NKC, 512], F32, 'xT3') for _ in range(2)]
            mx_t = [mem.alloc([128, NKC, 512], BF16, 'mx') for _ in range(2)]
            ya_t = [mem.alloc([128, NKC, 512], F32, 'ya') for _ in range(2)]
            tmp = [mem.alloc([128, 2, 512], BF16, 'sq'), mem.alloc([128, 512], F32, 'mean'), mem.alloc([128, 512], F32, 'rstd'),
                   mem.alloc([128, 512], F32, 'msq'), mem.alloc([128, 2, 512], BF16, 'ybf')]
            load_w_bf16(w_out[l], wo, kwo, NKC, D, stage)
            for i in range(NTILE):
                t0 = i * 512
                v = tile_v(i)
                xa, kx = xT_t[i % 2]
                ma, kmx = mx_t[i % 2]
                ya, ky = ya_t[i % 2]
                s.dma('sp', xa, fm_view(XT, t0, 512), reads=[('XT', i)], writes=[(kx, kc) for kc in range(NKC)], sk=kx)
                s.dma('sp', ma, fm_view(MIXT, t0, 512), reads=[('MIXT', i)], writes=[kmx], sk=kmx)
                for oc in range(NKC):
                    ps, kp = bank(0, 4)
                    for kc in range(NKC):
                        s.op('pe', lambda e, ps=ps, kc=kc, oc=oc, ma=ma: e.matmul(
                            ps[:, :], lhsT=wo[:, kc, oc * 128:(oc + 1) * 128], rhs=ma[:, kc, :], start=(kc == 0), stop=(kc == NKC - 1)),
                            reads=[kwo, kmx], writes=[kp])
                    s.op('dve', lambda e, ps=ps, oc=oc, xa=xa, v=v, ya=ya: e.scalar_tensor_tensor(
                        out=ya[:, oc, :], in0=ps[:, :], scalar=modT[:, G1 + oc, v:v + 1], in1=xa[:, oc, :], op0=ALU.mult, op1=ALU.add),
                        reads=[kp, (kx, oc), k_mod], writes=[(ky, oc)])
                ln_apply(ya, ky, xa, kx, l, 0, 1, tmp)
                s.dma('pool', fm_view(XT, t0, 512), xa, reads=[(kx, kc) for kc in range(NKC)], writes=[('XT', i)], sk=kx)
            s.barrier()
            mem.release(m0)

        def phase_p3b(l):
            m0 = mem.mark()
            wf, kwf = mem.alloc([128, NKC, 2 * FF], BF16, 'wf')
            stage = [mem.alloc([128, 1024], F32, 'wst') for _ in range(3)]
            xT_t = [mem.alloc([128, NKC, 512], F32, 'xT3b') for _ in range(1)]
            h2, kh2 = mem.alloc([128, NKC, 512], BF16, 'h2')
            sg_t = [mem.alloc([128, 512], F32, 'sg') for _ in range(2)]
            u_t = [mem.alloc([128, NFC, 512], BF16, 'u') for _ in range(2)]
            load_w_bf16(w_ffn_in[l], wf, kwf, NKC, 2 * FF, stage)
            for i in range(NTILE):
                t0 = i * 512
                v = tile_v(i)
                xa, kx = xT_t[0]
                s.dma('sp', xa, fm_view(XT, t0, 512), reads=[('XT', i)], writes=[kx], sk=kx)
                for kc in range(NKC):
                    s.op('dve', lambda e, kc=kc, xa=xa, v=v: e.tensor_scalar(
                        out=h2[:, kc, :], in0=xa[:, kc, :], scalar1=modT[:, SC2 + kc, v:v + 1], scalar2=modT[:, SH2 + kc, v:v + 1],
                        op0=ALU.mult, op1=ALU.add), reads=[kx, k_mod], writes=[(kh2, kc)])
                ua, ku = u_t[i % 2]
                for f in range(NFC):
                    pg, kpg = bank(0, 4)
                    pu, kpu = bank(4, 8)
                    for kc in range(NKC):
                        s.op('pe', lambda e, pg=pg, kc=kc, f=f: e.matmul(
                            pg[:, :], lhsT=wf[:, kc, f * 128:(f + 1) * 128], rhs=h2[:, kc, :], start=(kc == 0), stop=(kc == NKC - 1)),
                            reads=[kwf, (kh2, kc)], writes=[kpg])
                    for kc in range(NKC):
                        s.op('pe', lambda e, pu=pu, kc=kc, f=f: e.matmul(
                            pu[:, :], lhsT=wf[:, kc, FF + f * 128:FF + (f + 1) * 128], rhs=h2[:, kc, :], start=(kc == 0), stop=(kc == NKC - 1)),
                            reads=[kwf, (kh2, kc)], writes=[kpu])
                    sg, ksg = sg_t[f % 2]
                    s.op('act', lambda e, pg=pg, sg=sg: e.activation(out=sg, in_=pg[:, :], func=AF.Silu), reads=[kpg], writes=[ksg])
                    s.op('dve', lambda e, pu=pu, sg=sg, f=f, ua=ua: e.tensor_tensor(out=ua[:, f, :], in0=pu[:, :], in1=sg, op=ALU.mult),
                         reads=[kpu, ksg], writes=[(ku, f)])
                s.dma('pool', fm_view(UT, t0, 512), ua, reads=[(ku, f) for f in range(NFC)], writes=[('UT', i)], sk=ku)
            s.barrier()
            mem.release(m0)

        def phase_p4(l):
            m0 = mem.mark()
            w2, kw2 = mem.alloc([128, NFC, D], BF16, 'w2')
            stage = [mem.alloc([128, 1024], F32, 'wst') for _ in range(3)]
            xT_t = [mem.alloc([128, NKC, 512], F32, 'xT4') for _ in range(2)]
            u_t = [mem.alloc([128, NFC, 512], BF16, 'u4') for _ in range(2)]
            ya_t = [mem.alloc([128, NKC, 512], F32, 'ya4') for _ in range(2)]
            tmp = [mem.alloc([128, 2, 512], BF16, 'sq'), mem.alloc([128, 512], F32, 'mean'), mem.alloc([128, 512], F32, 'rstd'),
                   mem.alloc([128, 512], F32, 'msq'), mem.alloc([128, 2, 512], BF16, 'ybf')]
            load_w_bf16(w_ffn_out[l], w2, kw2, NFC, D, stage)
            for i in range(NTILE):
                t0 = i * 512
                v = tile_v(i)
                xa, kx = xT_t[i % 2]
                ua, ku = u_t[i % 2]
                ya, ky = ya_t[i % 2]
                s.dma('sp', xa, fm_view(XT, t0, 512), reads=[('XT', i)], writes=[(kx, kc) for kc in range(NKC)], sk=kx)
                s.dma('sp', ua, fm_view(UT, t0, 512), reads=[('UT', i)], writes=[ku], sk=ku)
                for oc in range(NKC):
                    ps, kp = bank(0, 4)
                    for f in range(NFC):
                        s.op('pe', lambda e, ps=ps, f=f, oc=oc, ua=ua: e.matmul(
                            ps[:, :], lhsT=w2[:, f, oc * 128:(oc + 1) * 128], rhs=ua[:, f, :], start=(f == 0), stop=(f == NFC - 1)),
                            reads=[kw2, ku], writes=[kp])
                    s.op('dve', lambda e, ps=ps, oc=oc, xa=xa, v=v, ya=ya: e.scalar_tensor_tensor(
                        out=ya[:, oc, :], in0=ps[:, :], scalar=modT[:, G2 + oc, v:v + 1], in1=xa[:, oc, :], op0=ALU.mult, op1=ALU.add),
                        reads=[kp, (kx, oc), k_mod], writes=[(ky, oc)])
                ln_apply(ya, ky, xa, kx, l, 2, 3, tmp)
                s.dma('pool', fm_view(XT, t0, 512), xa, reads=[(kx, kc) for kc in range(NKC)], writes=[('XT', i)], sk=kx)
            s.barrier()
            mem.release(m0)

        SEQS = ([dict(t0=0, T=TS, sample=True, j=-1)] if NST else []) + \
               [dict(t0=TS + 256 * j, T=256, sample=False, j=j) for j in range(NPR)]

        def small_w(dram2d, rows, cols, dt, name, stage):
            dst, kd = mem.alloc([rows, cols], dt, name)
            sa, sk_ = stage
            s.dma('sp', sa[0:rows, 0:cols], dram2d, writes=[sk_], sk=sk_)
            s.op('pool', lambda e: e.tensor_copy(out=dst, in_=sa[0:rows, 0:cols]), reads=[sk_], writes=[kd])
            return dst, kd

        def evac(i, out, in_, reads, writes):
            if i % 2 == 0:
                s.op('dve', lambda e: e.tensor_copy(out=out, in_=in_), reads=reads, writes=writes)
            else:
                s.op('act', lambda e: e.copy(out=out, in_=in_), reads=reads, writes=writes)

        def phase_mla(l):
            m0 = mem.mark()
            stage = mem.alloc([128, 1024], F32, 'wst')
            wq0, kwq0 = small_w(wq_d[l, 0:128, :], 128, 768, BF16, 'wq0', stage)
            wq1, kwq1 = small_w(wq_d[l, 128:256, :], 128, 768, BF16, 'wq1', stage)
            wp0, kwp0 = small_w(wqp_d[l, 0:128, :], 128, 768, BF16, 'wp0', stage)
            wp1, kwp1 = small_w(wqp_d[l, 128:256, :], 128, 768, BF16, 'wp1', stage)
            wqs, kwqs, wps, kwps = (wq0, wq1), (kwq0, kwq1), (wp0, wp1), (kwp0, kwp1)
            wk, kwk = small_w(wk_d[l], 128, 768, BF16, 'wk', stage)
            wv, kwv = small_w(wv_d[l], 128, 512, BF16, 'wv', stage)
            e96, ke96 = small_w(e96_d, 32, 96, BF16, 'e96', stage)
            qkn, kqkn = mem.alloc([128, 3], F32, 'qkn')
            s.dma('sp', qkn, qkn_d[l], writes=[kqkn], sk=kqkn)
            KMAX = 4608 if NST else 256
            KT, kKT = mem.alloc([96, 8, KMAX], BF16, 'KT')
            VA, kVA = mem.alloc([128, KMAX // 128, 8, 65], BF16, 'VA')
            s.op('pool', lambda e: e.memset(VA[:, :, :, 64:65], 1.0), writes=[(kVA, 'ones')])
            ckv, kckv = mem.alloc([128, 512], F32, 'ckv')
            sq, ksq = mem.alloc([128, 2, 512], F32, 'sqm')
            rstd, krstd = mem.alloc([128, 512], F32, 'rstdm')
            ckvn, kckvn = mem.alloc([128, 512], F32, 'ckvn')
            ckvb, kckvb = mem.alloc([128, 512], BF16, 'ckvb')
            kpe, kkpe = mem.alloc([32, 512], F32, 'kpe')
            kpp, kkpp = mem.alloc([32, 512], F32, 'kpp')
            kcs, kkcs = mem.alloc([32, 2, 512], F32, 'kcs')
            krb, kkrb = mem.alloc([32, 512], BF16, 'krb')
            ctm, kctm = mem.alloc([128, 4, 128], F32, 'ctm')
            ktm, kktm = mem.alloc([128, 4, 32], F32, 'ktm')
            cq, kcq = mem.alloc([128, 2, 512], F32, 'cq')
            cqn, kcqn = mem.alloc([128, 2, 512], BF16, 'cqn')
            cs96, kcs96 = mem.alloc([96, 2, 512], F32, 'cs96')
            crsr, kcrsr = mem.alloc([96, 2, 512], F32, 'crsr')
            t12, kt12 = mem.alloc([96, 2, 512], F32, 't12')
            qT_t = [mem.alloc([96, 512], BF16, 'qT') for _ in range(2)]
            PT_t = [mem.alloc([128, 512], BF16, 'PT') for _ in range(4)]
            rec, krec = mem.alloc([128, 4], F32, 'rec')
            att, katt = mem.alloc([128, 4, 512], BF16, 'att')
            mixo_t = [mem.alloc([128, 4, 512], BF16, 'mixo') for _ in range(2)]
            nev = [0]

            def rms_rstd(src2d_list, keys, n, div, extra_scale):
                ps, kp = bank(7, 8)
                for c, (a, ka) in enumerate(zip(src2d_list, keys)):
                    s.op('act', lambda e, a=a, c=c: e.activation(out=sq[:, c, 0:n], in_=a, func=AF.Square), reads=[ka], writes=[(ksq, c)])
                    s.op('pe', lambda e, c=c: e.matmul(ps[:, 0:n], lhsT=onesf, rhs=sq[:, c, 0:n], start=(c == 0), stop=(c == len(src2d_list) - 1)),
                         reads=[k_onesf, (ksq, c)], writes=[kp])
                s.op('act', lambda e: e.activation(out=rstd[:, 0:n], in_=ps[:, 0:n], func=AF.Sqrt, bias=epsln[:, 1:2], scale=1.0 / div),
                     reads=[kp, k_eps], writes=[krstd])
                s.op('dve', lambda e: e.reciprocal(out=rstd[:, 0:n], in_=rstd[:, 0:n]), reads=[krstd], writes=[krstd])
                if extra_scale != 1.0:
                    s.op('dve', lambda e: e.tensor_scalar_mul(out=rstd[:, 0:n], in0=rstd[:, 0:n], scalar1=extra_scale), reads=[krstd], writes=[krstd])

            def make_kv(k0, n):
                for h in range(8):
                    ps, kp = bank(0, 4)
                    s.op('pe', lambda e, ps=ps, h=h: e.matmul(ps[0:96, 0:n], lhsT=wk[:, h * 96:(h + 1) * 96], rhs=ckvb[:, 0:n], start=True, stop=False),
                         reads=[kwk, kckvb], writes=[kp])
                    s.op('pe', lambda e, ps=ps: e.matmul(ps[0:96, 0:n], lhsT=e96, rhs=krb[:, 0:n], start=False, stop=True),
                         reads=[ke96, kkrb], writes=[kp])
                    nev[0] += 1
                    evac(nev[0], KT[:, h, k0:k0 + n], ps[0:96, 0:n], [kp], [(kKT, h, k0)])
                for b in range(n // 128):
                    ps, kp = bank(0, 4)
                    s.op('pe', lambda e, ps=ps, b=b: e.matmul(ps[:, :], lhsT=ckvb[:, b * 128:(b + 1) * 128], rhs=wv, start=True, stop=True),
                         reads=[kwv, kckvb], writes=[kp])
                    kb = k0 // 128 + b
                    nev[0] += 1
                    evac(nev[0], VA[:, kb, :, 0:64], ps[:, :].rearrange("p (h e) -> p h e", e=64), [kp], [(kVA, kb)])

            for sq_ in SEQS:
                T, S0, samp, j = sq_['T'], sq_['t0'], sq_['sample'], sq_['j']
                nkeys = T + (512 if samp else 0)
                nkt = nkeys // 128
                for k0 in range(0, T, 512):
                    n = min(512, T - k0)
                    t0 = S0 + k0
                    s.dma('sp', ckv[:, 0:n], SCR['CKVT'][:, t0:t0 + n], reads=[('CKVT', 0, t0 // 512)], writes=[kckv], sk=kckv)
                    s.dma('sp', kpe[:, 0:n], SCR['KPET'][:, t0:t0 + n], reads=[('KPET', 0, t0 // 512)], writes=[kkpe], sk=kkpe)
                    rms_rstd([ckv[:, 0:n]], [kckv], n, 128.0, 1.0)
                    s.op('dve', lambda e, n=n: e.scalar_tensor_tensor(out=ckvn[:, 0:n], in0=ckv[:, 0:n], scalar=qkn[:, 2:3], in1=rstd[:, 0:n],
                                                                     op0=ALU.mult, op1=ALU.mult), reads=[kckv, kqkn, krstd], writes=[kckvn])
                    s.op('act', lambda e, n=n: e.copy(out=ckvb[:, 0:n], in_=ckvn[:, 0:n]), reads=[kckvn], writes=[kckvb])
                    if samp:
                        s.dma('sp', kpp[:, 0:n], SCR['KPEP'][:, t0:t0 + n], reads=[('KPEP', 0, t0 // 512)], writes=[kkpp], sk=kkpp)
                        s.dma_group('sp', [(kcs[:, 0, 0:n], kcos_d[:, k0:k0 + n]), (kcs[:, 1, 0:n], ksin_d[:, k0:k0 + n])], writes=[kkcs], sk=kkcs)
                        s.op('dve', lambda e, n=n: e.tensor_tensor(out=kpe[:, 0:n], in0=kpe[:, 0:n], in1=kcs[:, 0, 0:n], op=ALU.mult),
                             reads=[kkpe, kkcs], writes=[kkpe])
                        s.op('dve', lambda e, n=n: e.tensor_tensor(out=kpp[:, 0:n], in0=kpp[:, 0:n], in1=kcs[:, 1, 0:n], op=ALU.mult),
                             reads=[kkpp, kkcs], writes=[kkpp])
                        s.op('dve', lambda e, n=n: e.tensor_tensor(out=krb[:, 0:n], in0=kpe[:, 0:n], in1=kpp[:, 0:n], op=ALU.add),
                             reads=[kkpe, kkpp], writes=[kkrb])
                    else:
                        s.op('dve', lambda e, n=n: e.tensor_copy(out=krb[:, 0:n], in_=kpe[:, 0:n]), reads=[kkpe], writes=[kkrb])
                        for b in range(n // 128):
                            ps, kp = bank(4, 7)
                            s.op('pe', lambda e, ps=ps, b=b: e.transpose(out=ps[:, 0:128], in_=ckvn[:, b * 128:(b + 1) * 128], identity=idf),
                                 reads=[kckvn, k_idf], writes=[kp])
                            s.op('pe', lambda e, ps=ps, b=b: e.transpose(out=ps[:, 128:160], in_=kpe[0:32, b * 128:(b + 1) * 128], identity=idf[0:32, 0:32]),
                                 reads=[kkpe, k_idf], writes=[kp])
                            s.op('dve', lambda e, ps=ps, b=b: e.tensor_copy(out=ctm[:, b, :], in_=ps[:, 0:128]), reads=[kp], writes=[(kctm, b)])
                            s.op('act', lambda e, ps=ps, b=b: e.copy(out=ktm[:, b, :], in_=ps[:, 128:160]), reads=[kp], writes=[(kktm, b)])
                        nb = n // 128
                        s.dma('pool', ckv_out[j, l, k0:k0 + n, :].rearrange("(b p) c -> p b c", p=128), ctm[:, 0:nb, :],
                              reads=[(kctm, b) for b in range(nb)], sk=kctm)
                        s.dma('pool', kpe_out[j, l, k0:k0 + n, :].rearrange("(b p) c -> p b c", p=128), ktm[:, 0:nb, :],
                              reads=[(kktm, b) for b in range(nb)], sk=kktm)
                    make_kv(k0, n)
                if samp:
                    s.dma('sp', ctm, cckv_d[l].rearrange("(b p) c -> p b c", p=128), writes=[(kctm, b) for b in range(4)], sk=kctm)
                    s.dma('sp', ktm, ckpe_d[l].rearrange("(b p) c -> p b c", p=128), writes=[(kktm, b) for b in range(4)], sk=kktm)
                    ps, kp = bank(4, 7)
                    ps2, kp2 = bank(4, 7)
                    for b in range(4):
                        s.op('pe', lambda e, b=b, ps=ps: e.transpose(out=ps[:, b * 128:(b + 1) * 128], in_=ctm[:, b, :], identity=idf),
                             reads=[(kctm, b), k_idf], writes=[kp])
                        s.op('pe', lambda e, b=b, ps2=ps2: e.transpose(out=ps2[0:32, b * 128:(b + 1) * 128], in_=ktm[:, b, :], identity=idf),
                             reads=[(kktm, b), k_idf], writes=[kp2])
                    s.op('dve', lambda e, ps=ps: e.tensor_copy(out=ckvb, in_=ps[:, :]), reads=[kp], writes=[kckvb])
                    s.op('act', lambda e, ps2=ps2: e.copy(out=krb, in_=ps2[0:32, :]), reads=[kp2], writes=[kkrb])
                    make_kv(T, 512)
                for q0 in range(0, T, 512):
                    n = min(512, T - q0)
                    nqb = n // 128
                    t0 = S0 + q0
                    ti = t0 // 512
                    s.dma('sp', cq[:, :, 0:n], SCR['CQT'].rearrange("(c p) t -> p c t", p=128)[:, :, t0:t0 + n],
                          reads=[('CQT', 0, ti), ('CQT', 128, ti)], writes=[kcq], sk=kcq)
                    rms_rstd([cq[:, 0, 0:n], cq[:, 1, 0:n]], [kcq, kcq], n, 256.0, ATT_SCALE)
                    for kc in range(2):
                        s.op('dve', lambda e, kc=kc, n=n: e.tensor_scalar_mul(out=cqn[:, kc, 0:n], in0=cq[:, kc, 0:n], scalar1=qkn[:, kc:kc + 1]),
                             reads=[kcq, kqkn], writes=[(kcqn, kc)])
                    if samp:
                        s.dma_group('sp', [(cs96[:, 0, 0:n], cos96_d[:, q0:q0 + n]), (cs96[:, 1, 0:n], sin96_d[:, q0:q0 + n])], writes=[kcs96], sk=kcs96)
                        for c in range(2):
                            s.op('dve', lambda e, c=c, n=n: e.tensor_tensor(out=crsr[:, c, 0:n], in0=cs96[:, c, 0:n], in1=rstd[0:96, 0:n], op=ALU.mult),
                                 reads=[kcs96, krstd], writes=[(kcrsr, c)])
                    mo_, kmo = mixo_t[(t0 // 512) % 2]
                    def qproj(h):
                        qT, kqT = qT_t[h % 2]
                        psA, kpA = bank(0, 2)
                        for kc in range(2):
                            s.op('pe', lambda e, psA=psA, kc=kc, h=h, n=n: e.matmul(
                                psA[0:96, 0:n], lhsT=wqs[kc][:, h * 96:(h + 1) * 96], rhs=cqn[:, kc, 0:n], start=(kc == 0), stop=(kc == 1)),
                                reads=[kwqs[kc], (kcqn, kc)], writes=[kpA])
                        if samp:
                            psB, kpB = bank(0, 2)
                            for kc in range(2):
                                s.op('pe', lambda e, psB=psB, kc=kc, h=h, n=n: e.matmul(
                                    psB[0:96, 0:n], lhsT=wps[kc][:, h * 96:(h + 1) * 96], rhs=cqn[:, kc, 0:n], start=(kc == 0), stop=(kc == 1)),
                                    reads=[kwps[kc], (kcqn, kc)], writes=[kpB])
                            s.op('dve', lambda e, psA=psA, n=n: e.tensor_tensor(out=t12[:, 0, 0:n], in0=psA[0:96, 0:n], in1=crsr[:, 0, 0:n], op=ALU.mult),
                                 reads=[kpA, (kcrsr, 0)], writes=[(kt12, 0)])
                            s.op('dve', lambda e, psB=psB, n=n: e.tensor_tensor(out=t12[:, 1, 0:n], in0=psB[0:96, 0:n], in1=crsr[:, 1, 0:n], op=ALU.mult),
                                 reads=[kpB, (kcrsr, 1)], writes=[(kt12, 1)])
                            s.op('pool', lambda e, qT=qT, n=n: e.tensor_tensor(out=qT[:, 0:n], in0=t12[:, 0, 0:n], in1=t12[:, 1, 0:n], op=ALU.add),
                                 reads=[(kt12, 0), (kt12, 1)], writes=[kqT])
                        else:
                            s.op('dve', lambda e, psA=psA, qT=qT, n=n: e.tensor_tensor(out=qT[:, 0:n], in0=psA[0:96, 0:n], in1=rstd[0:96, 0:n], op=ALU.mult),
                                 reads=[kpA, krstd], writes=[kqT])

                    def attend(h):
                        qT, kqT = qT_t[h % 2]
                        psO, kpO = bank(5, 7)
                        LA = 2
                        pend = {}
                        for it in range(nkt + LA):
                            if it < nkt:
                                kt = it
                                psS, kpS = bank(2, 5)
                                kvk = [(kKT, h, (kt * 128) // 512 * 512 if kt * 128 < T else T)]
                                s.op('pe', lambda e, psS=psS, kt=kt, h=h, qT=qT, n=n: e.matmul(
                                    psS[:, 0:n], lhsT=KT[:, h, kt * 128:(kt + 1) * 128], rhs=qT[:, 0:n], start=True, stop=True),
                                    reads=kvk + [kqT], writes=[kpS])
                                PT, kPT = PT_t[kt % len(PT_t)]
                                s.op('act', lambda e, psS=psS, PT=PT, n=n: e.activation(out=PT[:, 0:n], in_=psS[:, 0:n], func=AF.Exp),
                                     reads=[kpS], writes=[kPT])
                                pend[kt] = (PT, kPT)
                            if it >= LA:
                                kt = it - LA
                                PT, kPT = pend.pop(kt)
                                for qb in range(nqb):
                                    s.op('pe', lambda e, psO=psO, PT=PT, qb=qb, kt=kt, h=h: e.matmul(
                                        psO[:, qb * 65:(qb + 1) * 65], lhsT=PT[:, qb * 128:(qb + 1) * 128], rhs=VA[:, kt, h, :],
                                        start=(kt == 0 and qb == 0), stop=(kt == nkt - 1), skip_group_check=True),
                                        reads=[kPT, (kVA, kt), (kVA, 'ones')], writes=[kpO])
                        pO3 = psO[:, 0:nqb * 65].rearrange("p (q e) -> p q e", e=65)
                        s.op('dve', lambda e, pO3=pO3, nqb=nqb: e.reciprocal(out=rec[:, 0:nqb], in_=pO3[:, :, 64]), reads=[kpO], writes=[krec])
                        s.op('dve', lambda e, pO3=pO3, nqb=nqb, h=h: e.tensor_tensor(
                            out=att[:, 0:nqb, h * 64:(h + 1) * 64], in0=pO3[:, :, 0:64], in1=rec[:, 0:nqb].unsqueeze(2).to_broadcast([128, nqb, 64]),
                            op=ALU.mult), reads=[kpO, krec], writes=[(katt, h)])

                    qproj(0)
                    for h in range(8):
                        if h + 1 < 8:
                            qproj(h + 1)
                        attend(h)
                    for qb in range(nqb):
                        ps, kp = bank(7, 8)
                        pbf = ps.bitcast(BF16)
                        for c in range(4):
                            s.op('pe', lambda e, pbf=pbf, c=c, qb=qb: e.transpose(out=pbf[:, c * 128:(c + 1) * 128], in_=att[:, qb, c * 128:(c + 1) * 128], identity=idb),
                                 reads=[(katt, 2 * c), (katt, 2 * c + 1), k_idb], writes=[kp])
                        nev[0] += 1
                        evac(nev[0], mo_[:, :, qb * 128:(qb + 1) * 128], pbf[:, 0:512].rearrange("p (c t) -> p c t", t=128), [kp], [(kmo, qb)])
                    s.dma('pool', MIXT.rearrange("(c p) t -> p c t", p=128)[:, 0:4, t0:t0 + n], mo_[:, :, 0:n],
                          reads=[(kmo, qb) for qb in range(nqb)], writes=[('MIXT_A', t0)], sk=kmo)
            s.barrier()
            mem.release(m0)

        def phase_mlstm(l):
            m0_ = mem.mark()
            mc, kmc = mem.alloc([64, 4, 64], F32, 'mconst')
            ng, kng = mem.alloc([64, 256], F32, 'mlng')
            s.dma('sp', mc, mconst_d, writes=[kmc], sk=kmc)
            s.dma('sp', ng, mlng_d[l], writes=[kng], sk=kng)
            rmask, krm = mem.alloc([64, 8, 64], F32, 'rmask')
            s.op('pool', lambda e: e.memset(rmask, 1.0), writes=[krm])
            s.op('pool', lambda e: e.memset(rmask[:, :, 0:1], 0.0), writes=[krm])
            A = lambda shape, dt=F32, name='m': mem.alloc(shape, dt, name)
            GI, kGI = A([64, 8, 64]); GF, kGF = A([64, 8, 64]); SP, kSP = A([64, 8, 64]); CS, kCS = A([64, 8, 64])
            NB, kNB = A([64, 8, 64]); Cc, kCc = A([64, 8, 64]); W, kW = A([64, 8, 64]); THR, kTHR = A([64, 8, 64])
            TOT, kTOT = A([64, 8]); CMAX, kCMAX = A([64, 8]); Gn, kGn = A([64, 8])
            ROW, kROW = A([4, 4, 64]); m0t, km0t = A([4, 2]); MN, kMN = A([4, 2, 64]); Rr, kRr = A([4, 2, 64])
            MP, kMP = A([4, 2, 64]); SCr, kSCr = A([4, 2, 64])
            TMPc, kTMPc = A([64, 16]); Rc, kRc = A([64, 8]); SCc, kSCc = A([64, 8])
            Wt, kWt = A([64, 8, 64]); THRt, kTHRt = A([64, 8, 64]); BD, kBD = A([64, 64, 8]); scB, kscB = A([64, 64, 8])
            Caug, kC = A([64, 4, 65]); Cs, kCs = A([64, 4, 65]); Csb, kCsb = A([64, 4, 65], BF16)
            q32_t = [A([64, 4, 512]) for _ in range(1)]
            k32_t = [A([64, 4, 512]) for _ in range(1)]
            qb_t = [A([64, 4, 512], BF16) for _ in range(2)]
            kb_t = [A([64, 4, 512], BF16) for _ in range(2)]
            vv_t = [A([64, 8, 256]) for _ in range(1)]
            kk_t = [A([64, 8, 256]) for _ in range(1)]
            kkb_t = [A([64, 8, 256], BF16) for _ in range(2)]
            va_t = [A([64, 8, 4, 65], BF16) for _ in range(2)]
            for va, kva in va_t:
                s.op('pool', lambda e, va=va: e.memset(va[:, :, :, 64:65], 1.0), writes=[(kva, 'ones')])
            vw_t = [A([64, 8, 4, 65], BF16) for _ in range(2)]
            PTg_t = [A([64, 8, 4, 64], BF16) for _ in range(2)]
            dC_t = [A([64, 8, 4, 65]) for _ in range(2)]
            hF_t = [A([64, 8, 256]) for _ in range(2)]
            mo_t = [A([64, 8, 256]) for _ in range(2)]
            hb_t = [A([64, 8, 256]) for _ in range(2)]
            den, kden = A([64, 2, 4]); rec, krec = A([64, 2, 4])
            s1, ks1 = A([64, 32]); s2, ks2 = A([64, 32]); s3, ks3 = A([64, 32])
            sqx, ksqx = q32_t[0]
            sqx = sqx.rearrange("p h (a b) -> p (h a) b", b=256)
            sig, ksig = k32_t[0]
            sig = sig.rearrange("p h (a b) -> p (h a) b", b=256)
            xo_t = [A([64, 8, 256], BF16) for _ in range(1)]
            mixo_t = [A([128, 2, 512], BF16) for _ in range(1)]
            maskF, maskB, J64, J4 = mc[:, 0, :], mc[:, 1, :], mc[:, 2, :], mc[0:4, 3, 0:4]
            gcount = [0]
            for sq_ in SEQS:
                T, S0, samp, j = sq_['T'], sq_['t0'], sq_['sample'], sq_['j']
                nch = T // 64
                Jn = J64 if nch == 64 else J4
                In = idf[0:nch, 0:nch]
                def gview(name):
                    return SCR[name][:, S0:S0 + T].rearrange("h (j t) -> j h t", t=64)
                rk = lambda nm: [(nm, 0, i) for i in range(S0 // 512, (S0 + T + 511) // 512)]
                s.dma_group('sp', [(GI[0:nch, 0:4, :], gview('MIF')), (GI[0:nch, 4:8, :], gview('MIB'))], reads=rk('MIF') + rk('MIB'), writes=[kGI], sk=kGI)
                s.dma_group('sp', [(GF[0:nch, 0:4, :], gview('MFF')), (GF[0:nch, 4:8, :], gview('MFB'))], reads=rk('MFF') + rk('MFB'), writes=[kGF], sk=kGF)
                s.op('act', lambda e, nch=nch: e.activation(out=SP[0:nch], in_=GF[0:nch], func=AF.Exp, scale=-1.0), reads=[kGF], writes=[kSP])
                s.op('act', lambda e, nch=nch: e.activation(out=SP[0:nch], in_=SP[0:nch], func=AF.Ln, bias=1.0), reads=[kSP], writes=[kSP])
                fl = lambda a, nch=nch: a[0:nch].rearrange("p a b -> p (a b)")
                s.op('dve', lambda e, nch=nch, fl=fl: e.tensor_tensor_scan(out=fl(CS), data0=fl(rmask), data1=fl(SP), initial=0.0, op0=ALU.mult, op1=ALU.add),
                     reads=[krm, kSP], writes=[kCS])
                s.op('dve', lambda e, nch=nch: e.tensor_copy(out=TOT[0:nch], in_=CS[0:nch, :, 63]), reads=[kCS], writes=[kTOT])
                s.op('dve', lambda e, nch=nch: e.tensor_copy(out=NB[0:nch, 0:4, :], in_=CS[0:nch, 0:4, :]), reads=[kCS], writes=[kNB])
                s.op('dve', lambda e, nch=nch: e.tensor_tensor(out=NB[0:nch, 4:8, :], in0=SP[0:nch, 4:8, :], in1=CS[0:nch, 4:8, :], op=ALU.subtract),
                     reads=[kCS, kSP, kNB], writes=[kNB])
                s.op('dve', lambda e, nch=nch: e.tensor_tensor(out=NB[0:nch, 4:8, :], in0=NB[0:nch, 4:8, :],
                                                              in1=TOT[0:nch, 4:8].unsqueeze(2).to_broadcast([nch, 4, 64]), op=ALU.add),
                     reads=[kNB, kTOT], writes=[kNB])
                s.op('dve', lambda e, nch=nch: e.tensor_tensor(out=Cc[0:nch], in0=GI[0:nch], in1=NB[0:nch], op=ALU.add), reads=[kGI, kNB], writes=[kCc])
                s.op('dve', lambda e, nch=nch: e.tensor_reduce(out=CMAX[0:nch], in_=Cc[0:nch], axis=AX.X, op=ALU.max), reads=[kCc], writes=[kCMAX])
                s.op('dve', lambda e, nch=nch: e.tensor_scalar_mul(out=Gn[0:nch], in0=TOT[0:nch], scalar1=-1.0), reads=[kTOT], writes=[kGn])
                ps, kp = bank(0, 8)
                for qi, (src_, ksrc, lo, mat) in enumerate(((CMAX, kCMAX, 0, In), (Gn, kGn, 0, In), (CMAX, kCMAX, 4, Jn), (Gn, kGn, 4, Jn))):
                    s.op('pe', lambda e, ps=ps, qi=qi, src_=src_, lo=lo, mat=mat, nch=nch: e.matmul(
                        ps[0:4, qi * nch:(qi + 1) * nch], lhsT=src_[0:nch, lo:lo + 4], rhs=mat, start=True, stop=True),
                        reads=[ksrc, k_idf, kmc], writes=[kp])
                s.op('dve', lambda e, ps=ps, nch=nch: e.tensor_copy(out=ROW[:, :, 0:nch], in_=ps[0:4, 0:4 * nch].rearrange("p (a b) -> p a b", b=nch)),
                     reads=[kp], writes=[kROW])
                if samp:
                    s.dma('sp', m0t, stm_d[l].rearrange("d h -> h d"), writes=[km0t], sk=km0t, allow_slow_non_contiguous=True)
                else:
                    s.op('dve', lambda e: e.memset(m0t, 0.0), writes=[km0t])
                for d in range(2):
                    s.op('dve', lambda e, d=d, nch=nch: e.tensor_tensor_scan(out=MN[:, d, 0:nch], data0=ROW[:, 2 * d, 0:nch], data1=ROW[:, 2 * d + 1, 0:nch],
                                                                            initial=m0t[:, d:d + 1], op0=ALU.max, op1=ALU.add),
                         reads=[kROW, km0t], writes=[(kMN, d)])
                    s.op('dve', lambda e, d=d, nch=nch: e.tensor_tensor(out=Rr[:, d, 0:nch], in0=MN[:, d, 0:nch], in1=ROW[:, 2 * d + 1, 0:nch], op=ALU.subtract),
                         reads=[(kMN, d), kROW], writes=[(kRr, d)])
                    s.op('dve', lambda e, d=d: e.tensor_copy(out=MP[:, d, 0:1], in_=m0t[:, d:d + 1]), reads=[km0t], writes=[(kMP, d, 0)])
                    s.op('dve', lambda e, d=d, nch=nch: e.tensor_copy(out=MP[:, d, 1:nch], in_=MN[:, d, 0:nch - 1]), reads=[(kMN, d)], writes=[(kMP, d, 1)])
                    s.op('dve', lambda e, d=d, nch=nch: e.tensor_tensor(out=SCr[:, d, 0:nch], in0=MP[:, d, 0:nch], in1=Rr[:, d, 0:nch], op=ALU.subtract),
                         reads=[(kMP, d, 0), (kMP, d, 1), (kRr, d)], writes=[(kSCr, d)])
                    s.op('act', lambda e, d=d, nch=nch: e.activation(out=SCr[:, d, 0:nch], in_=SCr[:, d, 0:nch], func=AF.Exp), reads=[(kSCr, d)], writes=[(kSCr, d)])
                    if not samp:
                        s.dma('pool', m_out[j, l, d, :].rearrange("(h o) -> h o", o=1), MN[:, d, nch - 1:nch], reads=[(kMN, d)], sk=(kMN, d))
                ps, kp = bank(0, 8)
                for qi, (src_, ksrc, d) in enumerate(((Rr, kRr, 0), (Rr, kRr, 1), (SCr, kSCr, 0), (SCr, kSCr, 1))):
                    s.op('pe', lambda e, ps=ps, qi=qi, src_=src_, d=d, nch=nch: e.matmul(
                        ps[0:nch, qi * 4:(qi + 1) * 4], lhsT=src_[:, d, 0:nch], rhs=idf[0:4, 0:4], start=True, stop=True),
                        reads=[(ksrc, d), k_idf], writes=[kp])
                s.op('dve', lambda e, ps=ps, nch=nch: e.tensor_copy(out=TMPc[0:nch], in_=ps[0:nch, 0:16]), reads=[kp], writes=[kTMPc])
                ps2, kp2 = bank(0, 8)
                for qi, lo in enumerate((4, 12)):
                    s.op('pe', lambda e, ps2=ps2, qi=qi, lo=lo, nch=nch, Jn=Jn: e.matmul(
                        ps2[0:nch, qi * 4:(qi + 1) * 4], lhsT=Jn, rhs=TMPc[0:nch, lo:lo + 4], start=True, stop=True),
                        reads=[kTMPc, kmc], writes=[kp2])
                s.op('dve', lambda e, nch=nch: e.tensor_copy(out=Rc[0:nch, 0:4], in_=TMPc[0:nch, 0:4]), reads=[kTMPc], writes=[(kRc, 0)])
                s.op('dve', lambda e, nch=nch, ps2=ps2: e.tensor_copy(out=Rc[0:nch, 4:8], in_=ps2[0:nch, 0:4]), reads=[kp2], writes=[(kRc, 1)])
                s.op('dve', lambda e, nch=nch: e.tensor_copy(out=SCc[0:nch, 0:4], in_=TMPc[0:nch, 8:12]), reads=[kTMPc], writes=[(kSCc, 0)])
                s.op('dve', lambda e, nch=nch, ps2=ps2: e.tensor_copy(out=SCc[0:nch, 4:8], in_=ps2[0:nch, 4:8]), reads=[kp2], writes=[(kSCc, 1)])
                rcb = lambda nch=nch: Rc[0:nch].unsqueeze(2).to_broadcast([nch, 8, 64])
                s.op('dve', lambda e, nch=nch, rcb=rcb: e.tensor_tensor(out=W[0:nch], in0=Cc[0:nch], in1=rcb(), op=ALU.subtract),
                     reads=[kCc, (kRc, 0), (kRc, 1)], writes=[kW])
                s.op('act', lambda e, nch=nch: e.activation(out=W[0:nch], in_=W[0:nch], func=AF.Exp), reads=[kW], writes=[kW])
                s.op('dve', lambda e, nch=nch, rcb=rcb: e.tensor_tensor(out=THR[0:nch], in0=NB[0:nch], in1=rcb(), op=ALU.subtract),
                     reads=[kNB, (kRc, 0), (kRc, 1)], writes=[kTHR])
                s.op('act', lambda e, nch=nch: e.activation(out=THR[0:nch], in_=THR[0:nch], func=AF.Exp), reads=[kTHR], writes=[kTHR])
                for src_, ksrc, dst, kdst in ((W, kW, Wt, kWt), (THR, kTHR, THRt, kTHRt)):
                    ps, kp = bank(0, 8)
                    for r in range(8):
                        s.op('pe', lambda e, ps=ps, r=r, src_=src_, nch=nch, In=In: e.transpose(
                            out=ps[0:64, r * nch:(r + 1) * nch], in_=src_[0:nch, r, :], identity=In), reads=[ksrc, k_idf], writes=[kp])
                    s.op('dve', lambda e, ps=ps, dst=dst, nch=nch: e.tensor_copy(out=dst[:, :, 0:nch], in_=ps[0:64, 0:8 * nch].rearrange("p (a b) -> p a b", b=nch)),
                         reads=[kp], writes=[kdst])
                s.op('dve', lambda e, nch=nch, In=In: e.tensor_tensor(
                    out=BD[0:nch, 0:nch, :], in0=In.unsqueeze(2).to_broadcast([nch, nch, 8]),
                    in1=SCc[0:nch].unsqueeze(1).to_broadcast([nch, nch, 8]), op=ALU.mult), reads=[k_idf, (kSCc, 0), (kSCc, 1)], writes=[kBD])
                ps, kp = bank(0, 8)
                s.op('pe', lambda e, ps=ps, nch=nch: e.matmul(ps[0:64, 0:nch * 8], lhsT=onesf[0:nch, 0:64],
                                                             rhs=BD[0:nch, 0:nch, :].rearrange("p a b -> p (a b)"), start=True, stop=True),
                     reads=[k_onesf, kBD], writes=[kp])
                s.op('dve', lambda e, ps=ps, nch=nch: e.tensor_copy(out=scB[:, 0:nch, :], in_=ps[0:64, 0:nch * 8].rearrange("p (a b) -> p a b", b=8)),
                     reads=[kp], writes=[kscB])
                G = min(8, nch)
                GT = 64 * G
                ngrp = nch // G
                for d in range(2):
                    if samp:
                        s.dma('sp', Caug[:, :, 0:64], stC_d[l, d].rearrange("h a b -> a h b"), writes=[kC], sk=kC)
                        s.dma('sp', Caug[:, :, 64], stn_d[l, d].rearrange("h a -> a h"), writes=[(kC, 'n')], sk=(kC, 'n'), allow_slow_non_contiguous=True)
                    else:
                        s.op('dve', lambda e: e.memset(Caug, 0.0), writes=[kC, (kC, 'n')])
                    mask = maskF if d == 0 else maskB
                    gorder = list(range(ngrp)) if d == 0 else list(range(ngrp - 1, -1, -1))
                    corder = list(range(G)) if d == 0 else list(range(G - 1, -1, -1))

                    def load_group(gi, d=d, G=G, GT=GT, S0=S0):
                        t0 = S0 + gi * GT
                        ti = t0 // 512
                        b = gcount[0] % 2
                        gcount[0] += 1
                        cx = dict(gi=gi, t0=t0, ti=ti, b=b)
                        q32, kq32 = q32_t[0]; k32, kk32 = k32_t[0]; vv, kvv = vv_t[0]; kk, kkk = kk_t[0]
                        qb, kqb = qb_t[b]; kb, kkb = kb_t[b]; va, kva = va_t[b]; kkb2, kkkb2 = kkb_t[b]
                        cx.update(qb=qb, kqb=kqb, kb=kb, kkb=kkb, va=va, kva=kva, kkb2=kkb2, kkkb2=kkkb2,
                                  vw=vw_t[b], PT=PTg_t[b], dC=dC_t[b], hb=hb_t[b])
                        s.dma('sp', q32[:, :, 0:GT], SCR['MQT'][:, t0:t0 + GT].rearrange("(h p) t -> p h t", p=64),
                              reads=[('MQT', 0, ti), ('MQT', 128, ti)], writes=[kq32], sk=kq32)
                        s.dma('sp', k32[:, :, 0:GT], SCR['MKT'][:, t0:t0 + GT].rearrange("(h p) t -> p h t", p=64),
                              reads=[('MKT', 0, ti), ('MKT', 128, ti)], writes=[kk32], sk=kk32)
                        s.dma('sp', vv[:, 0:G, :], TMV[t0:t0 + GT, 0:256].rearrange("(j t) c -> t j c", t=64), reads=[('TMV', ti)], writes=[kvv], sk=kvv)
                        s.dma('sp', kk[:, 0:G, :], TMV[t0:t0 + GT, 512:768].rearrange("(j t) c -> t j c", t=64), reads=[('TMV', ti)], writes=[kkk], sk=kkk)
                        s.op('act', lambda e: e.copy(out=qb[:, :, 0:GT], in_=q32[:, :, 0:GT]), reads=[kq32], writes=[kqb])
                        s.op('act', lambda e: e.mul(out=kb[:, :, 0:GT], in_=k32[:, :, 0:GT], mul=0.125), reads=[kk32], writes=[kkb])
                        s.op('pool', lambda e: e.tensor_copy(out=va[:, 0:G, :, 0:64], in_=vv[:, 0:G, :].rearrange("p g (h e) -> p g h e", e=64)),
                             reads=[kvv], writes=[kva])
                        s.op('act', lambda e: e.mul(out=kkb2[:, 0:G, :], in_=kk[:, 0:G, :], mul=0.125), reads=[kkk], writes=[kkkb2])
                        if d == 1:
                            hF, khF = hF_t[b]; mo_, kmo = mo_t[b]
                            cx.update(hF=hF, khF=khF, mo=mo_, kmo=kmo)
                            s.dma('sp', hF[:, 0:G, :], HF[t0:t0 + GT, :].rearrange("(j t) c -> t j c", t=64), reads=[('HF', t0)], writes=[khF], sk=khF)
                            s.dma('sp', mo_[:, 0:G, :], TMV[t0:t0 + GT, 256:512].rearrange("(j t) c -> t j c", t=64), reads=[('TMV', ti)], writes=[kmo], sk=kmo)
                        return cx

                    def stage1(cx, jl, d=d, mask=mask, G=G, GT=GT):
                        jg = cx['gi'] * G + jl
                        cs_ = slice(jl * 64, (jl + 1) * 64)
                        qb, kb, va, kkb2 = cx['qb'], cx['kb'], cx['va'], cx['kkb2']
                        (vw, kvw), (PT, kPT), (dC, kdC) = cx['vw'], cx['PT'], cx['dC']
                        psA, kpA = bank(0, 2)
                        for h in range(4):
                            s.op('pe', lambda e, h=h: e.matmul(psA[0:64, h * 64:(h + 1) * 64], lhsT=kb[:, h, cs_], rhs=qb[:, h, cs_], start=True, stop=True),
                                 reads=[cx['kkb'], cx['kqb']], writes=[kpA])
                        s.op('dve', lambda e: e.tensor_tensor(out=PT[:, jl], in0=psA[0:64, 0:256].rearrange("p (h t) -> p h t", t=64),
                                                              in1=mask.unsqueeze(1).to_broadcast([64, 4, 64]), op=ALU.mult),
                             reads=[kpA, kmc], writes=[(kPT, jl)])
                        s.op('pool', lambda e: e.tensor_tensor(out=vw[:, jl], in0=va[:, jl], in1=Wt[:, d * 4:d * 4 + 4, jg].unsqueeze(2).to_broadcast([64, 4, 65]), op=ALU.mult),
                             reads=[cx['kva'], (cx['kva'], 'ones'), kWt], writes=[(kvw, jl)])
                        psC, kpC = bank(2, 4)
                        for h in range(4):
                            s.op('pe', lambda e, h=h: e.matmul(psC[0:64, h * 65:(h + 1) * 65], lhsT=kkb2[:, jl, h * 64:(h + 1) * 64], rhs=vw[:, jl, h, :], start=True, stop=True),
                                 reads=[cx['kkkb2'], (kvw, jl)], writes=[kpC])
                        s.op('act', lambda e: e.copy(out=dC[:, jl], in_=psC[0:64, 0:260].rearrange("p (h e) -> p h e", e=65)), reads=[kpC], writes=[(kdC, jl)])

                    def stage2(cx, jl, d=d, G=G, GT=GT):
                        jg = cx['gi'] * G + jl
                        cs_ = slice(jl * 64, (jl + 1) * 64)
                        qb = cx['qb']
                        (vw, kvw), (PT, kPT), (dC, kdC), (hb, khb) = cx['vw'], cx['PT'], cx['dC'], cx['hb']
                        s.op('dve', lambda e: e.tensor_tensor(out=Cs, in0=Caug, in1=scB[:, jg, d * 4:d * 4 + 4].unsqueeze(2).to_broadcast([64, 4, 65]), op=ALU.mult),
                             reads=[kC, (kC, 'n'), kscB], writes=[kCs])
                        s.op('act', lambda e: e.copy(out=Csb, in_=Cs), reads=[kCs], writes=[kCsb])
                        s.op('dve', lambda e: e.tensor_tensor(out=Caug, in0=Cs, in1=dC[:, jl], op=ALU.add), reads=[kCs, (kdC, jl)], writes=[kC, (kC, 'n')])
                        psO, kpO = bank(4, 7)
                        for h in range(4):
                            s.op('pe', lambda e, h=h: e.matmul(psO[0:64, h * 65:(h + 1) * 65], lhsT=PT[:, jl, h, :], rhs=vw[:, jl, h, :], start=(h == 0), stop=False, skip_group_check=True),
                                 reads=[(kPT, jl), (kvw, jl)], writes=[kpO])
                            s.op('pe', lambda e, h=h: e.matmul(psO[0:64, h * 65:(h + 1) * 65], lhsT=qb[:, h, cs_], rhs=Csb[:, h, :], start=False, stop=True, skip_group_check=True),
                                 reads=[cx['kqb'], kCsb], writes=[kpO])
                        pO3 = psO[0:64, 0:260].rearrange("p (h e) -> p h e", e=65)
                        return lambda: stage2b(cx, jl, pO3, kpO)

                    def stage2b(cx, jl, pO3, kpO, d=d, G=G, GT=GT):
                        jg = cx['gi'] * G + jl
                        hb, khb = cx['hb']
                        s.op('act', lambda e: e.activation(out=den[:, jl % 2, :], in_=pO3[:, :, 64], func=AF.Abs), reads=[kpO], writes=[(kden, jl % 2)])
                        s.op('dve', lambda e: e.tensor_tensor(out=den[:, jl % 2, :], in0=den[:, jl % 2, :], in1=THRt[:, d * 4:d * 4 + 4, jg], op=ALU.max),
                             reads=[(kden, jl % 2), kTHRt], writes=[(kden, jl % 2)])
                        s.op('dve', lambda e: e.reciprocal(out=rec[:, jl % 2, :], in_=den[:, jl % 2, :]), reads=[(kden, jl % 2)], writes=[(krec, jl % 2)])
                        s.op('dve', lambda e: e.tensor_tensor(out=hb[:, jl, :].rearrange("p (h e) -> p h e", e=64), in0=pO3[:, :, 0:64],
                                                              in1=rec[:, jl % 2, :].unsqueeze(2).to_broadcast([64, 4, 64]), op=ALU.mult),
                             reads=[kpO, (krec, jl % 2)], writes=[(khb, jl)])

                    def finish_group(cx, d=d, G=G, GT=GT):
                        t0 = cx['t0']
                        hb, khb = cx['hb']
                        hk = [(khb, jl) for jl in range(G)]
                        if d == 0:
                            s.dma('pool', HF[t0:t0 + GT, :].rearrange("(j t) c -> t j c", t=64), hb[:, 0:G, :], reads=hk, writes=[('HF', t0)], sk=khb)
                            return
                        hF, khF, mo_, kmo = cx['hF'], cx['khF'], cx['mo'], cx['kmo']
                        s.op('pool', lambda e: e.tensor_tensor(out=hb[:, 0:G, :], in0=hb[:, 0:G, :], in1=hF[:, 0:G, :], op=ALU.add), reads=hk + [khF], writes=hk)
                        X4 = hb[:, 0:G, :].rearrange("p g (h e) -> p (g h) e", e=64)
                        n4 = G * 4
                        s.op('dve', lambda e: e.tensor_reduce(out=s1[:, 0:n4], in_=X4, axis=AX.X, op=ALU.add), reads=hk, writes=[ks1])
                        s.op('pool', lambda e: e.tensor_tensor(out=sqx[:, 0:G, :], in0=hb[:, 0:G, :], in1=hb[:, 0:G, :], op=ALU.mult), reads=hk, writes=[ksqx])
                        s.op('dve', lambda e: e.tensor_reduce(out=s2[:, 0:n4], in_=sqx[:, 0:G, :].rearrange("p g (h e) -> p (g h) e", e=64), axis=AX.X, op=ALU.add),
                             reads=[ksqx], writes=[ks2])
                        s.op('dve', lambda e: e.tensor_scalar_mul(out=s1[:, 0:n4], in0=s1[:, 0:n4], scalar1=1.0 / 64), reads=[ks1], writes=[ks1])
                        s.op('dve', lambda e: e.tensor_tensor(out=s3[:, 0:n4], in0=s1[:, 0:n4], in1=s1[:, 0:n4], op=ALU.mult), reads=[ks1], writes=[ks3])
                        s.op('dve', lambda e: e.scalar_tensor_tensor(out=s2[:, 0:n4], in0=s2[:, 0:n4], scalar=1.0 / 64, in1=s3[:, 0:n4], op0=ALU.mult, op1=ALU.subtract),
                             reads=[ks2, ks3], writes=[ks2])
                        s.op('act', lambda e: e.activation(out=s2[:, 0:n4], in_=s2[:, 0:n4], func=AF.Sqrt, bias=epsln[0:64, 1:2], scale=1.0), reads=[ks2, k_eps], writes=[ks2])
                        s.op('dve', lambda e: e.reciprocal(out=s2[:, 0:n4], in_=s2[:, 0:n4]), reads=[ks2], writes=[ks2])
                        s.op('dve', lambda e: e.tensor_tensor(out=X4, in0=X4, in1=s1[:, 0:n4].unsqueeze(2).to_broadcast([64, n4, 64]), op=ALU.subtract),
                             reads=hk + [ks1], writes=hk)
                        s.op('dve', lambda e: e.tensor_tensor(out=X4, in0=X4, in1=s2[:, 0:n4].unsqueeze(2).to_broadcast([64, n4, 64]), op=ALU.mult),
                             reads=hk + [ks2], writes=hk)
                        s.op('pool', lambda e: e.tensor_tensor(out=hb[:, 0:G, :], in0=hb[:, 0:G, :], in1=ng.unsqueeze(1).to_broadcast([64, G, 256]), op=ALU.mult),
                             reads=hk + [kng], writes=hk)
                        s.op('act', lambda e: e.activation(out=sig[:, 0:G, :], in_=mo_[:, 0:G, :], func=AF.Sigmoid), reads=[kmo], writes=[ksig])
                        xo, kxo = xo_t[0]
                        s.op('dve', lambda e: e.tensor_tensor(out=xo[:, 0:G, :], in0=hb[:, 0:G, :], in1=sig[:, 0:G, :], op=ALU.mult), reads=hk + [ksig], writes=[kxo])
                        mx, kmx = mixo_t[0]
                        ps, kp = bank(7, 8)
                        pbf = ps.bitcast(BF16)
                        for c in range(2):
                            for jl in range(G):
                                s.op('pe', lambda e, c=c, jl=jl: e.transpose(
                                    out=pbf[:, c * 512 + jl * 64:c * 512 + (jl + 1) * 64], in_=xo[:, jl, c * 128:(c + 1) * 128], identity=idb[0:64, 0:64]),
                                    reads=[kxo, k_idb], writes=[kp])
                        for c in range(2):
                            s.op('dve', lambda e, c=c: e.tensor_copy(out=mx[:, c, 0:GT], in_=pbf[:, c * 512:c * 512 + GT]), reads=[kp], writes=[(kmx, c)])
                        s.dma('pool', MIXT.rearrange("(c p) t -> p c t", p=128)[:, 4:6, t0:t0 + GT], mx[:, :, 0:GT],
                              reads=[(kmx, 0), (kmx, 1)], writes=[('MIXT_M', t0)], sk=kmx)

                    cur = load_group(gorder[0])
                    for jl in corder:
                        stage1(cur, jl)
                    for gidx, gi in enumerate(gorder):
                        nxt = load_group(gorder[gidx + 1]) if gidx + 1 < len(gorder) else None
                        pend = None
                        for jl in corder:
                            p2 = stage2(cur, jl)
                            if pend is not None:
                                pend()
                            pend = p2
                            if nxt is not None:
                                stage1(nxt, jl)
                        pend()
                        finish_group(cur)
                        cur = nxt
                    if not samp:
                        s.dma('pool', C_out[j, l, d].rearrange("h a b -> a h b"), Caug[:, :, 0:64], reads=[kC], sk=(kC, 0))
                        s.dma('pool', n_out[j, l, d].rearrange("h a -> a h"), Caug[:, :, 64], reads=[kC], sk=(kC, 1), allow_slow_non_contiguous=True)
            s.barrier()
            mem.release(m0_)

        def phase_hgrn(l):
            m0_ = mem.mark()
            A = lambda shape, dt=F32, name='g': mem.alloc(shape, dt, name)
            hc, khc = A([32, 2, 32]); ng, kng = A([32, 256]); lbl, klbl = A([64, 4, 4]); lbp, klbp = A([64, 4, 4])
            lbs, klbs = A([64, 4]); lbv, klbv = A([64, 3, 4])
            s.dma('sp', hc, hconst_d, writes=[khc], sk=khc)
            s.dma('sp', ng, hgng_d[l], writes=[kng], sk=kng)
            s.dma('sp', lbl, lbl_d, writes=[klbl], sk=klbl)
            s.op('act', lambda e: e.activation(out=lbp, in_=lbl, func=AF.Exp), reads=[klbl], writes=[klbp])
            s.op('dve', lambda e: e.tensor_reduce(out=lbs, in_=lbp, axis=AX.X, op=ALU.add), reads=[klbp], writes=[klbs])
            s.op('dve', lambda e: e.reciprocal(out=lbs, in_=lbs), reads=[klbs], writes=[klbs])
            s.op('dve', lambda e: e.tensor_tensor(out=lbp, in0=lbp, in1=lbs.unsqueeze(2).to_broadcast([64, 4, 4]), op=ALU.mult), reads=[klbp, klbs], writes=[klbp])
            if l == 0:
                s.op('dve', lambda e: e.memset(lbv[:, 0, :], 0.0), writes=[klbv])
            else:
                s.op('dve', lambda e: e.tensor_reduce(out=lbv[:, 0, :], in_=lbp[:, :, 1:l + 1], axis=AX.X, op=ALU.add), reads=[klbp], writes=[klbv])
            s.op('dve', lambda e: e.tensor_scalar(out=lbv[:, 1, :], in0=lbv[:, 0, :], scalar1=-1.0, scalar2=1.0, op0=ALU.mult, op1=ALU.add), reads=[klbv], writes=[klbv])
            s.op('dve', lambda e: e.tensor_scalar_mul(out=lbv[:, 2, :], in0=lbv[:, 1, :], scalar1=-1.0), reads=[klbv], writes=[klbv])
            rmask, krm = A([64, 512])
            s.op('pool', lambda e: e.memset(rmask, 1.0), writes=[krm])
            s.op('pool', lambda e: e.memset(rmask.rearrange("p (j t) -> p j t", t=32)[:, :, 0:1], 0.0), writes=[krm])
            GTM = 256
            gq32, kgq = A([64, 4, GTM]); gf32, kgf = A([64, 4, GTM]); qf, kqf = gq32, kgq
            sg, ksg = A([64, 4, GTM]); lg, klg = A([64, 4, GTM]); Bc, kBc = A([64, 4, GTM]); kf, kkf = A([64, 4, GTM])
            tot, ktot = A([64, 4, 8])
            egl_t = [A([64, 4, 8]) for _ in range(2)]
            qs_t = [A([96, 4, GTM], BF16) for _ in range(2)]
            RS_t = [A([96, 8, 4, 64], BF16) for _ in range(2)]
            v96, kv96 = A([96, 8, 256])
            hc96, khc96 = A([96, 2, 32])
            s.dma('sp', hc96[64:96], hconst_d, writes=[khc96], sk=khc96)
            ks_t = [A([64, 4, GTM], BF16) for _ in range(2)]
            v32, kv32 = A([32, 8, 256]); vb_t = [A([32, 8, 256], BF16) for _ in range(2)]
            PTg_t = [A([32, 8, 4, 32], BF16) for _ in range(2)]
            dS_t = [A([64, 8, 4, 64]) for _ in range(2)]
            oF, koF = A([32, 8, 256]); gg32, kgg = A([32, 8, 256]); ob_t = [A([32, 8, 256]) for _ in range(2)]
            S, kS = A([64, 4, 64]); Sb, kSb = A([64, 4, 64], BF16); St, kSt = A([64, 4, 64])
            kst_t = [A([32, 256], BF16) for _ in range(2)]
            s2, ks2 = A([32, 32]); sqx, ksqx = A([32, 8, 256]); xo, kxo = A([32, 8, 256], BF16)
            mx, kmx = A([128, 2, GTM], BF16)
            gcount = [0]
            for sq_ in SEQS:
                T, S0, samp, j = sq_['T'], sq_['t0'], sq_['sample'], sq_['j']
                GT = min(GTM, T)
                G = GT // 32
                ngrp = T // GT
                for d in range(2):
                    sview = lambda a: a.rearrange("h c e -> c h e")
                    if samp:
                        s.dma('sp', S, sview(stS_d[l, d]), writes=[kS], sk=kS)
                    else:
                        s.op('dve', lambda e: e.memset(S, 0.0), writes=[kS])
                    mask = hc[:, d, :]
                    gorder = list(range(ngrp)) if d == 0 else list(range(ngrp - 1, -1, -1))
                    corder = list(range(G)) if d == 0 else list(range(G - 1, -1, -1))

                    def load_group(gi, d=d, G=G, GT=GT, S0=S0):
                        t0 = S0 + gi * GT
                        ti = t0 // 512
                        b = gcount[0] % 2
                        gcount[0] += 1
                        LS, kqs = qs_t[b]; qs = LS[0:64]; ks, kks = ks_t[b]; vb, kvb = vb_t[b]; egl, kegl = egl_t[b]
                        RS, kRS = RS_t[b]
                        cx = dict(gi=gi, t0=t0, ti=ti, b=b, qs=qs, kqs=kqs, ks=ks, kks=kks, vb=vb, kvb=kvb, egl=egl, kegl=kegl,
                                  PT=PTg_t[b], dS=dS_t[b], ob=ob_t[b], LS=LS, RS=RS, kRS=kRS)
                        s.dma('sp', v96[64:96, 0:G, :], TMV[t0:t0 + GT, 768:1024].rearrange("(j t) c -> t j c", t=32), reads=[('TMV', ti)], writes=[kv96], sk=kv96)
                        s.op('act', lambda e: e.copy(out=RS[64:96, 0:G], in_=v96[64:96, 0:G, :].rearrange("p g (h e) -> p g h e", e=64)), reads=[kv96], writes=[(kRS, 'v')])
                        fsrc = 'GFF' if d == 0 else 'GFB'
                        s.dma('sp', gq32[:, :, 0:GT], SCR['GQT'].rearrange("(c p) t -> p c t", p=64)[:, :, t0:t0 + GT],
                              reads=[('GQT', 0, ti), ('GQT', 128, ti)], writes=[kgq], sk=kgq)
                        s.dma('sp', gf32[:, :, 0:GT], SCR[fsrc].rearrange("(c p) t -> p c t", p=64)[:, :, t0:t0 + GT],
                              reads=[(fsrc, 0, ti), (fsrc, 128, ti)], writes=[kgf], sk=kgf)
                        s.dma('sp', v32[:, 0:G, :], TMV[t0:t0 + GT, 768:1024].rearrange("(j t) c -> t j c", t=32), reads=[('TMV', ti)], writes=[kv32], sk=kv32)
                        s.op('act', lambda e: e.copy(out=vb[:, 0:G, :], in_=v32[:, 0:G, :]), reads=[kv32], writes=[kvb])
                        W_ = slice(0, GT)
                        klgs = [(klg, i_) for i_ in range(4)]
                        kkfs = [(kkf, i_) for i_ in range(4)]
                        s.op('act', lambda e: e.activation(out=qf[:, :, W_], in_=gq32[:, :, W_], func=AF.Silu), reads=[kgq], writes=[kgq])
                        s.op('act', lambda e: e.activation(out=sg[:, :, W_], in_=gf32[:, :, W_], func=AF.Sigmoid), reads=[kgf], writes=[ksg])
                        for cc in range(4):
                            s.op('dve', lambda e, cc=cc: e.tensor_scalar(out=lg[:, cc, W_], in0=sg[:, cc, W_], scalar1=lbv[:, 1, cc:cc + 1], scalar2=lbv[:, 0, cc:cc + 1],
                                                                        op0=ALU.mult, op1=ALU.add), reads=[ksg, klbv], writes=[(klg, cc)])
                            s.op('pool', lambda e, cc=cc: e.tensor_scalar(out=kf[:, cc, W_], in0=sg[:, cc, W_], scalar1=lbv[:, 2, cc:cc + 1], scalar2=lbv[:, 1, cc:cc + 1],
                                                                         op0=ALU.mult, op1=ALU.add), reads=[ksg, klbv], writes=[(kkf, cc)])
                        s.op('act', lambda e: e.activation(out=lg[:, :, W_], in_=lg[:, :, W_], func=AF.Ln), reads=klgs, writes=klgs)
                        for cc in range(4):
                            s.op('dve', lambda e, cc=cc: e.tensor_tensor_scan(out=Bc[:, cc, W_], data0=rmask[:, W_], data1=lg[:, cc, W_], initial=0.0,
                                                                             op0=ALU.mult, op1=ALU.add), reads=[krm, (klg, cc)], writes=[(kBc, cc)])
                        Bc4 = Bc[:, :, W_].rearrange("p c (j t) -> p c j t", t=32)
                        lg4 = lg[:, :, W_].rearrange("p c (j t) -> p c j t", t=32)
                        kBcs = [(kBc, i_) for i_ in range(4)]
                        s.op('dve', lambda e: e.tensor_copy(out=tot[:, :, 0:G], in_=Bc4[:, :, :, 31]), reads=kBcs, writes=[ktot])
                        if d == 1:
                            s.op('dve', lambda e: e.tensor_tensor(out=Bc4, in0=lg4, in1=Bc4, op=ALU.subtract), reads=kBcs + klgs, writes=kBcs)
                            s.op('dve', lambda e: e.tensor_tensor(out=Bc4, in0=Bc4, in1=tot[:, :, 0:G].unsqueeze(3).to_broadcast([64, 4, G, 32]), op=ALU.add),
                                 reads=kBcs + [ktot], writes=kBcs)
                        s.op('act', lambda e: e.activation(out=egl[:, :, 0:G], in_=tot[:, :, 0:G], func=AF.Exp), reads=[ktot], writes=[kegl])
                        s.op('act', lambda e: e.activation(out=sg[:, :, W_], in_=Bc[:, :, W_], func=AF.Exp), reads=kBcs + [ksg] + kkfs + klgs, writes=[ksg])
                        s.op('dve', lambda e: e.tensor_tensor(out=qs[:, :, W_], in0=qf[:, :, W_], in1=sg[:, :, W_], op=ALU.mult), reads=[kgq, ksg], writes=[kqs])
                        s.op('dve', lambda e: e.tensor_scalar_max(out=Bc[:, :, W_], in0=Bc[:, :, W_], scalar1=-80.0), reads=kBcs + [ksg], writes=kBcs)
                        s.op('act', lambda e: e.activation(out=lg[:, :, W_], in_=Bc[:, :, W_], func=AF.Exp, scale=-1.0), reads=kBcs + klgs, writes=klgs)
                        s.op('dve', lambda e: e.tensor_tensor(out=ks[:, :, W_], in0=kf[:, :, W_], in1=lg[:, :, W_], op=ALU.mult), reads=kkfs + klgs, writes=[kks])
                        return cx

                    def stage1(cx, jl, d=d, mask=mask, G=G, GT=GT):
                        cs_ = slice(jl * 32, (jl + 1) * 32)
                        qs, ks, vb, egl = cx['qs'], cx['ks'], cx['vb'], cx['egl']
                        (PT, kPT), (dS, kdS) = cx['PT'], cx['dS']
                        psA, kpA = bank(0, 2)
                        for h in range(4):
                            s.op('pe', lambda e, h=h: e.matmul(psA[64:96, h * 32:(h + 1) * 32], lhsT=ks[:, h, cs_], rhs=qs[:, h, cs_], start=True, stop=True),
                                 reads=[cx['kks'], cx['kqs']], writes=[kpA])
                        psT, kpT = bank(2, 4)
                        pbf = psT.bitcast(BF16)
                        for cc in range(4):
                            s.op('pe', lambda e, cc=cc: e.transpose(out=pbf[0:32, cc * 64:(cc + 1) * 64], in_=ks[:, cc, cs_], identity=idb[0:64, 0:64]),
                                 reads=[cx['kks'], k_idb], writes=[kpT])
                        kst, kkst = kst_t[jl % 2]
                        LS = cx['LS']
                        s.op('dve', lambda e: e.tensor_tensor(out=LS[64:96, :, cs_], in0=psA[64:96, 0:128].rearrange("p (h t) -> p h t", t=32),
                                                              in1=hc96[64:96, d, :].unsqueeze(1).to_broadcast([32, 4, 32]), op=ALU.mult),
                             reads=[kpA, khc96], writes=[(kPT, jl)])
                        s.op('act', lambda e: e.copy(out=kst, in_=pbf[0:32, 0:256]), reads=[kpT], writes=[kkst])
                        psS, kpS = bank(4, 6)
                        for h in range(4):
                            s.op('pe', lambda e, h=h: e.matmul(psS[0:64, h * 64:(h + 1) * 64], lhsT=kst[:, h * 64:(h + 1) * 64], rhs=vb[:, jl, h * 64:(h + 1) * 64], start=True, stop=True),
                                 reads=[kkst, cx['kvb']], writes=[kpS])
                        s.op('dve', lambda e: e.tensor_tensor(out=dS[:, jl], in0=psS[0:64, 0:256].rearrange("p (c e) -> p c e", e=64),
                                                              in1=egl[:, :, jl].unsqueeze(2).to_broadcast([64, 4, 64]), op=ALU.mult),
                             reads=[kpS, cx['kegl']], writes=[(kdS, jl)])

                    def stage2(cx, jl, tgt, d=d, G=G, GT=GT):
                        cs_ = slice(jl * 32, (jl + 1) * 32)
                        egl, LS, RS, kRS = cx['egl'], cx['LS'], cx['RS'], cx['kRS']
                        (PT, kPT), (dS, kdS), (ob, kob) = cx['PT'], cx['dS'], cx['ob']
                        psO, kpO = bank(6, 8)
                        for h in range(4):
                            s.op('pe', lambda e, h=h: e.matmul(psO[0:32, h * 64:(h + 1) * 64], lhsT=LS[0:96, h, cs_], rhs=RS[0:96, jl, h, :], start=True, stop=True),
                                 reads=[(kPT, jl), cx['kqs'], (kRS, 'v'), (kRS, 'S', jl)], writes=[kpO])
                        s.op('dve', lambda e: e.tensor_tensor(out=St, in0=S, in1=egl[:, :, jl].unsqueeze(2).to_broadcast([64, 4, 64]), op=ALU.mult),
                             reads=[kS, cx['kegl']], writes=[kSt])
                        s.op('dve', lambda e: e.tensor_tensor(out=S, in0=St, in1=dS[:, jl], op=ALU.add), reads=[kSt, (kdS, jl)], writes=[kS])
                        if tgt is not None:
                            tcx, tjl = tgt
                            tRS, tkRS = tcx['RS'], tcx['kRS']
                            s.op('act', lambda e: e.copy(out=tRS[0:64, tjl], in_=S), reads=[kS], writes=[(tkRS, 'S', tjl)])
                        return lambda: s.op('act', lambda e: e.copy(out=ob[:, jl, :], in_=psO[0:32, 0:256]), reads=[kpO], writes=[(kob, jl)])

                    def finish_group(cx, d=d, G=G, GT=GT):
                        t0, ti = cx['t0'], cx['ti']
                        ob, kob = cx['ob']
                        ok_ = [(kob, jl) for jl in range(G)]
                        if d == 0:
                            s.dma('pool', HF[t0:t0 + GT, :].rearrange("(j t) c -> t j c", t=32), ob[:, 0:G, :], reads=ok_, writes=[('HF', t0)], sk=kob)
                            return
                        s.dma('sp', oF[:, 0:G, :], HF[t0:t0 + GT, :].rearrange("(j t) c -> t j c", t=32), reads=[('HF', t0)], writes=[koF], sk=koF)
                        s.dma('sp', gg32[:, 0:G, :], TMV[t0:t0 + GT, 1024:1280].rearrange("(j t) c -> t j c", t=32), reads=[('TMV', ti)], writes=[kgg], sk=kgg)
                        s.op('pool', lambda e: e.tensor_tensor(out=ob[:, 0:G, :], in0=ob[:, 0:G, :], in1=oF[:, 0:G, :], op=ALU.add), reads=ok_ + [koF], writes=ok_)
                        n4 = G * 4
                        s.op('pool', lambda e: e.tensor_tensor(out=sqx[:, 0:G, :], in0=ob[:, 0:G, :], in1=ob[:, 0:G, :], op=ALU.mult), reads=ok_, writes=[ksqx])
                        s.op('dve', lambda e: e.tensor_reduce(out=s2[:, 0:n4], in_=sqx[:, 0:G, :].rearrange("p g (h e) -> p (g h) e", e=64), axis=AX.X, op=ALU.add),
                             reads=[ksqx], writes=[ks2])
                        s.op('act', lambda e: e.activation(out=s2[:, 0:n4], in_=s2[:, 0:n4], func=AF.Sqrt, bias=epsln[0:32, 1:2], scale=1.0 / 64), reads=[ks2, k_eps], writes=[ks2])
                        s.op('dve', lambda e: e.reciprocal(out=s2[:, 0:n4], in_=s2[:, 0:n4]), reads=[ks2], writes=[ks2])
                        X4 = ob[:, 0:G, :].rearrange("p g (h e) -> p (g h) e", e=64)
                        s.op('dve', lambda e: e.tensor_tensor(out=X4, in0=X4, in1=s2[:, 0:n4].unsqueeze(2).to_broadcast([32, n4, 64]), op=ALU.mult),
                             reads=ok_ + [ks2], writes=ok_)
                        s.op('pool', lambda e: e.tensor_tensor(out=ob[:, 0:G, :], in0=ob[:, 0:G, :], in1=ng.unsqueeze(1).to_broadcast([32, G, 256]), op=ALU.mult),
                             reads=ok_ + [kng], writes=ok_)
                        s.op('act', lambda e: e.activation(out=sqx[:, 0:G, :], in_=gg32[:, 0:G, :], func=AF.Silu), reads=[kgg, ksqx, ks2], writes=[ksqx])
                        s.op('dve', lambda e: e.tensor_tensor(out=xo[:, 0:G, :], in0=ob[:, 0:G, :], in1=sqx[:, 0:G, :], op=ALU.mult), reads=ok_ + [ksqx], writes=[kxo])
                        ps, kp = bank(0, 2)
                        pbf = ps.bitcast(BF16)
                        for c in range(2):
                            for jl in range(G):
                                s.op('pe', lambda e, c=c, jl=jl: e.transpose(
                                    out=pbf[:, c * 512 + jl * 32:c * 512 + (jl + 1) * 32], in_=xo[:, jl, c * 128:(c + 1) * 128], identity=idb[0:32, 0:32]),
                                    reads=[kxo, k_idb], writes=[kp])
                        for c in range(2):
                            s.op('dve', lambda e, c=c: e.tensor_copy(out=mx[:, c, 0:GT], in_=pbf[:, c * 512:c * 512 + GT]), reads=[kp], writes=[(kmx, c)])
                        s.dma('pool', MIXT.rearrange("(c p) t -> p c t", p=128)[:, 6:8, t0:t0 + GT], mx[:, :, 0:GT],
                              reads=[(kmx, 0), (kmx, 1)], writes=[('MIXT_G', t0)], sk=kmx)

                    cur = load_group(gorder[0])
                    for jl in corder:
                        stage1(cur, jl)
                    s.op('act', lambda e, cur=cur, j0=corder[0]: e.copy(out=cur['RS'][0:64, j0], in_=S), reads=[kS], writes=[(cur['kRS'], 'S', corder[0])])
                    for gidx, gi in enumerate(gorder):
                        nxt = load_group(gorder[gidx + 1]) if gidx + 1 < len(gorder) else None
                        pend = None
                        for ci, jl in enumerate(corder):
                            tgt = (cur, corder[ci + 1]) if ci + 1 < len(corder) else ((nxt, corder[0]) if nxt is not None else None)
                            p2 = stage2(cur, jl, tgt)
                            if pend is not None:
                                pend()
                            pend = p2
                            if nxt is not None and not HG_NOINT:
                                stage1(nxt, jl)
                        pend()
                        if nxt is not None and HG_NOINT:
                            for jl in corder:
                                stage1(nxt, jl)
                        finish_group(cur)
                        cur = nxt
                    if not samp:
                        s.dma('pool', sview(S_out[j, l, d]), S, reads=[kS], sk=(kS, 'o'))
            s.barrier()
            mem.release(m0_)

        phase_init()
        for l in range(NL):
            phase_mod(l)
            if 'p1' in PH:
                phase_p1(l)
            if 'mla' in PH:
                phase_mla(l)
            if 'mlstm' in PH:
                phase_mlstm(l)
            if 'hgrn' in PH:
                phase_hgrn(l)
            if 'dense' in PH:
                phase_p3(l)
                phase_p3b(l)
                phase_p4(l)
        phase_final()
        s.emit()
    return nc


def _bf(x):
    return np.ascontiguousarray(x, dtype=np.float32)


def make_in_maps(inputs, cfg):
    NL = cfg.get('n_layers', DEPTH)
    NST = cfg.get('n_sample_tiles', 8)
    NPR = cfg.get('n_prompts', 4)
    cores = cfg.get('cores', list(range(8)))
    g = {k: np.asarray(v) for k, v in inputs.items()}
    w1 = _bf(g['w_in'][:NL][:, :, W1_COLS])
    b1 = g['b_in'][:NL][:, W1_COLS]
    b1fm = np.zeros((NL, 128, NG), np.float32)
    for gi, (c0, M, _, _) in enumerate(FM_GROUPS):
        b1fm[:, :M, gi] = b1[:, c0:c0 + M]
    b1tm = _bf(np.broadcast_to(b1[:, None, NFM:], (NL, 128, NTM)))
    b_modT = _bf(g['b_mod'][:NL].reshape(NL, 48, 128).transpose(0, 2, 1))
    lnp = _bf(np.stack([g[k][:NL].reshape(NL, NKC, 128).transpose(0, 2, 1) for k in ('ln1_g', 'ln1_b', 'ln2_g', 'ln2_b')], axis=2))
    shared = dict(w_mod=_bf(g['w_mod'][:NL]), b_modT=b_modT, w1=w1, b1fm=b1fm, b1tm=b1tm, w_out=_bf(g['w_out'][:NL]),
                  lnp=lnp, w_ffn_in=_bf(g['w_ffn_in'][:NL]), w_ffn_out=_bf(g['w_ffn_out'][:NL]),
                  identf=np.eye(128, dtype=np.float32))
    wuq = g['w_uq'][:NL]
    wqp = np.zeros_like(wuq)
    for h in range(8):
        wqp[:, :, h * 96 + 64:h * 96 + 96] = wuq[:, :, h * 96 + 64 + _PERM]
    wukv = g['w_ukv'][:NL]
    wk = np.zeros((NL, 128, 768), np.float32)
    wv = np.zeros((NL, 128, 512), np.float32)
    for h in range(8):
        wk[:, :, h * 96:h * 96 + 64] = wukv[:, :, h * 128:h * 128 + 64]
        wv[:, :, h * 64:(h + 1) * 64] = wukv[:, :, h * 128 + 64:h * 128 + 128]
    e96 = np.zeros((32, 96), np.float32)
    e96[np.arange(32), 64 + np.arange(32)] = 1.0
    qkn = _bf(np.stack([g['mla_q_norm'][:NL, :128], g['mla_q_norm'][:NL, 128:], g['mla_kv_norm'][:NL]], axis=-1))
    pos = np.arange(4096)
    freqs = 10000.0 ** (-np.arange(8, dtype=np.float64) * 0.125)
    ar = (pos // 64)[:, None] * freqs
    ac = (pos % 64)[:, None] * freqs
    ang = np.concatenate([ar, ar, ac, ac], -1)
    cosT = np.cos(ang).T.astype(np.float32)
    sinT = (np.sin(ang) * _ROT_SIGN).T.astype(np.float32)
    cos96 = np.ones((96, 4096), np.float32)
    sin96 = np.zeros((96, 4096), np.float32)
    cos96[64:] = cosT
    sin96[64:] = sinT
    shared.update(wq=_bf(wuq), wqp=_bf(wqp), wk=wk, wv=wv, e96=e96, qkn=qkn, cos96=cos96, sin96=sin96,
                  kcos=_bf(cosT), ksin=_bf(sinT))
    mconst = np.zeros((64, 4, 64), np.float32)
    ii = np.arange(64)
    mconst[:, 0, :] = (ii[None, :] >= ii[:, None])
    mconst[:, 1, :] = (ii[:, None] >= ii[None, :])
    mconst[:, 2, :] = np.eye(64)[::-1]
    mconst[0:4, 3, 0:4] = np.eye(4)[::-1]
    shared.update(mconst=mconst, mlng=_bf(np.broadcast_to(g['mlstm_norm'][:NL, None, :], (NL, 64, 256))))
    hconst = np.zeros((32, 2, 32), np.float32)
    i32 = np.arange(32)
    hconst[:, 0, :] = (i32[None, :] >= i32[:, None])
    hconst[:, 1, :] = (i32[:, None] >= i32[None, :])
    lbl = _bf(g['hgrn_lb_logits'].reshape(4, 4, 64).transpose(2, 1, 0))
    shared.update(hconst=hconst, lbl=lbl, hgng=_bf(np.broadcast_to(g['hgrn_norm'][:NL, None, :], (NL, 32, 256))))
    maps = []
    for ci in cores:
        b = ci // 2
        parts = []
        if NST:
            parts.append(g['x_sample'][b, :NST * 512])
        for j in range(NPR):
            parts.append(g['x_prompt'][4 * ci + j])
        xin = _bf(np.concatenate(parts, axis=0))
        cv = np.stack([g['c'][b], g['c_ctx']], axis=-1)
        cvT = _bf(cv.reshape(NKC, 128, 2).transpose(1, 0, 2))
        m = dict(shared)
        m.update(stC=_bf(g['state_mlstm_C'][b, :NL]), stn=_bf(g['state_mlstm_n'][b, :NL]), stm=_bf(g['state_mlstm_m'][b, :NL]))
        m.update(stS=_bf(g['state_hgrn_S'][b, :NL]))
        m.update(xin=xin, cvT=cvT, cckv=_bf(g['cache_mla_ckv'][b, :NL]), ckpe=_bf(g['cache_mla_kpe'][b, :NL]))
        maps.append(m)
    return maps


def kernel(**inputs):
    cfg = {}
    nc = build_program(cfg)
    maps = make_in_maps(inputs, cfg)
    res = run_bass_kernel_spmd(nc, maps, core_ids=list(range(8)))
    r = res.results
    cat = lambda k: np.ascontiguousarray(np.concatenate([r[i][k] for i in range(8)], axis=0), dtype=np.float32)
    y_prompt = np.concatenate([r[i]['y_out'][4096:].reshape(4, 256, D) for i in range(8)], axis=0).astype(np.float32)
    y_sample = np.stack([r[2 * b]['y_out'][:4096] for b in range(4)], axis=0).astype(np.float32)
    return (y_prompt, y_sample, cat('ckv_out'), cat('kpe_out'), cat('C_out'), cat('n_out'), cat('m_out'), cat('S_out'))
```

```python
import math
from contextlib import ExitStack
import numpy as np
import concourse.bass as bass
import concourse.mybir as mybir
from concourse.bass_utils import run_bass_kernel_spmd

F32 = mybir.dt.float32
BF16 = mybir.dt.bfloat16
AF = mybir.ActivationFunctionType
ALU = mybir.AluOpType
AX = mybir.AxisListType

D = 1024
DEPTH = 4
NKC = 8
FF = 2816
NFC = 22
ALPHA = (2 * DEPTH) ** 0.25
EPS = 1e-6
EPS_LN = EPS / (ALPHA * ALPHA)
ATT_SCALE = 96 ** -0.5

_IN_SIZES = (256, 128, 32, 256, 256, 256, 256, 4, 4, 4, 4, 256, 256, 256, 256, 256)
_OFF = np.cumsum((0,) + _IN_SIZES)
_NAMES = ['cq', 'ckv', 'kpe', 'mq', 'mk', 'mv', 'mo', 'mi_f', 'mi_b', 'mf_f', 'mf_b', 'gq', 'gf_f', 'gf_b', 'gi', 'gg']
_COL = {n: np.arange(_OFF[i], _OFF[i + 1]) for i, n in enumerate(_NAMES)}
_PERM = np.concatenate([np.arange(8, 16), np.arange(0, 8), np.arange(24, 32), np.arange(16, 24)])
_ROT_SIGN = np.concatenate([-np.ones(8), np.ones(8), -np.ones(8), np.ones(8)]).astype(np.float32)
FM_GROUPS = []
_fm_cols = []


def _add_fm(parts):
    c0 = sum(len(c) for c in _fm_cols)
    outs = []
    p = 0
    for cols, dst, r0 in parts:
        _fm_cols.append(np.asarray(cols))
        outs.append((dst, r0, p, len(cols)))
        p += len(cols)
    FM_GROUPS.append((c0, p, outs))


_add_fm([(_COL['cq'][:128], 'CQT', 0)])
_add_fm([(_COL['cq'][128:], 'CQT', 128)])
_add_fm([(_COL['ckv'], 'CKVT', 0)])
_add_fm([(_COL['kpe'], 'KPET', 0), (_COL['kpe'][_PERM], 'KPEP', 0), (_COL['mi_f'], 'MIF', 0), (_COL['mi_b'], 'MIB', 0),
         (_COL['mf_f'], 'MFF', 0), (_COL['mf_b'], 'MFB', 0)])
for _n, _d in (('mq', 'MQT'), ('mk', 'MKT'), ('gq', 'GQT'), ('gf_f', 'GFF'), ('gf_b', 'GFB')):
    _add_fm([(_COL[_n][:128], _d, 0)])
    _add_fm([(_COL[_n][128:], _d, 128)])
NFM = sum(len(c) for c in _fm_cols)
_tm_cols = [_COL['mv'], _COL['mo'], _COL['mk'], _COL['gi'], _COL['gg']]
NTM = 1280
TM_GROUPS = [(0, 512), (512, 512), (1024, 256)]
W1_COLS = np.concatenate(_fm_cols + _tm_cols)
NW1 = len(W1_COLS)
NG = len(FM_GROUPS)
FM_SCR = {'CQT': 256, 'CKVT': 128, 'KPET': 32, 'KPEP': 32, 'MQT': 256, 'MKT': 256, 'GQT': 256, 'GFF': 256,
          'GFB': 256, 'MIF': 4, 'MIB': 4, 'MFF': 4, 'MFB': 4}


class Sched:
    EPOCH = 40000

    def __init__(self, nc, stack):
        self.nc = nc
        self.stack = stack
        self.eng = {'pe': nc.tensor, 'act': nc.scalar, 'dve': nc.vector, 'pool': nc.gpsimd, 'sp': nc.sync}
        self.prog = {k: [] for k in self.eng}
        self.seq = {k: 0 for k in self.eng}
        self.esems = {k: [] for k in self.eng}
        self.dsem = {}
        self.free_dsems = {}
        self.lastw = {}
        self.readers = {}
        self.waited = {}
        self.nsem = 0

    def _newsem(self, name):
        self.nsem += 1
        return self.stack.enter_context(self.nc.semaphore(name))

    def _deps(self, engine, reads, writes):
        deps = {}

        def add(ev):
            k = id(ev[0])
            if k not in deps or deps[k][1] < ev[1]:
                deps[k] = ev
        for k in reads:
            if k in self.lastw:
                add(self.lastw[k])
        for k in writes:
            if k in self.lastw:
                add(self.lastw[k])
            for ev in self.readers.get(k, {}).values():
                add(ev)
        waits = []
        own = set(id(s) for s in self.esems[engine]) if engine == 'pe' else set()
        for k, (sem, val) in deps.items():
            if k in own:
                continue
            wk = (engine, k)
            if self.waited.get(wk, 0) < val:
                self.waited[wk] = val
                waits.append((sem, val))
        return waits

    def _commit(self, ev, reads, writes):
        for k in writes:
            self.lastw[k] = ev
            self.readers[k] = {}
        for k in reads:
            self.readers.setdefault(k, {})[id(ev[0])] = ev

    def op(self, engine, fn, reads=(), writes=()):
        waits = self._deps(engine, reads, writes)
        n = self.seq[engine]
        e = n // self.EPOCH
        while len(self.esems[engine]) <= e:
            self.esems[engine].append(self._newsem(f"e_{engine}_{len(self.esems[engine])}"))
        sem = self.esems[engine][e]
        val = n % self.EPOCH + 1
        self.seq[engine] = n + 1
        self.prog[engine].append((waits, fn, sem, 1))
        self._commit((sem, val), reads, writes)

    def dma(self, queue, out, in_, reads=(), writes=(), sk=None, **kw):
        self.dma_group(queue, [(out, in_)], reads, writes, sk, **kw)

    def dma_group(self, queue, pairs, reads=(), writes=(), sk=None, **kw):
        assert sk is not None
        sk = (queue, sk)
        if sk not in self.dsem:
            fl = self.free_dsems.setdefault(queue, [])
            self.dsem[sk] = fl.pop() if fl else [self._newsem(f"d_{self.nsem}"), 0]
        ent = self.dsem[sk]
        waits = self._deps(queue, reads, writes)
        for i, (o, a) in enumerate(pairs):
            ent[1] += 16
            self.prog[queue].append((waits if i == 0 else [],
                                     (lambda eng, o=o, a=a: eng.dma_start(out=o, in_=a, **kw)), ent[0], 16))
        self._commit((ent[0], ent[1]), reads, writes)

    def barrier(self):
        evs = []
        for e, sems in self.esems.items():
            n = self.seq[e]
            if n > 0:
                evs.append((sems[(n - 1) // self.EPOCH], (n - 1) % self.EPOCH + 1))
        for sk, (sem, cnt) in self.dsem.items():
            if cnt > 0:
                evs.append((sem, cnt))
        for fl in self.free_dsems.values():
            for sem, cnt in fl:
                if cnt > 0:
                    evs.append((sem, cnt))
        for engine in self.eng:
            waits = []
            for sem, val in evs:
                wk = (engine, id(sem))
                if self.waited.get(wk, 0) < val:
                    self.waited[wk] = val
                    waits.append((sem, val))
            if waits:
                self.prog[engine].append((waits, None, None, 0))
        for (q, _), ent in self.dsem.items():
            self.free_dsems.setdefault(q, []).append(ent)
        self.dsem = {}
        self.lastw = {}
        self.readers = {}

    def emit(self):
        self.barrier()
        nc = self.nc
        with nc.Block() as block:
            def run(name):
                def f(eng):
                    for waits, fn, sem, amt in self.prog[name]:
                        for ws, wv in waits:
                            eng.wait_ge(ws, wv)
                        if fn is not None:
                            fn(eng).then_inc(sem, amt)
                return f
            block.sync(run('sp'))
            block.tensor(run('pe'))
            block.scalar(run('act'))
            block.vector(run('dve'))
            block.gpsimd(run('pool'))


class Mem:
    def __init__(self, big, words):
        self.big = big
        self.words = words
        self.top = 0
        self.n = 0

    def mark(self):
        return self.top

    def release(self, m):
        self.top = m

    def alloc(self, shape, dt=F32, name='t'):
        P = shape[0]
        n = int(np.prod(shape[1:]))
        w = n if dt == F32 else (n + 1) // 2
        w = (w + 15) // 16 * 16
        assert self.top + w <= self.words, f"SBUF overflow allocating {name} {shape}"
        a = self.big[0:P, self.top:self.top + w]
        self.top += w
        if dt != F32:
            a = a.bitcast(dt)
        a = a[:, 0:n]
        if len(shape) == 3:
            a = a.rearrange("p (a b) -> p a b", a=shape[1], b=shape[2])
        elif len(shape) == 4:
            a = a.rearrange("p (a b c) -> p a b c", a=shape[1], b=shape[2], c=shape[3])
        self.n += 1
        return a, f"{name}#{self.n}"


def build_program(cfg):
    NL = cfg.get('n_layers', DEPTH)
    NST = cfg.get('n_sample_tiles', 8)
    NPR = cfg.get('n_prompts', 4)
    DBG = cfg.get('debug', [])
    PH = cfg.get('phases', ['p1', 'mla', 'mlstm', 'hgrn', 'dense'])
    MIXIN = cfg.get('mix_input', False)
    HG_NOINT = cfg.get('hg_noint', False)
    TS = NST * 512
    TT = TS + NPR * 256
    NTILE = TT // 512
    nc = bass.Bass("TRN2", target_bir_lowering=False)

    def din(name, shape, dt=F32):
        return nc.dram_tensor(name, list(shape), dt, kind="ExternalInput").ap()

    def dout(name, shape, dt=F32):
        return nc.dram_tensor(name, list(shape), dt, kind="ExternalOutput").ap()

    def dscr(name, shape, dt=F32):
        kind = "ExternalOutput" if name in DBG else "Internal"
        return nc.dram_tensor(name, list(shape), dt, kind=kind).ap()

    xin = din("xin", [TT, D])
    cvT = din("cvT", [128, NKC, 2])
    w_mod = din("w_mod", [NL, D, 6 * D])
    b_modT = din("b_modT", [NL, 128, 48])
    w1 = din("w1", [NL, D, NW1])
    b1fm = din("b1fm", [NL, 128, NG])
    b1tm = din("b1tm", [NL, 128, NTM])
    w_out = din("w_out", [NL, D, D])
    lnp = din("lnp", [NL, 128, 4, NKC])
    w_ffn_in = din("w_ffn_in", [NL, D, 2 * FF])
    w_ffn_out = din("w_ffn_out", [NL, FF, D])
    identf = din("identf", [128, 128])
    wq_d = din("wq", [NL, 256, 768])
    wqp_d = din("wqp", [NL, 256, 768])
    wk_d = din("wk", [NL, 128, 768])
    wv_d = din("wv", [NL, 128, 512])
    e96_d = din("e96", [32, 96])
    qkn_d = din("qkn", [NL, 128, 3])
    cos96_d = din("cos96", [96, 4096])
    sin96_d = din("sin96", [96, 4096])
    kcos_d = din("kcos", [32, 4096])
    ksin_d = din("ksin", [32, 4096])
    cckv_d = din("cckv", [NL, 512, 128])
    ckpe_d = din("ckpe", [NL, 512, 32])
    mconst_d = din("mconst", [64, 4, 64])
    mlng_d = din("mlng", [NL, 64, 256])
    stC_d = din("stC", [NL, 2, 4, 64, 64])
    stn_d = din("stn", [NL, 2, 4, 64])
    stm_d = din("stm", [NL, 2, 4])
    C_out = dout("C_out", [max(NPR, 1), NL, 2, 4, 64, 64])
    n_out = dout("n_out", [max(NPR, 1), NL, 2, 4, 64])
    m_out = dout("m_out", [max(NPR, 1), NL, 2, 4])
    HF = dscr("HF", [TT, 256])
    hconst_d = din("hconst", [32, 2, 32])
    hgng_d = din("hgng", [NL, 32, 256])
    lbl_d = din("lbl", [64, 4, 4])
    stS_d = din("stS", [NL, 2, 4, 64, 64])
    S_out = dout("S_out", [max(NPR, 1), NL, 2, 4, 64, 64])
    ckv_out = dout("ckv_out", [max(NPR, 1), NL, 256, 128])
    kpe_out = dout("kpe_out", [max(NPR, 1), NL, 256, 32])
    y_out = dout("y_out", [TT, D])

    XT = dscr("XT", [D, TT])
    MIXT = din("MIXT", [D, TT], BF16) if MIXIN else dscr("MIXT", [D, TT], BF16)
    UT = dscr("UT", [FF, TT], BF16)
    SCR = {k: dscr(k, [r, TT]) for k, r in FM_SCR.items()}
    TMV = dscr("TMV", [TT, NTM])

    with ExitStack() as st:
        WORDS = 47 * 1024
        big = st.enter_context(nc.sbuf_tensor("big", [128, WORDS], F32))
        psb = [st.enter_context(nc.psum_tensor(f"ps{i}", [128, 512], F32)) for i in range(8)]
        s = Sched(nc, st)
        mem = Mem(big, WORDS)
        psi = {}

        def bank(lo=0, hi=8):
            c = psi.get((lo, hi), 0)
            psi[(lo, hi)] = c + 1
            i = lo + c % (hi - lo)
            return psb[i], ('ps', i)

        idf, k_idf = mem.alloc([128, 128], F32, 'idf')
        idb, k_idb = mem.alloc([128, 128], BF16, 'idb')
        onesf, k_onesf = mem.alloc([128, 128], F32, 'onesf')
        modT, k_mod = mem.alloc([128, 48, 2], F32, 'modT')
        lnt, k_lnt = mem.alloc([128, 4, NKC], F32, 'lnt')
        s.dma('sp', idf, identf, writes=[k_idf], sk='idf')
        s.op('dve', lambda e: e.tensor_copy(out=idb, in_=idf), reads=[k_idf], writes=[k_idb])
        s.op('pool', lambda e: e.memset(onesf, 1.0), writes=[k_onesf])
        onesb, k_onesb = mem.alloc([128, 128], BF16, 'onesb')
        s.op('pool', lambda e: e.memset(onesb, 1.0), writes=[k_onesb])
        epsln, k_eps = mem.alloc([128, 2], F32, 'epsln')
        s.op('pool', lambda e: e.memset(epsln[:, 0:1], EPS_LN), writes=[k_eps])
        s.op('pool', lambda e: e.memset(epsln[:, 1:2], EPS), writes=[k_eps])
        base_mark = mem.mark()

        def load_w_bf16(dram2d, dst, kdst, nkc, ncols, stage):
            i = 0
            cw = stage[0][0].shape[1]
            for kc in range(nkc):
                for c0 in range(0, ncols, cw):
                    w = min(cw, ncols - c0)
                    sa, sk_ = stage[i % len(stage)]
                    i += 1
                    s.dma('sp', sa[:, 0:w], dram2d[kc * 128:(kc + 1) * 128, c0:c0 + w], writes=[sk_], sk=sk_)
                    ce = ('pool', 'dve', 'act')[i % 3]
                    if ce == 'act':
                        s.op('act', lambda e, kc=kc, c0=c0, w=w, sa=sa: e.copy(out=dst[:, kc, c0:c0 + w], in_=sa[:, 0:w]), reads=[sk_], writes=[kdst])
                    else:
                        s.op(ce, lambda e, kc=kc, c0=c0, w=w, sa=sa: e.tensor_copy(out=dst[:, kc, c0:c0 + w], in_=sa[:, 0:w]), reads=[sk_], writes=[kdst])

        def fm_view(dram2d, t0, n):
            return dram2d.rearrange("(c p) t -> p c t", p=128)[:, :, t0:t0 + n]

        def phase_init():
            m0 = mem.mark()
            xin_t = [mem.alloc([128, 4, D], F32, 'xin_t') for _ in range(2)]
            xT_t = [mem.alloc([128, NKC, 512], F32, 'xT_t') for _ in range(2)]
            for i in range(NTILE):
                t0 = i * 512
                xa, kx = xin_t[i % 2]
                xo, ko = xT_t[i % 2]
                s.dma('sp', xa, xin[t0:t0 + 512, :].rearrange("(b p) c -> p b c", p=128), writes=[kx], sk=kx)
                for kc in range(NKC):
                    ps, kp = bank()
                    for b in range(4):
                        s.op('pe', lambda e, ps=ps, b=b, kc=kc, xa=xa: e.transpose(
                            out=ps[:, b * 128:(b + 1) * 128], in_=xa[:, b, kc * 128:(kc + 1) * 128], identity=idf),
                            reads=[kx, k_idf], writes=[kp])
                    eng = 'dve' if kc % 2 == 0 else 'act'
                    if eng == 'dve':
                        s.op('dve', lambda e, ps=ps, kc=kc, xo=xo: e.tensor_copy(out=xo[:, kc, :], in_=ps[:, :]),
                             reads=[kp], writes=[(ko, kc)])
                    else:
                        s.op('act', lambda e, ps=ps, kc=kc, xo=xo: e.copy(out=xo[:, kc, :], in_=ps[:, :]),
                             reads=[kp], writes=[(ko, kc)])
                s.dma('pool', fm_view(XT, t0, 512), xo, reads=[(ko, kc) for kc in range(NKC)], writes=[('XT', i)], sk=ko)
            s.barrier()
            mem.release(m0)

        def phase_final():
            m0 = mem.mark()
            xT_t = [mem.alloc([128, NKC, 512], F32, 'xT_f') for _ in range(2)]
            y_t = [mem.alloc([128, 4, D], F32, 'y_t') for _ in range(2)]
            for i in range(NTILE):
                t0 = i * 512
                xa, kx = xT_t[i % 2]
                ya, ky = y_t[i % 2]
                s.dma('sp', xa, fm_view(XT, t0, 512), reads=[('XT', i)], writes=[kx], sk=kx)
                for b in range(4):
                    for half in range(2):
                        ps, kp = bank()
                        for q in range(4):
                            kc = half * 4 + q
                            s.op('pe', lambda e, ps=ps, q=q, kc=kc, b=b, xa=xa: e.transpose(
                                out=ps[:, q * 128:(q + 1) * 128], in_=xa[:, kc, b * 128:(b + 1) * 128], identity=idf),
                                reads=[kx, k_idf], writes=[kp])
                        if half == 0:
                            s.op('dve', lambda e, ps=ps, b=b, ya=ya: e.tensor_copy(out=ya[:, b, 0:512], in_=ps[:, :]),
                                 reads=[kp], writes=[(ky, b, 0)])
                        else:
                            s.op('act', lambda e, ps=ps, b=b, ya=ya: e.copy(out=ya[:, b, 512:1024], in_=ps[:, :]),
                                 reads=[kp], writes=[(ky, b, 1)])
                s.dma('pool', y_out[t0:t0 + 512, :].rearrange("(b p) c -> p b c", p=128), ya,
                      reads=[(ky, b, h) for b in range(4) for h in range(2)], sk=ky)
            s.barrier()
            mem.release(m0)

        def phase_mod(l):
            m0 = mem.mark()
            cv, kcv = mem.alloc([128, NKC, 2], F32, 'cv')
            sil, ksil = mem.alloc([128, NKC, 2], F32, 'sil')
            bm, kbm = mem.alloc([128, 48], F32, 'bm')
            stage = [mem.alloc([128, 6 * D], F32, 'wm') for _ in range(2)]
            s.dma('sp', cv, cvT, writes=[kcv], sk=kcv)
            s.dma('sp', bm, b_modT[l], writes=[kbm], sk=kbm)
            s.dma('sp', lnt, lnp[l], writes=[k_lnt], sk=k_lnt)
            s.op('act', lambda e: e.activation(out=sil, in_=cv, func=AF.Silu), reads=[kcv], writes=[ksil])
            ps, kp = bank()
            for kc in range(NKC):
                sa, ks = stage[kc % 2]
                s.dma('sp', sa, w_mod[l, kc * 128:(kc + 1) * 128, :], writes=[ks], sk=ks)
                for g in range(48):
                    s.op('pe', lambda e, sa=sa, g=g, kc=kc: e.matmul(
                        ps[:, 2 * g:2 * g + 2], lhsT=sa[:, g * 128:(g + 1) * 128], rhs=sil[:, kc, :],
                        start=(kc == 0 and g == 0), stop=(kc == NKC - 1), skip_group_check=True),
                        reads=[ks, ksil], writes=[kp])
            s.op('dve', lambda e: e.tensor_tensor(
                out=modT, in0=ps[:, 0:96].rearrange("p (g v) -> p g v", v=2),
                in1=bm.unsqueeze(2).to_broadcast([128, 48, 2]), op=ALU.add), reads=[kp, kbm], writes=[k_mod])
            for lo, add in ((8, True), (16, False), (32, True), (40, False)):
                if add:
                    s.op('dve', lambda e, lo=lo: e.tensor_scalar_add(out=modT[:, lo:lo + 8, :], in0=modT[:, lo:lo + 8, :], scalar1=1.0),
                         reads=[k_mod], writes=[k_mod])
                else:
                    s.op('dve', lambda e, lo=lo: e.tensor_scalar_mul(out=modT[:, lo:lo + 8, :], in0=modT[:, lo:lo + 8, :], scalar1=1.0 / ALPHA),
                         reads=[k_mod], writes=[k_mod])
            s.barrier()
            mem.release(m0)

        SH1, SC1, G1, SH2, SC2, G2 = 0, 8, 16, 24, 32, 40

        def tile_v(i):
            return 0 if i < NST else 1

        def phase_p1(l):
            m0 = mem.mark()
            w1t, kw1 = mem.alloc([128, NKC, NW1], BF16, 'w1t')
            bfm, kbfm = mem.alloc([128, NG], F32, 'bfm')
            btm, kbtm = mem.alloc([128, NTM], F32, 'btm')
            stage = [mem.alloc([128, 2048], F32, 'wst') for _ in range(3)]
            xT_t = [mem.alloc([128, NKC, 512], F32, 'xT1') for _ in range(2)]
            hT_t = [mem.alloc([128, NKC, 512], BF16, 'hT') for _ in range(2)]
            fo_t = [mem.alloc([128, 512], F32, 'fo') for _ in range(4)]
            to_t = [mem.alloc([128, 4, NTM], F32, 'to') for _ in range(1)]
            s.dma('sp', bfm, b1fm[l], writes=[kbfm], sk=kbfm)
            s.dma('sp', btm, b1tm[l], writes=[kbtm], sk=kbtm)
            load_w_bf16(w1[l], w1t, kw1, NKC, NW1, stage)
            nfo = 0
            for i in range(NTILE):
                t0 = i * 512
                v = tile_v(i)
                xa, kx = xT_t[i % 2]
                ha, kh = hT_t[i % 2]
                s.dma('sp', xa, fm_view(XT, t0, 512), reads=[('XT', i)], writes=[kx], sk=kx)
                for kc in range(NKC):
                    s.op('dve', lambda e, kc=kc, xa=xa, ha=ha, v=v: e.tensor_scalar(
                        out=ha[:, kc, :], in0=xa[:, kc, :], scalar1=modT[:, SC1 + kc, v:v + 1], scalar2=modT[:, SH1 + kc, v:v + 1],
                        op0=ALU.mult, op1=ALU.add), reads=[kx, k_mod], writes=[(kh, kc)])
                for g, (c0, M, outs) in enumerate(FM_GROUPS):
                    ps, kp = bank(0, 4)
                    for kc in range(NKC):
                        s.op('pe', lambda e, ps=ps, kc=kc, c0=c0, M=M, ha=ha: e.matmul(
                            ps[0:M, :], lhsT=w1t[:, kc, c0:c0 + M], rhs=ha[:, kc, :], start=(kc == 0), stop=(kc == NKC - 1)),
                            reads=[kw1, (kh, kc)], writes=[kp])
                    fo, kfo = fo_t[nfo % 4]
                    nfo += 1
                    s.op('act', lambda e, ps=ps, M=M, g=g, fo=fo: e.activation(
                        out=fo[0:M, :], in_=ps[0:M, :], func=AF.Identity, bias=bfm[0:M, g:g + 1], scale=1.0),
                        reads=[kp, kbfm], writes=[kfo])
                    s.dma_group('pool', [(SCR[dst][r0:r0 + nr, t0:t0 + 512], fo[p0:p0 + nr, :]) for (dst, r0, p0, nr) in outs],
                                reads=[kfo], writes=[(dst, r0, i) for (dst, r0, p0, nr) in outs], sk=kfo)
                ta, kt = to_t[0]
                for b in range(4):
                    for (c0, N) in TM_GROUPS:
                        ps, kp = bank(4, 8)
                        for kc in range(NKC):
                            s.op('pe', lambda e, ps=ps, kc=kc, c0=c0, N=N, b=b, ha=ha: e.matmul(
                                ps[:, 0:N], lhsT=ha[:, kc, b * 128:(b + 1) * 128], rhs=w1t[:, kc, NFM + c0:NFM + c0 + N],
                                start=(kc == 0), stop=(kc == NKC - 1)), reads=[kw1, (kh, kc)], writes=[kp])
                        s.op('dve', lambda e, ps=ps, c0=c0, N=N, b=b, ta=ta: e.tensor_tensor(
                            out=ta[:, b, c0:c0 + N], in0=ps[:, 0:N], in1=btm[:, c0:c0 + N], op=ALU.add),
                            reads=[kp, kbtm], writes=[(kt, b, c0)])
                s.dma('pool', TMV[t0:t0 + 512, :].rearrange("(b p) c -> p b c", p=128), ta,
                      reads=[(kt, b, c0) for b in range(4) for (c0, N) in TM_GROUPS], writes=[('TMV', i)], sk=kt)
            s.barrier()
            mem.release(m0)

        def ln_apply(ya, ky, xo, ko, l, gi, bi, tmp):
            (sq, ksq), (mean, kmean), (rstd, krstd), (msq, kmsq), (ybf, kybf) = tmp
            p1, kp1 = bank(4, 6)
            p2, kp2 = bank(6, 8)
            for kc in range(NKC):
                s.op('act', lambda e, kc=kc: e.copy(out=ybf[:, kc % 2, :], in_=ya[:, kc, :]), reads=[(ky, kc)], writes=[(kybf, kc % 2)])
                s.op('pe', lambda e, kc=kc: e.matmul(p1[:, :], lhsT=onesb, rhs=ybf[:, kc % 2, :], start=(kc == 0), stop=(kc == NKC - 1)),
                     reads=[k_onesb, (kybf, kc % 2)], writes=[kp1])
                s.op('act', lambda e, kc=kc: e.activation(out=sq[:, kc % 2, :], in_=ya[:, kc, :], func=AF.Square),
                     reads=[(ky, kc)], writes=[(ksq, kc % 2)])
                s.op('pe', lambda e, kc=kc: e.matmul(p2[:, :], lhsT=onesb, rhs=sq[:, kc % 2, :], start=(kc == 0), stop=(kc == NKC - 1)),
                     reads=[k_onesb, (ksq, kc % 2)], writes=[kp2])
            s.op('dve', lambda e: e.tensor_scalar_mul(out=mean, in0=p1[:, :], scalar1=1.0 / D), reads=[kp1], writes=[kmean])
            s.op('dve', lambda e: e.tensor_tensor(out=msq, in0=mean, in1=mean, op=ALU.mult), reads=[kmean], writes=[kmsq])
            s.op('dve', lambda e: e.scalar_tensor_tensor(out=msq, in0=p2[:, :], scalar=1.0 / D, in1=msq, op0=ALU.mult, op1=ALU.subtract),
                 reads=[kp2, kmsq], writes=[kmsq])
            s.op('act', lambda e: e.activation(out=rstd, in_=msq, func=AF.Sqrt, bias=epsln[:, 0:1], scale=1.0), reads=[kmsq, k_eps], writes=[krstd])
            s.op('dve', lambda e: e.reciprocal(out=rstd, in_=rstd), reads=[krstd], writes=[krstd])
            for kc in range(NKC):
                s.op('dve', lambda e, kc=kc: e.tensor_tensor(out=ya[:, kc, :], in0=ya[:, kc, :], in1=mean, op=ALU.subtract),
                     reads=[(ky, kc), kmean], writes=[(ky, kc)])
                s.op('pool', lambda e, kc=kc: e.tensor_tensor(out=ya[:, kc, :], in0=ya[:, kc, :], in1=rstd, op=ALU.mult),
                     reads=[(ky, kc), krstd], writes=[(ky, kc)])
                s.op('dve', lambda e, kc=kc: e.tensor_scalar(out=xo[:, kc, :], in0=ya[:, kc, :], scalar1=lnt[:, gi, kc:kc + 1],
                                                             scalar2=lnt[:, bi, kc:kc + 1], op0=ALU.mult, op1=ALU.add),
                     reads=[(ky, kc), k_lnt], writes=[(ko, kc)])

        def phase_p3(l):
            m0 = mem.mark()
            wo, kwo = mem.alloc([128, NKC, D], BF16, 'wo')
            stage = [mem.alloc([128, 2048], F32, 'wst') for _ in range(3)]
            xT_t = [mem.alloc([128, NKC, 512], F32, 'xT3') for _ in range(2)]
            mx_t = [mem.alloc([128, NKC, 512], BF16, 'mx') for _ in range(2)]
            ya_t = [mem.alloc([128, NKC, 512], F32, 'ya') for _ in range(2)]
            tmp = [mem.alloc([128, 2, 512], BF16, 'sq'), mem.alloc([128, 512], F32, 'mean'), mem.alloc([128, 512], F32, 'rstd'),
                   mem.alloc([128, 512], F32, 'msq'), mem.alloc([128, 2, 512], BF16, 'ybf')]
            load_w_bf16(w_out[l], wo, kwo, NKC, D, stage)
            for i in range(NTILE):
                t0 = i * 512
                v = tile_v(i)
                xa, kx = xT_t[i % 2]
                ma, kmx = mx_t[i % 2]
                ya, ky = ya_t[i % 2]
                s.dma('sp', xa, fm_view(XT, t0, 512), reads=[('XT', i)], writes=[(kx, kc) for kc in range(NKC)], sk=kx)
                s.dma('sp', ma, fm_view(MIXT, t0, 512), reads=[('MIXT', i)], writes=[kmx], sk=kmx)
                for oc in range(NKC):
                    ps, kp = bank(0, 4)
                    for kc in range(NKC):
                        s.op('pe', lambda e, ps=ps, kc=kc, oc=oc, ma=ma: e.matmul(
                            ps[:, :], lhsT=wo[:, kc, oc * 128:(oc + 1) * 128], rhs=ma[:, kc, :], start=(kc == 0), stop=(kc == NKC - 1)),
                            reads=[kwo, kmx], writes=[kp])
                    s.op('dve', lambda e, ps=ps, oc=oc, xa=xa, v=v, ya=ya: e.scalar_tensor_tensor(
                        out=ya[:, oc, :], in0=ps[:, :], scalar=modT[:, G1 + oc, v:v + 1], in1=xa[:, oc, :], op0=ALU.mult, op1=ALU.add),
                        reads=[kp, (kx, oc), k_mod], writes=[(ky, oc)])
                ln_apply(ya, ky, xa, kx, l, 0, 1, tmp)
                s.dma('pool', fm_view(XT, t0, 512), xa, reads=[(kx, kc) for kc in range(NKC)], writes=[('XT', i)], sk=kx)
            s.barrier()
            mem.release(m0)

        def phase_p3b(l):
            m0 = mem.mark()
            wf, kwf = mem.alloc([128, NKC, 2 * FF], BF16, 'wf')
            stage = [mem.alloc([128, 1024], F32, 'wst') for _ in range(3)]
            xT_t = [mem.alloc([128, NKC, 512], F32, 'xT3b') for _ in range(1)]
            h2, kh2 = mem.alloc([128, NKC, 512], BF16, 'h2')
            sg_t = [mem.alloc([128, 512], F32, 'sg') for _ in range(2)]
            u_t = [mem.alloc([128, NFC, 512], BF16, 'u') for _ in range(2)]
            load_w_bf16(w_ffn_in[l], wf, kwf, NKC, 2 * FF, stage)
            for i in range(NTILE):
                t0 = i * 512
                v = tile_v(i)
                xa, kx = xT_t[0]
                s.dma('sp', xa, fm_view(XT, t0, 512), reads=[('XT', i)], writes=[kx], sk=kx)
                for kc in range(NKC):
                    s.op('dve', lambda e, kc=kc, xa=xa, v=v: e.tensor_scalar(
                        out=h2[:, kc, :], in0=xa[:, kc, :], scalar1=modT[:, SC2 + kc, v:v + 1], scalar2=modT[:, SH2 + kc, v:v + 1],
                        op0=ALU.mult, op1=ALU.add), reads=[kx, k_mod], writes=[(kh2, kc)])
                ua, ku = u_t[i % 2]
                for f in range(NFC):
                    pg, kpg = bank(0, 4)
                    pu, kpu = bank(4, 8)
                    for kc in range(NKC):
                        s.op('pe', lambda e, pg=pg, kc=kc, f=f: e.matmul(
                            pg[:, :], lhsT=wf[:, kc, f * 128:(f + 1) * 128], rhs=h2[:, kc, :], start=(kc == 0), stop=(kc == NKC - 1)),
                            reads=[kwf, (kh2, kc)], writes=[kpg])
                    for kc in range(NKC):
                        s.op('pe', lambda e, pu=pu, kc=kc, f=f: e.matmul(
                            pu[:, :], lhsT=wf[:, kc, FF + f * 128:FF + (f + 1) * 128], rhs=h2[:, kc, :], start=(kc == 0), stop=(kc == NKC - 1)),
                            reads=[kwf, (kh2, kc)], writes=[kpu])
                    sg, ksg = sg_t[f % 2]
                    s.op('act', lambda e, pg=pg, sg=sg: e.activation(out=sg, in_=pg[:, :], func=AF.Silu), reads=[kpg], writes=[ksg])
                    s.op('dve', lambda e, pu=pu, sg=sg, f=f, ua=ua: e.tensor_tensor(out=ua[:, f, :], in0=pu[:, :], in1=sg, op=ALU.mult),
                         reads=[kpu, ksg], writes=[(ku, f)])
                s.dma('pool', fm_view(UT, t0, 512), ua, reads=[(ku, f) for f in range(NFC)], writes=[('UT', i)], sk=ku)
            s.barrier()
            mem.release(m0)

        def phase_p4(l):
            m0 = mem.mark()
            w2, kw2 = mem.alloc([128, NFC, D], BF16, 'w2')
            stage = [mem.alloc([128, 1024], F32, 'wst') for _ in range(3)]
            xT_t = [mem.alloc([128, NKC, 512], F32, 'xT4') for _ in range(2)]
            u_t = [mem.alloc([128, NFC, 512], BF16, 'u4') for _ in range(2)]
            ya_t = [mem.alloc([128, NKC, 512], F32, 'ya4') for _ in range(2)]
            tmp = [mem.alloc([128, 2, 512], BF16, 'sq'), mem.alloc([128, 512], F32, 'mean'), mem.alloc([128, 512], F32, 'rstd'),
                   mem.alloc([128, 512], F32, 'msq'), mem.alloc([128, 2, 512], BF16, 'ybf')]
            load_w_bf16(w_ffn_out[l], w2, kw2, NFC, D, stage)
            for i in range(NTILE):
                t0 = i * 512
                v = tile_v(i)
                xa, kx = xT_t[i % 2]
                ua, ku = u_t[i % 2]
                ya, ky = ya_t[i % 2]
                s.dma('sp', xa, fm_view(XT, t0, 512), reads=[('XT', i)], writes=[(kx, kc) for kc in range(NKC)], sk=kx)
                s.dma('sp', ua, fm_view(UT, t0, 512), reads=[('UT', i)], writes=[ku], sk=ku)
                for oc in range(NKC):
                    ps, kp = bank(0, 4)
                    for f in range(NFC):
                        s.op('pe', lambda e, ps=ps, f=f, oc=oc, ua=ua: e.matmul(
                            ps[:, :], lhsT=w2[:, f, oc * 128:(oc + 1) * 128], rhs=ua[:, f, :], start=(f == 0), stop=(f == NFC - 1)),
                            reads=[kw2, ku], writes=[kp])
                    s.op('dve', lambda e, ps=ps, oc=oc, xa=xa, v=v, ya=ya: e.scalar_tensor_tensor(
                        out=ya[:, oc, :], in0=ps[:, :], scalar=modT[:, G2 + oc, v:v + 1], in1=xa[:, oc, :], op0=ALU.mult, op1=ALU.add),
                        reads=[kp, (kx, oc), k_mod], writes=[(ky, oc)])
                ln_apply(ya, ky, xa, kx, l, 2, 3, tmp)
                s.dma('pool', fm_view(XT, t0, 512), xa, reads=[(kx, kc) for kc in range(NKC)], writes=[('XT', i)], sk=kx)
            s.barrier()
            mem.release(m0)

        SEQS = ([dict(t0=0, T=TS, sample=True, j=-1)] if NST else []) + \
               [dict(t0=TS + 256 * j, T=256, sample=False, j=j) for j in range(NPR)]

        def small_w(dram2d, rows, cols, dt, name, stage):
            dst, kd = mem.alloc([rows, cols], dt, name)
            sa, sk_ = stage
            s.dma('sp', sa[0:rows, 0:cols], dram2d, writes=[sk_], sk=sk_)
            s.op('pool', lambda e: e.tensor_copy(out=dst, in_=sa[0:rows, 0:cols]), reads=[sk_], writes=[kd])
            return dst, kd

        def evac(i, out, in_, reads, writes):
            if i % 2 == 0:
                s.op('dve', lambda e: e.tensor_copy(out=out, in_=in_), reads=reads, writes=writes)
            else:
                s.op('act', lambda e: e.copy(out=out, in_=in_), reads=reads, writes=writes)

        def phase_mla(l):
            m0 = mem.mark()
            stage = mem.alloc([128, 1024], F32, 'wst')
            wq0, kwq0 = small_w(wq_d[l, 0:128, :], 128, 768, BF16, 'wq0', stage)
            wq1, kwq1 = small_w(wq_d[l, 128:256, :], 128, 768, BF16, 'wq1', stage)
            wp0, kwp0 = small_w(wqp_d[l, 0:128, :], 128, 768, BF16, 'wp0', stage)
            wp1, kwp1 = small_w(wqp_d[l, 128:256, :], 128, 768, BF16, 'wp1', stage)
            wqs, kwqs, wps, kwps = (wq0, wq1), (kwq0, kwq1), (wp0, wp1), (kwp0, kwp1)
            wk, kwk = small_w(wk_d[l], 128, 768, BF16, 'wk', stage)
            wv, kwv = small_w(wv_d[l], 128, 512, BF16, 'wv', stage)
            e96, ke96 = small_w(e96_d, 32, 96, BF16, 'e96', stage)
            qkn, kqkn = mem.alloc([128, 3], F32, 'qkn')
            s.dma('sp', qkn, qkn_d[l], writes=[kqkn], sk=kqkn)
            KMAX = 4608 if NST else 256
            KT, kKT = mem.alloc([96, 8, KMAX], BF16, 'KT')
            VA, kVA = mem.alloc([128, KMAX // 128, 8, 65], BF16, 'VA')
            s.op('pool', lambda e: e.memset(VA[:, :, :, 64:65], 1.0), writes=[(kVA, 'ones')])
            ckv, kckv = mem.alloc([128, 512], F32, 'ckv')
            sq, ksq = mem.alloc([128, 2, 512], F32, 'sqm')
            rstd, krstd = mem.alloc([128, 512], F32, 'rstdm')
            ckvn, kckvn = mem.alloc([128, 512], F32, 'ckvn')
            ckvb, kckvb = mem.alloc([128, 512], BF16, 'ckvb')
            kpe, kkpe = mem.alloc([32, 512], F32, 'kpe')
            kpp, kkpp = mem.alloc([32, 512], F32, 'kpp')
            kcs, kkcs = mem.alloc([32, 2, 512], F32, 'kcs')
            krb, kkrb = mem.alloc([32, 512], BF16, 'krb')
            ctm, kctm = mem.alloc([128, 4, 128], F32, 'ctm')
            ktm, kktm = mem.alloc([128, 4, 32], F32, 'ktm')
            cq, kcq = mem.alloc([128, 2, 512], F32, 'cq')
            cqn, kcqn = mem.alloc([128, 2, 512], BF16, 'cqn')
            cs96, kcs96 = mem.alloc([96, 2, 512], F32, 'cs96')
            crsr, kcrsr = mem.alloc([96, 2, 512], F32, 'crsr')
            t12, kt12 = mem.alloc([96, 2, 512], F32, 't12')
            qT_t = [mem.alloc([96, 512], BF16, 'qT') for _ in range(2)]
            PT_t = [mem.alloc([128, 512], BF16, 'PT') for _ in range(4)]
            rec, krec = mem.alloc([128, 4], F32, 'rec')
            att, katt = mem.alloc([128, 4, 512], BF16, 'att')
            mixo_t = [mem.alloc([128, 4, 512], BF16, 'mixo') for _ in range(2)]
            nev = [0]

            def rms_rstd(src2d_list, keys, n, div, extra_scale):
                ps, kp = bank(7, 8)
                for c, (a, ka) in enumerate(zip(src2d_list, keys)):
                    s.op('act', lambda e, a=a, c=c: e.activation(out=sq[:, c, 0:n], in_=a, func=AF.Square), reads=[ka], writes=[(ksq, c)])
                    s.op('pe', lambda e, c=c: e.matmul(ps[:, 0:n], lhsT=onesf, rhs=sq[:, c, 0:n], start=(c == 0), stop=(c == len(src2d_list) - 1)),
                         reads=[k_onesf, (ksq, c)], writes=[kp])
                s.op('act', lambda e: e.activation(out=rstd[:, 0:n], in_=ps[:, 0:n], func=AF.Sqrt, bias=epsln[:, 1:2], scale=1.0 / div),
                     reads=[kp, k_eps], writes=[krstd])
                s.op('dve', lambda e: e.reciprocal(out=rstd[:, 0:n], in_=rstd[:, 0:n]), reads=[krstd], writes=[krstd])
                if extra_scale != 1.0:
                    s.op('dve', lambda e: e.tensor_scalar_mul(out=rstd[:, 0:n], in0=rstd[:, 0:n], scalar1=extra_scale), reads=[krstd], writes=[krstd])

            def make_kv(k0, n):
                for h in range(8):
                    ps, kp = bank(0, 4)
                    s.op('pe', lambda e, ps=ps, h=h: e.matmul(ps[0:96, 0:n], lhsT=wk[:, h * 96:(h + 1) * 96], rhs=ckvb[:, 0:n], start=True, stop=False),
                         reads=[kwk, kckvb], writes=[kp])
                    s.op('pe', lambda e, ps=ps: e.matmul(ps[0:96, 0:n], lhsT=e96, rhs=krb[:, 0:n], start=False, stop=True),
                         reads=[ke96, kkrb], writes=[kp])
                    nev[0] += 1
                    evac(nev[0], KT[:, h, k0:k0 + n], ps[0:96, 0:n], [kp], [(kKT, h, k0)])
                for b in range(n // 128):
                    ps, kp = bank(0, 4)
                    s.op('pe', lambda e, ps=ps, b=b: e.matmul(ps[:, :], lhsT=ckvb[:, b * 128:(b + 1) * 128], rhs=wv, start=True, stop=True),
                         reads=[kwv, kckvb], writes=[kp])
                    kb = k0 // 128 + b
                    nev[0] += 1
                    evac(nev[0], VA[:, kb, :, 0:64], ps[:, :].rearrange("p (h e) -> p h e", e=64), [kp], [(kVA, kb)])

            for sq_ in SEQS:
                T, S0, samp, j = sq_['T'], sq_['t0'], sq_['sample'], sq_['j']
                nkeys = T + (512 if samp else 0)
                nkt = nkeys // 128
                for k0 in range(0, T, 512):
                    n = min(512, T - k0)
                    t0 = S0 + k0
                    s.dma('sp', ckv[:, 0:n], SCR['CKVT'][:, t0:t0 + n], reads=[('CKVT', 0, t0 // 512)], writes=[kckv], sk=kckv)
                    s.dma('sp', kpe[:, 0:n], SCR['KPET'][:, t0:t0 + n], reads=[('KPET', 0, t0 // 512)], writes=[kkpe], sk=kkpe)
                    rms_rstd([ckv[:, 0:n]], [kckv], n, 128.0, 1.0)
                    s.op('dve', lambda e, n=n: e.scalar_tensor_tensor(out=ckvn[:, 0:n], in0=ckv[:, 0:n], scalar=qkn[:, 2:3], in1=rstd[:, 0:n],
                                                                     op0=ALU.mult, op1=ALU.mult), reads=[kckv, kqkn, krstd], writes=[kckvn])
                    s.op('act', lambda e, n=n: e.copy(out=ckvb[:, 0:n], in_=ckvn[:, 0:n]), reads=[kckvn], writes=[kckvb])
                    if samp:
                        s.dma('sp', kpp[:, 0:n], SCR['KPEP'][:, t0:t0 + n], reads=[('KPEP', 0, t0 // 512)], writes=[kkpp], sk=kkpp)
                        s.dma_group('sp', [(kcs[:, 0, 0:n], kcos_d[:, k0:k0 + n]), (kcs[:, 1, 0:n], ksin_d[:, k0:k0 + n])], writes=[kkcs], sk=kkcs)
                        s.op('dve', lambda e, n=n: e.tensor_tensor(out=kpe[:, 0:n], in0=kpe[:, 0:n], in1=kcs[:, 0, 0:n], op=ALU.mult),
                             reads=[kkpe, kkcs], writes=[kkpe])
                        s.op('dve', lambda e, n=n: e.tensor_tensor(out=kpp[:, 0:n], in0=kpp[:, 0:n], in1=kcs[:, 1, 0:n], op=ALU.mult),
                             reads=[kkpp, kkcs], writes=[kkpp])
                        s.op('dve', lambda e, n=n: e.tensor_tensor(out=krb[:, 0:n], in0=kpe[:, 0:n], in1=kpp[:, 0:n], op=ALU.add),
                             reads=[kkpe, kkpp], writes=[kkrb])
                    else:
                        s.op('dve', lambda e, n=n: e.tensor_copy(out=krb[:, 0:n], in_=kpe[:, 0:n]), reads=[kkpe], writes=[kkrb])
                        for b in range(n // 128):
                            ps, kp = bank(4, 7)
                            s.op('pe', lambda e, ps=ps, b=b: e.transpose(out=ps[:, 0:128], in_=ckvn[:, b * 128:(b + 1) * 128], identity=idf),
                                 reads=[kckvn, k_idf], writes=[kp])
                            s.op('pe', lambda e, ps=ps, b=b: e.transpose(out=ps[:, 128:160], in_=kpe[0:32, b * 128:(b + 1) * 128], identity=idf[0:32, 0:32]),
                                 reads=[kkpe, k_idf], writes=[kp])
                            s.op('dve', lambda e, ps=ps, b=b: e.tensor_copy(out=ctm[:, b, :], in_=ps[:, 0:128]), reads=[kp], writes=[(kctm, b)])
                            s.op('act', lambda e, ps=ps, b=b: e.copy(out=ktm[:, b, :], in_=ps[:, 128:160]), reads=[kp], writes=[(kktm, b)])
                        nb = n // 128
                        s.dma('pool', ckv_out[j, l, k0:k0 + n, :].rearrange("(b p) c -> p b c", p=128), ctm[:, 0:nb, :],
                              reads=[(kctm, b) for b in range(nb)], sk=kctm)
                        s.dma('pool', kpe_out[j, l, k0:k0 + n, :].rearrange("(b p) c -> p b c", p=128), ktm[:, 0:nb, :],
                              reads=[(kktm, b) for b in range(nb)], sk=kktm)
                    make_kv(k0, n)
                if samp:
                    s.dma('sp', ctm, cckv_d[l].rearrange("(b p) c -> p b c", p=128), writes=[(kctm, b) for b in range(4)], sk=kctm)
                    s.dma('sp', ktm, ckpe_d[l].rearrange("(b p) c -> p b c", p=128), writes=[(kktm, b) for b in range(4)], sk=kktm)
                    ps, kp = bank(4, 7)
                    ps2, kp2 = bank(4, 7)
                    for b in range(4):
                        s.op('pe', lambda e, b=b, ps=ps: e.transpose(out=ps[:, b * 128:(b + 1) * 128], in_=ctm[:, b, :], identity=idf),
                             reads=[(kctm, b), k_idf], writes=[kp])
                        s.op('pe', lambda e, b=b, ps2=ps2: e.transpose(out=ps2[0:32, b * 128:(b + 1) * 128], in_=ktm[:, b, :], identity=idf),
                             reads=[(kktm, b), k_idf], writes=[kp2])
                    s.op('dve', lambda e, ps=ps: e.tensor_copy(out=ckvb, in_=ps[:, :]), reads=[kp], writes=[kckvb])
                    s.op('act', lambda e, ps2=ps2: e.copy(out=krb, in_=ps2[0:32, :]), reads=[kp2], writes=[kkrb])
                    make_kv(T, 512)
                for q0 in range(0, T, 512):
                    n = min(512, T - q0)
                    nqb = n // 128
                    t0 = S0 + q0
                    ti = t0 // 512
                    s.dma('sp', cq[:, :, 0:n], SCR['CQT'].rearrange("(c p) t -> p c t", p=128)[:, :, t0:t0 + n],
                          reads=[('CQT', 0, ti), ('CQT', 128, ti)], writes=[kcq], sk=kcq)
                    rms_rstd([cq[:, 0, 0:n], cq[:, 1, 0:n]], [kcq, kcq], n, 256.0, ATT_SCALE)
                    for kc in range(2):
                        s.op('dve', lambda e, kc=kc, n=n: e.tensor_scalar_mul(out=cqn[:, kc, 0:n], in0=cq[:, kc, 0:n], scalar1=qkn[:, kc:kc + 1]),
                             reads=[kcq, kqkn], writes=[(kcqn, kc)])
                    if samp:
                        s.dma_group('sp', [(cs96[:, 0, 0:n], cos96_d[:, q0:q0 + n]), (cs96[:, 1, 0:n], sin96_d[:, q0:q0 + n])], writes=[kcs96], sk=kcs96)
                        for c in range(2):
                            s.op('dve', lambda e, c=c, n=n: e.tensor_tensor(out=crsr[:, c, 0:n], in0=cs96[:, c, 0:n], in1=rstd[0:96, 0:n], op=ALU.mult),
                                 reads=[kcs96, krstd], writes=[(kcrsr, c)])
                    mo_, kmo = mixo_t[(t0 // 512) % 2]
                    def qproj(h):
                        qT, kqT = qT_t[h % 2]
                        psA, kpA = bank(0, 2)
                        for kc in range(2):
                            s.op('pe', lambda e, psA=psA, kc=kc, h=h, n=n: e.matmul(
                                psA[0:96, 0:n], lhsT=wqs[kc][:, h * 96:(h + 1) * 96], rhs=cqn[:, kc, 0:n], start=(kc == 0), stop=(kc == 1)),
                                reads=[kwqs[kc], (kcqn, kc)], writes=[kpA])
                        if samp:
                            psB, kpB = bank(0, 2)
                            for kc in range(2):
                                s.op('pe', lambda e, psB=psB, kc=kc, h=h, n=n: e.matmul(
                                    psB[0:96, 0:n], lhsT=wps[kc][:, h * 96:(h + 1) * 96], rhs=cqn[:, kc, 0:n], start=(kc == 0), stop=(kc == 1)),
                                    reads=[kwps[kc], (kcqn, kc)], writes=[kpB])
                            s.op('dve', lambda e, psA=psA, n=n: e.tensor_tensor(out=t12[:, 0, 0:n], in0=psA[0:96, 0:n], in1=crsr[:, 0, 0:n], op=ALU.mult),
                                 reads=[kpA, (kcrsr, 0)], writes=[(kt12, 0)])
                            s.op('dve', lambda e, psB=psB, n=n: e.tensor_tensor(out=t12[:, 1, 0:n], in0=psB[0:96, 0:n], in1=crsr[:, 1, 0:n], op=ALU.mult),
                                 reads=[kpB, (kcrsr, 1)], writes=[(kt12, 1)])
                            s.op('pool', lambda e, qT=qT, n=n: e.tensor_tensor(out=qT[:, 0:n], in0=t12[:, 0, 0:n], in1=t12[:, 1, 0:n], op=ALU.add),
                                 reads=[(kt12, 0), (kt12, 1)], writes=[kqT])
                        else:
                            s.op('dve', lambda e, psA=psA, qT=qT, n=n: e.tensor_tensor(out=qT[:, 0:n], in0=psA[0:96, 0:n], in1=rstd[0:96, 0:n], op=ALU.mult),
                                 reads=[kpA, krstd], writes=[kqT])

                    def attend(h):
                        qT, kqT = qT_t[h % 2]
                        psO, kpO = bank(5, 7)
                        LA = 2
                        pend = {}
                        for it in range(nkt + LA):
                            if it < nkt:
                                kt = it
                                psS, kpS = bank(2, 5)
                                kvk = [(kKT, h, (kt * 128) // 512 * 512 if kt * 128 < T else T)]
                                s.op('pe', lambda e, psS=psS, kt=kt, h=h, qT=qT, n=n: e.matmul(
                                    psS[:, 0:n], lhsT=KT[:, h, kt * 128:(kt + 1) * 128], rhs=qT[:, 0:n], start=True, stop=True),
                                    reads=kvk + [kqT], writes=[kpS])
                                PT, kPT = PT_t[kt % len(PT_t)]
                                s.op('act', lambda e, psS=psS, PT=PT, n=n: e.activation(out=PT[:, 0:n], in_=psS[:, 0:n], func=AF.Exp),
                                     reads=[kpS], writes=[kPT])
                                pend[kt] = (PT, kPT)
                            if it >= LA:
                                kt = it - LA
                                PT, kPT = pend.pop(kt)
                                for qb in range(nqb):
                                    s.op('pe', lambda e, psO=psO, PT=PT, qb=qb, kt=kt, h=h: e.matmul(
                                        psO[:, qb * 65:(qb + 1) * 65], lhsT=PT[:, qb * 128:(qb + 1) * 128], rhs=VA[:, kt, h, :],
                                        start=(kt == 0 and qb == 0), stop=(kt == nkt - 1), skip_group_check=True),
                                        reads=[kPT, (kVA, kt), (kVA, 'ones')], writes=[kpO])
                        pO3 = psO[:, 0:nqb * 65].rearrange("p (q e) -> p q e", e=65)
                        s.op('dve', lambda e, pO3=pO3, nqb=nqb: e.reciprocal(out=rec[:, 0:nqb], in_=pO3[:, :, 64]), reads=[kpO], writes=[krec])
                        s.op('dve', lambda e, pO3=pO3, nqb=nqb, h=h: e.tensor_tensor(
                            out=att[:, 0:nqb, h * 64:(h + 1) * 64], in0=pO3[:, :, 0:64], in1=rec[:, 0:nqb].unsqueeze(2).to_broadcast([128, nqb, 64]),
                            op=ALU.mult), reads=[kpO, krec], writes=[(katt, h)])

                    qproj(0)
                    for h in range(8):
                        if h + 1 < 8:
                            qproj(h + 1)
                        attend(h)
                    for qb in range(nqb):
                        ps, kp = bank(7, 8)
                        pbf = ps.bitcast(BF16)
                        for c in range(4):
                            s.op('pe', lambda e, pbf=pbf, c=c, qb=qb: e.transpose(out=pbf[:, c * 128:(c + 1) * 128], in_=att[:, qb, c * 128:(c + 1) * 128], identity=idb),
                                 reads=[(katt, 2 * c), (katt, 2 * c + 1), k_idb], writes=[kp])
                        nev[0] += 1
                        evac(nev[0], mo_[:, :, qb * 128:(qb + 1) * 128], pbf[:, 0:512].rearrange("p (c t) -> p c t", t=128), [kp], [(kmo, qb)])
                    s.dma('pool', MIXT.rearrange("(c p) t -> p c t", p=128)[:, 0:4, t0:t0 + n], mo_[:, :, 0:n],
                          reads=[(kmo, qb) for qb in range(nqb)], writes=[('MIXT_A', t0)], sk=kmo)
            s.barrier()
            mem.release(m0)

        def phase_mlstm(l):
            m0_ = mem.mark()
            mc, kmc = mem.alloc([64, 4, 64], F32, 'mconst')
            ng, kng = mem.alloc([64, 256], F32, 'mlng')
            s.dma('sp', mc, mconst_d, writes=[kmc], sk=kmc)
            s.dma('sp', ng, mlng_d[l], writes=[kng], sk=kng)
            rmask, krm = mem.alloc([64, 8, 64], F32, 'rmask')
            s.op('pool', lambda e: e.memset(rmask, 1.0), writes=[krm])
            s.op('pool', lambda e: e.memset(rmask[:, :, 0:1], 0.0), writes=[krm])
            A = lambda shape, dt=F32, name='m': mem.alloc(shape, dt, name)
            GI, kGI = A([64, 8, 64]); GF, kGF = A([64, 8, 64]); SP, kSP = A([64, 8, 64]); CS, kCS = A([64, 8, 64])
            NB, kNB = A([64, 8, 64]); Cc, kCc = A([64, 8, 64]); W, kW = A([64, 8, 64]); THR, kTHR = A([64, 8, 64])
            TOT, kTOT = A([64, 8]); CMAX, kCMAX = A([64, 8]); Gn, kGn = A([64, 8])
            ROW, kROW = A([4, 4, 64]); m0t, km0t = A([4, 2]); MN, kMN = A([4, 2, 64]); Rr, kRr = A([4, 2, 64])
            MP, kMP = A([4, 2, 64]); SCr, kSCr = A([4, 2, 64])
            TMPc, kTMPc = A([64, 16]); Rc, kRc = A([64, 8]); SCc, kSCc = A([64, 8])
            Wt, kWt = A([64, 8, 64]); THRt, kTHRt = A([64, 8, 64]); BD, kBD = A([64, 64, 8]); scB, kscB = A([64, 64, 8])
            Caug, kC = A([64, 4, 65]); Cs, kCs = A([64, 4, 65]); Csb, kCsb = A([64, 4, 65], BF16)
            q32_t = [A([64, 4, 512]) for _ in range(1)]
            k32_t = [A([64, 4, 512]) for _ in range(1)]
            qb_t = [A([64, 4, 512], BF16) for _ in range(2)]
            kb_t = [A([64, 4, 512], BF16) for _ in range(2)]
            vv_t = [A([64, 8, 256]) for _ in range(1)]
            kk_t = [A([64, 8, 256]) for _ in range(1)]
            kkb_t = [A([64, 8, 256], BF16) for _ in range(2)]
            va_t = [A([64, 8, 4, 65], BF16) for _ in range(2)]
            for va, kva in va_t:
                s.op('pool', lambda e, va=va: e.memset(va[:, :, :, 64:65], 1.0), writes=[(kva, 'ones')])
            vw_t = [A([64, 8, 4, 65], BF16) for _ in range(2)]
            PTg_t = [A([64, 8, 4, 64], BF16) for _ in range(2)]
            dC_t = [A([64, 8, 4, 65]) for _ in range(2)]
            hF_t = [A([64, 8, 256]) for _ in range(2)]
            mo_t = [A([64, 8, 256]) for _ in range(2)]
            hb_t = [A([64, 8, 256]) for _ in range(2)]
            den, kden = A([64, 2, 4]); rec, krec = A([64, 2, 4])
            s1, ks1 = A([64, 32]); s2, ks2 = A([64, 32]); s3, ks3 = A([64, 32])
            sqx, ksqx = q32_t[0]
            sqx = sqx.rearrange("p h (a b) -> p (h a) b", b=256)
            sig, ksig = k32_t[0]
            sig = sig.rearrange("p h (a b) -> p (h a) b", b=256)
            xo_t = [A([64, 8, 256], BF16) for _ in range(1)]
            mixo_t = [A([128, 2, 512], BF16) for _ in range(1)]
            maskF, maskB, J64, J4 = mc[:, 0, :], mc[:, 1, :], mc[:, 2, :], mc[0:4, 3, 0:4]
            gcount = [0]
            for sq_ in SEQS:
                T, S0, samp, j = sq_['T'], sq_['t0'], sq_['sample'], sq_['j']
                nch = T // 64
                Jn = J64 if nch == 64 else J4
                In = idf[0:nch, 0:nch]
                def gview(name):
                    return SCR[name][:, S0:S0 + T].rearrange("h (j t) -> j h t", t=64)
                rk = lambda nm: [(nm, 0, i) for i in range(S0 // 512, (S0 + T + 511) // 512)]
                s.dma_group('sp', [(GI[0:nch, 0:4, :], gview('MIF')), (GI[0:nch, 4:8, :], gview('MIB'))], reads=rk('MIF') + rk('MIB'), writes=[kGI], sk=kGI)
                s.dma_group('sp', [(GF[0:nch, 0:4, :], gview('MFF')), (GF[0:nch, 4:8, :], gview('MFB'))], reads=rk('MFF') + rk('MFB'), writes=[kGF], sk=kGF)
                s.op('act', lambda e, nch=nch: e.activation(out=SP[0:nch], in_=GF[0:nch], func=AF.Exp, scale=-1.0), reads=[kGF], writes=[kSP])
                s.op('act', lambda e, nch=nch: e.activation(out=SP[0:nch], in_=SP[0:nch], func=AF.Ln, bias=1.0), reads=[kSP], writes=[kSP])
                fl = lambda a, nch=nch: a[0:nch].rearrange("p a b -> p (a b)")
                s.op('dve', lambda e, nch=nch, fl=fl: e.tensor_tensor_scan(out=fl(CS), data0=fl(rmask), data1=fl(SP), initial=0.0, op0=ALU.mult, op1=ALU.add),
                     reads=[krm, kSP], writes=[kCS])
                s.op('dve', lambda e, nch=nch: e.tensor_copy(out=TOT[0:nch], in_=CS[0:nch, :, 63]), reads=[kCS], writes=[kTOT])
                s.op('dve', lambda e, nch=nch: e.tensor_copy(out=NB[0:nch, 0:4, :], in_=CS[0:nch, 0:4, :]), reads=[kCS], writes=[kNB])
                s.op('dve', lambda e, nch=nch: e.tensor_tensor(out=NB[0:nch, 4:8, :], in0=SP[0:nch, 4:8, :], in1=CS[0:nch, 4:8, :], op=ALU.subtract),
                     reads=[kCS, kSP, kNB], writes=[kNB])
                s.op('dve', lambda e, nch=nch: e.tensor_tensor(out=NB[0:nch, 4:8, :], in0=NB[0:nch, 4:8, :],
                                                              in1=TOT[0:nch, 4:8].unsqueeze(2).to_broadcast([nch, 4, 64]), op=ALU.add),
                     reads=[kNB, kTOT], writes=[kNB])
                s.op('dve', lambda e, nch=nch: e.tensor_tensor(out=Cc[0:nch], in0=GI[0:nch], in1=NB[0:nch], op=ALU.add), reads=[kGI, kNB], writes=[kCc])
                s.op('dve', lambda e, nch=nch: e.tensor_reduce(out=CMAX[0:nch], in_=Cc[0:nch], axis=AX.X, op=ALU.max), reads=[kCc], writes=[kCMAX])
                s.op('dve', lambda e, nch=nch: e.tensor_scalar_mul(out=Gn[0:nch], in0=TOT[0:nch], scalar1=-1.0), reads=[kTOT], writes=[kGn])
                ps, kp = bank(0, 8)
                for qi, (src_, ksrc, lo, mat) in enumerate(((CMAX, kCMAX, 0, In), (Gn, kGn, 0, In), (CMAX, kCMAX, 4, Jn), (Gn, kGn, 4, Jn))):
                    s.op('pe', lambda e, ps=ps, qi=qi, src_=src_, lo=lo, mat=mat, nch=nch: e.matmul(
                        ps[0:4, qi * nch:(qi + 1) * nch], lhsT=src_[0:nch, lo:lo + 4], rhs=mat, start=True, stop=True),
                        reads=[ksrc, k_idf, kmc], writes=[kp])
                s.op('dve', lambda e, ps=ps, nch=nch: e.tensor_copy(out=ROW[:, :, 0:nch], in_=ps[0:4, 0:4 * nch].rearrange("p (a b) -> p a b", b=nch)),
                     reads=[kp], writes=[kROW])
                if samp:
                    s.dma('sp', m0t, stm_d[l].rearrange("d h -> h d"), writes=[km0t], sk=km0t, allow_slow_non_contiguous=True)
                else:
                    s.op('dve', lambda e: e.memset(m0t, 0.0), writes=[km0t])
                for d in range(2):
                    s.op('dve', lambda e, d=d, nch=nch: e.tensor_tensor_scan(out=MN[:, d, 0:nch], data0=ROW[:, 2 * d, 0:nch], data1=ROW[:, 2 * d + 1, 0:nch],
                                                                            initial=m0t[:, d:d + 1], op0=ALU.max, op1=ALU.add),
                         reads=[kROW, km0t], writes=[(kMN, d)])
                    s.op('dve', lambda e, d=d, nch=nch: e.tensor_tensor(out=Rr[:, d, 0:nch], in0=MN[:, d, 0:nch], in1=ROW[:, 2 * d + 1, 0:nch], op=ALU.subtract),
                         reads=[(kMN, d), kROW], writes=[(kRr, d)])
                    s.op('dve', lambda e, d=d: e.tensor_copy(out=MP[:, d, 0:1], in_=m0t[:, d:d + 1]), reads=[km0t], writes=[(kMP, d, 0)])
                    s.op('dve', lambda e, d=d, nch=nch: e.tensor_copy(out=MP[:, d, 1:nch], in_=MN[:, d, 0:nch - 1]), reads=[(kMN, d)], writes=[(kMP, d, 1)])
                    s.op('dve', lambda e, d=d, nch=nch: e.tensor_tensor(out=SCr[:, d, 0:nch], in0=MP[:, d, 0:nch], in1=Rr[:, d, 0:nch], op=ALU.subtract),
                         reads=[(kMP, d, 0), (kMP, d, 1), (kRr, d)], writes=[(kSCr, d)])
                    s.op('act', lambda e, d=d, nch=nch: e.activation(out=SCr[:, d, 0:nch], in_=SCr[:, d, 0:nch], func=AF.Exp), reads=[(kSCr, d)], writes=[(kSCr, d)])
                    if not samp:
                        s.dma('pool', m_out[j, l, d, :].rearrange("(h o) -> h o", o=1), MN[:, d, nch - 1:nch], reads=[(kMN, d)], sk=(kMN, d))
                ps, kp = bank(0, 8)
                for qi, (src_, ksrc, d) in enumerate(((Rr, kRr, 0), (Rr, kRr, 1), (SCr, kSCr, 0), (SCr, kSCr, 1))):
                    s.op('pe', lambda e, ps=ps, qi=qi, src_=src_, d=d, nch=nch: e.matmul(
                        ps[0:nch, qi * 4:(qi + 1) * 4], lhsT=src_[:, d, 0:nch], rhs=idf[0:4, 0:4], start=True, stop=True),
                        reads=[(ksrc, d), k_idf], writes=[kp])
                s.op('dve', lambda e, ps=ps, nch=nch: e.tensor_copy(out=TMPc[0:nch], in_=ps[0:nch, 0:16]), reads=[kp], writes=[kTMPc])
                ps2, kp2 = bank(0, 8)
                for qi, lo in enumerate((4, 12)):
                    s.op('pe', lambda e, ps2=ps2, qi=qi, lo=lo, nch=nch, Jn=Jn: e.matmul(
                        ps2[0:nch, qi * 4:(qi + 1) * 4], lhsT=Jn, rhs=TMPc[0:nch, lo:lo + 4], start=True, stop=True),
                        reads=[kTMPc, kmc], writes=[kp2])
                s.op('dve', lambda e, nch=nch: e.tensor_copy(out=Rc[0:nch, 0:4], in_=TMPc[0:nch, 0:4]), reads=[kTMPc], writes=[(kRc, 0)])
                s.op('dve', lambda e, nch=nch, ps2=ps2: e.tensor_copy(out=Rc[0:nch, 4:8], in_=ps2[0:nch, 0:4]), reads=[kp2], writes=[(kRc, 1)])
                s.op('dve', lambda e, nch=nch: e.tensor_copy(out=SCc[0:nch, 0:4], in_=TMPc[0:nch, 8:12]), reads=[kTMPc], writes=[(kSCc, 0)])
                s.op('dve', lambda e, nch=nch, ps2=ps2: e.tensor_copy(out=SCc[0:nch, 4:8], in_=ps2[0:nch, 4:8]), reads=[kp2], writes=[(kSCc, 1)])
                rcb = lambda nch=nch: Rc[0:nch].unsqueeze(2).to_broadcast([nch, 8, 64])
                s.op('dve', lambda e, nch=nch, rcb=rcb: e.tensor_tensor(out=W[0:nch], in0=Cc[0:nch], in1=rcb(), op=ALU.subtract),
                     reads=[kCc, (kRc, 0), (kRc, 1)], writes=[kW])
                s.op('act', lambda e, nch=nch: e.activation(out=W[0:nch], in_=W[0:nch], func=AF.Exp), reads=[kW], writes=[kW])
                s.op('dve', lambda e, nch=nch, rcb=rcb: e.tensor_tensor(out=THR[0:nch], in0=NB[0:nch], in1=rcb(), op=ALU.subtract),
                     reads=[kNB, (kRc, 0), (kRc, 1)], writes=[kTHR])
                s.op('act', lambda e, nch=nch: e.activation(out=THR[0:nch], in_=THR[0:nch], func=AF.Exp), reads=[kTHR], writes=[kTHR])
                for src_, ksrc, dst, kdst in ((W, kW, Wt, kWt), (THR, kTHR, THRt, kTHRt)):
                    ps, kp = bank(0, 8)
                    for r in range(8):
                        s.op('pe', lambda e, ps=ps, r=r, src_=src_, nch=nch, In=In: e.transpose(
                            out=ps[0:64, r * nch:(r + 1) * nch], in_=src_[0:nch, r, :], identity=In), reads=[ksrc, k_idf], writes=[kp])
                    s.op('dve', lambda e, ps=ps, dst=dst, nch=nch: e.tensor_copy(out=dst[:, :, 0:nch], in_=ps[0:64, 0:8 * nch].rearrange("p (a b) -> p a b", b=nch)),
                         reads=[kp], writes=[kdst])
                s.op('dve', lambda e, nch=nch, In=In: e.tensor_tensor(
                    out=BD[0:nch, 0:nch, :], in0=In.unsqueeze(2).to_broadcast([nch, nch, 8]),
                    in1=SCc[0:nch].unsqueeze(1).to_broadcast([nch, nch, 8]), op=ALU.mult), reads=[k_idf, (kSCc, 0), (kSCc, 1)], writes=[kBD])
                ps, kp = bank(0, 8)
                s.op('pe', lambda e, ps=ps, nch=nch: e.matmul(ps[0:64, 0:nch * 8], lhsT=onesf[0:nch, 0:64],
                                                             rhs=BD[0:nch, 0:nch, :].rearrange("p a b -> p (a b)"), start=True, stop=True),
                     reads=[k_onesf, kBD], writes=[kp])
                s.op('dve', lambda e, ps=ps, nch=nch: e.tensor_copy(out=scB[:, 0:nch, :], in_=ps[0:64, 0:nch * 8].rearrange("p (a b) -> p a b", b=8)),
                     reads=[kp], writes=[kscB])
                G = min(8, nch)
                GT = 64 * G
                ngrp = nch // G
                for d in range(2):
                    if samp:
                        s.dma('sp', Caug[:, :, 0:64], stC_d[l, d].rearrange("h a b -> a h b"), writes=[kC], sk=kC)
                        s.dma('sp', Caug[:, :, 64], stn_d[l, d].rearrange("h a -> a h"), writes=[(kC, 'n')], sk=(kC, 'n'), allow_slow_non_contiguous=True)
                    else:
                        s.op('dve', lambda e: e.memset(Caug, 0.0), writes=[kC, (kC, 'n')])
                    mask = maskF if d == 0 else maskB
                    gorder = list(range(ngrp)) if d == 0 else list(range(ngrp - 1, -1, -1))
                    corder = list(range(G)) if d == 0 else list(range(G - 1, -1, -1))

                    def load_group(gi, d=d, G=G, GT=GT, S0=S0):
                        t0 = S0 + gi * GT
                        ti = t0 // 512
                        b = gcount[0] % 2
                        gcount[0] += 1
                        cx = dict(gi=gi, t0=t0, ti=ti, b=b)
                        q32, kq32 = q32_t[0]; k32, kk32 = k32_t[0]; vv, kvv = vv_t[0]; kk, kkk = kk_t[0]
                        qb, kqb = qb_t[b]; kb, kkb = kb_t[b]; va, kva = va_t[b]; kkb2, kkkb2 = kkb_t[b]
                        cx.update(qb=qb, kqb=kqb, kb=kb, kkb=kkb, va=va, kva=kva, kkb2=kkb2, kkkb2=kkkb2,
                                  vw=vw_t[b], PT=PTg_t[b], dC=dC_t[b], hb=hb_t[b])
                        s.dma('sp', q32[:, :, 0:GT], SCR['MQT'][:, t0:t0 + GT].rearrange("(h p) t -> p h t", p=64),
                              reads=[('MQT', 0, ti), ('MQT', 128, ti)], writes=[kq32], sk=kq32)
                        s.dma('sp', k32[:, :, 0:GT], SCR['MKT'][:, t0:t0 + GT].rearrange("(h p) t -> p h t", p=64),
                              reads=[('MKT', 0, ti), ('MKT', 128, ti)], writes=[kk32], sk=kk32)
                        s.dma('sp', vv[:, 0:G, :], TMV[t0:t0 + GT, 0:256].rearrange("(j t) c -> t j c", t=64), reads=[('TMV', ti)], writes=[kvv], sk=kvv)
                        s.dma('sp', kk[:, 0:G, :], TMV[t0:t0 + GT, 512:768].rearrange("(j t) c -> t j c", t=64), reads=[('TMV', ti)], writes=[kkk], sk=kkk)
                        s.op('act', lambda e: e.copy(out=qb[:, :, 0:GT], in_=q32[:, :, 0:GT]), reads=[kq32], writes=[kqb])
                        s.op('act', lambda e: e.mul(out=kb[:, :, 0:GT], in_=k32[:, :, 0:GT], mul=0.125), reads=[kk32], writes=[kkb])
                        s.op('pool', lambda e: e.tensor_copy(out=va[:, 0:G, :, 0:64], in_=vv[:, 0:G, :].rearrange("p g (h e) -> p g h e", e=64)),
                             reads=[kvv], writes=[kva])
                        s.op('act', lambda e: e.mul(out=kkb2[:, 0:G, :], in_=kk[:, 0:G, :], mul=0.125), reads=[kkk], writes=[kkkb2])
                        if d == 1:
                            hF, khF = hF_t[b]; mo_, kmo = mo_t[b]
                            cx.update(hF=hF, khF=khF, mo=mo_, kmo=kmo)
                            s.dma('sp', hF[:, 0:G, :], HF[t0:t0 + GT, :].rearrange("(j t) c -> t j c", t=64), reads=[('HF', t0)], writes=[khF], sk=khF)
                            s.dma('sp', mo_[:, 0:G, :], TMV[t0:t0 + GT, 256:512].rearrange("(j t) c -> t j c", t=64), reads=[('TMV', ti)], writes=[kmo], sk=kmo)
                        return cx

                    def stage1(cx, jl, d=d, mask=mask, G=G, GT=GT):
                        jg = cx['gi'] * G + jl
                        cs_ = slice(jl * 64, (jl + 1) * 64)
                        qb, kb, va, kkb2 = cx['qb'], cx['kb'], cx['va'], cx['kkb2']
                        (vw, kvw), (PT, kPT), (dC, kdC) = cx['vw'], cx['PT'], cx['dC']
                        psA, kpA = bank(0, 2)
                        for h in range(4):
                            s.op('pe', lambda e, h=h: e.matmul(psA[0:64, h * 64:(h + 1) * 64], lhsT=kb[:, h, cs_], rhs=qb[:, h, cs_], start=True, stop=True),
                                 reads=[cx['kkb'], cx['kqb']], writes=[kpA])
                        s.op('dve', lambda e: e.tensor_tensor(out=PT[:, jl], in0=psA[0:64, 0:256].rearrange("p (h t) -> p h t", t=64),
                                                              in1=mask.unsqueeze(1).to_broadcast([64, 4, 64]), op=ALU.mult),
                             reads=[kpA, kmc], writes=[(kPT, jl)])
                        s.op('pool', lambda e: e.tensor_tensor(out=vw[:, jl], in0=va[:, jl], in1=Wt[:, d * 4:d * 4 + 4, jg].unsqueeze(2).to_broadcast([64, 4, 65]), op=ALU.mult),
                             reads=[cx['kva'], (cx['kva'], 'ones'), kWt], writes=[(kvw, jl)])
                        psC, kpC = bank(2, 4)
                        for h in range(4):
                            s.op('pe', lambda e, h=h: e.matmul(psC[0:64, h * 65:(h + 1) * 65], lhsT=kkb2[:, jl, h * 64:(h + 1) * 64], rhs=vw[:, jl, h, :], start=True, stop=True),
                                 reads=[cx['kkkb2'], (kvw, jl)], writes=[kpC])
                        s.op('act', lambda e: e.copy(out=dC[:, jl], in_=psC[0:64, 0:260].rearrange("p (h e) -> p h e", e=65)), reads=[kpC], writes=[(kdC, jl)])

                    def stage2(cx, jl, d=d, G=G, GT=GT):
                        jg = cx['gi'] * G + jl
                        cs_ = slice(jl * 64, (jl + 1) * 64)
                        qb = cx['qb']
                        (vw, kvw), (PT, kPT), (dC, kdC), (hb, khb) = cx['vw'], cx['PT'], cx['dC'], cx['hb']
                        s.op('dve', lambda e: e.tensor_tensor(out=Cs, in0=Caug, in1=scB[:, jg, d * 4:d * 4 + 4].unsqueeze(2).to_broadcast([64, 4, 65]), op=ALU.mult),
                             reads=[kC, (kC, 'n'), kscB], writes=[kCs])
                        s.op('act', lambda e: e.copy(out=Csb, in_=Cs), reads=[kCs], writes=[kCsb])
                        s.op('dve', lambda e: e.tensor_tensor(out=Caug, in0=Cs, in1=dC[:, jl], op=ALU.add), reads=[kCs, (kdC, jl)], writes=[kC, (kC, 'n')])
                        psO, kpO = bank(4, 7)
                        for h in range(4):
                            s.op('pe', lambda e, h=h: e.matmul(psO[0:64, h * 65:(h + 1) * 65], lhsT=PT[:, jl, h, :], rhs=vw[:, jl, h, :], start=(h == 0), stop=False, skip_group_check=True),
                                 reads=[(kPT, jl), (kvw, jl)], writes=[kpO])
                            s.op('pe', lambda e, h=h: e.matmul(psO[0:64, h * 65:(h + 1) * 65], lhsT=qb[:, h, cs_], rhs=Csb[:, h, :], start=False, stop=True, skip_group_check=True),
                                 reads=[cx['kqb'], kCsb], writes=[kpO])
                        pO3 = psO[0:64, 0:260].rearrange("p (h e) -> p h e", e=65)
                        return lambda: stage2b(cx, jl, pO3, kpO)

                    def stage2b(cx, jl, pO3, kpO, d=d, G=G, GT=GT):
                        jg = cx['gi'] * G + jl
                        hb, khb = cx['hb']
                        s.op('act', lambda e: e.activation(out=den[:, jl % 2, :], in_=pO3[:, :, 64], func=AF.Abs), reads=[kpO], writes=[(kden, jl % 2)])
                        s.op('dve', lambda e: e.tensor_tensor(out=den[:, jl % 2, :], in0=den[:, jl % 2, :], in1=THRt[:, d * 4:d * 4 + 4, jg], op=ALU.max),
                             reads=[(kden, jl % 2), kTHRt], writes=[(kden, jl % 2)])
                        s.op('dve', lambda e: e.reciprocal(out=rec[:, jl % 2, :], in_=den[:, jl % 2, :]), reads=[(kden, jl % 2)], writes=[(krec, jl % 2)])
                        s.op('dve', lambda e: e.tensor_tensor(out=hb[:, jl, :].rearrange("p (h e) -> p h e", e=64), in0=pO3[:, :, 0:64],
                                                              in1=rec[:, jl % 2, :].unsqueeze(2).to_broadcast([64, 4, 64]), op=ALU.mult),
                             reads=[kpO, (krec, jl % 2)], writes=[(khb, jl)])

                    def finish_group(cx, d=d, G=G, GT=GT):
                        t0 = cx['t0']
                        hb, khb = cx['hb']
                        hk = [(khb, jl) for jl in range(G)]
                        if d == 0:
                            s.dma('pool', HF[t0:t0 + GT, :].rearrange("(j t) c -> t j c", t=64), hb[:, 0:G, :], reads=hk, writes=[('HF', t0)], sk=khb)
                            return
                        hF, khF, mo_, kmo = cx['hF'], cx['khF'], cx['mo'], cx['kmo']
                        s.op('pool', lambda e: e.tensor_tensor(out=hb[:, 0:G, :], in0=hb[:, 0:G, :], in1=hF[:, 0:G, :], op=ALU.add), reads=hk + [khF], writes=hk)
                        X4 = hb[:, 0:G, :].rearrange("p g (h e) -> p (g h) e", e=64)
                        n4 = G * 4
                        s.op('dve', lambda e: e.tensor_reduce(out=s1[:, 0:n4], in_=X4, axis=AX.X, op=ALU.add), reads=hk, writes=[ks1])
                        s.op('pool', lambda e: e.tensor_tensor(out=sqx[:, 0:G, :], in0=hb[:, 0:G, :], in1=hb[:, 0:G, :], op=ALU.mult), reads=hk, writes=[ksqx])
                        s.op('dve', lambda e: e.tensor_reduce(out=s2[:, 0:n4], in_=sqx[:, 0:G, :].rearrange("p g (h e) -> p (g h) e", e=64), axis=AX.X, op=ALU.add),
                             reads=[ksqx], writes=[ks2])
                        s.op('dve', lambda e: e.tensor_scalar_mul(out=s1[:, 0:n4], in0=s1[:, 0:n4], scalar1=1.0 / 64), reads=[ks1], writes=[ks1])
                        s.op('dve', lambda e: e.tensor_tensor(out=s3[:, 0:n4], in0=s1[:, 0:n4], in1=s1[:, 0:n4], op=ALU.mult), reads=[ks1], writes=[ks3])
                        s.op('dve', lambda e: e.scalar_tensor_tensor(out=s2[:, 0:n4], in0=s2[:, 0:n4], scalar=1.0 / 64, in1=s3[:, 0:n4], op0=ALU.mult, op1=ALU.subtract),
                             reads=[ks2, ks3], writes=[ks2])
                        s.op('act', lambda e: e.activation(out=s2[:, 0:n4], in_=s2[:, 0:n4], func=AF.Sqrt, bias=epsln[0:64, 1:2], scale=1.0), reads=[ks2, k_eps], writes=[ks2])
                        s.op('dve', lambda e: e.reciprocal(out=s2[:, 0:n4], in_=s2[:, 0:n4]), reads=[ks2], writes=[ks2])
                        s.op('dve', lambda e: e.tensor_tensor(out=X4, in0=X4, in1=s1[:, 0:n4].unsqueeze(2).to_broadcast([64, n4, 64]), op=ALU.subtract),
                             reads=hk + [ks1], writes=hk)
                        s.op('dve', lambda e: e.tensor_tensor(out=X4, in0=X4, in1=s2[:, 0:n4].unsqueeze(2).to_broadcast([64, n4, 64]), op=ALU.mult),
                             reads=hk + [ks2], writes=hk)
                        s.op('pool', lambda e: e.tensor_tensor(out=hb[:, 0:G, :], in0=hb[:, 0:G, :], in1=ng.unsqueeze(1).to_broadcast([64, G, 256]), op=ALU.mult),
                             reads=hk + [kng], writes=hk)
                        s.op('act', lambda e: e.activation(out=sig[:, 0:G, :], in_=mo_[:, 0:G, :], func=AF.Sigmoid), reads=[kmo], writes=[ksig])
                        xo, kxo = xo_t[0]
                        s.op('dve', lambda e: e.tensor_tensor(out=xo[:, 0:G, :], in0=hb[:, 0:G, :], in1=sig[:, 0:G, :], op=ALU.mult), reads=hk + [ksig], writes=[kxo])
                        mx, kmx = mixo_t[0]
                        ps, kp = bank(7, 8)
                        pbf = ps.bitcast(BF16)
                        for c in range(2):
                            for jl in range(G):
                                s.op('pe', lambda e, c=c, jl=jl: e.transpose(
                                    out=pbf[:, c * 512 + jl * 64:c * 512 + (jl + 1) * 64], in_=xo[:, jl, c * 128:(c + 1) * 128], identity=idb[0:64, 0:64]),
                                    reads=[kxo, k_idb], writes=[kp])
                        for c in range(2):
                            s.op('dve', lambda e, c=c: e.tensor_copy(out=mx[:, c, 0:GT], in_=pbf[:, c * 512:c * 512 + GT]), reads=[kp], writes=[(kmx, c)])
                        s.dma('pool', MIXT.rearrange("(c p) t -> p c t", p=128)[:, 4:6, t0:t0 + GT], mx[:, :, 0:GT],
                              reads=[(kmx, 0), (kmx, 1)], writes=[('MIXT_M', t0)], sk=kmx)

                    cur = load_group(gorder[0])
                    for jl in corder:
                        stage1(cur, jl)
                    for gidx, gi in enumerate(gorder):
                        nxt = load_group(gorder[gidx + 1]) if gidx + 1 < len(gorder) else None
                        pend = None
                        for jl in corder:
                            p2 = stage2(cur, jl)
                            if pend is not None:
                                pend()
                            pend = p2
                            if nxt is not None:
                                stage1(nxt, jl)
                        pend()
                        finish_group(cur)
                        cur = nxt
                    if not samp:
                        s.dma('pool', C_out[j, l, d].rearrange("h a b -> a h b"), Caug[:, :, 0:64], reads=[kC], sk=(kC, 0))
                        s.dma('pool', n_out[j, l, d].rearrange("h a -> a h"), Caug[:, :, 64], reads=[kC], sk=(kC, 1), allow_slow_non_contiguous=True)
            s.barrier()
            mem.release(m0_)

        def phase_hgrn(l):
            m0_ = mem.mark()
            A = lambda shape, dt=F32, name='g': mem.alloc(shape, dt, name)
            hc, khc = A([32, 2, 32]); ng, kng = A([32, 256]); lbl, klbl = A([64, 4, 4]); lbp, klbp = A([64, 4, 4])
            lbs, klbs = A([64, 4]); lbv, klbv = A([64, 3, 4])
            s.dma('sp', hc, hconst_d, writes=[khc], sk=khc)
            s.dma('sp', ng, hgng_d[l], writes=[kng], sk=kng)
            s.dma('sp', lbl, lbl_d, writes=[klbl], sk=klbl)
            s.op('act', lambda e: e.activation(out=lbp, in_=lbl, func=AF.Exp), reads=[klbl], writes=[klbp])
            s.op('dve', lambda e: e.tensor_reduce(out=lbs, in_=lbp, axis=AX.X, op=ALU.add), reads=[klbp], writes=[klbs])
            s.op('dve', lambda e: e.reciprocal(out=lbs, in_=lbs), reads=[klbs], writes=[klbs])
            s.op('dve', lambda e: e.tensor_tensor(out=lbp, in0=lbp, in1=lbs.unsqueeze(2).to_broadcast([64, 4, 4]), op=ALU.mult), reads=[klbp, klbs], writes=[klbp])
            if l == 0:
                s.op('dve', lambda e: e.memset(lbv[:, 0, :], 0.0), writes=[klbv])
            else:
                s.op('dve', lambda e: e.tensor_reduce(out=lbv[:, 0, :], in_=lbp[:, :, 1:l + 1], axis=AX.X, op=ALU.add), reads=[klbp], writes=[klbv])
            s.op('dve', lambda e: e.tensor_scalar(out=lbv[:, 1, :], in0=lbv[:, 0, :], scalar1=-1.0, scalar2=1.0, op0=ALU.mult, op1=ALU.add), reads=[klbv], writes=[klbv])
            s.op('dve', lambda e: e.tensor_scalar_mul(out=lbv[:, 2, :], in0=lbv[:, 1, :], scalar1=-1.0), reads=[klbv], writes=[klbv])
            rmask, krm = A([64, 512])
            s.op('pool', lambda e: e.memset(rmask, 1.0), writes=[krm])
            s.op('pool', lambda e: e.memset(rmask.rearrange("p (j t) -> p j t", t=32)[:, :, 0:1], 0.0), writes=[krm])
            GTM = 256
            gq32, kgq = A([64, 4, GTM]); gf32, kgf = A([64, 4, GTM]); qf, kqf = gq32, kgq
            sg, ksg = A([64, 4, GTM]); lg, klg = A([64, 4, GTM]); Bc, kBc = A([64, 4, GTM]); kf, kkf = A([64, 4, GTM])
            tot, ktot = A([64, 4, 8])
            egl_t = [A([64, 4, 8]) for _ in range(2)]
            qs_t = [A([96, 4, GTM], BF16) for _ in range(2)]
            RS_t = [A([96, 8, 4, 64], BF16) for _ in range(2)]
            v96, kv96 = A([96, 8, 256])
            hc96, khc96 = A([96, 2, 32])
            s.dma('sp', hc96[64:96], hconst_d, writes=[khc96], sk=khc96)
            ks_t = [A([64, 4, GTM], BF16) for _ in range(2)]
            v32, kv32 = A([32, 8, 256]); vb_t = [A([32, 8, 256], BF16) for _ in range(2)]
            PTg_t = [A([32, 8, 4, 32], BF16) for _ in range(2)]
            dS_t = [A([64, 8, 4, 64]) for _ in range(2)]
            oF, koF = A([32, 8, 256]); gg32, kgg = A([32, 8, 256]); ob_t = [A([32, 8, 256]) for _ in range(2)]
            S, kS = A([64, 4, 64]); Sb, kSb = A([64, 4, 64], BF16); St, kSt = A([64, 4, 64])
            kst_t = [A([32, 256], BF16) for _ in range(2)]
            s2, ks2 = A([32, 32]); sqx, ksqx = A([32, 8, 256]); xo, kxo = A([32, 8, 256], BF16)
            mx, kmx = A([128, 2, GTM], BF16)
            gcount = [0]
            for sq_ in SEQS:
                T, S0, samp, j = sq_['T'], sq_['t0'], sq_['sample'], sq_['j']
                GT = min(GTM, T)
                G = GT // 32
                ngrp = T // GT
                for d in range(2):
                    sview = lambda a: a.rearrange("h c e -> c h e")
                    if samp:
                        s.dma('sp', S, sview(stS_d[l, d]), writes=[kS], sk=kS)
                    else:
                        s.op('dve', lambda e: e.memset(S, 0.0), writes=[kS])
                    mask = hc[:, d, :]
                    gorder = list(range(ngrp)) if d == 0 else list(range(ngrp - 1, -1, -1))
                    corder = list(range(G)) if d == 0 else list(range(G - 1, -1, -1))

                    def load_group(gi, d=d, G=G, GT=GT, S0=S0):
                        t0 = S0 + gi * GT
                        ti = t0 // 512
                        b = gcount[0] % 2
                        gcount[0] += 1
                        LS, kqs = qs_t[b]; qs = LS[0:64]; ks, kks = ks_t[b]; vb, kvb = vb_t[b]; egl, kegl = egl_t[b]
                        RS, kRS = RS_t[b]
                        cx = dict(gi=gi, t0=t0, ti=ti, b=b, qs=qs, kqs=kqs, ks=ks, kks=kks, vb=vb, kvb=kvb, egl=egl, kegl=kegl,
                                  PT=PTg_t[b], dS=dS_t[b], ob=ob_t[b], LS=LS, RS=RS, kRS=kRS)
                        s.dma('sp', v96[64:96, 0:G, :], TMV[t0:t0 + GT, 768:1024].rearrange("(j t) c -> t j c", t=32), reads=[('TMV', ti)], writes=[kv96], sk=kv96)
                        s.op('act', lambda e: e.copy(out=RS[64:96, 0:G], in_=v96[64:96, 0:G, :].rearrange("p g (h e) -> p g h e", e=64)), reads=[kv96], writes=[(kRS, 'v')])
                        fsrc = 'GFF' if d == 0 else 'GFB'
                        s.dma('sp', gq32[:, :, 0:GT], SCR['GQT'].rearrange("(c p) t -> p c t", p=64)[:, :, t0:t0 + GT],
                              reads=[('GQT', 0, ti), ('GQT', 128, ti)], writes=[kgq], sk=kgq)
                        s.dma('sp', gf32[:, :, 0:GT], SCR[fsrc].rearrange("(c p) t -> p c t", p=64)[:, :, t0:t0 + GT],
                              reads=[(fsrc, 0, ti), (fsrc, 128, ti)], writes=[kgf], sk=kgf)
                        s.dma('sp', v32[:, 0:G, :], TMV[t0:t0 + GT, 768:1024].rearrange("(j t) c -> t j c", t=32), reads=[('TMV', ti)], writes=[kv32], sk=kv32)
                        s.op('act', lambda e: e.copy(out=vb[:, 0:G, :], in_=v32[:, 0:G, :]), reads=[kv32], writes=[kvb])
                        W_ = slice(0, GT)
                        klgs = [(klg, i_) for i_ in range(4)]
                        kkfs = [(kkf, i_) for i_ in range(4)]
                        s.op('act', lambda e: e.activation(out=qf[:, :, W_], in_=gq32[:, :, W_], func=AF.Silu), reads=[kgq], writes=[kgq])
                        s.op('act', lambda e: e.activation(out=sg[:, :, W_], in_=gf32[:, :, W_], func=AF.Sigmoid), reads=[kgf], writes=[ksg])
                        for cc in range(4):
                            s.op('dve', lambda e, cc=cc: e.tensor_scalar(out=lg[:, cc, W_], in0=sg[:, cc, W_], scalar1=lbv[:, 1, cc:cc + 1], scalar2=lbv[:, 0, cc:cc + 1],
                                                                        op0=ALU.mult, op1=ALU.add), reads=[ksg, klbv], writes=[(klg, cc)])
                            s.op('pool', lambda e, cc=cc: e.tensor_scalar(out=kf[:, cc, W_], in0=sg[:, cc, W_], scalar1=lbv[:, 2, cc:cc + 1], scalar2=lbv[:, 1, cc:cc + 1],
                                                                         op0=ALU.mult, op1=ALU.add), reads=[ksg, klbv], writes=[(kkf, cc)])
                        s.op('act', lambda e: e.activation(out=lg[:, :, W_], in_=lg[:, :, W_], func=AF.Ln), reads=klgs, writes=klgs)
                        for cc in range(4):
                            s.op('dve', lambda e, cc=cc: e.tensor_tensor_scan(out=Bc[:, cc, W_], data0=rmask[:, W_], data1=lg[:, cc, W_], initial=0.0,
                                                                             op0=ALU.mult, op1=ALU.add), reads=[krm, (klg, cc)], writes=[(kBc, cc)])
                        Bc4 = Bc[:, :, W_].rearrange("p c (j t) -> p c j t", t=32)
                        lg4 = lg[:, :, W_].rearrange("p c (j t) -> p c j t", t=32)
                        kBcs = [(kBc, i_) for i_ in range(4)]
                        s.op('dve', lambda e: e.tensor_copy(out=tot[:, :, 0:G], in_=Bc4[:, :, :, 31]), reads=kBcs, writes=[ktot])
                        if d == 1:
                            s.op('dve', lambda e: e.tensor_tensor(out=Bc4, in0=lg4, in1=Bc4, op=ALU.subtract), reads=kBcs + klgs, writes=kBcs)
                            s.op('dve', lambda e: e.tensor_tensor(out=Bc4, in0=Bc4, in1=tot[:, :, 0:G].unsqueeze(3).to_broadcast([64, 4, G, 32]), op=ALU.add),
                                 reads=kBcs + [ktot], writes=kBcs)
                        s.op('act', lambda e: e.activation(out=egl[:, :, 0:G], in_=tot[:, :, 0:G], func=AF.Exp), reads=[ktot], writes=[kegl])
                        s.op('act', lambda e: e.activation(out=sg[:, :, W_], in_=Bc[:, :, W_], func=AF.Exp), reads=kBcs + [ksg] + kkfs + klgs, writes=[ksg])
                        s.op('dve', lambda e: e.tensor_tensor(out=qs[:, :, W_], in0=qf[:, :, W_], in1=sg[:, :, W_], op=ALU.mult), reads=[kgq, ksg], writes=[kqs])
                        s.op('dve', lambda e: e.tensor_scalar_max(out=Bc[:, :, W_], in0=Bc[:, :, W_], scalar1=-80.0), reads=kBcs + [ksg], writes=kBcs)
                        s.op('act', lambda e: e.activation(out=lg[:, :, W_], in_=Bc[:, :, W_], func=AF.Exp, scale=-1.0), reads=kBcs + klgs, writes=klgs)
                        s.op('dve', lambda e: e.tensor_tensor(out=ks[:, :, W_], in0=kf[:, :, W_], in1=lg[:, :, W_], op=ALU.mult), reads=kkfs + klgs, writes=[kks])
                        return cx

                    def stage1(cx, jl, d=d, mask=mask, G=G, GT=GT):
                        cs_ = slice(jl * 32, (jl + 1) * 32)
                        qs, ks, vb, egl = cx['qs'], cx['ks'], cx['vb'], cx['egl']
                        (PT, kPT), (dS, kdS) = cx['PT'], cx['dS']
                        psA, kpA = bank(0, 2)
                        for h in range(4):
                            s.op('pe', lambda e, h=h: e.matmul(psA[64:96, h * 32:(h + 1) * 32], lhsT=ks[:, h, cs_], rhs=qs[:, h, cs_], start=True, stop=True),
                                 reads=[cx['kks'], cx['kqs']], writes=[kpA])
                        psT, kpT = bank(2, 4)
                        pbf = psT.bitcast(BF16)
                        for cc in range(4):
                            s.op('pe', lambda e, cc=cc: e.transpose(out=pbf[0:32, cc * 64:(cc + 1) * 64], in_=ks[:, cc, cs_], identity=idb[0:64, 0:64]),
                                 reads=[cx['kks'], k_idb], writes=[kpT])
                        kst, kkst = kst_t[jl % 2]
                        LS = cx['LS']
                        s.op('dve', lambda e: e.tensor_tensor(out=LS[64:96, :, cs_], in0=psA[64:96, 0:128].rearrange("p (h t) -> p h t", t=32),
                                                              in1=hc96[64:96, d, :].unsqueeze(1).to_broadcast([32, 4, 32]), op=ALU.mult),
                             reads=[kpA, khc96], writes=[(kPT, jl)])
                        s.op('act', lambda e: e.copy(out=kst, in_=pbf[0:32, 0:256]), reads=[kpT], writes=[kkst])
                        psS, kpS = bank(4, 6)
                        for h in range(4):
                            s.op('pe', lambda e, h=h: e.matmul(psS[0:64, h * 64:(h + 1) * 64], lhsT=kst[:, h * 64:(h + 1) * 64], rhs=vb[:, jl, h * 64:(h + 1) * 64], start=True, stop=True),
                                 reads=[kkst, cx['kvb']], writes=[kpS])
                        s.op('dve', lambda e: e.tensor_tensor(out=dS[:, jl], in0=psS[0:64, 0:256].rearrange("p (c e) -> p c e", e=64),
                                                              in1=egl[:, :, jl].unsqueeze(2).to_broadcast([64, 4, 64]), op=ALU.mult),
                             reads=[kpS, cx['kegl']], writes=[(kdS, jl)])

                    def stage2(cx, jl, tgt, d=d, G=G, GT=GT):
                        cs_ = slice(jl * 32, (jl + 1) * 32)
                        egl, LS, RS, kRS = cx['egl'], cx['LS'], cx['RS'], cx['kRS']
                        (PT, kPT), (dS, kdS), (ob, kob) = cx['PT'], cx['dS'], cx['ob']
                        psO, kpO = bank(6, 8)
                        for h in range(4):
                            s.op('pe', lambda e, h=h: e.matmul(psO[0:32, h * 64:(h + 1) * 64], lhsT=LS[0:96, h, cs_], rhs=RS[0:96, jl, h, :], start=True, stop=True),
                                 reads=[(kPT, jl), cx['kqs'], (kRS, 'v'), (kRS, 'S', jl)], writes=[kpO])
                        s.op('dve', lambda e: e.tensor_tensor(out=St, in0=S, in1=egl[:, :, jl].unsqueeze(2).to_broadcast([64, 4, 64]), op=ALU.mult),
                             reads=[kS, cx['kegl']], writes=[kSt])
                        s.op('dve', lambda e: e.tensor_tensor(out=S, in0=St, in1=dS[:, jl], op=ALU.add), reads=[kSt, (kdS, jl)], writes=[kS])
                        if tgt is not None:
                            tcx, tjl = tgt
                            tRS, tkRS = tcx['RS'], tcx['kRS']
                            s.op('act', lambda e: e.copy(out=tRS[0:64, tjl], in_=S), reads=[kS], writes=[(tkRS, 'S', tjl)])
                        return lambda: s.op('act', lambda e: e.copy(out=ob[:, jl, :], in_=psO[0:32, 0:256]), reads=[kpO], writes=[(kob, jl)])

                    def finish_group(cx, d=d, G=G, GT=GT):
                        t0, ti = cx['t0'], cx['ti']
                        ob, kob = cx['ob']
                        ok_ = [(kob, jl) for jl in range(G)]
                        if d == 0:
                            s.dma('pool', HF[t0:t0 + GT, :].rearrange("(j t) c -> t j c", t=32), ob[:, 0:G, :], reads=ok_, writes=[('HF', t0)], sk=kob)
                            return
                        s.dma('sp', oF[:, 0:G, :], HF[t0:t0 + GT, :].rearrange("(j t) c -> t j c", t=32), reads=[('HF', t0)], writes=[koF], sk=koF)
                        s.dma('sp', gg32[:, 0:G, :], TMV[t0:t0 + GT, 1024:1280].rearrange("(j t) c -> t j c", t=32), reads=[('TMV', ti)], writes=[kgg], sk=kgg)
                        s.op('pool', lambda e: e.tensor_tensor(out=ob[:, 0:G, :], in0=ob[:, 0:G, :], in1=oF[:, 0:G, :], op=ALU.add), reads=ok_ + [koF], writes=ok_)
                        n4 = G * 4
                        s.op('pool', lambda e: e.tensor_tensor(out=sqx[:, 0:G, :], in0=ob[:, 0:G, :], in1=ob[:, 0:G, :], op=ALU.mult), reads=ok_, writes=[ksqx])
                        s.op('dve', lambda e: e.tensor_reduce(out=s2[:, 0:n4], in_=sqx[:, 0:G, :].rearrange("p g (h e) -> p (g h) e", e=64), axis=AX.X, op=ALU.add),
                             reads=[ksqx], writes=[ks2])
                        s.op('act', lambda e: e.activation(out=s2[:, 0:n4], in_=s2[:, 0:n4], func=AF.Sqrt, bias=epsln[0:32, 1:2], scale=1.0 / 64), reads=[ks2, k_eps], writes=[ks2])
                        s.op('dve', lambda e: e.reciprocal(out=s2[:, 0:n4], in_=s2[:, 0:n4]), reads=[ks2], writes=[ks2])
                        X4 = ob[:, 0:G, :].rearrange("p g (h e) -> p (g h) e", e=64)
                        s.op('dve', lambda e: e.tensor_tensor(out=X4, in0=X4, in1=s2[:, 0:n4].unsqueeze(2).to_broadcast([32, n4, 64]), op=ALU.mult),
                             reads=ok_ + [ks2], writes=ok_)
                        s.op('pool', lambda e: e.tensor_tensor(out=ob[:, 0:G, :], in0=ob[:, 0:G, :], in1=ng.unsqueeze(1).to_broadcast([32, G, 256]), op=ALU.mult),
                             reads=ok_ + [kng], writes=ok_)
                        s.op('act', lambda e: e.activation(out=sqx[:, 0:G, :], in_=gg32[:, 0:G, :], func=AF.Silu), reads=[kgg, ksqx, ks2], writes=[ksqx])
                        s.op('dve', lambda e: e.tensor_tensor(out=xo[:, 0:G, :], in0=ob[:, 0:G, :], in1=sqx[:, 0:G, :], op=ALU.mult), reads=ok_ + [ksqx], writes=[kxo])
                        ps, kp = bank(0, 2)
                        pbf = ps.bitcast(BF16)
                        for c in range(2):
                            for jl in range(G):
                                s.op('pe', lambda e, c=c, jl=jl: e.transpose(
                                    out=pbf[:, c * 512 + jl * 32:c * 512 + (jl + 1) * 32], in_=xo[:, jl, c * 128:(c + 1) * 128], identity=idb[0:32, 0:32]),
                                    reads=[kxo, k_idb], writes=[kp])
                        for c in range(2):
                            s.op('dve', lambda e, c=c: e.tensor_copy(out=mx[:, c, 0:GT], in_=pbf[:, c * 512:c * 512 + GT]), reads=[kp], writes=[(kmx, c)])
                        s.dma('pool', MIXT.rearrange("(c p) t -> p c t", p=128)[:, 6:8, t0:t0 + GT], mx[:, :, 0:GT],
                              reads=[(kmx, 0), (kmx, 1)], writes=[('MIXT_G', t0)], sk=kmx)

                    cur = load_group(gorder[0])
                    for jl in corder:
                        stage1(cur, jl)
                    s.op('act', lambda e, cur=cur, j0=corder[0]: e.copy(out=cur['RS'][0:64, j0], in_=S), reads=[kS], writes=[(cur['kRS'], 'S', corder[0])])
                    for gidx, gi in enumerate(gorder):
                        nxt = load_group(gorder[gidx + 1]) if gidx + 1 < len(gorder) else None
                        pend = None
                        for ci, jl in enumerate(corder):
                            tgt = (cur, corder[ci + 1]) if ci + 1 < len(corder) else ((nxt, corder[0]) if nxt is not None else None)
                            p2 = stage2(cur, jl, tgt)
                            if pend is not None:
                                pend()
                            pend = p2
                            if nxt is not None and not HG_NOINT:
                                stage1(nxt, jl)
                        pend()
                        if nxt is not None and HG_NOINT:
                            for jl in corder:
                                stage1(nxt, jl)
                        finish_group(cur)
                        cur = nxt
                    if not samp:
                        s.dma('pool', sview(S_out[j, l, d]), S, reads=[kS], sk=(kS, 'o'))
            s.barrier()
            mem.release(m0_)

        phase_init()
        for l in range(NL):
            phase_mod(l)
            if 'p1' in PH:
                phase_p1(l)
            if 'mla' in PH:
                phase_mla(l)
            if 'mlstm' in PH:
                phase_mlstm(l)
            if 'hgrn' in PH:
                phase_hgrn(l)
            if 'dense' in PH:
                phase_p3(l)
                phase_p3b(l)
                phase_p4(l)
        phase_final()
        s.emit()
    return nc


def _bf(x):
    return np.ascontiguousarray(x, dtype=np.float32)


def make_in_maps(inputs, cfg):
    NL = cfg.get('n_layers', DEPTH)
    NST = cfg.get('n_sample_tiles', 8)
    NPR = cfg.get('n_prompts', 4)
    cores = cfg.get('cores', list(range(8)))
    g = {k: np.asarray(v) for k, v in inputs.items()}
    w1 = _bf(g['w_in'][:NL][:, :, W1_COLS])
    b1 = g['b_in'][:NL][:, W1_COLS]
    b1fm = np.zeros((NL, 128, NG), np.float32)
    for gi, (c0, M, _) in enumerate(FM_GROUPS):
        b1fm[:, :M, gi] = b1[:, c0:c0 + M]
    b1tm = _bf(np.broadcast_to(b1[:, None, NFM:], (NL, 128, NTM)))
    b_modT = _bf(g['b_mod'][:NL].reshape(NL, 48, 128).transpose(0, 2, 1))
    lnp = _bf(np.stack([g[k][:NL].reshape(NL, NKC, 128).transpose(0, 2, 1) for k in ('ln1_g', 'ln1_b', 'ln2_g', 'ln2_b')], axis=2))
    shared = dict(w_mod=_bf(g['w_mod'][:NL]), b_modT=b_modT, w1=w1, b1fm=b1fm, b1tm=b1tm, w_out=_bf(g['w_out'][:NL]),
                  lnp=lnp, w_ffn_in=_bf(g['w_ffn_in'][:NL]), w_ffn_out=_bf(g['w_ffn_out'][:NL]),
                  identf=np.eye(128, dtype=np.float32))
    wuq = g['w_uq'][:NL]
    wqp = np.zeros_like(wuq)
    for h in range(8):
        wqp[:, :, h * 96 + 64:h * 96 + 96] = wuq[:, :, h * 96 + 64 + _PERM]
    wukv = g['w_ukv'][:NL]
    wk = np.zeros((NL, 128, 768), np.float32)
    wv = np.zeros((NL, 128, 512), np.float32)
    for h in range(8):
        wk[:, :, h * 96:h * 96 + 64] = wukv[:, :, h * 128:h * 128 + 64]
        wv[:, :, h * 64:(h + 1) * 64] = wukv[:, :, h * 128 + 64:h * 128 + 128]
    e96 = np.zeros((32, 96), np.float32)
    e96[np.arange(32), 64 + np.arange(32)] = 1.0
    qkn = _bf(np.stack([g['mla_q_norm'][:NL, :128], g['mla_q_norm'][:NL, 128:], g['mla_kv_norm'][:NL]], axis=-1))
    pos = np.arange(4096)
    freqs = 10000.0 ** (-np.arange(8, dtype=np.float64) * 0.125)
    ar = (pos // 64)[:, None] * freqs
    ac = (pos % 64)[:, None] * freqs
    ang = np.concatenate([ar, ar, ac, ac], -1)
    cosT = np.cos(ang).T.astype(np.float32)
    sinT = (np.sin(ang) * _ROT_SIGN).T.astype(np.float32)
    cos96 = np.ones((96, 4096), np.float32)
    sin96 = np.zeros((96, 4096), np.float32)
    cos96[64:] = cosT
    sin96[64:] = sinT
    shared.update(wq=_bf(wuq), wqp=_bf(wqp), wk=wk, wv=wv, e96=e96, qkn=qkn, cos96=cos96, sin96=sin96,
                  kcos=_bf(cosT), ksin=_bf(sinT))
    mconst = np.zeros((64, 4, 64), np.float32)
    ii = np.arange(64)
    mconst[:, 0, :] = (ii[None, :] >= ii[:, None])
    mconst[:, 1, :] = (ii[:, None] >= ii[None, :])
    mconst[:, 2, :] = np.eye(64)[::-1]
    mconst[0:4, 3, 0:4] = np.eye(4)[::-1]
    shared.update(mconst=mconst, mlng=_bf(np.broadcast_to(g['mlstm_norm'][:NL, None, :], (NL, 64, 256))))
    hconst = np.zeros((32, 2, 32), np.float32)
    i32 = np.arange(32)
    hconst[:, 0, :] = (i32[None, :] >= i32[:, None])
    hconst[:, 1, :] = (i32[:, None] >= i32[None, :])
    lbl = _bf(g['hgrn_lb_logits'].reshape(4, 4, 64).transpose(2, 1, 0))
    shared.update(hconst=hconst, lbl=lbl, hgng=_bf(np.broadcast_to(g['hgrn_norm'][:NL, None, :], (NL, 32, 256))))
    maps = []
    for ci in cores:
        b = ci // 2
        parts = []
        if NST:
            parts.append(g['x_sample'][b, :NST * 512])
        for j in range(NPR):
            parts.append(g['x_prompt'][4 * ci + j])
        xin = _bf(np.concatenate(parts, axis=0))
        cv = np.stack([g['c'][b], g['c_ctx']], axis=-1)
        cvT = _bf(cv.reshape(NKC, 128, 2).transpose(1, 0, 2))
        m = dict(shared)
        m.update(stC=_bf(g['state_mlstm_C'][b, :NL]), stn=_bf(g['state_mlstm_n'][b, :NL]), stm=_bf(g['state_mlstm_m'][b, :NL]))
        m.update(stS=_bf(g['state_hgrn_S'][b, :NL]))
        m.update(xin=xin, cvT=cvT, cckv=_bf(g['cache_mla_ckv'][b, :NL]), ckpe=_bf(g['cache_mla_kpe'][b, :NL]))
        maps.append(m)
    return maps


def kernel(**inputs):
    cfg = {}
    nc = build_program(cfg)
    maps = make_in_maps(inputs, cfg)
    res = run_bass_kernel_spmd(nc, maps, core_ids=list(range(8)))
    r = res.results
    cat = lambda k: np.ascontiguousarray(np.concatenate([r[i][k] for i in range(8)], axis=0), dtype=np.float32)
    y_prompt = np.concatenate([r[i]['y_out'][4096:].reshape(4, 256, D) for i in range(8)], axis=0).astype(np.float32)
    y_sample = np.stack([r[2 * b]['y_out'][:4096] for b in range(4)], axis=0).astype(np.float32)
    return (y_prompt, y_sample, cat('ckv_out'), cat('kpe_out'), cat('C_out'), cat('n_out'), cat('m_out'), cat('S_out'))
```
